# Optimizing a Trainium2 kernel written in Bass

```python
import jax, jax.numpy as jnp
from jax import lax
import numpy as np

D_MODEL = 1024
BATCH = 16
SEQ = 2048
DEPTH = 2

N_EVEN = (DEPTH + 1) // 2
N_ODD = DEPTH // 2
D_A = D_MODEL // 2
CONV_A_WIDTH = 31
D_B = D_MODEL // 2
CONV_B_WIDTH = 3
CONV_IN = 2 * D_A + 3 * D_B
GLA_HEADS = 4
GLA_DK = D_MODEL // 2 // GLA_HEADS
GLA_DV = D_MODEL // GLA_HEADS
GATE_RANK = 16
GATE_TAU = 16.0
CHUNK = 64
GLA_IN = 2 * GLA_HEADS * GLA_DK + 2 * GLA_HEADS * GLA_DV + 2 * GATE_RANK
D_FF = ((8 * D_MODEL // 3 + 255) // 256) * 256
EPS = 1e-6

kernel_name = "hybrid_conv_gla_encoder"


def rms_norm(x, g):
    xf = x.astype(jnp.float32)
    y = xf * lax.rsqrt(jnp.mean(xf * xf, axis=-1, keepdims=True) + EPS)
    return (y * g.astype(jnp.float32)).astype(x.dtype)


def layer_norm(x, g, b):
    xf = x.astype(jnp.float32)
    mu = jnp.mean(xf, axis=-1, keepdims=True)
    xc = xf - mu
    y = xc * lax.rsqrt(jnp.mean(xc * xc, axis=-1, keepdims=True) + EPS)
    return (y * g.astype(jnp.float32) + b.astype(jnp.float32)).astype(x.dtype)


def depthwise_conv(x, w):
    k = w.shape[0]
    pad = (k - 1) // 2
    return lax.conv_general_dilated(
        x, w[:, None, :].astype(x.dtype), window_strides=(1,), padding=[(pad, pad)],
        dimension_numbers=("NWC", "WIO", "NWC"), feature_group_count=x.shape[-1])


def conv_hybrid_mixer(h, w_in, dw_w, dw_b, ln_g, ln_b, sc_w, w_out):
    u = h @ w_in
    a_val, a_gate, b_gate, c_gate, v = jnp.split(
        u, [D_A, 2 * D_A, 2 * D_A + D_B, 2 * D_A + 2 * D_B], axis=-1)
    a = a_val * jax.nn.sigmoid(a_gate)
    a = depthwise_conv(a, dw_w) + dw_b
    a = jax.nn.silu(layer_norm(a, ln_g, ln_b))
    bb = b_gate * depthwise_conv(c_gate * v, sc_w)
    return jnp.concatenate([a, bb], axis=-1) @ w_out


def gla_one_direction(q, k, v, log_a):
    b_, s_, h_, dk = q.shape
    dv = v.shape[-1]
    n = s_ // CHUNK
    qf = q.astype(jnp.float32).reshape(b_, n, CHUNK, h_, dk) * (dk ** -0.5)
    kf = k.astype(jnp.float32).reshape(b_, n, CHUNK, h_, dk)
    vf = v.astype(jnp.float32).reshape(b_, n, CHUNK, h_, dv)
    cum = jnp.cumsum(log_a.reshape(b_, n, CHUNK, h_, dk), axis=2)
    cum_last = cum[:, :, -1:]
    q_t = qf * jnp.exp(cum)
    k_t = kf * jnp.exp(-cum)
    k_end = kf * jnp.exp(cum_last - cum)
    scores = jnp.einsum("bnihk,bnjhk->bnhij", q_t, k_t)
    mask = jnp.tril(jnp.ones((CHUNK, CHUNK), dtype=bool))
    scores = jnp.where(mask, scores, 0.0)
    o_intra = jnp.einsum("bnhij,bnjhv->bnihv", scores, vf)
    kv = jnp.einsum("bnjhk,bnjhv->nbhkv", k_end, vf)
    decay = jnp.transpose(jnp.exp(cum_last[:, :, 0]), (1, 0, 2, 3))

    def step(state, inp):
        d, kv_n = inp
        return state * d[..., None] + kv_n, state

    _, states = lax.scan(step, jnp.zeros((b_, h_, dk, dv), jnp.float32), (decay, kv))
    o_inter = jnp.einsum("bnihk,nbhkv->bnihv", q_t, states)
    return (o_intra + o_inter).reshape(b_, s_, h_, dv)


def gla_mixer(h, w_in, wa2_f, ba2_f, wa2_b, ba2_b, gn_g, w_out):
    b_, s_, _ = h.shape
    u = h @ w_in
    qd = GLA_HEADS * GLA_DK
    vd = GLA_HEADS * GLA_DV
    q, k, v, r, g_f, g_b = jnp.split(
        u, [qd, 2 * qd, 2 * qd + vd, 2 * qd + 2 * vd, 2 * qd + 2 * vd + GATE_RANK], axis=-1)
    q = q.reshape(b_, s_, GLA_HEADS, GLA_DK)
    k = k.reshape(b_, s_, GLA_HEADS, GLA_DK)
    v = v.reshape(b_, s_, GLA_HEADS, GLA_DV)
    la_f = (jax.nn.log_sigmoid((g_f @ wa2_f + ba2_f).astype(jnp.float32)) / GATE_TAU)
    la_b = (jax.nn.log_sigmoid((g_b @ wa2_b + ba2_b).astype(jnp.float32)) / GATE_TAU)
    la_f = la_f.reshape(b_, s_, GLA_HEADS, GLA_DK)
    la_b = la_b.reshape(b_, s_, GLA_HEADS, GLA_DK)
    o_fwd = gla_one_direction(q, k, v, la_f)
    o_bwd = jnp.flip(gla_one_direction(jnp.flip(q, 1), jnp.flip(k, 1), jnp.flip(v, 1),
                                       jnp.flip(la_b, 1)), 1)
    o = o_fwd + o_bwd
    o = o * lax.rsqrt(jnp.mean(o * o, axis=-1, keepdims=True) + EPS)
    o = o * gn_g.astype(jnp.float32).reshape(GLA_HEADS, GLA_DV)
    o = o.reshape(b_, s_, GLA_HEADS * GLA_DV) * jax.nn.silu(r.astype(jnp.float32))
    return o.astype(h.dtype) @ w_out


def swiglu_ffn(h, w_gu, w_down):
    gu = h @ w_gu
    g, u = jnp.split(gu, 2, axis=-1)
    return (jax.nn.silu(g) * u) @ w_down


def setup_inputs(seed: int = 0) -> dict:
    key = jax.random.key(seed)
    ks = jax.random.split(key, 24)
    f32 = jnp.float32

    def nrm(k, shape, scale):
        return jax.random.normal(k, shape, f32) * scale

    def gain(k, shape):
        return 1.0 + 0.02 * jax.random.normal(k, shape, f32)

    return {
        "x": jax.random.normal(ks[0], (BATCH, SEQ, D_MODEL), f32),
        "mix_pre_g": gain(ks[1], (DEPTH, D_MODEL)),
        "mix_post_g": gain(ks[2], (DEPTH, D_MODEL)),
        "ffn_pre_g": gain(ks[3], (DEPTH, D_MODEL)),
        "ffn_post_g": gain(ks[4], (DEPTH, D_MODEL)),
        "cv_w_in": nrm(ks[5], (N_EVEN, D_MODEL, CONV_IN), D_MODEL ** -0.5),
        "cv_dw_w": nrm(ks[6], (N_EVEN, CONV_A_WIDTH, D_A), CONV_A_WIDTH ** -0.5),
        "cv_dw_b": nrm(ks[7], (N_EVEN, D_A), 0.02),
        "cv_ln_g": gain(ks[8], (N_EVEN, D_A)),
        "cv_ln_b": nrm(ks[9], (N_EVEN, D_A), 0.02),
        "cv_sc_w": nrm(ks[10], (N_EVEN, CONV_B_WIDTH, D_B), CONV_B_WIDTH ** -0.5),
        "cv_w_out": nrm(ks[11], (N_EVEN, D_A + D_B, D_MODEL), (D_A + D_B) ** -0.5),
        "gla_w_in": nrm(ks[12], (N_ODD, D_MODEL, GLA_IN), D_MODEL ** -0.5),
        "gla_wa2_f": nrm(ks[13], (N_ODD, GATE_RANK, GLA_HEADS * GLA_DK), GATE_RANK ** -0.5),
        "gla_ba2_f": nrm(ks[14], (N_ODD, GLA_HEADS * GLA_DK), 0.1),
        "gla_wa2_b": nrm(ks[15], (N_ODD, GATE_RANK, GLA_HEADS * GLA_DK), GATE_RANK ** -0.5),
        "gla_ba2_b": nrm(ks[16], (N_ODD, GLA_HEADS * GLA_DK), 0.1),
        "gla_gn_g": gain(ks[17], (N_ODD, GLA_HEADS * GLA_DV)),
        "gla_w_out": nrm(ks[18], (N_ODD, GLA_HEADS * GLA_DV, D_MODEL), (GLA_HEADS * GLA_DV) ** -0.5),
        "ffn_w_gu": nrm(ks[19], (DEPTH, D_MODEL, 2 * D_FF), D_MODEL ** -0.5),
        "ffn_w_down": nrm(ks[20], (DEPTH, D_FF, D_MODEL), D_FF ** -0.5),
    }


def reference(x, mix_pre_g, mix_post_g, ffn_pre_g, ffn_post_g,
              cv_w_in, cv_dw_w, cv_dw_b, cv_ln_g, cv_ln_b, cv_sc_w, cv_w_out,
              gla_w_in, gla_wa2_f, gla_ba2_f, gla_wa2_b, gla_ba2_b, gla_gn_g, gla_w_out,
              ffn_w_gu, ffn_w_down):
    for layer in range(DEPTH):
        h = rms_norm(x, mix_pre_g[layer])
        if layer % 2 == 0:
            i = layer // 2
            m = conv_hybrid_mixer(h, cv_w_in[i], cv_dw_w[i], cv_dw_b[i], cv_ln_g[i],
                                  cv_ln_b[i], cv_sc_w[i], cv_w_out[i])
        else:
            i = layer // 2
            m = gla_mixer(h, gla_w_in[i], gla_wa2_f[i], gla_ba2_f[i], gla_wa2_b[i],
                          gla_ba2_b[i], gla_gn_g[i], gla_w_out[i])
        x = x + rms_norm(m, mix_post_g[layer])
        h = rms_norm(x, ffn_pre_g[layer])
        f = swiglu_ffn(h, ffn_w_gu[layer], ffn_w_down[layer])
        x = x + rms_norm(f, ffn_post_g[layer])
    return x
```

```python
import math
from contextlib import ExitStack

import numpy as np
import concourse.bass as bass
import concourse.mybir as mybir
from concourse.bass_utils import run_bass_kernel_spmd
from concourse.alu_op_type import AluOpType as ALU

F32 = mybir.dt.float32
BF16 = mybir.dt.bfloat16
AF = mybir.ActivationFunctionType

NCORES = 8
D = 1024
SEQ = 2048
TOK = 4096
DFF = 2816
EPS = 1e-6
SLOT_ELEMS = 5632
NSLOT = 4
NPRM = 220

P_MIXPRE, P_MIXPOST, P_FFNPRE, P_FFNPOST = 0, 16, 32, 48
P_DWW = 64
P_DWB = 188
P_LNG = 192
P_LNB = 196
P_SCW = 200
P_GNG = 212


class Sched:
    def __init__(self, nc, es):
        self.nc = nc
        self.E = {"pe": nc.tensor, "act": nc.scalar, "dve": nc.vector, "pool": nc.gpsimd, "sp": nc.sync}
        self.psem = {}
        for e in ("pe", "act", "dve", "pool"):
            self.psem[e] = es.enter_context(nc.semaphore(f"p_{e}"))
        self.pcnt = {e: 0 for e in self.psem}
        self.waited = {e: {} for e in self.E}
        self.state = {}
        self.dsems = {}
        for q in ("sp", "pool"):
            self.dsems[q] = [[es.enter_context(nc.semaphore(f"d_{q}{i}")), 0, f"d_{q}{i}"] for i in range(10)]
        self.drr = {q: 0 for q in self.dsems}
        self.slotsem = [[es.enter_context(nc.semaphore(f"w_{i}")), 0, f"w_{i}"] for i in range(NSLOT)]

    def _collect(self, reads, writes):
        need = {}

        def add(n, s, v):
            if n not in need or need[n][1] < v:
                need[n] = (s, v)

        for k in reads:
            st = self.state.get(k)
            if st and st[0] is not None:
                add(*st[0])
        for k in writes:
            st = self.state.get(k)
            if st:
                if st[0] is not None:
                    add(*st[0])
                for n, (s, v) in st[1].items():
                    add(n, s, v)
        return need

    def _wait(self, e, need):
        for n, (s, v) in need.items():
            if self.waited[e].get(n, 0) >= v:
                continue
            self.E[e].wait_ge(s, v)
            self.waited[e][n] = v

    def _commit(self, tok, reads, writes):
        n, s, v = tok
        for k in reads:
            st = self.state.setdefault(k, [None, {}])
            st[1][n] = (s, v)
        for k in writes:
            self.state[k] = [tok, {}]

    def op(self, e, fn, reads=(), writes=()):
        need = self._collect(reads, writes)
        self._wait(e, need)
        ins = fn()
        self.pcnt[e] += 1
        ins.then_inc(self.psem[e], 1)
        tok = (f"p_{e}", self.psem[e], self.pcnt[e])
        self._commit(tok, reads, writes)
        return tok

    def mm(self, out, pairs, reads, wkey):
        need = self._collect(reads, [wkey])
        self._wait("pe", need)
        n = len(pairs)
        ins = None
        for i, (l, r) in enumerate(pairs):
            ins = self.nc.tensor.matmul(out, l, r, start=(i == 0), stop=(i == n - 1))
        self.pcnt["pe"] += 1
        ins.then_inc(self.psem["pe"], 1)
        tok = ("p_pe", self.psem["pe"], self.pcnt["pe"])
        self._commit(tok, reads, [wkey])
        return tok

    def mm_multi(self, groups, reads, wkey):
        need = self._collect(reads, [wkey])
        self._wait("pe", need)
        ins = None
        for out, pairs in groups:
            n = len(pairs)
            for i, (l, r) in enumerate(pairs):
                ins = self.nc.tensor.matmul(out, l, r, start=(i == 0), stop=(i == n - 1))
        self.pcnt["pe"] += 1
        ins.then_inc(self.psem["pe"], 1)
        tok = ("p_pe", self.psem["pe"], self.pcnt["pe"])
        self._commit(tok, reads, [wkey])
        return tok

    def tr(self, items, ident, reads, wkey):
        need = self._collect(reads, [wkey])
        self._wait("pe", need)
        ins = None
        for out, in_ in items:
            ins = self.nc.tensor.transpose(out, in_, ident)
        self.pcnt["pe"] += 1
        ins.then_inc(self.psem["pe"], 1)
        tok = ("p_pe", self.psem["pe"], self.pcnt["pe"])
        self._commit(tok, reads, [wkey])
        return tok

    def dma(self, q, out, in_, reads=(), writes=(), semrec=None, nonc=False):
        if semrec is None:
            semrec = self.dsems[q][self.drr[q]]
            self.drr[q] = (self.drr[q] + 1) % len(self.dsems[q])
            need = self._collect(reads, writes)
            if semrec[1] > 0:
                need[semrec[2]] = (semrec[0], semrec[1])
        else:
            need = self._collect(reads, writes)
        self._wait(q, need)
        if nonc:
            ins = self.E[q].dma_start(out=out, in_=in_, allow_slow_non_contiguous=True)
        else:
            ins = self.E[q].dma_start(out=out, in_=in_)
        semrec[1] += 16
        ins.then_inc(semrec[0], 16)
        tok = (semrec[2], semrec[0], semrec[1])
        self._commit(tok, reads, writes)
        return tok

    def barrier(self, engines=("pe", "act", "dve", "sp")):
        need = {}
        for e in self.psem:
            if self.pcnt[e] > 0:
                need[f"p_{e}"] = (self.psem[e], self.pcnt[e])
        for q in self.dsems:
            for rec in self.dsems[q]:
                if rec[1] > 0:
                    need[rec[2]] = (rec[0], rec[1])
        for e in engines:
            self._wait(e, need)


class Arena:
    def __init__(self, big, base_b, limit_b):
        self.big = big
        self.base = base_b
        self.limit = limit_b
        self.off = base_b

    def reset(self):
        self.off = self.base

    def alloc(self, shape, dt):
        esz = 2 if dt == BF16 else 4
        n = 1
        for s in shape[1:]:
            n *= s
        nb = (n * esz + 63) // 64 * 64
        assert self.off + nb <= self.limit, f"arena overflow {self.off + nb} > {self.limit}"
        a = self.big[0:shape[0], self.off // 2: self.off // 2 + n * esz // 2]
        self.off += nb
        if dt == F32:
            a = a.bitcast(F32)
        if len(shape) == 3:
            a = a.rearrange("p (a b) -> p a b", b=shape[2])
        elif len(shape) == 4:
            a = a.rearrange("p (a b c) -> p a b c", b=shape[2], c=shape[3])
        return a


class Builder:
    def __init__(self, plan):
        self.plan = plan
        self.nc = bass.Bass("TRN2", target_bir_lowering=False)
        self.es = ExitStack()

    def declare(self):
        nc = self.nc
        dt = lambda name, shape, kind="ExternalInput": nc.dram_tensor(name, shape, F32, kind=kind).ap()
        self.x_in = dt("x", [TOK, D])
        self.y_out = dt("y", [TOK, D], "ExternalOutput")
        self.xs = dt("xs", [TOK, D], "Internal")
        self.prm_d = dt("prm", [128, NPRM])
        self.cst_d = dt("cst", [128, 6 * 128])
        self.gvec_d = dt("gvec", [8, D])
        self.wa_d = dt("wa", [2, 17, 512])
        self.cv_w_in = dt("cv_w_in", [D, 2560])
        self.cv_w_out = dt("cv_w_out", [D, D])
        self.gla_w_in = dt("gla_w_in", [D, 3104])
        self.gla_w_out = dt("gla_w_out", [D, D])
        self.ffn_w_gu = dt("ffn_w_gu", [2, D, 2 * DFF])
        self.ffn_w_down = dt("ffn_w_down", [2, DFF, D])

    def setup(self):
        nc, es = self.nc, self.es
        self.S = Sched(nc, es)
        S = self.S
        total_b = 212000
        self.big = es.enter_context(nc.sbuf_tensor("big", [128, total_b // 2], BF16))
        self.ps = [es.enter_context(nc.psum_tensor(f"ps{i}", [128, 512], F32)) for i in range(8)]
        carve = Arena(self.big, 0, total_b)
        self.slots = [carve.alloc([128, SLOT_ELEMS], BF16) for _ in range(NSLOT)]
        self.identb = carve.alloc([128, 128], BF16)
        self.onesb = carve.alloc([128, 128], BF16)
        self.maskf = carve.alloc([128, 128], BF16)
        self.maskb = carve.alloc([128, 128], BF16)
        self.trif = carve.alloc([128, 128], F32)
        self.trib = carve.alloc([128, 128], F32)
        self.prm = carve.alloc([128, NPRM], F32)
        self.arena = Arena(self.big, carve.off, total_b)
        c = self.cst_d
        S.dma("pool", self.identb, c[:, 0:128], writes=["c0"])
        S.dma("pool", self.onesb, c[:, 128:256], writes=["c1"])
        S.dma("pool", self.maskf, c[:, 256:384], writes=["c2"])
        S.dma("pool", self.maskb, c[:, 384:512], writes=["c3"])
        S.dma("sp", self.trif, c[:, 512:640], writes=["c4"])
        S.dma("sp", self.trib, c[:, 640:768], writes=["c5"])
        S.dma("sp", self.prm, self.prm_d[:, :], writes=["c6"])
        S.barrier()
        self.wplan = []
        self.wnext_issue = 0
        self.wnext_use = 0
        self.psrr = {}

    def bank(self, role, banks):
        i = self.psrr.get(role, 0)
        self.psrr[role] = i + 1
        b = banks[i % len(banks)]
        return b, ("ps", b)

    def pcol(self, c0, n=1):
        return self.prm[:, c0:c0 + n]

    def w_issue(self, idx):
        if idx >= len(self.wplan):
            return
        S = self.S
        slot = idx % NSLOT
        rec = S.slotsem[slot]
        S._wait("pool", S._collect([], [("w", slot)]))
        for (eoff, shp, src) in self.wplan[idx]:
            n = shp[0] * shp[1]
            dst = self.slots[slot][:, eoff:eoff + n].rearrange("p (a b) -> p a b", b=shp[1])
            ins = self.nc.gpsimd.dma_start(out=dst, in_=src)
            rec[1] += 16
            ins.then_inc(rec[0], 16)
        S._commit((rec[2], rec[0], rec[1]), [], [("w", slot)])

    def w_prime(self):
        for i in range(NSLOT):
            self.w_issue(i)
        self.wnext_issue = NSLOT

    def w_get(self, tag):
        idx = self.wnext_use
        assert self.wtags[idx] == tag, (idx, self.wtags[idx], tag)
        self.wnext_use += 1
        slot = idx % NSLOT
        return self.slots[slot], ("w", slot)

    def w_release(self, n=1):
        for _ in range(n):
            self.w_issue(self.wnext_issue)
            self.wnext_issue += 1

    def add_load(self, tag, parts):
        self.wplan.append(parts)
        self.wtags.append(tag)

    @staticmethod
    def wsrc(w2d, r0, nr, c0, ncol):
        return w2d[r0:r0 + nr, c0:c0 + ncol].rearrange("(kc p) n -> p kc n", p=128)

    def make_plan(self):
        self.wtags = []
        for s in range(2):
            for sub in self.plan:
                if sub == "mix0":
                    w = self.cv_w_in
                    for nm, c0 in (("aval", 0), ("agate", 512), ("cgate", 1536), ("v", 2048), ("bgate", 1024)):
                        self.add_load(nm, [(0, (8, 512), self.wsrc(w, 0, D, c0, 512))])
                    for h in range(2):
                        self.add_load(f"wout{h}", [(0, (8, 512), self.wsrc(self.cv_w_out, 0, D, h * 512, 512))])
                elif sub == "mix1":
                    w = self.gla_w_in
                    self.add_load("gates", [(0, (8, 32), self.wsrc(w, 0, D, 3072, 32))])
                    for h in range(2):
                        self.add_load(f"r{h}", [(0, (8, 512), self.wsrc(w, 0, D, 2048 + h * 512, 512))])
                    for h in range(4):
                        self.add_load(f"qk{h}", [(0, (8, 128), self.wsrc(w, 0, D, h * 128, 128)),
                                                 (1024, (8, 128), self.wsrc(w, 0, D, 512 + h * 128, 128))])
                        self.add_load(f"v{h}", [(0, (8, 256), self.wsrc(w, 0, D, 1024 + h * 256, 256))])
                    for h in range(2):
                        self.add_load(f"wout{h}", [(0, (8, 512), self.wsrc(self.gla_w_out, 0, D, h * 512, 512))])
                elif sub in ("ffn0", "ffn1"):
                    l = int(sub[3])
                    wg = self.ffn_w_gu[l]
                    wd = self.ffn_w_down[l]
                    for hb in range(2):
                        for L in range(11):
                            self.add_load(f"gu{L}", [(0, (8, 256), self.wsrc(wg, 0, D, L * 256, 256)),
                                                     (2048, (8, 256), self.wsrc(wg, 0, D, DFF + L * 256, 256))])
                        for half in range(2):
                            for part in range(2):
                                self.add_load(f"dn{half}{part}",
                                              [(0, (11, 512), self.wsrc(wd, part * 1408, 1408, half * 512, 512))])

    def load_gb(self, A, row):
        g = A.alloc([128, D], F32)
        self.S.dma("sp", g, self.gvec_d[row:row + 1, :].to_broadcast([128, D]), writes=[("gb", row)])
        return g, ("gb", row)

    def prenorm_tile(self, xt, xkey, rs_col, rskey, gB, gkey, htm, htmkey, sqj, ss_col, sskey, dst_cols, hkey):
        S, nc = self.S, self.nc
        S.op("act", lambda: nc.scalar.activation(out=sqj, in_=xt, func=AF.Square, accum_out=ss_col),
             reads=[xkey], writes=[sskey])
        S.op("act", lambda: nc.scalar.activation(out=rs_col, in_=ss_col, func=AF.Sqrt, scale=1.0 / D, bias=self.epsc),
             reads=[sskey], writes=[rskey])
        S.op("dve", lambda: nc.vector.reciprocal(out=rs_col, in_=rs_col), reads=[rskey], writes=[rskey])
        S.op("dve", lambda: nc.vector.scalar_tensor_tensor(out=htm, in0=xt, scalar=rs_col, in1=gB,
                                                            op0=ALU.mult, op1=ALU.mult),
             reads=[xkey, rskey, gkey], writes=[htmkey])
        b, bkey = self.bank("pt", [0, 1])
        pv = self.ps[b][:, :].bitcast(BF16)
        S.tr([(pv[:, c * 128:(c + 1) * 128], htm[:, c * 128:(c + 1) * 128]) for c in range(8)], self.identb,
             reads=[htmkey], wkey=bkey)
        S.op("act", lambda: nc.scalar.copy(out=dst_cols, in_=pv.rearrange("p (c t) -> p c t", t=128)),
             reads=[bkey], writes=[hkey])

    def postnorm_tile(self, ops, okeys, xt, xkey, gB, gkey, t1, t1key, ssb, idx):
        S, nc = self.S, self.nc
        c0 = ssb[:, 4 * idx:4 * idx + 1]
        c1 = ssb[:, 4 * idx + 1:4 * idx + 2]
        c2 = ssb[:, 4 * idx + 2:4 * idx + 3]
        k = ("ssb", idx)
        S.op("act", lambda: nc.scalar.activation(out=self.sqj[:, 0:512], in_=ops[0], func=AF.Square, accum_out=c0),
             reads=[okeys[0]], writes=[(k, 0)])
        S.op("act", lambda: nc.scalar.activation(out=self.sqj[:, 512:1024], in_=ops[1], func=AF.Square, accum_out=c1),
             reads=[okeys[1]], writes=[(k, 1)])
        S.op("dve", lambda: nc.vector.tensor_tensor(out=c2, in0=c0, in1=c1, op=ALU.add),
             reads=[(k, 0), (k, 1)], writes=[(k, 2)])
        S.op("act", lambda: nc.scalar.activation(out=c2, in_=c2, func=AF.Sqrt, scale=1.0 / D, bias=self.epsc),
             reads=[(k, 2)], writes=[(k, 2)])
        S.op("dve", lambda: nc.vector.reciprocal(out=c2, in_=c2), reads=[(k, 2)], writes=[(k, 2)])
        for h in range(2):
            S.op("dve", lambda h=h: nc.vector.scalar_tensor_tensor(
                out=t1[:, h * 512:(h + 1) * 512], in0=ops[h], scalar=c2, in1=gB[:, h * 512:(h + 1) * 512],
                op0=ALU.mult, op1=ALU.mult), reads=[okeys[h], (k, 2), gkey], writes=[(t1key, h)])
        S.op("dve", lambda: nc.vector.tensor_tensor(out=xt, in0=xt, in1=t1, op=ALU.add),
             reads=[xkey, (t1key, 0), (t1key, 1)], writes=[xkey])

    def common_tmps(self, A):
        self.sqj = A.alloc([128, D], BF16)
        self.epsc = A.alloc([128, 1], F32)
        self.S.op("dve", lambda: self.nc.vector.memset(self.epsc, EPS), writes=["epsc"])
        self.S.barrier()

    def ffn(self, l, src, dst, seq):
        S, nc, A = self.S, self.nc, self.arena
        S.barrier()
        A.reset()
        self.common_tmps(A)
        gBpre, gpk = self.load_gb(A, 4 + l)
        gBpost, gqk = self.load_gb(A, 6 + l)
        XH = A.alloc([128, 8, D], F32)
        hT = A.alloc([128, 8, 1024], BF16)
        M0 = hT.rearrange("p a b -> p (a b)").bitcast(F32).rearrange("p (a b) -> p a b", b=512)
        ACTB = A.alloc([128, 22, 1024], BF16)
        HTM = [A.alloc([128, D], BF16) for _ in range(2)]
        SG = [A.alloc([128, 512], BF16) for _ in range(2)]
        T1 = [A.alloc([128, D], F32) for _ in range(2)]
        SS = A.alloc([128, 16], F32)
        RS = A.alloc([128, 16], F32)
        SSB = A.alloc([128, 64], F32)
        for hb in range(2):
            S.barrier()
            t0 = seq * SEQ + hb * 1024
            for i in range(8):
                r0 = t0 + i * 128
                S.dma("sp", XH[:, i, :], src[r0:r0 + 128, :], reads=[("xd", r0)], writes=[("XH", i)])
            for i in range(8):
                self.prenorm_tile(XH[:, i, :], ("XH", i), RS[:, i:i + 1], ("rs", i), gBpre, gpk,
                                  HTM[i % 2], ("htm", i % 2), self.sqj, SS[:, i:i + 1], ("ss", i),
                                  hT[:, :, i * 128:(i + 1) * 128], ("hT", i // 4))
            for L in range(11):
                W, wk = self.w_get(f"gu{L}")
                Wg = W[:, 0:2048].rearrange("p (a b) -> p a b", b=256)
                Wu = W[:, 2048:4096].rearrange("p (a b) -> p a b", b=256)
                for cc in range(2):
                    c = 2 * L + cc
                    for tb in range(2):
                        bg, kg = self.bank("g", [2, 3])
                        bu, ku = self.bank("u", [4, 5])
                        rhs = lambda kc: hT[:, kc, tb * 512:(tb + 1) * 512]
                        S.mm(self.ps[bg][:, :], [(Wg[:, kc, cc * 128:(cc + 1) * 128], rhs(kc)) for kc in range(8)],
                             reads=[wk, ("hT", tb)], wkey=kg)
                        S.mm(self.ps[bu][:, :], [(Wu[:, kc, cc * 128:(cc + 1) * 128], rhs(kc)) for kc in range(8)],
                             reads=[wk, ("hT", tb)], wkey=ku)
                        sg = SG[(2 * c + tb) % 2]
                        sgk = ("sg", (2 * c + tb) % 2)
                        S.op("act", lambda: nc.scalar.activation(out=sg, in_=self.ps[bg][:, :], func=AF.Silu),
                             reads=[kg], writes=[sgk])
                        S.op("dve", lambda: nc.vector.tensor_tensor(out=ACTB[:, c, tb * 512:(tb + 1) * 512], in0=sg,
                                                                    in1=self.ps[bu][:, :], op=ALU.mult),
                             reads=[sgk, ku], writes=[("actb", c, tb)])
                self.w_release()
            for half in range(2):
                Wa, wka = self.w_get(f"dn{half}0")
                Wb, wkb = self.w_get(f"dn{half}1")
                Wa3 = Wa[:, 0:5632].rearrange("p (a b) -> p a b", b=512)
                Wb3 = Wb[:, 0:5632].rearrange("p (a b) -> p a b", b=512)
                for i in range(8):
                    bf, kf = self.bank("f", [6, 7])
                    tb = i // 4
                    pairs = []
                    for kc in range(22):
                        w3 = Wa3 if kc < 11 else Wb3
                        pairs.append((ACTB[:, kc, i * 128:(i + 1) * 128], w3[:, kc % 11, :]))
                    S.mm(self.ps[bf][:, :], pairs, reads=[wka, wkb] + [("actb", kc, tb) for kc in range(22)], wkey=kf)
                    c0 = SSB[:, 4 * i + half:4 * i + half + 1]
                    k = ("ssb", i)
                    if half == 0:
                        S.op("act", lambda: nc.scalar.activation(out=self.sqj[:, 0:512], in_=self.ps[bf][:, :],
                                                                 func=AF.Square, accum_out=c0),
                             reads=[kf], writes=[(k, 0)])
                        S.op("act", lambda: nc.scalar.copy(out=M0[:, i, :], in_=self.ps[bf][:, :]),
                             reads=[kf], writes=[("m0", i), ("hT", 0), ("hT", 1)])
                    else:
                        c2 = SSB[:, 4 * i + 2:4 * i + 3]
                        S.op("act", lambda: nc.scalar.activation(out=self.sqj[:, 512:1024], in_=self.ps[bf][:, :],
                                                                 func=AF.Square, accum_out=c0),
                             reads=[kf], writes=[(k, 1)])
                        S.op("dve", lambda: nc.vector.tensor_tensor(out=c2, in0=SSB[:, 4 * i:4 * i + 1], in1=c0,
                                                                    op=ALU.add),
                             reads=[(k, 0), (k, 1)], writes=[(k, 2)])
                        S.op("act", lambda: nc.scalar.activation(out=c2, in_=c2, func=AF.Sqrt, scale=1.0 / D,
                                                                 bias=self.epsc),
                             reads=[(k, 2)], writes=[(k, 2)])
                        S.op("dve", lambda: nc.vector.reciprocal(out=c2, in_=c2), reads=[(k, 2)], writes=[(k, 2)])
                        t1 = T1[i % 2]
                        tk = ("t1", i % 2)
                        S.op("dve", lambda: nc.vector.scalar_tensor_tensor(
                            out=t1[:, 512:1024], in0=self.ps[bf][:, :], scalar=c2, in1=gBpost[:, 512:1024],
                            op0=ALU.mult, op1=ALU.mult), reads=[kf, (k, 2), gqk], writes=[(tk, 1)])
                        S.op("dve", lambda: nc.vector.scalar_tensor_tensor(
                            out=t1[:, 0:512], in0=M0[:, i, :], scalar=c2, in1=gBpost[:, 0:512],
                            op0=ALU.mult, op1=ALU.mult), reads=[("m0", i), (k, 2), gqk], writes=[(tk, 0)])
                        S.op("dve", lambda: nc.vector.tensor_tensor(out=XH[:, i, :], in0=XH[:, i, :], in1=t1,
                                                                    op=ALU.add),
                             reads=[("XH", i), (tk, 0), (tk, 1)], writes=[("XH", i)])
                        r0 = t0 + i * 128
                        S.dma("sp", dst[r0:r0 + 128, :], XH[:, i, :], reads=[("XH", i)], writes=[("xd", r0)])
                self.w_release(2)

    def mixer_prenorm(self, A, src, seq, gB, gk, hT):
        S = self.S
        XT = [A.alloc([128, D], F32) for _ in range(3)]
        HTM = [A.alloc([128, D], BF16) for _ in range(2)]
        SS = A.alloc([128, 16], F32)
        RS = A.alloc([128, 16], F32)
        for i in range(16):
            r0 = seq * SEQ + i * 128
            S.dma("sp", XT[i % 3], src[r0:r0 + 128, :], reads=[("xd", r0)], writes=[("XT", i % 3)])
            self.prenorm_tile(XT[i % 3], ("XT", i % 3), RS[:, i:i + 1], ("rs", i), gB, gk, HTM[i % 2],
                              ("htm", i % 2), self.sqj, SS[:, i:i + 1], ("ss", i),
                              hT[:, :, i * 128:(i + 1) * 128], ("hT", i // 4))
        return XT

    def mixer_out(self, A, CAT, catkeys, src, dst, seq, gB, gk, XT):
        S, nc = self.S, self.nc
        T1 = [A.alloc([128, D], F32) for _ in range(2)]
        SSB = A.alloc([128, 64], F32)
        W0, wk0 = self.w_get("wout0")
        W1, wk1 = self.w_get("wout1")
        W3 = [W0[:, 0:4096].rearrange("p (a b) -> p a b", b=512), W1[:, 0:4096].rearrange("p (a b) -> p a b", b=512)]
        wks = [wk0, wk1]
        for i in range(16):
            r0 = seq * SEQ + i * 128
            xt = XT[i % 3]
            xk = ("XT", i % 3)
            S.dma("sp", xt, src[r0:r0 + 128, :], reads=[("xd", r0)], writes=[xk])
            ops, oks = [], []
            for h in range(2):
                b, bk = self.bank("o", [2, 3, 4, 5])
                S.mm(self.ps[b][:, :], [(CAT[:, kc, i * 128:(i + 1) * 128], W3[h][:, kc, :]) for kc in range(8)],
                     reads=[wks[h]] + catkeys(i // 4), wkey=bk)
                ops.append(self.ps[b][:, :])
                oks.append(bk)
            self.postnorm_tile(ops, oks, xt, xk, gB, gk, T1[i % 2], ("t1", i % 2), SSB, i)
            S.dma("sp", dst[r0:r0 + 128, :], xt, reads=[xk], writes=[("xd", r0)])
        self.w_release(2)

    def conv_mixer(self, src, dst, seq):
        S, nc, A = self.S, self.nc, self.arena
        S.barrier()
        A.reset()
        self.common_tmps(A)
        gBpre, gpk = self.load_gb(A, 0)
        gBpost, gqk = self.load_gb(A, 2)
        hT = A.alloc([128, 8, SEQ], BF16)
        AB = A.alloc([128, 4, 2080], BF16)
        CV = A.alloc([128, 4, 2052], BF16)
        BG = A.alloc([128, 4, SEQ], BF16)
        DG = A.alloc([128, 31, 128], BF16)
        DGB = A.alloc([128, 3, 128], BF16)
        SIG = [A.alloc([128, 512], F32) for _ in range(2)]
        VS = [A.alloc([128, 512], BF16) for _ in range(2)]
        YSQ = A.alloc([128, 4, 512], BF16)
        MEAN = A.alloc([128, 512], F32)
        MSQ = A.alloc([128, 512], F32)
        SDv = A.alloc([128, 512], F32)
        Dt = [A.alloc([128, 512], F32) for _ in range(2)]
        Zt = [A.alloc([128, 512], F32) for _ in range(2)]
        S.op("dve", lambda: nc.vector.memset(AB[:, :, 0:15], 0.0), writes=["abh0"])
        S.op("dve", lambda: nc.vector.memset(AB[:, :, 2063:2080], 0.0), writes=["abh1"])
        S.op("dve", lambda: nc.vector.memset(CV[:, :, 0:1], 0.0), writes=["cvh0"])
        S.op("dve", lambda: nc.vector.memset(CV[:, :, 2049:2052], 0.0), writes=["cvh1"])
        XT = self.mixer_prenorm(A, src, seq, gBpre, gpk, hT)

        def proj(W, wk, j, tb, role, banks):
            b, bk = self.bank(role, banks)
            S.mm(self.ps[b][:, :], [(W[:, kc, j * 128:(j + 1) * 128], hT[:, kc, tb * 512:(tb + 1) * 512])
                                    for kc in range(8)], reads=[wk, ("hT", tb)], wkey=bk)
            return self.ps[b][:, :], bk

        def w3(tag):
            W, wk = self.w_get(tag)
            return W[:, 0:4096].rearrange("p (a b) -> p a b", b=512), wk

        Wv, wkv = w3("aval")
        Wg, wkg = w3("agate")
        n = 0
        for j in range(4):
            for tb in range(4):
                pv, kv = proj(Wv, wkv, j, tb, "pa", [2, 3])
                pg, kg = proj(Wg, wkg, j, tb, "pb", [4, 5])
                sg, sk = SIG[n % 2], ("sig", n % 2)
                S.op("act", lambda: nc.scalar.activation(out=sg, in_=pg, func=AF.Sigmoid), reads=[kg], writes=[sk])
                S.op("dve", lambda: nc.vector.tensor_tensor(out=AB[:, j, 15 + tb * 512:15 + (tb + 1) * 512],
                                                            in0=pv, in1=sg, op=ALU.mult),
                     reads=[kv, sk], writes=[("Ag", j, tb)])
                n += 1
        self.w_release(2)
        Wc, wkc = w3("cgate")
        Wvv, wkvv = w3("v")
        for j in range(4):
            for tb in range(4):
                pc, kc_ = proj(Wc, wkc, j, tb, "pa", [2, 3])
                pv, kv = proj(Wvv, wkvv, j, tb, "pb", [4, 5])
                vs, vk = VS[n % 2], ("vs", n % 2)
                S.op("act", lambda: nc.scalar.copy(out=vs, in_=pv), reads=[kv], writes=[vk])
                S.op("dve", lambda: nc.vector.tensor_tensor(out=CV[:, j, 1 + tb * 512:1 + (tb + 1) * 512],
                                                            in0=pc, in1=vs, op=ALU.mult),
                     reads=[kc_, vk], writes=[("CV", j, tb)])
                n += 1
        self.w_release(2)
        Wb, wkb = w3("bgate")
        for j in range(4):
            for tb in range(4):
                pb, kb = proj(Wb, wkb, j, tb, "pa", [2, 3])
                S.op("act", lambda: nc.scalar.copy(out=BG[:, j, tb * 512:(tb + 1) * 512], in_=pb),
                     reads=[kb], writes=[("BG", j, tb)])
        self.w_release(1)
        S.barrier()
        CAT = hT
        for j in range(4):
            for k in range(31):
                S.op("dve", lambda k=k: nc.vector.tensor_scalar(out=DG[:, k, :], in0=self.identb,
                                                                 scalar1=self.pcol(P_DWW + j * 31 + k), scalar2=None,
                                                                 op0=ALU.mult),
                     writes=[("DG", k)])
            for tb in range(4):
                b, bk = self.bank("cv", [2, 3])
                rd = [("DG", k) for k in range(31)] + [("Ag", j, t) for t in (tb - 1, tb, tb + 1) if 0 <= t < 4]
                rd += [("Ar", j, tb), ("Ar", j, tb + 1), "abh0", "abh1"]
                S.mm(self.ps[b][:, :], [(DG[:, k, :], AB[:, j, tb * 512 + k:tb * 512 + k + 512]) for k in range(31)],
                     reads=rd, wkey=bk)
                S.op("act", lambda: nc.scalar.activation(out=AB[:, j, tb * 512:(tb + 1) * 512], in_=self.ps[b][:, :],
                                                         func=AF.Identity, bias=self.pcol(P_DWB + j), scale=1.0),
                     reads=[bk], writes=[("Ar", j, tb)])
        for tb in range(4):
            for j in range(4):
                S.op("act", lambda j=j: nc.scalar.activation(out=YSQ[:, j, :], in_=AB[:, j, tb * 512:(tb + 1) * 512],
                                                             func=AF.Square),
                     reads=[("Ar", j, tb)], writes=[("ysq", j)])
            bm, km = self.bank("st", [6, 7])
            S.mm(self.ps[bm][:, :], [(self.onesb, AB[:, j, tb * 512:(tb + 1) * 512]) for j in range(4)],
                 reads=[("Ar", j, tb) for j in range(4)], wkey=km)
            be, ke = self.bank("st", [6, 7])
            S.mm(self.ps[be][:, :], [(self.onesb, YSQ[:, j, :]) for j in range(4)],
                 reads=[("ysq", j) for j in range(4)], wkey=ke)
            S.op("act", lambda: nc.scalar.activation(out=MEAN, in_=self.ps[bm][:, :], func=AF.Copy, scale=1.0 / 512),
                 reads=[km], writes=["mean"])
            S.op("act", lambda: nc.scalar.activation(out=MSQ, in_=self.ps[bm][:, :], func=AF.Square, scale=1.0 / 512),
                 reads=[km], writes=["msq"])
            S.op("dve", lambda: nc.vector.scalar_tensor_tensor(out=SDv, in0=self.ps[be][:, :], scalar=1.0 / 512,
                                                                in1=MSQ, op0=ALU.mult, op1=ALU.subtract),
                 reads=[ke, "msq"], writes=["sd"])
            S.op("act", lambda: nc.scalar.activation(out=SDv, in_=SDv, func=AF.Sqrt, bias=self.epsc, scale=1.0),
                 reads=["sd"], writes=["sd"])
            S.op("dve", lambda: nc.vector.reciprocal(out=SDv, in_=SDv), reads=["sd"], writes=["sd"])
            for j in range(4):
                d, dk_ = Dt[j % 2], ("dt", j % 2)
                z, zk = Zt[j % 2], ("zt", j % 2)
                S.op("dve", lambda: nc.vector.tensor_tensor(out=d, in0=AB[:, j, tb * 512:(tb + 1) * 512], in1=MEAN,
                                                            op=ALU.subtract),
                     reads=[("Ar", j, tb), "mean"], writes=[dk_])
                S.op("dve", lambda: nc.vector.tensor_tensor(out=z, in0=d, in1=SDv, op=ALU.mult),
                     reads=[dk_, "sd"], writes=[zk])
                S.op("act", lambda: nc.scalar.activation(out=CAT[:, j, tb * 512:(tb + 1) * 512], in_=z, func=AF.Silu,
                                                         scale=self.pcol(P_LNG + j), bias=self.pcol(P_LNB + j)),
                     reads=[zk], writes=[("cat", j, tb)])
        for j in range(4):
            for k in range(3):
                S.op("dve", lambda k=k: nc.vector.tensor_scalar(out=DGB[:, k, :], in0=self.identb,
                                                                 scalar1=self.pcol(P_SCW + j * 3 + k), scalar2=None,
                                                                 op0=ALU.mult),
                     writes=[("DGB", k)])
            for tb in range(4):
                b, bk = self.bank("cv", [2, 3])
                rd = [("DGB", k) for k in range(3)] + [("CV", j, t) for t in (tb - 1, tb, tb + 1) if 0 <= t < 4]
                rd += ["cvh0", "cvh1"]
                S.mm(self.ps[b][:, :], [(DGB[:, k, :], CV[:, j, tb * 512 + k:tb * 512 + k + 512]) for k in range(3)],
                     reads=rd, wkey=bk)
                S.op("dve", lambda: nc.vector.tensor_tensor(out=CAT[:, 4 + j, tb * 512:(tb + 1) * 512],
                                                            in0=self.ps[b][:, :], in1=BG[:, j, tb * 512:(tb + 1) * 512],
                                                            op=ALU.mult),
                     reads=[bk, ("BG", j, tb)], writes=[("cat", 4 + j, tb)])
        self.mixer_out(A, CAT, lambda tb: [("cat", c, tb) for c in range(8)], src, dst, seq, gBpost, gqk, XT)

    def gla_mixer(self, src, dst, seq):
        S, nc, A = self.S, self.nc, self.arena
        S.barrier()
        A.reset()
        self.common_tmps(A)
        gBpre, gpk = self.load_gb(A, 1)
        gBpost, gqk = self.load_gb(A, 3)
        hT = A.alloc([128, 8, SEQ], BF16)
        OG = A.alloc([128, 8, SEQ], BF16)
        GT = [A.alloc([17, SEQ], BF16) for _ in range(2)]
        WA = [A.alloc([17, 512], BF16) for _ in range(2)]
        QT = A.alloc([128, SEQ], BF16)
        KT = A.alloc([128, SEQ], BF16)
        VTM = A.alloc([128, 16, 256], BF16)
        OF = A.alloc([128, 2, SEQ], F32)
        NB = 2
        Et = [A.alloc([128, 128], F32) for _ in range(NB)]
        LP = [A.alloc([128, 128], F32) for _ in range(NB)]
        EQ = [A.alloc([128, 128], F32) for _ in range(NB)]
        EK = [A.alloc([128, 128], F32) for _ in range(NB)]
        DEC = [A.alloc([128, 1], F32) for _ in range(NB)]
        QTt = [A.alloc([128, 128], BF16) for _ in range(NB)]
        KTt = [A.alloc([128, 128], BF16) for _ in range(NB)]
        KE = [A.alloc([128, 128], BF16) for _ in range(NB)]
        ST = [A.alloc([128, 128], BF16) for _ in range(NB)]
        KTM = [A.alloc([128, 128], BF16) for _ in range(NB)]
        Sst = A.alloc([128, 256], F32)
        SBF = [A.alloc([128, 256], BF16) for _ in range(NB)]
        OS = [A.alloc([128, 2, 128], F32) for _ in range(NB)]
        OSQ = [A.alloc([128, 2, 128], BF16) for _ in range(NB)]
        RG = [A.alloc([128, 128], F32) for _ in range(NB)]
        ON = [A.alloc([128, 2, 128], F32) for _ in range(NB)]
        lnq = A.alloc([128, 1], F32)
        S.op("dve", lambda: nc.vector.memset(lnq, math.log(128.0 ** -0.5)), writes=["lnq"])
        for d in range(2):
            S.dma("pool", WA[d], self.wa_d[d], writes=[("WA", d)])
            S.op("dve", lambda d=d: nc.vector.memset(GT[d], 1.0), writes=[("GT", d, t) for t in range(4)])
        XT = self.mixer_prenorm(A, src, seq, gBpre, gpk, hT)
        W, wk = self.w_get("gates")
        Wg3 = W[:, 0:256].rearrange("p (a b) -> p a b", b=32)
        for tb in range(4):
            for d in range(2):
                b, bk = self.bank("pa", [2, 3])
                S.mm(self.ps[b][0:16, :], [(Wg3[:, kc, d * 16:(d + 1) * 16], hT[:, kc, tb * 512:(tb + 1) * 512])
                                            for kc in range(8)], reads=[wk, ("hT", tb)], wkey=bk)
                S.op("act", lambda: nc.scalar.copy(out=GT[d][0:16, tb * 512:(tb + 1) * 512], in_=self.ps[b][0:16, :]),
                     reads=[bk], writes=[("GT", d, tb)])
        self.w_release(1)
        for h in range(2):
            W, wk = self.w_get(f"r{h}")
            W3 = W[:, 0:4096].rearrange("p (a b) -> p a b", b=512)
            for cc in range(4):
                for tb in range(4):
                    b, bk = self.bank("pb", [4, 5])
                    S.mm(self.ps[b][:, :], [(W3[:, kc, cc * 128:(cc + 1) * 128], hT[:, kc, tb * 512:(tb + 1) * 512])
                                            for kc in range(8)], reads=[wk, ("hT", tb)], wkey=bk)
                    S.op("act", lambda: nc.scalar.activation(out=OG[:, h * 4 + cc, tb * 512:(tb + 1) * 512],
                                                             in_=self.ps[b][:, :], func=AF.Silu),
                         reads=[bk], writes=[("og", h * 4 + cc, tb)])
            self.w_release(1)
        n = 0
        for h in range(4):
            W, wk = self.w_get(f"qk{h}")
            Wq = W[:, 0:1024].rearrange("p (a b) -> p a b", b=128)
            Wk = W[:, 1024:2048].rearrange("p (a b) -> p a b", b=128)
            for tb in range(4):
                for (Wx, Xt, nm) in ((Wq, QT, "QT"), (Wk, KT, "KT")):
                    b, bk = self.bank("pa", [2, 3])
                    S.mm(self.ps[b][:, :], [(Wx[:, kc, :], hT[:, kc, tb * 512:(tb + 1) * 512]) for kc in range(8)],
                         reads=[wk, ("hT", tb)], wkey=bk)
                    S.op("act", lambda: nc.scalar.copy(out=Xt[:, tb * 512:(tb + 1) * 512], in_=self.ps[b][:, :]),
                         reads=[bk], writes=[(nm, tb)])
            self.w_release(1)
            W, wk = self.w_get(f"v{h}")
            Wv = W[:, 0:2048].rearrange("p (a b) -> p a b", b=256)
            for i in range(16):
                b, bk = self.bank("pb", [4, 5])
                S.mm(self.ps[b][:, 0:256], [(hT[:, kc, i * 128:(i + 1) * 128], Wv[:, kc, :]) for kc in range(8)],
                     reads=[wk, ("hT", i // 4)], wkey=bk)
                S.op("dve", lambda: nc.vector.tensor_copy(out=VTM[:, i, :], in_=self.ps[b][:, 0:256]),
                     reads=[bk], writes=[("VTM", i)])
            self.w_release(1)
            for d in range(2):
                order = range(16) if d == 0 else range(15, -1, -1)
                tri = self.trif if d == 0 else self.trib
                mask = self.maskf if d == 0 else self.maskb
                first = True
                for c in order:
                    u = n % NB
                    n += 1
                    cs = slice(c * 128, (c + 1) * 128)
                    tb = c // 4
                    b, kz = self.bank("gz", [6, 7])
                    zps = self.ps[b][:, 0:128]
                    S.mm(zps, [(GT[d][0:17, cs], WA[d][0:17, h * 128:(h + 1) * 128])],
                         reads=[("GT", d, tb), ("WA", d)], wkey=kz)
                    S.op("act", lambda: nc.scalar.activation(out=Et[u], in_=zps, func=AF.Exp, scale=-1.0),
                         reads=[kz], writes=[("E", u)])
                    S.op("act", lambda: nc.scalar.activation(out=LP[u], in_=Et[u], func=AF.Ln, bias=1.0, scale=1.0),
                         reads=[("E", u)], writes=[("LP", u)])
                    b2, kc_ = self.bank("gz", [6, 7])
                    cps = self.ps[b2][:, 0:128]
                    S.mm(cps, [(LP[u], tri)], reads=[("LP", u)], wkey=kc_)
                    S.op("act", lambda: nc.scalar.activation(out=EQ[u], in_=cps, func=AF.Exp, bias=lnq, scale=1.0),
                         reads=[kc_, "lnq"], writes=[("EQ", u)])
                    S.op("act", lambda: nc.scalar.activation(out=EK[u], in_=cps, func=AF.Exp, scale=-1.0),
                         reads=[kc_], writes=[("EK", u)])
                    lastc = 127 if d == 0 else 0
                    S.op("act", lambda: nc.scalar.activation(out=DEC[u], in_=cps[:, lastc:lastc + 1], func=AF.Exp),
                         reads=[kc_], writes=[("DEC", u)])
                    S.op("dve", lambda: nc.vector.tensor_tensor(out=QTt[u], in0=QT[:, cs], in1=EQ[u], op=ALU.mult),
                         reads=[("QT", tb), ("EQ", u)], writes=[("QTt", u)])
                    S.op("dve", lambda: nc.vector.tensor_tensor(out=KTt[u], in0=KT[:, cs], in1=EK[u], op=ALU.mult),
                         reads=[("KT", tb), ("EK", u)], writes=[("KTt", u)])
                    S.op("act", lambda: nc.scalar.activation(out=KE[u], in_=KTt[u], func=AF.Copy, scale=DEC[u]),
                         reads=[("KTt", u), ("DEC", u)], writes=[("KE", u)])
                    b3, ks = self.bank("sc", [0, 1])
                    sps = self.ps[b3][:, 0:128]
                    S.mm(sps, [(KTt[u], QTt[u])], reads=[("KTt", u), ("QTt", u)], wkey=ks)
                    S.op("dve", lambda: nc.vector.tensor_tensor(out=ST[u], in0=sps, in1=mask, op=ALU.mult),
                         reads=[ks], writes=[("ST", u)])
                    b4, kt = self.bank("sc", [0, 1])
                    tps = self.ps[b4][:, :].bitcast(BF16)[:, 0:128]
                    S.tr([(tps, KE[u])], self.identb, reads=[("KE", u)], wkey=kt)
                    S.op("act", lambda: nc.scalar.copy(out=KTM[u], in_=tps), reads=[kt], writes=[("KTM", u)])
                    b5, ko = self.bank("o", [2, 3])
                    groups = []
                    for vc in range(2):
                        prs = [(VTM[:, c, vc * 128:(vc + 1) * 128], ST[u])]
                        if not first:
                            prs.append((SBF[(n - 2) % NB][:, vc * 128:(vc + 1) * 128], QTt[u]))
                        groups.append((self.ps[b5][:, vc * 128:(vc + 1) * 128], prs))
                    rd = [("VTM", c), ("ST", u), ("QTt", u)]
                    if not first:
                        rd.append(("SBF", (n - 2) % NB))
                    S.mm_multi(groups, reads=rd, wkey=ko)
                    ops3 = self.ps[b5][:, 0:256].rearrange("p (a b) -> p a b", b=128)
                    b6, kkv = self.bank("kv", [4, 5])
                    kvps = self.ps[b6][:, 0:256]
                    S.mm(kvps, [(KTM[u], VTM[:, c, :])], reads=[("KTM", u), ("VTM", c)], wkey=kkv)
                    if first:
                        S.op("dve", lambda: nc.vector.tensor_copy(out=Sst, in_=kvps), reads=[kkv], writes=["S"])
                    else:
                        S.op("dve", lambda: nc.vector.scalar_tensor_tensor(out=Sst, in0=Sst, scalar=DEC[u], in1=kvps,
                                                                            op0=ALU.mult, op1=ALU.add),
                             reads=["S", ("DEC", u), kkv], writes=["S"])
                    S.op("act", lambda: nc.scalar.copy(out=SBF[(n - 1) % NB], in_=Sst), reads=["S"],
                         writes=[("SBF", (n - 1) % NB)])
                    if d == 0:
                        S.op("act", lambda: nc.scalar.copy(out=OF[:, :, cs], in_=ops3), reads=[ko],
                             writes=[("OF", c)])
                    else:
                        S.op("dve", lambda: nc.vector.tensor_tensor(out=OS[u], in0=ops3, in1=OF[:, :, cs], op=ALU.add),
                             reads=[ko, ("OF", c)], writes=[("OS", u)])
                        S.op("act", lambda: nc.scalar.activation(out=OSQ[u], in_=OS[u], func=AF.Square),
                             reads=[("OS", u)], writes=[("OSQ", u)])
                        b7, kq = self.bank("gz", [6, 7])
                        qps = self.ps[b7][:, 0:128]
                        S.mm(qps, [(self.onesb, OSQ[u][:, vc, :]) for vc in range(2)], reads=[("OSQ", u)], wkey=kq)
                        S.op("act", lambda: nc.scalar.activation(out=RG[u], in_=qps, func=AF.Sqrt, scale=1.0 / 256,
                                                                 bias=self.epsc),
                             reads=[kq], writes=[("RG", u)])
                        S.op("dve", lambda: nc.vector.reciprocal(out=RG[u], in_=RG[u]), reads=[("RG", u)],
                             writes=[("RG", u)])
                        for vc in range(2):
                            S.op("dve", lambda vc=vc: nc.vector.tensor_tensor(out=ON[u][:, vc, :], in0=OS[u][:, vc, :],
                                                                              in1=RG[u], op=ALU.mult),
                                 reads=[("OS", u), ("RG", u)], writes=[("ON", u, vc)])
                            S.op("dve", lambda vc=vc: nc.vector.scalar_tensor_tensor(
                                out=OG[:, h * 2 + vc, cs], in0=ON[u][:, vc, :], scalar=self.pcol(P_GNG + h * 2 + vc),
                                in1=OG[:, h * 2 + vc, cs], op0=ALU.mult, op1=ALU.mult),
                                reads=[("ON", u, vc), ("og", h * 2 + vc, tb)], writes=[("og", h * 2 + vc, tb)])
                    first = False
        self.mixer_out(A, OG, lambda tb: [("og", c, tb) for c in range(8)], src, dst, seq, gBpost, gqk, XT)

    def build(self):
        self.declare()
        self.setup()
        self.make_plan()
        self.w_prime()
        S = self.S
        nsub = len(self.plan)
        for seq in range(2):
            for si, sub in enumerate(self.plan):
                src = self.x_in if si == 0 else self.xs
                dst = self.y_out if si == nsub - 1 else self.xs
                if sub == "mix0":
                    self.conv_mixer(src, dst, seq)
                elif sub == "mix1":
                    self.gla_mixer(src, dst, seq)
                else:
                    self.ffn(int(sub[3]), src, dst, seq)
        S.barrier(engines=("sp",))
        self.es.close()
        return self.nc


def host_inputs(inp):
    f = lambda a: np.ascontiguousarray(np.asarray(a, dtype=np.float32))
    pm = lambda v: f(v).reshape(-1, 128).T
    prm = np.zeros((128, NPRM), np.float32)
    for l in range(2):
        prm[:, P_MIXPRE + 8 * l:P_MIXPRE + 8 * l + 8] = pm(inp["mix_pre_g"][l])
        prm[:, P_MIXPOST + 8 * l:P_MIXPOST + 8 * l + 8] = pm(inp["mix_post_g"][l])
        prm[:, P_FFNPRE + 8 * l:P_FFNPRE + 8 * l + 8] = pm(inp["ffn_pre_g"][l])
        prm[:, P_FFNPOST + 8 * l:P_FFNPOST + 8 * l + 8] = pm(inp["ffn_post_g"][l])
    dww = f(inp["cv_dw_w"][0])
    prm[:, P_DWW:P_DWW + 124] = dww.reshape(31, 4, 128).transpose(2, 1, 0).reshape(128, 124)
    prm[:, P_DWB:P_DWB + 4] = pm(inp["cv_dw_b"][0])
    prm[:, P_LNG:P_LNG + 4] = pm(inp["cv_ln_g"][0])
    prm[:, P_LNB:P_LNB + 4] = pm(inp["cv_ln_b"][0])
    scw = f(inp["cv_sc_w"][0])
    prm[:, P_SCW:P_SCW + 12] = scw.reshape(3, 4, 128).transpose(2, 1, 0).reshape(128, 12)
    prm[:, P_GNG:P_GNG + 8] = pm(inp["gla_gn_g"][0])
    j = np.arange(128)[:, None]
    i = np.arange(128)[None, :]
    cst = np.concatenate([
        np.eye(128), np.ones((128, 128)), (j <= i) * 1.0, (j >= i) * 1.0,
        (j <= i) * (-1.0 / 16.0), (j >= i) * (-1.0 / 16.0)], axis=1).astype(np.float32)
    gvec = np.stack([f(inp["mix_pre_g"][0]), f(inp["mix_pre_g"][1]), f(inp["mix_post_g"][0]),
                     f(inp["mix_post_g"][1]), f(inp["ffn_pre_g"][0]), f(inp["ffn_pre_g"][1]),
                     f(inp["ffn_post_g"][0]), f(inp["ffn_post_g"][1])], axis=0)
    wa = np.stack([np.concatenate([f(inp["gla_wa2_f"][0]), f(inp["gla_ba2_f"])[0:1]], axis=0),
                   np.concatenate([f(inp["gla_wa2_b"][0]), f(inp["gla_ba2_b"])[0:1]], axis=0)], axis=0)
    shared = {
        "prm": prm, "cst": cst, "gvec": f(gvec), "wa": f(wa),
        "cv_w_in": f(inp["cv_w_in"][0]), "cv_w_out": f(inp["cv_w_out"][0]),
        "gla_w_in": f(inp["gla_w_in"][0]), "gla_w_out": f(inp["gla_w_out"][0]),
        "ffn_w_gu": f(inp["ffn_w_gu"]), "ffn_w_down": f(inp["ffn_w_down"]),
    }
    x = f(inp["x"]).reshape(NCORES, TOK, D)
    return [dict(shared, x=x[c]) for c in range(NCORES)]


_CACHE = {}


def run(inp, plan=("mix0", "ffn0", "mix1", "ffn1")):
    plan = tuple(plan)
    if plan not in _CACHE:
        _CACHE[plan] = Builder(plan).build()
    nc = _CACHE[plan]
    in_maps = host_inputs(inp)
    res = run_bass_kernel_spmd(nc, in_maps, core_ids=list(range(NCORES)))
    out = np.stack([np.asarray(r["y"], dtype=np.float32) for r in res.results], axis=0)
    return out.reshape(16, SEQ, D)


def kernel(**inputs):
    return run(inputs)
```

```python
import math
from contextlib import ExitStack

import numpy as np
import concourse.bass as bass
import concourse.mybir as mybir
from concourse.bass_utils import run_bass_kernel_spmd
from concourse.alu_op_type import AluOpType as ALU

F32 = mybir.dt.float32
BF16 = mybir.dt.bfloat16
AF = mybir.ActivationFunctionType

NCORES = 8
D = 1024
SEQ = 2048
TOK = 4096
DFF = 2816
EPS = 1e-6
SLOT_ELEMS = 5632
NSLOT = 4
NPRM = 220

P_MIXPRE, P_MIXPOST, P_FFNPRE, P_FFNPOST = 0, 16, 32, 48
P_DWW = 64
P_DWB = 188
P_LNG = 192
P_LNB = 196
P_SCW = 200
P_GNG = 212


class Sched:
    def __init__(self, nc, es):
        self.nc = nc
        self.E = {"pe": nc.tensor, "act": nc.scalar, "dve": nc.vector, "pool": nc.gpsimd, "sp": nc.sync}
        self.psem = {}
        for e in ("pe", "act", "dve", "pool"):
            self.psem[e] = es.enter_context(nc.semaphore(f"p_{e}"))
        self.pcnt = {e: 0 for e in self.psem}
        self.waited = {e: {} for e in self.E}
        self.state = {}
        self.dsems = {}
        for q in ("sp", "pool"):
            self.dsems[q] = [[es.enter_context(nc.semaphore(f"d_{q}{i}")), 0, f"d_{q}{i}"] for i in range(10)]
        self.drr = {q: 0 for q in self.dsems}
        self.slotsem = [[es.enter_context(nc.semaphore(f"w_{i}")), 0, f"w_{i}"] for i in range(NSLOT)]

    def _collect(self, reads, writes):
        need = {}

        def add(n, s, v):
            if n not in need or need[n][1] < v:
                need[n] = (s, v)

        for k in reads:
            st = self.state.get(k)
            if st and st[0] is not None:
                add(*st[0])
        for k in writes:
            st = self.state.get(k)
            if st:
                if st[0] is not None:
                    add(*st[0])
                for n, (s, v) in st[1].items():
                    add(n, s, v)
        return need

    def _wait(self, e, need):
        for n, (s, v) in need.items():
            if self.waited[e].get(n, 0) >= v:
                continue
            self.E[e].wait_ge(s, v)
            self.waited[e][n] = v

    def _commit(self, tok, reads, writes):
        n, s, v = tok
        for k in reads:
            st = self.state.setdefault(k, [None, {}])
            st[1][n] = (s, v)
        for k in writes:
            self.state[k] = [tok, {}]

    def op(self, e, fn, reads=(), writes=()):
        need = self._collect(reads, writes)
        self._wait(e, need)
        ins = fn()
        self.pcnt[e] += 1
        ins.then_inc(self.psem[e], 1)
        tok = (f"p_{e}", self.psem[e], self.pcnt[e])
        self._commit(tok, reads, writes)
        return tok

    def mm(self, out, pairs, reads, wkey):
        need = self._collect(reads, [wkey])
        self._wait("pe", need)
        n = len(pairs)
        ins = None
        for i, (l, r) in enumerate(pairs):
            ins = self.nc.tensor.matmul(out, l, r, start=(i == 0), stop=(i == n - 1))
        self.pcnt["pe"] += 1
        ins.then_inc(self.psem["pe"], 1)
        tok = ("p_pe", self.psem["pe"], self.pcnt["pe"])
        self._commit(tok, reads, [wkey])
        return tok

    def mm_multi(self, groups, reads, wkey):
        need = self._collect(reads, [wkey])
        self._wait("pe", need)
        ins = None
        for out, pairs in groups:
            n = len(pairs)
            for i, (l, r) in enumerate(pairs):
                ins = self.nc.tensor.matmul(out, l, r, start=(i == 0), stop=(i == n - 1))
        self.pcnt["pe"] += 1
        ins.then_inc(self.psem["pe"], 1)
        tok = ("p_pe", self.psem["pe"], self.pcnt["pe"])
        self._commit(tok, reads, [wkey])
        return tok

    def tr(self, items, ident, reads, wkey):
        need = self._collect(reads, [wkey])
        self._wait("pe", need)
        ins = None
        for out, in_ in items:
            ins = self.nc.tensor.transpose(out, in_, ident)
        self.pcnt["pe"] += 1
        ins.then_inc(self.psem["pe"], 1)
        tok = ("p_pe", self.psem["pe"], self.pcnt["pe"])
        self._commit(tok, reads, [wkey])
        return tok

    def dma(self, q, out, in_, reads=(), writes=(), semrec=None, nonc=False):
        if semrec is None:
            semrec = self.dsems[q][self.drr[q]]
            self.drr[q] = (self.drr[q] + 1) % len(self.dsems[q])
            need = self._collect(reads, writes)
            if semrec[1] > 0:
                need[semrec[2]] = (semrec[0], semrec[1])
        else:
            need = self._collect(reads, writes)
        self._wait(q, need)
        if nonc:
            ins = self.E[q].dma_start(out=out, in_=in_, allow_slow_non_contiguous=True)
        else:
            ins = self.E[q].dma_start(out=out, in_=in_)
        semrec[1] += 16
        ins.then_inc(semrec[0], 16)
        tok = (semrec[2], semrec[0], semrec[1])
        self._commit(tok, reads, writes)
        return tok

    def barrier(self, engines=("pe", "act", "dve", "sp")):
        need = {}
        for e in self.psem:
            if self.pcnt[e] > 0:
                need[f"p_{e}"] = (self.psem[e], self.pcnt[e])
        for q in self.dsems:
            for rec in self.dsems[q]:
                if rec[1] > 0:
                    need[rec[2]] = (rec[0], rec[1])
        for e in engines:
            self._wait(e, need)


class Arena:
    def __init__(self, big, base_b, limit_b):
        self.big = big
        self.base = base_b
        self.limit = limit_b
        self.off = base_b

    def reset(self):
        self.off = self.base

    def alloc(self, shape, dt):
        esz = 2 if dt == BF16 else 4
        n = 1
        for s in shape[1:]:
            n *= s
        nb = (n * esz + 63) // 64 * 64
        assert self.off + nb <= self.limit, f"arena overflow {self.off + nb} > {self.limit}"
        a = self.big[0:shape[0], self.off // 2: self.off // 2 + n * esz // 2]
        self.off += nb
        if dt == F32:
            a = a.bitcast(F32)
        if len(shape) == 3:
            a = a.rearrange("p (a b) -> p a b", b=shape[2])
        elif len(shape) == 4:
            a = a.rearrange("p (a b c) -> p a b c", b=shape[2], c=shape[3])
        return a


class Builder:
    def __init__(self, plan):
        self.plan = plan
        self.nc = bass.Bass("TRN2", target_bir_lowering=False)
        self.es = ExitStack()

    def declare(self):
        nc = self.nc
        dt = lambda name, shape, kind="ExternalInput": nc.dram_tensor(name, shape, F32, kind=kind).ap()
        self.x_in = dt("x", [TOK, D])
        self.y_out = dt("y", [TOK, D], "ExternalOutput")
        self.xs = dt("xs", [TOK, D], "Internal")
        self.prm_d = dt("prm", [128, NPRM])
        self.cst_d = dt("cst", [128, 6 * 128])
        self.gvec_d = dt("gvec", [8, D])
        self.wa_d = dt("wa", [2, 17, 512])
        self.cv_w_in = dt("cv_w_in", [D, 2560])
        self.cv_w_out = dt("cv_w_out", [D, D])
        self.gla_w_in = dt("gla_w_in", [D, 3104])
        self.gla_w_out = dt("gla_w_out", [D, D])
        self.ffn_w_gu = dt("ffn_w_gu", [2, D, 2 * DFF])
        self.ffn_w_down = dt("ffn_w_down", [2, DFF, D])

    def setup(self):
        nc, es = self.nc, self.es
        self.S = Sched(nc, es)
        S = self.S
        total_b = 212800
        self.big = es.enter_context(nc.sbuf_tensor("big", [128, total_b // 2], BF16))
        self.ps = [es.enter_context(nc.psum_tensor(f"ps{i}", [128, 512], F32)) for i in range(8)]
        carve = Arena(self.big, 0, total_b)
        self.slots = [carve.alloc([128, SLOT_ELEMS], BF16) for _ in range(NSLOT)]
        self.identb = carve.alloc([128, 128], BF16)
        self.onesb = carve.alloc([128, 128], BF16)
        self.maskf = carve.alloc([128, 128], BF16)
        self.maskb = carve.alloc([128, 128], BF16)
        self.trif = carve.alloc([128, 128], F32)
        self.trib = carve.alloc([128, 128], F32)
        self.prm = carve.alloc([128, NPRM], F32)
        self.arena = Arena(self.big, carve.off, total_b)
        c = self.cst_d
        S.dma("pool", self.identb, c[:, 0:128], writes=["c0"])
        S.dma("pool", self.onesb, c[:, 128:256], writes=["c1"])
        S.dma("pool", self.maskf, c[:, 256:384], writes=["c2"])
        S.dma("pool", self.maskb, c[:, 384:512], writes=["c3"])
        S.dma("sp", self.trif, c[:, 512:640], writes=["c4"])
        S.dma("sp", self.trib, c[:, 640:768], writes=["c5"])
        S.dma("sp", self.prm, self.prm_d[:, :], writes=["c6"])
        S.barrier()
        self.wplan = []
        self.wnext_issue = 0
        self.wnext_use = 0
        self.psrr = {}

    def bank(self, role, banks):
        i = self.psrr.get(role, 0)
        self.psrr[role] = i + 1
        b = banks[i % len(banks)]
        return b, ("ps", b)

    def pcol(self, c0, n=1):
        return self.prm[:, c0:c0 + n]

    def w_issue(self, idx):
        if idx >= len(self.wplan):
            return
        S = self.S
        slot = idx % NSLOT
        rec = S.slotsem[slot]
        S._wait("pool", S._collect([], [("w", slot)]))
        for (eoff, shp, src) in self.wplan[idx]:
            n = shp[0] * shp[1]
            dst = self.slots[slot][:, eoff:eoff + n].rearrange("p (a b) -> p a b", b=shp[1])
            ins = self.nc.gpsimd.dma_start(out=dst, in_=src)
            rec[1] += 16
            ins.then_inc(rec[0], 16)
        S._commit((rec[2], rec[0], rec[1]), [], [("w", slot)])

    def w_prime(self):
        for i in range(NSLOT):
            self.w_issue(i)
        self.wnext_issue = NSLOT

    def w_get(self, tag):
        idx = self.wnext_use
        assert self.wtags[idx] == tag, (idx, self.wtags[idx], tag)
        self.wnext_use += 1
        slot = idx % NSLOT
        return self.slots[slot], ("w", slot)

    def w_release(self, n=1):
        for _ in range(n):
            self.w_issue(self.wnext_issue)
            self.wnext_issue += 1

    def add_load(self, tag, parts):
        self.wplan.append(parts)
        self.wtags.append(tag)

    @staticmethod
    def wsrc(w2d, r0, nr, c0, ncol):
        return w2d[r0:r0 + nr, c0:c0 + ncol].rearrange("(kc p) n -> p kc n", p=128)

    def make_plan(self):
        self.wtags = []
        for s in range(2):
            for sub in self.plan:
                if sub == "mix0":
                    w = self.cv_w_in
                    for nm, c0 in (("aval", 0), ("agate", 512), ("cgate", 1536), ("v", 2048), ("bgate", 1024)):
                        self.add_load(nm, [(0, (8, 512), self.wsrc(w, 0, D, c0, 512))])
                    for h in range(2):
                        self.add_load(f"wout{h}", [(0, (8, 512), self.wsrc(self.cv_w_out, 0, D, h * 512, 512))])
                elif sub == "mix1":
                    w = self.gla_w_in
                    self.add_load("gates", [(0, (8, 32), self.wsrc(w, 0, D, 3072, 32))])
                    for h in range(2):
                        self.add_load(f"r{h}", [(0, (8, 512), self.wsrc(w, 0, D, 2048 + h * 512, 512))])
                    for h in range(4):
                        self.add_load(f"qk{h}", [(0, (8, 128), self.wsrc(w, 0, D, h * 128, 128)),
                                                 (1024, (8, 128), self.wsrc(w, 0, D, 512 + h * 128, 128))])
                        self.add_load(f"v{h}", [(0, (8, 256), self.wsrc(w, 0, D, 1024 + h * 256, 256))])
                    for h in range(2):
                        self.add_load(f"wout{h}", [(0, (8, 512), self.wsrc(self.gla_w_out, 0, D, h * 512, 512))])
                elif sub in ("ffn0", "ffn1"):
                    l = int(sub[3])
                    wg = self.ffn_w_gu[l]
                    wd = self.ffn_w_down[l]
                    for hb in range(2):
                        for L in range(11):
                            self.add_load(f"gu{L}", [(0, (8, 256), self.wsrc(wg, 0, D, L * 256, 256)),
                                                     (2048, (8, 256), self.wsrc(wg, 0, D, DFF + L * 256, 256))])
                        for half in range(2):
                            for part in range(2):
                                self.add_load(f"dn{half}{part}",
                                              [(0, (11, 512), self.wsrc(wd, part * 1408, 1408, half * 512, 512))])

    def load_gb(self, A, row):
        g = A.alloc([128, D], F32)
        self.S.dma("sp", g, self.gvec_d[row:row + 1, :].to_broadcast([128, D]), writes=[("gb", row)])
        return g, ("gb", row)

    def pn_stats(self, xt, xkey, rs_col, rskey, ss_col, sskey):
        S, nc = self.S, self.nc
        S.op("act", lambda: nc.scalar.activation(out=self.sqj, in_=xt, func=AF.Square, accum_out=ss_col),
             reads=[xkey], writes=[sskey])
        S.op("act", lambda: nc.scalar.activation(out=rs_col, in_=ss_col, func=AF.Sqrt, scale=1.0 / D, bias=self.epsc),
             reads=[sskey], writes=[rskey])
        S.op("dve", lambda: nc.vector.reciprocal(out=rs_col, in_=rs_col), reads=[rskey], writes=[rskey])

    def pn_apply(self, xt, xkey, rs_col, rskey, gB, gkey, htm, htmkey, dst_cols, hkey):
        S, nc = self.S, self.nc
        S.op("dve", lambda: nc.vector.scalar_tensor_tensor(out=htm, in0=xt, scalar=rs_col, in1=gB,
                                                            op0=ALU.mult, op1=ALU.mult),
             reads=[xkey, rskey, gkey], writes=[htmkey])
        b, bkey = self.bank("pt", [0, 1])
        pv = self.ps[b][:, :].bitcast(BF16)
        S.tr([(pv[:, c * 128:(c + 1) * 128], htm[:, c * 128:(c + 1) * 128]) for c in range(8)], self.identb,
             reads=[htmkey], wkey=bkey)
        S.op("act", lambda: nc.scalar.copy(out=dst_cols, in_=pv.rearrange("p (c t) -> p c t", t=128)),
             reads=[bkey], writes=[hkey])

    def postnorm_tile(self, ops, okeys, xt, xkey, gB, gkey, t1, t1key, ssb, idx):
        S, nc = self.S, self.nc
        c0 = ssb[:, 4 * idx:4 * idx + 1]
        c1 = ssb[:, 4 * idx + 1:4 * idx + 2]
        c2 = ssb[:, 4 * idx + 2:4 * idx + 3]
        k = ("ssb", idx)
        S.op("act", lambda: nc.scalar.activation(out=self.sqj[:, 0:512], in_=ops[0], func=AF.Square, accum_out=c0),
             reads=[okeys[0]], writes=[(k, 0)])
        S.op("act", lambda: nc.scalar.activation(out=self.sqj[:, 512:1024], in_=ops[1], func=AF.Square, accum_out=c1),
             reads=[okeys[1]], writes=[(k, 1)])
        S.op("dve", lambda: nc.vector.tensor_tensor(out=c2, in0=c0, in1=c1, op=ALU.add),
             reads=[(k, 0), (k, 1)], writes=[(k, 2)])
        S.op("act", lambda: nc.scalar.activation(out=c2, in_=c2, func=AF.Sqrt, scale=1.0 / D, bias=self.epsc),
             reads=[(k, 2)], writes=[(k, 2)])
        S.op("dve", lambda: nc.vector.reciprocal(out=c2, in_=c2), reads=[(k, 2)], writes=[(k, 2)])
        for h in range(2):
            S.op("dve", lambda h=h: nc.vector.scalar_tensor_tensor(
                out=t1[:, h * 512:(h + 1) * 512], in0=ops[h], scalar=c2, in1=gB[:, h * 512:(h + 1) * 512],
                op0=ALU.mult, op1=ALU.mult), reads=[okeys[h], (k, 2), gkey], writes=[(t1key, h)])
        S.op("dve", lambda: nc.vector.tensor_tensor(out=xt, in0=xt, in1=t1, op=ALU.add),
             reads=[xkey, (t1key, 0), (t1key, 1)], writes=[xkey])

    def common_tmps(self, A):
        self.sqj = A.alloc([128, D], BF16)
        self.epsc = A.alloc([128, 1], F32)
        self.S.op("dve", lambda: self.nc.vector.memset(self.epsc, EPS), writes=["epsc"])
        self.S.barrier()

    def ffn(self, l, src, dst, seq):
        S, nc, A = self.S, self.nc, self.arena
        S.barrier()
        A.reset()
        self.common_tmps(A)
        gBpre, gpk = self.load_gb(A, 4 + l)
        gBpost, gqk = self.load_gb(A, 6 + l)
        XH = A.alloc([128, 8, D], F32)
        hT = A.alloc([128, 8, 1024], BF16)
        M0 = hT.rearrange("p a b -> p (a b)").bitcast(F32).rearrange("p (a b) -> p a b", b=512)
        ACTB = A.alloc([128, 22, 1024], BF16)
        HTM = [A.alloc([128, D], BF16) for _ in range(2)]
        SG = [A.alloc([128, 512], BF16) for _ in range(2)]
        T1 = [A.alloc([128, D], F32) for _ in range(2)]
        SS = A.alloc([128, 16], F32)
        RS = A.alloc([128, 16], F32)
        SSB = A.alloc([128, 64], F32)
        for hb in range(2):
            S.barrier()
            t0 = seq * SEQ + hb * 1024
            for i in range(8):
                r0 = t0 + i * 128
                S.dma("sp", XH[:, i, :], src[r0:r0 + 128, :], reads=[("xd", r0)], writes=[("XH", i)])
            for i in range(8):
                self.pn_stats(XH[:, i, :], ("XH", i), RS[:, i:i + 1], ("rs", i), SS[:, i:i + 1], ("ss", i))
            for i in range(8):
                self.pn_apply(XH[:, i, :], ("XH", i), RS[:, i:i + 1], ("rs", i), gBpre, gpk,
                              HTM[i % 2], ("htm", i % 2), hT[:, :, i * 128:(i + 1) * 128], ("hT", i, "w"))
            for L in range(11):
                W, wk = self.w_get(f"gu{L}")
                Wg = W[:, 0:2048].rearrange("p (a b) -> p a b", b=256)
                Wu = W[:, 2048:4096].rearrange("p (a b) -> p a b", b=256)
                for cc in range(2):
                    c = 2 * L + cc
                    for tb in range(2):
                        bg, kg = self.bank("g", [2, 3])
                        bu, ku = self.bank("u", [4, 5])
                        rhs = lambda kc: hT[:, kc, tb * 512:(tb + 1) * 512]
                        S.mm(self.ps[bg][:, :], [(Wg[:, kc, cc * 128:(cc + 1) * 128], rhs(kc)) for kc in range(8)],
                             reads=[wk] + [("hT", 4 * tb + q, "w") for q in range(4)], wkey=kg)
                        S.mm(self.ps[bu][:, :], [(Wu[:, kc, cc * 128:(cc + 1) * 128], rhs(kc)) for kc in range(8)],
                             reads=[wk] + [("hT", 4 * tb + q, "w") for q in range(4)], wkey=ku)
                        sg = SG[(2 * c + tb) % 2]
                        sgk = ("sg", (2 * c + tb) % 2)
                        S.op("act", lambda: nc.scalar.activation(out=sg, in_=self.ps[bg][:, :], func=AF.Silu),
                             reads=[kg], writes=[sgk])
                        S.op("dve", lambda: nc.vector.tensor_tensor(out=ACTB[:, c, tb * 512:(tb + 1) * 512], in0=sg,
                                                                    in1=self.ps[bu][:, :], op=ALU.mult),
                             reads=[sgk, ku], writes=[("actb", c, tb)])
                self.w_release()
            for half in range(2):
                Wa, wka = self.w_get(f"dn{half}0")
                Wb, wkb = self.w_get(f"dn{half}1")
                Wa3 = Wa[:, 0:5632].rearrange("p (a b) -> p a b", b=512)
                Wb3 = Wb[:, 0:5632].rearrange("p (a b) -> p a b", b=512)
                for i in range(8):
                    bf, kf = self.bank("f", [6, 7])
                    tb = i // 4
                    pairs = []
                    for kc in range(22):
                        w3 = Wa3 if kc < 11 else Wb3
                        pairs.append((ACTB[:, kc, i * 128:(i + 1) * 128], w3[:, kc % 11, :]))
                    S.mm(self.ps[bf][:, :], pairs, reads=[wka, wkb] + [("actb", kc, tb) for kc in range(22)], wkey=kf)
                    c0 = SSB[:, 4 * i + half:4 * i + half + 1]
                    k = ("ssb", i)
                    if half == 0:
                        S.op("act", lambda: nc.scalar.activation(out=self.sqj[:, 0:512], in_=self.ps[bf][:, :],
                                                                 func=AF.Square, accum_out=c0),
                             reads=[kf], writes=[(k, 0)])
                        S.op("act", lambda: nc.scalar.copy(out=M0[:, i, :], in_=self.ps[bf][:, :]),
                             reads=[kf], writes=[("m0", i)] + [("hT", q, "w") for q in range(8)])
                    else:
                        c2 = SSB[:, 4 * i + 2:4 * i + 3]
                        S.op("act", lambda: nc.scalar.activation(out=self.sqj[:, 512:1024], in_=self.ps[bf][:, :],
                                                                 func=AF.Square, accum_out=c0),
                             reads=[kf], writes=[(k, 1)])
                        S.op("dve", lambda: nc.vector.tensor_tensor(out=c2, in0=SSB[:, 4 * i:4 * i + 1], in1=c0,
                                                                    op=ALU.add),
                             reads=[(k, 0), (k, 1)], writes=[(k, 2)])
                        S.op("act", lambda: nc.scalar.activation(out=c2, in_=c2, func=AF.Sqrt, scale=1.0 / D,
                                                                 bias=self.epsc),
                             reads=[(k, 2)], writes=[(k, 2)])
                        S.op("dve", lambda: nc.vector.reciprocal(out=c2, in_=c2), reads=[(k, 2)], writes=[(k, 2)])
                        t1 = T1[i % 2]
                        tk = ("t1", i % 2)
                        S.op("dve", lambda: nc.vector.scalar_tensor_tensor(
                            out=t1[:, 512:1024], in0=self.ps[bf][:, :], scalar=c2, in1=gBpost[:, 512:1024],
                            op0=ALU.mult, op1=ALU.mult), reads=[kf, (k, 2), gqk], writes=[(tk, 1)])
                        S.op("dve", lambda: nc.vector.scalar_tensor_tensor(
                            out=t1[:, 0:512], in0=M0[:, i, :], scalar=c2, in1=gBpost[:, 0:512],
                            op0=ALU.mult, op1=ALU.mult), reads=[("m0", i), (k, 2), gqk], writes=[(tk, 0)])
                        S.op("dve", lambda: nc.vector.tensor_tensor(out=XH[:, i, :], in0=XH[:, i, :], in1=t1,
                                                                    op=ALU.add),
                             reads=[("XH", i), (tk, 0), (tk, 1)], writes=[("XH", i)])
                        r0 = t0 + i * 128
                        S.dma("sp", dst[r0:r0 + 128, :], XH[:, i, :], reads=[("XH", i)], writes=[("xd", r0)])
                self.w_release(2)

    def mixer_prenorm(self, A, src, seq, gB, gk, hT):
        S = self.S
        XT = [A.alloc([128, D], F32) for _ in range(3)]
        HTM = [A.alloc([128, D], BF16) for _ in range(2)]
        SS = A.alloc([128, 16], F32)
        RS = A.alloc([128, 16], F32)
        def stats(i):
            r0 = seq * SEQ + i * 128
            S.dma("sp", XT[i % 3], src[r0:r0 + 128, :], reads=[("xd", r0)], writes=[("XT", i % 3)])
            self.pn_stats(XT[i % 3], ("XT", i % 3), RS[:, i:i + 1], ("rs", i), SS[:, i:i + 1], ("ss", i))

        stats(0)
        for i in range(16):
            if i + 1 < 16:
                stats(i + 1)
            self.pn_apply(XT[i % 3], ("XT", i % 3), RS[:, i:i + 1], ("rs", i), gB, gk, HTM[i % 2],
                          ("htm", i % 2), hT[:, :, i * 128:(i + 1) * 128], ("hT", i, "w"))
        return XT

    def mixer_out(self, A, CAT, catkeys, src, dst, seq, gB, gk, XT):
        S, nc = self.S, self.nc
        T1 = [A.alloc([128, D], F32) for _ in range(1)]
        SSB = A.alloc([128, 64], F32)
        W0, wk0 = self.w_get("wout0")
        W1, wk1 = self.w_get("wout1")
        W3 = [W0[:, 0:4096].rearrange("p (a b) -> p a b", b=512), W1[:, 0:4096].rearrange("p (a b) -> p a b", b=512)]
        wks = [wk0, wk1]
        for i in range(16):
            r0 = seq * SEQ + i * 128
            xt = XT[i % 3]
            xk = ("XT", i % 3)
            S.dma("sp", xt, src[r0:r0 + 128, :], reads=[("xd", r0)], writes=[xk])
            ops, oks = [], []
            for h in range(2):
                b, bk = self.bank("o", [2, 3, 4, 5])
                S.mm(self.ps[b][:, :], [(CAT[:, kc, i * 128:(i + 1) * 128], W3[h][:, kc, :]) for kc in range(8)],
                     reads=[wks[h]] + catkeys(i // 4), wkey=bk)
                ops.append(self.ps[b][:, :])
                oks.append(bk)
            self.postnorm_tile(ops, oks, xt, xk, gB, gk, T1[0], ("t1", 0), SSB, i)
            S.dma("sp", dst[r0:r0 + 128, :], xt, reads=[xk], writes=[("xd", r0)])
        self.w_release(2)

    def conv_mixer(self, src, dst, seq):
        S, nc, A = self.S, self.nc, self.arena
        S.barrier()
        A.reset()
        self.common_tmps(A)
        gBpre, gpk = self.load_gb(A, 0)
        gBpost, gqk = self.load_gb(A, 2)
        hT = A.alloc([128, 8, SEQ], BF16)
        AB = A.alloc([128, 4, 2080], BF16)
        CV = A.alloc([128, 4, 2052], BF16)
        BG = A.alloc([128, 4, SEQ], BF16)
        DG = A.alloc([128, 31, 128], BF16)
        DGB = A.alloc([128, 3, 128], BF16)
        SIG = [A.alloc([128, 512], F32) for _ in range(2)]
        VS = [A.alloc([128, 512], BF16) for _ in range(2)]
        YSQ = A.alloc([128, 4, 512], BF16)
        MEAN = A.alloc([128, 512], F32)
        MSQ = A.alloc([128, 512], F32)
        SDv = A.alloc([128, 512], F32)
        Dt = [A.alloc([128, 512], F32) for _ in range(2)]
        Zt = [A.alloc([128, 512], F32) for _ in range(2)]
        S.op("dve", lambda: nc.vector.memset(AB[:, :, 0:15], 0.0), writes=["abh0"])
        S.op("dve", lambda: nc.vector.memset(AB[:, :, 2063:2080], 0.0), writes=["abh1"])
        S.op("dve", lambda: nc.vector.memset(CV[:, :, 0:1], 0.0), writes=["cvh0"])
        S.op("dve", lambda: nc.vector.memset(CV[:, :, 2049:2052], 0.0), writes=["cvh1"])
        XT = self.mixer_prenorm(A, src, seq, gBpre, gpk, hT)

        def proj(W, wk, j, tb, role, banks):
            b, bk = self.bank(role, banks)
            S.mm(self.ps[b][:, :], [(W[:, kc, j * 128:(j + 1) * 128], hT[:, kc, tb * 512:(tb + 1) * 512])
                                    for kc in range(8)], reads=[wk] + [("hT", 4 * tb + q_, "w") for q_ in range(4)], wkey=bk)
            return self.ps[b][:, :], bk

        def w3(tag):
            W, wk = self.w_get(tag)
            return W[:, 0:4096].rearrange("p (a b) -> p a b", b=512), wk

        Wv, wkv = w3("aval")
        Wg, wkg = w3("agate")
        n = 0
        for j in range(4):
            for tb in range(4):
                pv, kv = proj(Wv, wkv, j, tb, "pa", [2, 3])
                pg, kg = proj(Wg, wkg, j, tb, "pb", [4, 5])
                sg, sk = SIG[n % 2], ("sig", n % 2)
                S.op("act", lambda: nc.scalar.activation(out=sg, in_=pg, func=AF.Sigmoid), reads=[kg], writes=[sk])
                S.op("dve", lambda: nc.vector.tensor_tensor(out=AB[:, j, 15 + tb * 512:15 + (tb + 1) * 512],
                                                            in0=pv, in1=sg, op=ALU.mult),
                     reads=[kv, sk], writes=[("Ag", j, tb)])
                n += 1
        self.w_release(2)
        Wc, wkc = w3("cgate")
        Wvv, wkvv = w3("v")
        for j in range(4):
            for tb in range(4):
                pc, kc_ = proj(Wc, wkc, j, tb, "pa", [2, 3])
                pv, kv = proj(Wvv, wkvv, j, tb, "pb", [4, 5])
                vs, vk = VS[n % 2], ("vs", n % 2)
                S.op("act", lambda: nc.scalar.copy(out=vs, in_=pv), reads=[kv], writes=[vk])
                S.op("dve", lambda: nc.vector.tensor_tensor(out=CV[:, j, 1 + tb * 512:1 + (tb + 1) * 512],
                                                            in0=pc, in1=vs, op=ALU.mult),
                     reads=[kc_, vk], writes=[("CV", j, tb)])
                n += 1
        self.w_release(2)
        Wb, wkb = w3("bgate")
        for j in range(4):
            for tb in range(4):
                pb, kb = proj(Wb, wkb, j, tb, "pa", [2, 3])
                S.op("act", lambda: nc.scalar.copy(out=BG[:, j, tb * 512:(tb + 1) * 512], in_=pb),
                     reads=[kb], writes=[("BG", j, tb)])
        self.w_release(1)
        S.barrier()
        CAT = hT
        for j in range(4):
            for k in range(31):
                S.op("dve", lambda k=k: nc.vector.tensor_scalar(out=DG[:, k, :], in0=self.identb,
                                                                 scalar1=self.pcol(P_DWW + j * 31 + k), scalar2=None,
                                                                 op0=ALU.mult),
                     writes=[("DG", k)])
            for tb in range(4):
                b, bk = self.bank("cv", [2, 3])
                rd = [("DG", k) for k in range(31)] + [("Ag", j, t) for t in (tb - 1, tb, tb + 1) if 0 <= t < 4]
                rd += [("Ar", j, tb), ("Ar", j, tb + 1), "abh0", "abh1"]
                S.mm(self.ps[b][:, :], [(DG[:, k, :], AB[:, j, tb * 512 + k:tb * 512 + k + 512]) for k in range(31)],
                     reads=rd, wkey=bk)
                S.op("act", lambda: nc.scalar.activation(out=AB[:, j, tb * 512:(tb + 1) * 512], in_=self.ps[b][:, :],
                                                         func=AF.Identity, bias=self.pcol(P_DWB + j), scale=1.0),
                     reads=[bk], writes=[("Ar", j, tb)])
        for tb in range(4):
            for j in range(4):
                S.op("act", lambda j=j: nc.scalar.activation(out=YSQ[:, j, :], in_=AB[:, j, tb * 512:(tb + 1) * 512],
                                                             func=AF.Square),
                     reads=[("Ar", j, tb)], writes=[("ysq", j)])
            bm, km = self.bank("st", [6, 7])
            S.mm(self.ps[bm][:, :], [(self.onesb, AB[:, j, tb * 512:(tb + 1) * 512]) for j in range(4)],
                 reads=[("Ar", j, tb) for j in range(4)], wkey=km)
            be, ke = self.bank("st", [6, 7])
            S.mm(self.ps[be][:, :], [(self.onesb, YSQ[:, j, :]) for j in range(4)],
                 reads=[("ysq", j) for j in range(4)], wkey=ke)
            S.op("act", lambda: nc.scalar.activation(out=MEAN, in_=self.ps[bm][:, :], func=AF.Copy, scale=1.0 / 512),
                 reads=[km], writes=["mean"])
            S.op("act", lambda: nc.scalar.activation(out=MSQ, in_=self.ps[bm][:, :], func=AF.Square, scale=1.0 / 512),
                 reads=[km], writes=["msq"])
            S.op("dve", lambda: nc.vector.scalar_tensor_tensor(out=SDv, in0=self.ps[be][:, :], scalar=1.0 / 512,
                                                                in1=MSQ, op0=ALU.mult, op1=ALU.subtract),
                 reads=[ke, "msq"], writes=["sd"])
            S.op("act", lambda: nc.scalar.activation(out=SDv, in_=SDv, func=AF.Sqrt, bias=self.epsc, scale=1.0),
                 reads=["sd"], writes=["sd"])
            S.op("dve", lambda: nc.vector.reciprocal(out=SDv, in_=SDv), reads=["sd"], writes=["sd"])
            for j in range(4):
                d, dk_ = Dt[j % 2], ("dt", j % 2)
                z, zk = Zt[j % 2], ("zt", j % 2)
                S.op("dve", lambda: nc.vector.tensor_tensor(out=d, in0=AB[:, j, tb * 512:(tb + 1) * 512], in1=MEAN,
                                                            op=ALU.subtract),
                     reads=[("Ar", j, tb), "mean"], writes=[dk_])
                S.op("dve", lambda: nc.vector.tensor_tensor(out=z, in0=d, in1=SDv, op=ALU.mult),
                     reads=[dk_, "sd"], writes=[zk])
                S.op("act", lambda: nc.scalar.activation(out=CAT[:, j, tb * 512:(tb + 1) * 512], in_=z, func=AF.Silu,
                                                         scale=self.pcol(P_LNG + j), bias=self.pcol(P_LNB + j)),
                     reads=[zk], writes=[("cat", j, tb)])
        for j in range(4):
            for k in range(3):
                S.op("dve", lambda k=k: nc.vector.tensor_scalar(out=DGB[:, k, :], in0=self.identb,
                                                                 scalar1=self.pcol(P_SCW + j * 3 + k), scalar2=None,
                                                                 op0=ALU.mult),
                     writes=[("DGB", k)])
            for tb in range(4):
                b, bk = self.bank("cv", [2, 3])
                rd = [("DGB", k) for k in range(3)] + [("CV", j, t) for t in (tb - 1, tb, tb + 1) if 0 <= t < 4]
                rd += ["cvh0", "cvh1"]
                S.mm(self.ps[b][:, :], [(DGB[:, k, :], CV[:, j, tb * 512 + k:tb * 512 + k + 512]) for k in range(3)],
                     reads=rd, wkey=bk)
                S.op("dve", lambda: nc.vector.tensor_tensor(out=CAT[:, 4 + j, tb * 512:(tb + 1) * 512],
                                                            in0=self.ps[b][:, :], in1=BG[:, j, tb * 512:(tb + 1) * 512],
                                                            op=ALU.mult),
                     reads=[bk, ("BG", j, tb)], writes=[("cat", 4 + j, tb)])
        self.mixer_out(A, CAT, lambda tb: [("cat", c, tb) for c in range(8)], src, dst, seq, gBpost, gqk, XT)

    def gla_mixer(self, src, dst, seq):
        S, nc, A = self.S, self.nc, self.arena
        S.barrier()
        A.reset()
        self.common_tmps(A)
        gBpre, gpk = self.load_gb(A, 1)
        gBpost, gqk = self.load_gb(A, 3)
        hT = A.alloc([128, 8, SEQ], BF16)
        OG = A.alloc([128, 8, SEQ], BF16)
        GT = [A.alloc([17, SEQ], BF16) for _ in range(2)]
        WA = [A.alloc([17, 512], BF16) for _ in range(2)]
        QT = A.alloc([128, SEQ], BF16)
        KT = A.alloc([128, SEQ], BF16)
        VTM = A.alloc([128, 16, 256], BF16)
        OF = A.alloc([128, 2, SEQ], BF16)
        Et = A.alloc([128, 512], F32)
        LP = A.alloc([128, 512], F32)
        EQ = [A.alloc([128, 512], F32) for _ in range(2)]
        EK = A.alloc([128, 512], F32)
        QTt = [A.alloc([128, 512], BF16) for _ in range(2)]
        KTt = A.alloc([128, 512], BF16)
        KE = A.alloc([128, 512], BF16)
        ST = [A.alloc([128, 512], BF16) for _ in range(2)]
        KTM = [A.alloc([128, 512], BF16) for _ in range(2)]
        Sst = A.alloc([128, 256], F32)
        SBF = [A.alloc([128, 256], BF16) for _ in range(3)]
        OS = A.alloc([128, 2, 512], F32)
        OSQ = A.alloc([128, 2, 512], BF16)
        RG = A.alloc([128, 512], F32)
        for d in range(2):
            S.dma("pool", WA[d], self.wa_d[d], writes=[("WA", d)])
            S.op("dve", lambda d=d: nc.vector.memset(GT[d], 1.0), writes=[("GT", d, t) for t in range(4)])
        XT = self.mixer_prenorm(A, src, seq, gBpre, gpk, hT)
        hk = lambda tb: [("hT", 4 * tb + q_, "w") for q_ in range(4)]
        W, wk = self.w_get("gates")
        Wg3 = W[:, 0:256].rearrange("p (a b) -> p a b", b=32)
        for tb in range(4):
            for d in range(2):
                b, bk = self.bank("pa", [2, 3])
                S.mm(self.ps[b][0:16, :], [(Wg3[:, kc, d * 16:(d + 1) * 16], hT[:, kc, tb * 512:(tb + 1) * 512])
                                            for kc in range(8)], reads=[wk] + hk(tb), wkey=bk)
                S.op("act", lambda: nc.scalar.copy(out=GT[d][0:16, tb * 512:(tb + 1) * 512], in_=self.ps[b][0:16, :]),
                     reads=[bk], writes=[("GT", d, tb)])
        self.w_release(1)
        for h in range(2):
            W, wk = self.w_get(f"r{h}")
            W3 = W[:, 0:4096].rearrange("p (a b) -> p a b", b=512)
            for cc in range(4):
                for tb in range(4):
                    b, bk = self.bank("pb", [4, 5])
                    S.mm(self.ps[b][:, :], [(W3[:, kc, cc * 128:(cc + 1) * 128], hT[:, kc, tb * 512:(tb + 1) * 512])
                                            for kc in range(8)], reads=[wk] + hk(tb), wkey=bk)
                    S.op("act", lambda: nc.scalar.activation(out=OG[:, h * 4 + cc, tb * 512:(tb + 1) * 512],
                                                             in_=self.ps[b][:, :], func=AF.Silu),
                         reads=[bk], writes=[("og", h * 4 + cc, tb)])
            self.w_release(1)
        gctr = [0]
        sctr = [0]
        for h in range(4):
            W, wk = self.w_get(f"qk{h}")
            Wq = W[:, 0:1024].rearrange("p (a b) -> p a b", b=128)
            Wk = W[:, 1024:2048].rearrange("p (a b) -> p a b", b=128)
            for tb in range(4):
                for (Wx, Xt, nm, sc) in ((Wq, QT, "QT", 128.0 ** -0.5), (Wk, KT, "KT", 1.0)):
                    b, bk = self.bank("pa", [2, 3])
                    S.mm(self.ps[b][:, :], [(Wx[:, kc, :], hT[:, kc, tb * 512:(tb + 1) * 512]) for kc in range(8)],
                         reads=[wk] + hk(tb), wkey=bk)
                    S.op("act", lambda: nc.scalar.activation(out=Xt[:, tb * 512:(tb + 1) * 512], in_=self.ps[b][:, :],
                                                             func=AF.Copy, scale=sc),
                         reads=[bk], writes=[(nm, tb)])
            self.w_release(1)
            W, wk = self.w_get(f"v{h}")
            Wv = W[:, 0:2048].rearrange("p (a b) -> p a b", b=256)
            for i in range(16):
                b, bk = self.bank("pb", [4, 5])
                S.mm(self.ps[b][:, 0:256], [(hT[:, kc, i * 128:(i + 1) * 128], Wv[:, kc, :]) for kc in range(8)],
                     reads=[wk, ("hT", i, "w")], wkey=bk)
                S.op("dve", lambda: nc.vector.tensor_copy(out=VTM[:, i, :], in_=self.ps[b][:, 0:256]),
                     reads=[bk], writes=[("VTM", i)])
            self.w_release(1)

            items = [(0, g) for g in range(4)] + [(1, g) for g in range(3, -1, -1)]

            def SI(d, gi, p):
                tri = self.trif if d == 0 else self.trib
                mask = self.maskf if d == 0 else self.maskb
                gs = slice(gi * 512, (gi + 1) * 512)
                col = lambda tm: slice(tm * 128, (tm + 1) * 128)
                b, kz = self.bank("gz", [6, 7])
                zb = self.ps[b]
                S.mm_multi([(zb[:, col(tm)], [(GT[d][0:17, (gi * 4 + tm) * 128:(gi * 4 + tm + 1) * 128],
                                               WA[d][0:17, h * 128:(h + 1) * 128])]) for tm in range(4)],
                           reads=[("GT", d, gi), ("WA", d)], wkey=kz)
                S.op("act", lambda: nc.scalar.activation(out=Et, in_=zb[:, :], func=AF.Exp, scale=-1.0),
                     reads=[kz], writes=["E"])
                S.op("act", lambda: nc.scalar.activation(out=LP, in_=Et, func=AF.Ln, bias=1.0, scale=1.0),
                     reads=["E"], writes=["LP"])
                b2, kc_ = self.bank("gz", [6, 7])
                cb = self.ps[b2]
                S.mm_multi([(cb[:, col(tm)], [(LP[:, col(tm)], tri)]) for tm in range(4)], reads=["LP"], wkey=kc_)
                S.op("act", lambda: nc.scalar.activation(out=EQ[p], in_=cb[:, :], func=AF.Exp),
                     reads=[kc_], writes=[("EQ", p)])
                S.op("act", lambda: nc.scalar.activation(out=EK, in_=cb[:, :], func=AF.Exp, scale=-1.0),
                     reads=[kc_], writes=["EK"])
                S.op("dve", lambda: nc.vector.tensor_tensor(out=QTt[p], in0=QT[:, gs], in1=EQ[p], op=ALU.mult),
                     reads=[("QT", gi), ("EQ", p)], writes=[("QTt", p)])
                S.op("dve", lambda: nc.vector.tensor_tensor(out=KTt, in0=KT[:, gs], in1=EK, op=ALU.mult),
                     reads=[("KT", gi), "EK"], writes=["KTt"])
                lastc = 127 if d == 0 else 0
                eq3 = EQ[p].rearrange("p (a b) -> p a b", b=128)
                dec_b = eq3[:, :, lastc:lastc + 1].to_broadcast([128, 4, 128])
                S.op("dve", lambda: nc.vector.tensor_tensor(out=KE.rearrange("p (a b) -> p a b", b=128),
                                                            in0=KTt.rearrange("p (a b) -> p a b", b=128),
                                                            in1=dec_b, op=ALU.mult),
                     reads=["KTt", ("EQ", p)], writes=["KE"])
                b3, ks = self.bank("sc", [0, 1])
                sb = self.ps[b3]
                S.mm_multi([(sb[:, col(tm)], [(KTt[:, col(tm)], QTt[p][:, col(tm)])]) for tm in range(4)],
                           reads=["KTt", ("QTt", p)], wkey=ks)
                mask_b = mask.rearrange("p (o b) -> p o b", o=1).to_broadcast([128, 4, 128])
                S.op("dve", lambda: nc.vector.tensor_tensor(out=ST[p].rearrange("p (a b) -> p a b", b=128),
                                                            in0=sb[:, :].rearrange("p (a b) -> p a b", b=128),
                                                            in1=mask_b, op=ALU.mult),
                     reads=[ks], writes=[("ST", p)])
                b4, kt = self.bank("sc", [0, 1])
                tb16 = self.ps[b4][:, :].bitcast(BF16)
                S.tr([(tb16[:, col(tm)], KE[:, col(tm)]) for tm in range(4)], self.identb, reads=["KE"], wkey=kt)
                S.op("act", lambda: nc.scalar.copy(out=KTM[p], in_=tb16[:, 0:512]), reads=[kt],
                     writes=[("KTM", p)])

            def SD(d, gi, p, first_group):
                order = [0, 1, 2, 3] if d == 0 else [3, 2, 1, 0]
                col = lambda tm: slice(tm * 128, (tm + 1) * 128)
                lastc = 127 if d == 0 else 0
                kvb = {}

                def kv(tm):
                    b6, kkv = self.bank("kv", [4, 5])
                    S.mm(self.ps[b6][:, 0:256], [(KTM[p][:, col(tm)], VTM[:, gi * 4 + tm, :])],
                         reads=[("KTM", p), ("VTM", gi * 4 + tm)], wkey=kkv)
                    kvb[tm] = (self.ps[b6][:, 0:256], kkv)

                kv(order[0])
                kv(order[1])
                obanks = {}
                for half in range(2):
                    b5, ko = self.bank("o", [2, 3])
                    obanks[half] = (self.ps[b5], ko)
                pend = {0: [], 1: []}
                for half in range(2):
                    S._wait("pe", S._collect([], [obanks[half][1]]))
                for t, tm in enumerate(order):
                    c = gi * 4 + tm
                    has_state = not (first_group and t == 0)
                    ob, ko = obanks[tm // 2]
                    tm2 = tm % 2
                    groups = []
                    for vc in range(2):
                        prs = [(VTM[:, c, vc * 128:(vc + 1) * 128], ST[p][:, col(tm)])]
                        if has_state:
                            prs.append((SBF[sctr[0] % 3][:, vc * 128:(vc + 1) * 128], QTt[p][:, col(tm)]))
                        groups.append((ob[:, vc * 256 + tm2 * 128:vc * 256 + tm2 * 128 + 128], prs))
                    rd = [("VTM", c), ("ST", p), ("QTt", p)] + ([("SBF", sctr[0] % 3)] if has_state else [])
                    S.mm_multi(groups, reads=rd, wkey=(ko, "part", tm2))
                    pend[tm // 2].append((ko, "part", tm2))
                    kvps, kkv = kvb[tm]
                    dec = EQ[p][:, tm * 128 + lastc:tm * 128 + lastc + 1]
                    nxt = (sctr[0] + 1) % 3
                    if not has_state:
                        S.op("dve", lambda: nc.vector.tensor_copy(out=SBF[nxt], in_=kvps), reads=[kkv],
                             writes=[("SBF", nxt)])
                        S.op("dve", lambda: nc.vector.tensor_copy(out=Sst, in_=kvps), reads=[kkv], writes=["S"])
                    else:
                        S.op("dve", lambda: nc.vector.scalar_tensor_tensor(out=SBF[nxt], in0=Sst, scalar=dec, in1=kvps,
                                                                            op0=ALU.mult, op1=ALU.add),
                             reads=["S", ("EQ", p), kkv], writes=[("SBF", nxt)])
                        S.op("dve", lambda: nc.vector.scalar_tensor_tensor(out=Sst, in0=Sst, scalar=dec, in1=kvps,
                                                                            op0=ALU.mult, op1=ALU.add),
                             reads=["S", ("EQ", p), kkv], writes=["S"])
                    sctr[0] += 1
                    if t + 2 < 4:
                        kv(order[t + 2])
                gs = slice(gi * 512, (gi + 1) * 512)
                for half in range(2):
                    ob, ko = obanks[half]
                    o3 = ob[:, :].rearrange("p (a b) -> p a b", b=256)
                    cols = slice(gi * 512 + half * 256, gi * 512 + half * 256 + 256)
                    if d == 0:
                        S.op("act", lambda: nc.scalar.copy(out=OF[:, :, cols], in_=o3), reads=pend[half],
                             writes=[("OF", gi, half), ko])
                    else:
                        S.op("dve", lambda: nc.vector.tensor_tensor(out=OS[:, :, half * 256:(half + 1) * 256], in0=o3,
                                                                    in1=OF[:, :, cols], op=ALU.add),
                             reads=pend[half] + [("OF", gi, half)], writes=[("OS", half), ko])
                if d == 1:
                    S.op("act", lambda: nc.scalar.activation(out=OSQ, in_=OS, func=AF.Square),
                         reads=[("OS", 0), ("OS", 1)], writes=["OSQ"])
                    b7, kq = self.bank("gz", [6, 7])
                    qps = self.ps[b7]
                    S.mm(qps[:, :], [(self.onesb, OSQ[:, vc, :]) for vc in range(2)], reads=["OSQ"], wkey=kq)
                    S.op("act", lambda: nc.scalar.activation(out=RG, in_=qps[:, :], func=AF.Sqrt, scale=1.0 / 256,
                                                             bias=self.epsc),
                         reads=[kq], writes=["RG"])
                    S.op("dve", lambda: nc.vector.reciprocal(out=RG, in_=RG), reads=["RG"], writes=["RG"])
                    rg_b = RG.rearrange("p (o b) -> p o b", o=1).to_broadcast([128, 2, 512])
                    S.op("dve", lambda: nc.vector.tensor_tensor(out=OS, in0=OS, in1=rg_b, op=ALU.mult),
                         reads=[("OS", 0), ("OS", 1), "RG"], writes=[("OS", 0), ("OS", 1)])
                    for vc in range(2):
                        S.op("dve", lambda vc=vc: nc.vector.scalar_tensor_tensor(
                            out=OG[:, h * 2 + vc, gs], in0=OS[:, vc, :], scalar=self.pcol(P_GNG + h * 2 + vc),
                            in1=OG[:, h * 2 + vc, gs], op0=ALU.mult, op1=ALU.mult),
                            reads=[("OS", 0), ("OS", 1), ("og", h * 2 + vc, gi)], writes=[("og", h * 2 + vc, gi)])

            par = []
            for n_, (d, gi) in enumerate(items):
                par.append(gctr[0] % 2)
                gctr[0] += 1
            SI(items[0][0], items[0][1], par[0])
            for n_, (d, gi) in enumerate(items):
                if n_ + 1 < len(items):
                    SI(items[n_ + 1][0], items[n_ + 1][1], par[n_ + 1])
                SD(d, gi, par[n_], first_group=(n_ == 0 or n_ == 4))
        self.mixer_out(A, OG, lambda tb: [("og", c, tb) for c in range(8)], src, dst, seq, gBpost, gqk, XT)

    def build(self):
        self.declare()
        self.setup()
        self.make_plan()
        self.w_prime()
        S = self.S
        nsub = len(self.plan)
        for seq in range(2):
            for si, sub in enumerate(self.plan):
                src = self.x_in if si == 0 else self.xs
                dst = self.y_out if si == nsub - 1 else self.xs
                if sub == "mix0":
                    self.conv_mixer(src, dst, seq)
                elif sub == "mix1":
                    self.gla_mixer(src, dst, seq)
                else:
                    self.ffn(int(sub[3]), src, dst, seq)
        S.barrier(engines=("sp",))
        self.es.close()
        return self.nc


def host_inputs(inp):
    f = lambda a: np.ascontiguousarray(np.asarray(a, dtype=np.float32))
    pm = lambda v: f(v).reshape(-1, 128).T
    prm = np.zeros((128, NPRM), np.float32)
    for l in range(2):
        prm[:, P_MIXPRE + 8 * l:P_MIXPRE + 8 * l + 8] = pm(inp["mix_pre_g"][l])
        prm[:, P_MIXPOST + 8 * l:P_MIXPOST + 8 * l + 8] = pm(inp["mix_post_g"][l])
        prm[:, P_FFNPRE + 8 * l:P_FFNPRE + 8 * l + 8] = pm(inp["ffn_pre_g"][l])
        prm[:, P_FFNPOST + 8 * l:P_FFNPOST + 8 * l + 8] = pm(inp["ffn_post_g"][l])
    dww = f(inp["cv_dw_w"][0])
    prm[:, P_DWW:P_DWW + 124] = dww.reshape(31, 4, 128).transpose(2, 1, 0).reshape(128, 124)
    prm[:, P_DWB:P_DWB + 4] = pm(inp["cv_dw_b"][0])
    prm[:, P_LNG:P_LNG + 4] = pm(inp["cv_ln_g"][0])
    prm[:, P_LNB:P_LNB + 4] = pm(inp["cv_ln_b"][0])
    scw = f(inp["cv_sc_w"][0])
    prm[:, P_SCW:P_SCW + 12] = scw.reshape(3, 4, 128).transpose(2, 1, 0).reshape(128, 12)
    prm[:, P_GNG:P_GNG + 8] = pm(inp["gla_gn_g"][0])
    j = np.arange(128)[:, None]
    i = np.arange(128)[None, :]
    cst = np.concatenate([
        np.eye(128), np.ones((128, 128)), (j <= i) * 1.0, (j >= i) * 1.0,
        (j <= i) * (-1.0 / 16.0), (j >= i) * (-1.0 / 16.0)], axis=1).astype(np.float32)
    gvec = np.stack([f(inp["mix_pre_g"][0]), f(inp["mix_pre_g"][1]), f(inp["mix_post_g"][0]),
                     f(inp["mix_post_g"][1]), f(inp["ffn_pre_g"][0]), f(inp["ffn_pre_g"][1]),
                     f(inp["ffn_post_g"][0]), f(inp["ffn_post_g"][1])], axis=0)
    wa = np.stack([np.concatenate([f(inp["gla_wa2_f"][0]), f(inp["gla_ba2_f"])[0:1]], axis=0),
                   np.concatenate([f(inp["gla_wa2_b"][0]), f(inp["gla_ba2_b"])[0:1]], axis=0)], axis=0)
    shared = {
        "prm": prm, "cst": cst, "gvec": f(gvec), "wa": f(wa),
        "cv_w_in": f(inp["cv_w_in"][0]), "cv_w_out": f(inp["cv_w_out"][0]),
        "gla_w_in": f(inp["gla_w_in"][0]), "gla_w_out": f(inp["gla_w_out"][0]),
        "ffn_w_gu": f(inp["ffn_w_gu"]), "ffn_w_down": f(inp["ffn_w_down"]),
    }
    x = f(inp["x"]).reshape(NCORES, TOK, D)
    return [dict(shared, x=x[c]) for c in range(NCORES)]


_CACHE = {}


def run(inp, plan=("mix0", "ffn0", "mix1", "ffn1")):
    plan = tuple(plan)
    if plan not in _CACHE:
        _CACHE[plan] = Builder(plan).build()
    nc = _CACHE[plan]
    in_maps = host_inputs(inp)
    res = run_bass_kernel_spmd(nc, in_maps, core_ids=list(range(NCORES)))
    out = np.stack([np.asarray(r["y"], dtype=np.float32) for r in res.results], axis=0)
    return out.reshape(16, SEQ, D)


def kernel(**inputs):
    return run(inputs)
```

```python
import math
from contextlib import ExitStack

import numpy as np
import concourse.bass as bass
import concourse.mybir as mybir
from concourse.bass_utils import run_bass_kernel_spmd
from concourse.alu_op_type import AluOpType as ALU

F32 = mybir.dt.float32
BF16 = mybir.dt.bfloat16
AF = mybir.ActivationFunctionType

NCORES = 8
D = 1024
SEQ = 2048
TOK = 4096
DFF = 2816
EPS = 1e-6
SLOT_ELEMS = 5632
NSLOT = 4
NPRM = 220

P_MIXPRE, P_MIXPOST, P_FFNPRE, P_FFNPOST = 0, 16, 32, 48
P_DWW = 64
P_DWB = 188
P_LNG = 192
P_LNB = 196
P_SCW = 200
P_GNG = 212


class Sched:
    def __init__(self, nc, es):
        self.nc = nc
        self.E = {"pe": nc.tensor, "act": nc.scalar, "dve": nc.vector, "pool": nc.gpsimd, "sp": nc.sync}
        self.psem = {}
        for e in ("pe", "act", "dve", "pool"):
            self.psem[e] = es.enter_context(nc.semaphore(f"p_{e}"))
        self.pcnt = {e: 0 for e in self.psem}
        self.waited = {e: {} for e in self.E}
        self.state = {}
        self.dsems = {}
        for q in ("sp", "pool"):
            self.dsems[q] = [[es.enter_context(nc.semaphore(f"d_{q}{i}")), 0, f"d_{q}{i}"] for i in range(10)]
        self.drr = {q: 0 for q in self.dsems}
        self.slotsem = [[es.enter_context(nc.semaphore(f"w_{i}")), 0, f"w_{i}"] for i in range(NSLOT)]

    def _collect(self, reads, writes):
        need = {}

        def add(n, s, v):
            if n not in need or need[n][1] < v:
                need[n] = (s, v)

        for k in reads:
            st = self.state.get(k)
            if st and st[0] is not None:
                add(*st[0])
        for k in writes:
            st = self.state.get(k)
            if st:
                if st[0] is not None:
                    add(*st[0])
                for n, (s, v) in st[1].items():
                    add(n, s, v)
        return need

    def _wait(self, e, need):
        for n, (s, v) in need.items():
            if self.waited[e].get(n, 0) >= v:
                continue
            self.E[e].wait_ge(s, v)
            self.waited[e][n] = v

    def _commit(self, tok, reads, writes):
        n, s, v = tok
        for k in reads:
            st = self.state.setdefault(k, [None, {}])
            st[1][n] = (s, v)
        for k in writes:
            self.state[k] = [tok, {}]

    def op(self, e, fn, reads=(), writes=()):
        need = self._collect(reads, writes)
        self._wait(e, need)
        ins = fn()
        self.pcnt[e] += 1
        ins.then_inc(self.psem[e], 1)
        tok = (f"p_{e}", self.psem[e], self.pcnt[e])
        self._commit(tok, reads, writes)
        return tok

    def mm(self, out, pairs, reads, wkey):
        need = self._collect(reads, [wkey])
        self._wait("pe", need)
        n = len(pairs)
        ins = None
        for i, (l, r) in enumerate(pairs):
            ins = self.nc.tensor.matmul(out, l, r, start=(i == 0), stop=(i == n - 1))
        self.pcnt["pe"] += 1
        ins.then_inc(self.psem["pe"], 1)
        tok = ("p_pe", self.psem["pe"], self.pcnt["pe"])
        self._commit(tok, reads, [wkey])
        return tok

    def mm_multi(self, groups, reads, wkey):
        need = self._collect(reads, [wkey])
        self._wait("pe", need)
        ins = None
        for out, pairs in groups:
            n = len(pairs)
            for i, (l, r) in enumerate(pairs):
                ins = self.nc.tensor.matmul(out, l, r, start=(i == 0), stop=(i == n - 1))
        self.pcnt["pe"] += 1
        ins.then_inc(self.psem["pe"], 1)
        tok = ("p_pe", self.psem["pe"], self.pcnt["pe"])
        self._commit(tok, reads, [wkey])
        return tok

    def tr(self, items, ident, reads, wkey):
        need = self._collect(reads, [wkey])
        self._wait("pe", need)
        ins = None
        for out, in_ in items:
            ins = self.nc.tensor.transpose(out, in_, ident)
        self.pcnt["pe"] += 1
        ins.then_inc(self.psem["pe"], 1)
        tok = ("p_pe", self.psem["pe"], self.pcnt["pe"])
        self._commit(tok, reads, [wkey])
        return tok

    def dma(self, q, out, in_, reads=(), writes=(), semrec=None, nonc=False):
        if semrec is None:
            semrec = self.dsems[q][self.drr[q]]
            self.drr[q] = (self.drr[q] + 1) % len(self.dsems[q])
            need = self._collect(reads, writes)
            if semrec[1] > 0:
                need[semrec[2]] = (semrec[0], semrec[1])
        else:
            need = self._collect(reads, writes)
        self._wait(q, need)
        if nonc:
            ins = self.E[q].dma_start(out=out, in_=in_, allow_slow_non_contiguous=True)
        else:
            ins = self.E[q].dma_start(out=out, in_=in_)
        semrec[1] += 16
        ins.then_inc(semrec[0], 16)
        tok = (semrec[2], semrec[0], semrec[1])
        self._commit(tok, reads, writes)
        return tok

    def barrier(self, engines=("pe", "act", "dve", "sp")):
        need = {}
        for e in self.psem:
            if self.pcnt[e] > 0:
                need[f"p_{e}"] = (self.psem[e], self.pcnt[e])
        for q in self.dsems:
            for rec in self.dsems[q]:
                if rec[1] > 0:
                    need[rec[2]] = (rec[0], rec[1])
        for e in engines:
            self._wait(e, need)


class Arena:
    def __init__(self, big, base_b, limit_b):
        self.big = big
        self.base = base_b
        self.limit = limit_b
        self.off = base_b

    def reset(self):
        self.off = self.base

    def alloc(self, shape, dt):
        esz = 2 if dt == BF16 else 4
        n = 1
        for s in shape[1:]:
            n *= s
        nb = (n * esz + 63) // 64 * 64
        assert self.off + nb <= self.limit, f"arena overflow {self.off + nb} > {self.limit}"
        a = self.big[0:shape[0], self.off // 2: self.off // 2 + n * esz // 2]
        self.off += nb
        if dt == F32:
            a = a.bitcast(F32)
        if len(shape) == 3:
            a = a.rearrange("p (a b) -> p a b", b=shape[2])
        elif len(shape) == 4:
            a = a.rearrange("p (a b c) -> p a b c", b=shape[2], c=shape[3])
        return a


class Builder:
    def __init__(self, plan):
        self.plan = plan
        self.nc = bass.Bass("TRN2", target_bir_lowering=False)
        self.es = ExitStack()

    def declare(self):
        nc = self.nc
        dt = lambda name, shape, kind="ExternalInput": nc.dram_tensor(name, shape, F32, kind=kind).ap()
        self.x_in = dt("x", [TOK, D])
        self.y_out = dt("y", [TOK, D], "ExternalOutput")
        self.xs = dt("xs", [TOK, D], "Internal")
        self.prm_d = dt("prm", [128, NPRM])
        self.cst_d = dt("cst", [128, 6 * 128])
        self.gvec_d = dt("gvec", [8, D])
        self.wa_d = dt("wa", [2, 17, 512])
        self.cv_w_in = dt("cv_w_in", [D, 2560])
        self.cv_w_out = dt("cv_w_out", [D, D])
        self.gla_w_in = dt("gla_w_in", [D, 3104])
        self.gla_w_out = dt("gla_w_out", [D, D])
        self.ffn_w_gu = dt("ffn_w_gu", [2, D, 2 * DFF])
        self.ffn_w_down = dt("ffn_w_down", [2, DFF, D])

    def setup(self):
        nc, es = self.nc, self.es
        self.S = Sched(nc, es)
        S = self.S
        total_b = 212800
        self.big = es.enter_context(nc.sbuf_tensor("big", [128, total_b // 2], BF16))
        self.ps = [es.enter_context(nc.psum_tensor(f"ps{i}", [128, 512], F32)) for i in range(8)]
        carve = Arena(self.big, 0, total_b)
        self.slots = [carve.alloc([128, SLOT_ELEMS], BF16) for _ in range(NSLOT)]
        self.identb = carve.alloc([128, 128], BF16)
        self.onesb = carve.alloc([128, 128], BF16)
        self.maskf = carve.alloc([128, 128], BF16)
        self.maskb = carve.alloc([128, 128], BF16)
        self.trif = carve.alloc([128, 128], F32)
        self.trib = carve.alloc([128, 128], F32)
        self.prm = carve.alloc([128, NPRM], F32)
        self.arena = Arena(self.big, carve.off, total_b)
        c = self.cst_d
        S.dma("pool", self.identb, c[:, 0:128], writes=["c0"])
        S.dma("pool", self.onesb, c[:, 128:256], writes=["c1"])
        S.dma("pool", self.maskf, c[:, 256:384], writes=["c2"])
        S.dma("pool", self.maskb, c[:, 384:512], writes=["c3"])
        S.dma("sp", self.trif, c[:, 512:640], writes=["c4"])
        S.dma("sp", self.trib, c[:, 640:768], writes=["c5"])
        S.dma("sp", self.prm, self.prm_d[:, :], writes=["c6"])
        S.barrier()
        self.wplan = []
        self.wnext_issue = 0
        self.wnext_use = 0
        self.psrr = {}

    def bank(self, role, banks):
        i = self.psrr.get(role, 0)
        self.psrr[role] = i + 1
        b = banks[i % len(banks)]
        return b, ("ps", b)

    def pcol(self, c0, n=1):
        return self.prm[:, c0:c0 + n]

    def w_issue(self, idx):
        if idx >= len(self.wplan):
            return
        S = self.S
        slot = idx % NSLOT
        rec = S.slotsem[slot]
        S._wait("pool", S._collect([], [("w", slot)]))
        for (eoff, shp, src) in self.wplan[idx]:
            n = shp[0] * shp[1]
            dst = self.slots[slot][:, eoff:eoff + n].rearrange("p (a b) -> p a b", b=shp[1])
            ins = self.nc.gpsimd.dma_start(out=dst, in_=src)
            rec[1] += 16
            ins.then_inc(rec[0], 16)
        S._commit((rec[2], rec[0], rec[1]), [], [("w", slot)])

    def w_prime(self):
        for i in range(NSLOT):
            self.w_issue(i)
        self.wnext_issue = NSLOT

    def w_get(self, tag):
        idx = self.wnext_use
        assert self.wtags[idx] == tag, (idx, self.wtags[idx], tag)
        self.wnext_use += 1
        slot = idx % NSLOT
        return self.slots[slot], ("w", slot)

    def w_release(self, n=1):
        for _ in range(n):
            self.w_issue(self.wnext_issue)
            self.wnext_issue += 1

    def add_load(self, tag, parts):
        self.wplan.append(parts)
        self.wtags.append(tag)

    @staticmethod
    def wsrc(w2d, r0, nr, c0, ncol):
        return w2d[r0:r0 + nr, c0:c0 + ncol].rearrange("(kc p) n -> p kc n", p=128)

    def make_plan(self):
        self.wtags = []
        for s in range(2):
            for sub in self.plan:
                if sub == "mix0":
                    w = self.cv_w_in
                    for nm, c0 in (("aval", 0), ("agate", 512), ("cgate", 1536), ("v", 2048), ("bgate", 1024)):
                        self.add_load(nm, [(0, (8, 512), self.wsrc(w, 0, D, c0, 512))])
                    for h in range(2):
                        self.add_load(f"wout{h}", [(0, (8, 512), self.wsrc(self.cv_w_out, 0, D, h * 512, 512))])
                elif sub == "mix1":
                    w = self.gla_w_in
                    self.add_load("gates", [(0, (8, 32), self.wsrc(w, 0, D, 3072, 32))])
                    for h in range(2):
                        self.add_load(f"r{h}", [(0, (8, 512), self.wsrc(w, 0, D, 2048 + h * 512, 512))])
                    for h in range(4):
                        self.add_load(f"qk{h}", [(0, (8, 128), self.wsrc(w, 0, D, h * 128, 128)),
                                                 (1024, (8, 128), self.wsrc(w, 0, D, 512 + h * 128, 128))])
                        self.add_load(f"v{h}", [(0, (8, 256), self.wsrc(w, 0, D, 1024 + h * 256, 256))])
                    for h in range(2):
                        self.add_load(f"wout{h}", [(0, (8, 512), self.wsrc(self.gla_w_out, 0, D, h * 512, 512))])
                elif sub in ("ffn0", "ffn1"):
                    l = int(sub[3])
                    wg = self.ffn_w_gu[l]
                    wd = self.ffn_w_down[l]
                    for hb in range(2):
                        for L in range(11):
                            self.add_load(f"gu{L}", [(0, (8, 256), self.wsrc(wg, 0, D, L * 256, 256)),
                                                     (2048, (8, 256), self.wsrc(wg, 0, D, DFF + L * 256, 256))])
                        for half in range(2):
                            for part in range(2):
                                self.add_load(f"dn{half}{part}",
                                              [(0, (11, 512), self.wsrc(wd, part * 1408, 1408, half * 512, 512))])

    def load_gb(self, A, row):
        g = A.alloc([128, D], F32)
        self.S.dma("sp", g, self.gvec_d[row:row + 1, :].to_broadcast([128, D]), writes=[("gb", row)])
        return g, ("gb", row)

    def pn_stats(self, xt, xkey, rs_col, rskey, ss_col, sskey):
        S, nc = self.S, self.nc
        S.op("act", lambda: nc.scalar.activation(out=self.sqj, in_=xt, func=AF.Square, accum_out=ss_col),
             reads=[xkey], writes=[sskey])
        S.op("act", lambda: nc.scalar.activation(out=rs_col, in_=ss_col, func=AF.Sqrt, scale=1.0 / D, bias=self.epsc),
             reads=[sskey], writes=[rskey])
        S.op("dve", lambda: nc.vector.reciprocal(out=rs_col, in_=rs_col), reads=[rskey], writes=[rskey])

    def pn_apply(self, xt, xkey, rs_col, rskey, gB, gkey, htm, htmkey, dst_cols, hkey):
        S, nc = self.S, self.nc
        S.op("dve", lambda: nc.vector.scalar_tensor_tensor(out=htm, in0=xt, scalar=rs_col, in1=gB,
                                                            op0=ALU.mult, op1=ALU.mult),
             reads=[xkey, rskey, gkey], writes=[htmkey])
        b, bkey = self.bank("pt", [0, 1])
        pv = self.ps[b][:, :].bitcast(BF16)
        S.tr([(pv[:, c * 128:(c + 1) * 128], htm[:, c * 128:(c + 1) * 128]) for c in range(8)], self.identb,
             reads=[htmkey], wkey=bkey)
        S.op("act", lambda: nc.scalar.copy(out=dst_cols, in_=pv.rearrange("p (c t) -> p c t", t=128)),
             reads=[bkey], writes=[hkey])

    def postnorm_tile(self, ops, okeys, xt, xkey, gB, gkey, t1, t1key, ssb, idx):
        S, nc = self.S, self.nc
        c0 = ssb[:, 4 * idx:4 * idx + 1]
        c1 = ssb[:, 4 * idx + 1:4 * idx + 2]
        c2 = ssb[:, 4 * idx + 2:4 * idx + 3]
        k = ("ssb", idx)
        S.op("act", lambda: nc.scalar.activation(out=self.sqj[:, 0:512], in_=ops[0], func=AF.Square, accum_out=c0),
             reads=[okeys[0]], writes=[(k, 0)])
        S.op("act", lambda: nc.scalar.activation(out=self.sqj[:, 512:1024], in_=ops[1], func=AF.Square, accum_out=c1),
             reads=[okeys[1]], writes=[(k, 1)])
        S.op("dve", lambda: nc.vector.tensor_tensor(out=c2, in0=c0, in1=c1, op=ALU.add),
             reads=[(k, 0), (k, 1)], writes=[(k, 2)])
        S.op("act", lambda: nc.scalar.activation(out=c2, in_=c2, func=AF.Sqrt, scale=1.0 / D, bias=self.epsc),
             reads=[(k, 2)], writes=[(k, 2)])
        S.op("dve", lambda: nc.vector.reciprocal(out=c2, in_=c2), reads=[(k, 2)], writes=[(k, 2)])
        for h in range(2):
            S.op("dve", lambda h=h: nc.vector.scalar_tensor_tensor(
                out=t1[:, h * 512:(h + 1) * 512], in0=ops[h], scalar=c2, in1=gB[:, h * 512:(h + 1) * 512],
                op0=ALU.mult, op1=ALU.mult), reads=[okeys[h], (k, 2), gkey], writes=[(t1key, h)])
        S.op("dve", lambda: nc.vector.tensor_tensor(out=xt, in0=xt, in1=t1, op=ALU.add),
             reads=[xkey, (t1key, 0), (t1key, 1)], writes=[xkey])

    def common_tmps(self, A):
        self.sqj = A.alloc([128, D], BF16)
        self.epsc = A.alloc([128, 1], F32)
        self.S.op("dve", lambda: self.nc.vector.memset(self.epsc, EPS), writes=["epsc"])
        self.S.barrier()

    def ffn(self, l, src, dst, seq):
        S, nc, A = self.S, self.nc, self.arena
        S.barrier()
        A.reset()
        self.common_tmps(A)
        gBpre, gpk = self.load_gb(A, 4 + l)
        gBpost, gqk = self.load_gb(A, 6 + l)
        XH = A.alloc([128, 8, D], F32)
        hT = A.alloc([128, 8, 1024], BF16)
        M0 = hT.rearrange("p a b -> p (a b)").bitcast(F32).rearrange("p (a b) -> p a b", b=512)
        ACTB = A.alloc([128, 22, 1024], BF16)
        HTM = [A.alloc([128, D], BF16) for _ in range(2)]
        SG = [A.alloc([128, 512], BF16) for _ in range(2)]
        T1 = [A.alloc([128, D], F32) for _ in range(2)]
        SS = A.alloc([128, 16], F32)
        RS = A.alloc([128, 16], F32)
        SSB = A.alloc([128, 64], F32)
        for hb in range(2):
            S.barrier()
            t0 = seq * SEQ + hb * 1024
            for i in range(8):
                r0 = t0 + i * 128
                S.dma("sp", XH[:, i, :], src[r0:r0 + 128, :], reads=[("xd", r0)], writes=[("XH", i)])
            for i in range(8):
                self.pn_stats(XH[:, i, :], ("XH", i), RS[:, i:i + 1], ("rs", i), SS[:, i:i + 1], ("ss", i))
            for i in range(8):
                self.pn_apply(XH[:, i, :], ("XH", i), RS[:, i:i + 1], ("rs", i), gBpre, gpk,
                              HTM[i % 2], ("htm", i % 2), hT[:, :, i * 128:(i + 1) * 128], ("hT", i, "w"))
            for L in range(11):
                W, wk = self.w_get(f"gu{L}")
                Wg = W[:, 0:2048].rearrange("p (a b) -> p a b", b=256)
                Wu = W[:, 2048:4096].rearrange("p (a b) -> p a b", b=256)
                for cc in range(2):
                    c = 2 * L + cc
                    for tb in range(2):
                        bg, kg = self.bank("g", [2, 3])
                        bu, ku = self.bank("u", [4, 5])
                        rhs = lambda kc: hT[:, kc, tb * 512:(tb + 1) * 512]
                        S.mm(self.ps[bg][:, :], [(Wg[:, kc, cc * 128:(cc + 1) * 128], rhs(kc)) for kc in range(8)],
                             reads=[wk] + [("hT", 4 * tb + q, "w") for q in range(4)], wkey=kg)
                        S.mm(self.ps[bu][:, :], [(Wu[:, kc, cc * 128:(cc + 1) * 128], rhs(kc)) for kc in range(8)],
                             reads=[wk] + [("hT", 4 * tb + q, "w") for q in range(4)], wkey=ku)
                        sg = SG[(2 * c + tb) % 2]
                        sgk = ("sg", (2 * c + tb) % 2)
                        S.op("act", lambda: nc.scalar.activation(out=sg, in_=self.ps[bg][:, :], func=AF.Silu),
                             reads=[kg], writes=[sgk])
                        S.op("dve", lambda: nc.vector.tensor_tensor(out=ACTB[:, c, tb * 512:(tb + 1) * 512], in0=sg,
                                                                    in1=self.ps[bu][:, :], op=ALU.mult),
                             reads=[sgk, ku], writes=[("actb", c, tb)])
                self.w_release()
            for half in range(2):
                Wa, wka = self.w_get(f"dn{half}0")
                Wb, wkb = self.w_get(f"dn{half}1")
                Wa3 = Wa[:, 0:5632].rearrange("p (a b) -> p a b", b=512)
                Wb3 = Wb[:, 0:5632].rearrange("p (a b) -> p a b", b=512)
                for i in range(8):
                    bf, kf = self.bank("f", [6, 7])
                    tb = i // 4
                    pairs = []
                    for kc in range(22):
                        w3 = Wa3 if kc < 11 else Wb3
                        pairs.append((ACTB[:, kc, i * 128:(i + 1) * 128], w3[:, kc % 11, :]))
                    S.mm(self.ps[bf][:, :], pairs, reads=[wka, wkb] + [("actb", kc, tb) for kc in range(22)], wkey=kf)
                    c0 = SSB[:, 4 * i + half:4 * i + half + 1]
                    k = ("ssb", i)
                    if half == 0:
                        S.op("act", lambda: nc.scalar.activation(out=self.sqj[:, 0:512], in_=self.ps[bf][:, :],
                                                                 func=AF.Square, accum_out=c0),
                             reads=[kf], writes=[(k, 0)])
                        S.op("act", lambda: nc.scalar.copy(out=M0[:, i, :], in_=self.ps[bf][:, :]),
                             reads=[kf], writes=[("m0", i)] + [("hT", q, "w") for q in range(8)])
                    else:
                        c2 = SSB[:, 4 * i + 2:4 * i + 3]
                        S.op("act", lambda: nc.scalar.activation(out=self.sqj[:, 512:1024], in_=self.ps[bf][:, :],
                                                                 func=AF.Square, accum_out=c0),
                             reads=[kf], writes=[(k, 1)])
                        S.op("dve", lambda: nc.vector.tensor_tensor(out=c2, in0=SSB[:, 4 * i:4 * i + 1], in1=c0,
                                                                    op=ALU.add),
                             reads=[(k, 0), (k, 1)], writes=[(k, 2)])
                        S.op("act", lambda: nc.scalar.activation(out=c2, in_=c2, func=AF.Sqrt, scale=1.0 / D,
                                                                 bias=self.epsc),
                             reads=[(k, 2)], writes=[(k, 2)])
                        S.op("dve", lambda: nc.vector.reciprocal(out=c2, in_=c2), reads=[(k, 2)], writes=[(k, 2)])
                        t1 = T1[i % 2]
                        tk = ("t1", i % 2)
                        S.op("dve", lambda: nc.vector.scalar_tensor_tensor(
                            out=t1[:, 512:1024], in0=self.ps[bf][:, :], scalar=c2, in1=gBpost[:, 512:1024],
                            op0=ALU.mult, op1=ALU.mult), reads=[kf, (k, 2), gqk], writes=[(tk, 1)])
                        S.op("dve", lambda: nc.vector.scalar_tensor_tensor(
                            out=t1[:, 0:512], in0=M0[:, i, :], scalar=c2, in1=gBpost[:, 0:512],
                            op0=ALU.mult, op1=ALU.mult), reads=[("m0", i), (k, 2), gqk], writes=[(tk, 0)])
                        S.op("dve", lambda: nc.vector.tensor_tensor(out=XH[:, i, :], in0=XH[:, i, :], in1=t1,
                                                                    op=ALU.add),
                             reads=[("XH", i), (tk, 0), (tk, 1)], writes=[("XH", i)])
                        r0 = t0 + i * 128
                        S.dma("sp", dst[r0:r0 + 128, :], XH[:, i, :], reads=[("XH", i)], writes=[("xd", r0)])
                self.w_release(2)

    def mixer_prenorm(self, A, src, seq, gB, gk, hT):
        S = self.S
        XT = [A.alloc([128, D], F32) for _ in range(3)]
        HTM = [A.alloc([128, D], BF16) for _ in range(2)]
        SS = A.alloc([128, 16], F32)
        RS = A.alloc([128, 16], F32)
        def stats(i):
            r0 = seq * SEQ + i * 128
            S.dma("sp", XT[i % 3], src[r0:r0 + 128, :], reads=[("xd", r0)], writes=[("XT", i % 3)])
            self.pn_stats(XT[i % 3], ("XT", i % 3), RS[:, i:i + 1], ("rs", i), SS[:, i:i + 1], ("ss", i))

        stats(0)
        stats(1)
        for i in range(16):
            if i + 2 < 16:
                stats(i + 2)
            self.pn_apply(XT[i % 3], ("XT", i % 3), RS[:, i:i + 1], ("rs", i), gB, gk, HTM[i % 2],
                          ("htm", i % 2), hT[:, :, i * 128:(i + 1) * 128], ("hT", i, "w"))
        return XT

    def mixer_out(self, A, CAT, catkeys, src, dst, seq, gB, gk, XT):
        S, nc = self.S, self.nc
        T1 = [A.alloc([128, D], F32) for _ in range(1)]
        SSB = A.alloc([128, 64], F32)
        W0, wk0 = self.w_get("wout0")
        W1, wk1 = self.w_get("wout1")
        W3 = [W0[:, 0:4096].rearrange("p (a b) -> p a b", b=512), W1[:, 0:4096].rearrange("p (a b) -> p a b", b=512)]
        wks = [wk0, wk1]
        for i in range(16):
            r0 = seq * SEQ + i * 128
            xt = XT[i % 3]
            xk = ("XT", i % 3)
            S.dma("sp", xt, src[r0:r0 + 128, :], reads=[("xd", r0)], writes=[xk])
            ops, oks = [], []
            for h in range(2):
                b, bk = self.bank("o", [2, 3, 4, 5])
                S.mm(self.ps[b][:, :], [(CAT[:, kc, i * 128:(i + 1) * 128], W3[h][:, kc, :]) for kc in range(8)],
                     reads=[wks[h]] + catkeys(i // 4), wkey=bk)
                ops.append(self.ps[b][:, :])
                oks.append(bk)
            self.postnorm_tile(ops, oks, xt, xk, gB, gk, T1[0], ("t1", 0), SSB, i)
            S.dma("sp", dst[r0:r0 + 128, :], xt, reads=[xk], writes=[("xd", r0)])
        self.w_release(2)

    def conv_mixer(self, src, dst, seq):
        S, nc, A = self.S, self.nc, self.arena
        S.barrier()
        A.reset()
        self.common_tmps(A)
        gBpre, gpk = self.load_gb(A, 0)
        gBpost, gqk = self.load_gb(A, 2)
        hT = A.alloc([128, 8, SEQ], BF16)
        AB = A.alloc([128, 4, 2080], BF16)
        CV = A.alloc([128, 4, 2052], BF16)
        BG = A.alloc([128, 4, SEQ], BF16)
        DGs = [A.alloc([128, 31, 128], BF16) for _ in range(2)]
        DGB = A.alloc([128, 3, 128], BF16)
        SIG = [A.alloc([128, 512], F32) for _ in range(2)]
        VS = [A.alloc([128, 512], BF16) for _ in range(2)]
        YSQ = A.alloc([128, 4, 512], BF16)
        MEAN = A.alloc([128, 512], F32)
        MSQ = A.alloc([128, 512], F32)
        SDv = A.alloc([128, 512], F32)
        Dt = [A.alloc([128, 512], F32) for _ in range(2)]
        Zt = [A.alloc([128, 512], F32) for _ in range(2)]
        S.op("dve", lambda: nc.vector.memset(AB[:, :, 0:15], 0.0), writes=["abh0"])
        S.op("dve", lambda: nc.vector.memset(AB[:, :, 2063:2080], 0.0), writes=["abh1"])
        S.op("dve", lambda: nc.vector.memset(CV[:, :, 0:1], 0.0), writes=["cvh0"])
        S.op("dve", lambda: nc.vector.memset(CV[:, :, 2049:2052], 0.0), writes=["cvh1"])
        XT = self.mixer_prenorm(A, src, seq, gBpre, gpk, hT)

        def proj(W, wk, j, tb, role, banks):
            b, bk = self.bank(role, banks)
            S.mm(self.ps[b][:, :], [(W[:, kc, j * 128:(j + 1) * 128], hT[:, kc, tb * 512:(tb + 1) * 512])
                                    for kc in range(8)], reads=[wk] + [("hT", 4 * tb + q_, "w") for q_ in range(4)], wkey=bk)
            return self.ps[b][:, :], bk

        def w3(tag):
            W, wk = self.w_get(tag)
            return W[:, 0:4096].rearrange("p (a b) -> p a b", b=512), wk

        Wv, wkv = w3("aval")
        Wg, wkg = w3("agate")
        n = 0
        for j in range(4):
            for tb in range(4):
                pv, kv = proj(Wv, wkv, j, tb, "pa", [2, 3])
                pg, kg = proj(Wg, wkg, j, tb, "pb", [4, 5])
                sg, sk = SIG[n % 2], ("sig", n % 2)
                S.op("act", lambda: nc.scalar.activation(out=sg, in_=pg, func=AF.Sigmoid), reads=[kg], writes=[sk])
                S.op("dve", lambda: nc.vector.tensor_tensor(out=AB[:, j, 15 + tb * 512:15 + (tb + 1) * 512],
                                                            in0=pv, in1=sg, op=ALU.mult),
                     reads=[kv, sk], writes=[("Ag", j, tb)])
                n += 1
        self.w_release(2)
        Wc, wkc = w3("cgate")
        Wvv, wkvv = w3("v")
        for j in range(4):
            for tb in range(4):
                pc, kc_ = proj(Wc, wkc, j, tb, "pa", [2, 3])
                pv, kv = proj(Wvv, wkvv, j, tb, "pb", [4, 5])
                vs, vk = VS[n % 2], ("vs", n % 2)
                S.op("act", lambda: nc.scalar.copy(out=vs, in_=pv), reads=[kv], writes=[vk])
                S.op("dve", lambda: nc.vector.tensor_tensor(out=CV[:, j, 1 + tb * 512:1 + (tb + 1) * 512],
                                                            in0=pc, in1=vs, op=ALU.mult),
                     reads=[kc_, vk], writes=[("CV", j, tb)])
                n += 1
        self.w_release(2)
        Wb, wkb = w3("bgate")
        for j in range(4):
            for tb in range(4):
                pb, kb = proj(Wb, wkb, j, tb, "pa", [2, 3])
                S.op("act", lambda: nc.scalar.copy(out=BG[:, j, tb * 512:(tb + 1) * 512], in_=pb),
                     reads=[kb], writes=[("BG", j, tb)])
        self.w_release(1)
        S.barrier()
        CAT = hT
        for j in range(4):
            DG = DGs[j % 2]
            for k in range(31):
                S.op("dve", lambda k=k: nc.vector.tensor_scalar(out=DG[:, k, :], in0=self.identb,
                                                                 scalar1=self.pcol(P_DWW + j * 31 + k), scalar2=None,
                                                                 op0=ALU.mult),
                     writes=[("DG", j % 2, k)])
            for tb in range(4):
                b, bk = self.bank("cv", [2, 3])
                rd = [("DG", j % 2, k) for k in range(31)] + [("Ag", j, t) for t in (tb - 1, tb, tb + 1) if 0 <= t < 4]
                rd += [("Ar", j, tb), ("Ar", j, tb + 1), "abh0", "abh1"]
                S.mm(self.ps[b][:, :], [(DG[:, k, :], AB[:, j, tb * 512 + k:tb * 512 + k + 512]) for k in range(31)],
                     reads=rd, wkey=bk)
                S.op("act", lambda: nc.scalar.activation(out=AB[:, j, tb * 512:(tb + 1) * 512], in_=self.ps[b][:, :],
                                                         func=AF.Identity, bias=self.pcol(P_DWB + j), scale=1.0),
                     reads=[bk], writes=[("Ar", j, tb)])
        for tb in range(4):
            for j in range(4):
                S.op("act", lambda j=j: nc.scalar.activation(out=YSQ[:, j, :], in_=AB[:, j, tb * 512:(tb + 1) * 512],
                                                             func=AF.Square),
                     reads=[("Ar", j, tb)], writes=[("ysq", j)])
            bm, km = self.bank("st", [6, 7])
            S.mm(self.ps[bm][:, :], [(self.onesb, AB[:, j, tb * 512:(tb + 1) * 512]) for j in range(4)],
                 reads=[("Ar", j, tb) for j in range(4)], wkey=km)
            be, ke = self.bank("st", [6, 7])
            S.mm(self.ps[be][:, :], [(self.onesb, YSQ[:, j, :]) for j in range(4)],
                 reads=[("ysq", j) for j in range(4)], wkey=ke)
            S.op("act", lambda: nc.scalar.activation(out=MEAN, in_=self.ps[bm][:, :], func=AF.Copy, scale=1.0 / 512),
                 reads=[km], writes=["mean"])
            S.op("act", lambda: nc.scalar.activation(out=MSQ, in_=self.ps[bm][:, :], func=AF.Square, scale=1.0 / 512),
                 reads=[km], writes=["msq"])
            S.op("dve", lambda: nc.vector.scalar_tensor_tensor(out=SDv, in0=self.ps[be][:, :], scalar=1.0 / 512,
                                                                in1=MSQ, op0=ALU.mult, op1=ALU.subtract),
                 reads=[ke, "msq"], writes=["sd"])
            S.op("act", lambda: nc.scalar.activation(out=SDv, in_=SDv, func=AF.Sqrt, bias=self.epsc, scale=1.0),
                 reads=["sd"], writes=["sd"])
            S.op("dve", lambda: nc.vector.reciprocal(out=SDv, in_=SDv), reads=["sd"], writes=["sd"])
            for j in range(4):
                d, dk_ = Dt[j % 2], ("dt", j % 2)
                z, zk = Zt[j % 2], ("zt", j % 2)
                S.op("dve", lambda: nc.vector.tensor_tensor(out=d, in0=AB[:, j, tb * 512:(tb + 1) * 512], in1=MEAN,
                                                            op=ALU.subtract),
                     reads=[("Ar", j, tb), "mean"], writes=[dk_])
                S.op("dve", lambda: nc.vector.tensor_tensor(out=z, in0=d, in1=SDv, op=ALU.mult),
                     reads=[dk_, "sd"], writes=[zk])
                S.op("act", lambda: nc.scalar.activation(out=CAT[:, j, tb * 512:(tb + 1) * 512], in_=z, func=AF.Silu,
                                                         scale=self.pcol(P_LNG + j), bias=self.pcol(P_LNB + j)),
                     reads=[zk], writes=[("cat", j, tb)])
        for j in range(4):
            for k in range(3):
                S.op("dve", lambda k=k: nc.vector.tensor_scalar(out=DGB[:, k, :], in0=self.identb,
                                                                 scalar1=self.pcol(P_SCW + j * 3 + k), scalar2=None,
                                                                 op0=ALU.mult),
                     writes=[("DGB", k)])
            for tb in range(4):
                b, bk = self.bank("cv", [2, 3])
                rd = [("DGB", k) for k in range(3)] + [("CV", j, t) for t in (tb - 1, tb, tb + 1) if 0 <= t < 4]
                rd += ["cvh0", "cvh1"]
                S.mm(self.ps[b][:, :], [(DGB[:, k, :], CV[:, j, tb * 512 + k:tb * 512 + k + 512]) for k in range(3)],
                     reads=rd, wkey=bk)
                S.op("dve", lambda: nc.vector.tensor_tensor(out=CAT[:, 4 + j, tb * 512:(tb + 1) * 512],
                                                            in0=self.ps[b][:, :], in1=BG[:, j, tb * 512:(tb + 1) * 512],
                                                            op=ALU.mult),
                     reads=[bk, ("BG", j, tb)], writes=[("cat", 4 + j, tb)])
        self.mixer_out(A, CAT, lambda tb: [("cat", c, tb) for c in range(8)], src, dst, seq, gBpost, gqk, XT)

    def gla_mixer(self, src, dst, seq):
        S, nc, A = self.S, self.nc, self.arena
        S.barrier()
        A.reset()
        self.common_tmps(A)
        gBpre, gpk = self.load_gb(A, 1)
        gBpost, gqk = self.load_gb(A, 3)
        hT = A.alloc([128, 8, SEQ], BF16)
        OG = A.alloc([128, 8, SEQ], BF16)
        GT = [A.alloc([17, SEQ], BF16) for _ in range(2)]
        WA = [A.alloc([17, 512], BF16) for _ in range(2)]
        QT = A.alloc([128, SEQ], BF16)
        KT = A.alloc([128, SEQ], BF16)
        VTM = A.alloc([128, 16, 256], BF16)
        OF = A.alloc([128, 2, SEQ], BF16)
        Et = A.alloc([128, 512], F32)
        LP = [A.alloc([128, 512], F32) for _ in range(2)]
        EQ = [A.alloc([128, 512], F32) for _ in range(3)]
        EK = A.alloc([128, 512], F32)
        QTt = [A.alloc([128, 512], BF16) for _ in range(3)]
        KTt = [A.alloc([128, 512], BF16) for _ in range(2)]
        ST = [A.alloc([128, 512], BF16) for _ in range(2)]
        KTM = [A.alloc([128, 512], BF16) for _ in range(2)]
        Ust = A.alloc([128, 256], F32)
        Sst = A.alloc([128, 256], F32)
        SBF = [A.alloc([128, 256], BF16) for _ in range(3)]
        OS = A.alloc([128, 2, 512], F32)
        OSQ = self.sqj.rearrange("p (a b) -> p a b", b=512)
        RG = A.alloc([128, 512], F32)
        for d in range(2):
            S.dma("pool", WA[d], self.wa_d[d], writes=[("WA", d)])
            S.op("dve", lambda d=d: nc.vector.memset(GT[d], 1.0), writes=[("GT", d, t) for t in range(4)])
        XT = self.mixer_prenorm(A, src, seq, gBpre, gpk, hT)
        hk = lambda tb: [("hT", 4 * tb + q_, "w") for q_ in range(4)]
        W, wk = self.w_get("gates")
        Wg3 = W[:, 0:256].rearrange("p (a b) -> p a b", b=32)
        for tb in range(4):
            for d in range(2):
                b, bk = self.bank("pa", [2, 3])
                S.mm(self.ps[b][0:16, :], [(Wg3[:, kc, d * 16:(d + 1) * 16], hT[:, kc, tb * 512:(tb + 1) * 512])
                                            for kc in range(8)], reads=[wk] + hk(tb), wkey=bk)
                S.op("act", lambda: nc.scalar.copy(out=GT[d][0:16, tb * 512:(tb + 1) * 512], in_=self.ps[b][0:16, :]),
                     reads=[bk], writes=[("GT", d, tb)])
        self.w_release(1)
        for h in range(2):
            W, wk = self.w_get(f"r{h}")
            W3 = W[:, 0:4096].rearrange("p (a b) -> p a b", b=512)
            for cc in range(4):
                for tb in range(4):
                    b, bk = self.bank("pb", [4, 5])
                    S.mm(self.ps[b][:, :], [(W3[:, kc, cc * 128:(cc + 1) * 128], hT[:, kc, tb * 512:(tb + 1) * 512])
                                            for kc in range(8)], reads=[wk] + hk(tb), wkey=bk)
                    S.op("act", lambda: nc.scalar.activation(out=OG[:, h * 4 + cc, tb * 512:(tb + 1) * 512],
                                                             in_=self.ps[b][:, :], func=AF.Silu),
                         reads=[bk], writes=[("og", h * 4 + cc, tb)])
            self.w_release(1)
        gctr = [0]
        sctr = [0]
        for h in range(4):
            W, wk = self.w_get(f"qk{h}")
            Wq = W[:, 0:1024].rearrange("p (a b) -> p a b", b=128)
            Wk = W[:, 1024:2048].rearrange("p (a b) -> p a b", b=128)
            for tb in range(4):
                for (Wx, Xt, nm, sc) in ((Wq, QT, "QT", 128.0 ** -0.5), (Wk, KT, "KT", 1.0)):
                    b, bk = self.bank("pa", [2, 3])
                    S.mm(self.ps[b][:, :], [(Wx[:, kc, :], hT[:, kc, tb * 512:(tb + 1) * 512]) for kc in range(8)],
                         reads=[wk] + hk(tb), wkey=bk)
                    S.op("act", lambda: nc.scalar.activation(out=Xt[:, tb * 512:(tb + 1) * 512], in_=self.ps[b][:, :],
                                                             func=AF.Copy, scale=sc),
                         reads=[bk], writes=[(nm, tb)])
            self.w_release(1)
            W, wk = self.w_get(f"v{h}")
            Wv = W[:, 0:2048].rearrange("p (a b) -> p a b", b=256)
            for i in range(16):
                b, bk = self.bank("pb", [4, 5])
                S.mm(self.ps[b][:, 0:256], [(hT[:, kc, i * 128:(i + 1) * 128], Wv[:, kc, :]) for kc in range(8)],
                     reads=[wk, ("hT", i, "w")], wkey=bk)
                S.op("dve", lambda: nc.vector.tensor_copy(out=VTM[:, i, :], in_=self.ps[b][:, 0:256]),
                     reads=[bk], writes=[("VTM", i)])
            self.w_release(1)

            items = [(0, g) for g in range(4)] + [(1, g) for g in range(3, -1, -1)]
            col = lambda tm: slice(tm * 128, (tm + 1) * 128)

            def stA(d, gi, n):
                b, kz = self.bank("gz", [6, 7])
                zb = self.ps[b]
                S.mm_multi([(zb[:, col(tm)], [(GT[d][0:17, (gi * 4 + tm) * 128:(gi * 4 + tm + 1) * 128],
                                               WA[d][0:17, h * 128:(h + 1) * 128])]) for tm in range(4)],
                           reads=[("GT", d, gi), ("WA", d)], wkey=kz)
                S.op("act", lambda: nc.scalar.activation(out=Et, in_=zb[:, :], func=AF.Exp, scale=-1.0),
                     reads=[kz], writes=["E"])
                S.op("act", lambda: nc.scalar.activation(out=LP[n % 2], in_=Et, func=AF.Ln, bias=1.0, scale=1.0),
                     reads=["E"], writes=[("LP", n % 2)])

            def stB(d, gi, n):
                tri = self.trif if d == 0 else self.trib
                gs = slice(gi * 512, (gi + 1) * 512)
                b2, kc_ = self.bank("gz", [6, 7])
                cb = self.ps[b2]
                S.mm_multi([(cb[:, col(tm)], [(LP[n % 2][:, col(tm)], tri)]) for tm in range(4)],
                           reads=[("LP", n % 2)], wkey=kc_)
                S.op("act", lambda: nc.scalar.activation(out=EQ[n % 3], in_=cb[:, :], func=AF.Exp),
                     reads=[kc_], writes=[("EQ", n % 3)])
                S.op("act", lambda: nc.scalar.activation(out=EK, in_=cb[:, :], func=AF.Exp, scale=-1.0),
                     reads=[kc_], writes=["EK"])
                S.op("dve", lambda: nc.vector.tensor_tensor(out=QTt[n % 3], in0=QT[:, gs], in1=EQ[n % 3], op=ALU.mult),
                     reads=[("QT", gi), ("EQ", n % 3)], writes=[("QTt", n % 3)])
                S.op("dve", lambda: nc.vector.tensor_tensor(out=KTt[n % 2], in0=KT[:, gs], in1=EK, op=ALU.mult),
                     reads=[("KT", gi), "EK"], writes=[("KTt", n % 2)])

            def stC(d, gi, n):
                mask = self.maskf if d == 0 else self.maskb
                b3, ks = self.bank("sc", [0, 1])
                sb = self.ps[b3]
                S.mm_multi([(sb[:, col(tm)], [(KTt[n % 2][:, col(tm)], QTt[n % 3][:, col(tm)])]) for tm in range(4)],
                           reads=[("KTt", n % 2), ("QTt", n % 3)], wkey=ks)
                mask_b = mask.rearrange("p (o b) -> p o b", o=1).to_broadcast([128, 4, 128])
                S.op("dve", lambda: nc.vector.tensor_tensor(out=ST[n % 2].rearrange("p (a b) -> p a b", b=128),
                                                            in0=sb[:, :].rearrange("p (a b) -> p a b", b=128),
                                                            in1=mask_b, op=ALU.mult),
                     reads=[ks], writes=[("ST", n % 2)])
                b4, kt = self.bank("sc", [0, 1])
                tb16 = self.ps[b4][:, :].bitcast(BF16)
                S.tr([(tb16[:, col(tm)], KTt[n % 2][:, col(tm)]) for tm in range(4)], self.identb,
                     reads=[("KTt", n % 2)], wkey=kt)
                S.op("act", lambda: nc.scalar.copy(out=KTM[n % 2], in_=tb16[:, 0:512]), reads=[kt],
                     writes=[("KTM", n % 2)])

            def SD(d, gi, n, first_group, first_arrival):
                order = [0, 1, 2, 3] if d == 0 else [3, 2, 1, 0]
                lastc = 127 if d == 0 else 0
                p2, p3 = n % 2, n % 3
                decs = lambda tm: EQ[p3][:, tm * 128 + lastc:tm * 128 + lastc + 1]
                kvb = {}

                def kv(tm):
                    b6, kkv = self.bank("kv", [4, 5])
                    S.mm(self.ps[b6][:, 0:256], [(KTM[p2][:, col(tm)], VTM[:, gi * 4 + tm, :])],
                         reads=[("KTM", p2), ("VTM", gi * 4 + tm)], wkey=kkv)
                    kvb[tm] = (self.ps[b6][:, 0:256], kkv)

                kv(order[0])
                kv(order[1])
                obanks = {}
                for half in range(2):
                    b5, ko = self.bank("o", [2, 3])
                    obanks[half] = (self.ps[b5], ko)
                pend = {0: [], 1: []}
                for half in range(2):
                    S._wait("pe", S._collect([], [obanks[half][1]]))
                for t, tm in enumerate(order):
                    c = gi * 4 + tm
                    has_state = not (first_group and t == 0)
                    kvps, kkv = kvb[tm]
                    if t > 0:
                        nxt = (sctr[0] + 1) % 3
                        S.op("dve", lambda: nc.vector.tensor_scalar(out=SBF[nxt], in0=Ust, scalar1=decs(order[t - 1]),
                                                                    scalar2=None, op0=ALU.mult),
                             reads=["U", ("EQ", p3)], writes=[("SBF", nxt)])
                        sctr[0] += 1
                    ob, ko = obanks[tm // 2]
                    tm2 = tm % 2
                    groups = []
                    for vc in range(2):
                        prs = [(VTM[:, c, vc * 128:(vc + 1) * 128], ST[p2][:, col(tm)])]
                        if has_state:
                            prs.append((SBF[sctr[0] % 3][:, vc * 128:(vc + 1) * 128], QTt[p3][:, col(tm)]))
                        groups.append((ob[:, vc * 256 + tm2 * 128:vc * 256 + tm2 * 128 + 128], prs))
                    rd = [("VTM", c), ("ST", p2), ("QTt", p3)] + ([("SBF", sctr[0] % 3)] if has_state else [])
                    S.mm_multi(groups, reads=rd, wkey=(ko, "part", tm2))
                    pend[tm // 2].append((ko, "part", tm2))
                    if t == 0:
                        if has_state:
                            S.op("dve", lambda: nc.vector.tensor_tensor(out=Ust, in0=Sst, in1=kvps, op=ALU.add),
                                 reads=["S", kkv], writes=["U"])
                        else:
                            S.op("dve", lambda: nc.vector.tensor_copy(out=Ust, in_=kvps), reads=[kkv], writes=["U"])
                    else:
                        S.op("dve", lambda: nc.vector.scalar_tensor_tensor(out=Ust, in0=Ust, scalar=decs(order[t - 1]),
                                                                            in1=kvps, op0=ALU.mult, op1=ALU.add),
                             reads=["U", ("EQ", p3), kkv], writes=["U"])
                    if t + 2 < 4:
                        kv(order[t + 2])
                nxt = (sctr[0] + 1) % 3
                S.op("dve", lambda: nc.vector.tensor_scalar(out=SBF[nxt], in0=Ust, scalar1=decs(order[3]),
                                                            scalar2=None, op0=ALU.mult),
                     reads=["U", ("EQ", p3)], writes=[("SBF", nxt)])
                S.op("dve", lambda: nc.vector.tensor_scalar(out=Sst, in0=Ust, scalar1=decs(order[3]),
                                                            scalar2=None, op0=ALU.mult),
                     reads=["U", ("EQ", p3)], writes=["S"])
                sctr[0] += 1
                gs = slice(gi * 512, (gi + 1) * 512)
                for half in range(2):
                    ob, ko = obanks[half]
                    o3 = ob[:, :].rearrange("p (a b) -> p a b", b=256)
                    cols = slice(gi * 512 + half * 256, gi * 512 + half * 256 + 256)
                    if first_arrival:
                        S.op("act", lambda: nc.scalar.copy(out=OF[:, :, cols], in_=o3), reads=pend[half],
                             writes=[("OF", gi, half), ko])
                    else:
                        S.op("dve", lambda: nc.vector.tensor_tensor(out=OS[:, :, half * 256:(half + 1) * 256], in0=o3,
                                                                    in1=OF[:, :, cols], op=ALU.add),
                             reads=pend[half] + [("OF", gi, half)], writes=[("OS", half), ko])
                if not first_arrival:
                    S.op("act", lambda: nc.scalar.activation(out=OSQ, in_=OS, func=AF.Square),
                         reads=[("OS", 0), ("OS", 1)], writes=["OSQ"])
                    b7, kq = self.bank("gz", [6, 7])
                    qps = self.ps[b7]
                    S.mm(qps[:, :], [(self.onesb, OSQ[:, vc, :]) for vc in range(2)], reads=["OSQ"], wkey=kq)
                    S.op("act", lambda: nc.scalar.activation(out=RG, in_=qps[:, :], func=AF.Sqrt, scale=1.0 / 256,
                                                             bias=self.epsc),
                         reads=[kq], writes=["RG"])
                    S.op("dve", lambda: nc.vector.reciprocal(out=RG, in_=RG), reads=["RG"], writes=["RG"])
                    rg_b = RG.rearrange("p (o b) -> p o b", o=1).to_broadcast([128, 2, 512])
                    S.op("dve", lambda: nc.vector.tensor_tensor(out=OS, in0=OS, in1=rg_b, op=ALU.mult),
                         reads=[("OS", 0), ("OS", 1), "RG"], writes=[("OS", 0), ("OS", 1)])
                    for vc in range(2):
                        S.op("dve", lambda vc=vc: nc.vector.scalar_tensor_tensor(
                            out=OG[:, h * 2 + vc, gs], in0=OS[:, vc, :], scalar=self.pcol(P_GNG + h * 2 + vc),
                            in1=OG[:, h * 2 + vc, gs], op0=ALU.mult, op1=ALU.mult),
                            reads=[("OS", 0), ("OS", 1), ("og", h * 2 + vc, gi)], writes=[("og", h * 2 + vc, gi)])

            n0 = gctr[0]
            gctr[0] += len(items)
            L = len(items)
            stA(*items[0], n0)
            stB(*items[0], n0)
            stC(*items[0], n0)
            stA(*items[1], n0 + 1)
            stB(*items[1], n0 + 1)
            stA(*items[2], n0 + 2)
            for i_ in range(L):
                if i_ + 3 < L:
                    stA(*items[i_ + 3], n0 + i_ + 3)
                if i_ + 2 < L:
                    stB(*items[i_ + 2], n0 + i_ + 2)
                if i_ + 1 < L:
                    stC(*items[i_ + 1], n0 + i_ + 1)
                d, gi = items[i_]
                SD(d, gi, n0 + i_, first_group=(i_ == 0 or i_ == 4), first_arrival=(d == 0))
        self.mixer_out(A, OG, lambda tb: [("og", c, tb) for c in range(8)], src, dst, seq, gBpost, gqk, XT)

    def build(self):
        self.declare()
        self.setup()
        self.make_plan()
        self.w_prime()
        S = self.S
        nsub = len(self.plan)
        for seq in range(2):
            for si, sub in enumerate(self.plan):
                src = self.x_in if si == 0 else self.xs
                dst = self.y_out if si == nsub - 1 else self.xs
                if sub == "mix0":
                    self.conv_mixer(src, dst, seq)
                elif sub == "mix1":
                    self.gla_mixer(src, dst, seq)
                else:
                    self.ffn(int(sub[3]), src, dst, seq)
        S.barrier(engines=("sp",))
        self.es.close()
        return self.nc


def host_inputs(inp):
    f = lambda a: np.ascontiguousarray(np.asarray(a, dtype=np.float32))
    pm = lambda v: f(v).reshape(-1, 128).T
    prm = np.zeros((128, NPRM), np.float32)
    for l in range(2):
        prm[:, P_MIXPRE + 8 * l:P_MIXPRE + 8 * l + 8] = pm(inp["mix_pre_g"][l])
        prm[:, P_MIXPOST + 8 * l:P_MIXPOST + 8 * l + 8] = pm(inp["mix_post_g"][l])
        prm[:, P_FFNPRE + 8 * l:P_FFNPRE + 8 * l + 8] = pm(inp["ffn_pre_g"][l])
        prm[:, P_FFNPOST + 8 * l:P_FFNPOST + 8 * l + 8] = pm(inp["ffn_post_g"][l])
    dww = f(inp["cv_dw_w"][0])
    prm[:, P_DWW:P_DWW + 124] = dww.reshape(31, 4, 128).transpose(2, 1, 0).reshape(128, 124)
    prm[:, P_DWB:P_DWB + 4] = pm(inp["cv_dw_b"][0])
    prm[:, P_LNG:P_LNG + 4] = pm(inp["cv_ln_g"][0])
    prm[:, P_LNB:P_LNB + 4] = pm(inp["cv_ln_b"][0])
    scw = f(inp["cv_sc_w"][0])
    prm[:, P_SCW:P_SCW + 12] = scw.reshape(3, 4, 128).transpose(2, 1, 0).reshape(128, 12)
    prm[:, P_GNG:P_GNG + 8] = pm(inp["gla_gn_g"][0])
    j = np.arange(128)[:, None]
    i = np.arange(128)[None, :]
    cst = np.concatenate([
        np.eye(128), np.ones((128, 128)), (j <= i) * 1.0, (j >= i) * 1.0,
        (j <= i) * (-1.0 / 16.0), (j >= i) * (-1.0 / 16.0)], axis=1).astype(np.float32)
    gvec = np.stack([f(inp["mix_pre_g"][0]), f(inp["mix_pre_g"][1]), f(inp["mix_post_g"][0]),
                     f(inp["mix_post_g"][1]), f(inp["ffn_pre_g"][0]), f(inp["ffn_pre_g"][1]),
                     f(inp["ffn_post_g"][0]), f(inp["ffn_post_g"][1])], axis=0)
    wa = np.stack([np.concatenate([f(inp["gla_wa2_f"][0]), f(inp["gla_ba2_f"])[0:1]], axis=0),
                   np.concatenate([f(inp["gla_wa2_b"][0]), f(inp["gla_ba2_b"])[0:1]], axis=0)], axis=0)
    shared = {
        "prm": prm, "cst": cst, "gvec": f(gvec), "wa": f(wa),
        "cv_w_in": f(inp["cv_w_in"][0]), "cv_w_out": f(inp["cv_w_out"][0]),
        "gla_w_in": f(inp["gla_w_in"][0]), "gla_w_out": f(inp["gla_w_out"][0]),
        "ffn_w_gu": f(inp["ffn_w_gu"]), "ffn_w_down": f(inp["ffn_w_down"]),
    }
    x = f(inp["x"]).reshape(NCORES, TOK, D)
    return [dict(shared, x=x[c]) for c in range(NCORES)]


_CACHE = {}


def run(inp, plan=("mix0", "ffn0", "mix1", "ffn1")):
    plan = tuple(plan)
    if plan not in _CACHE:
        _CACHE[plan] = Builder(plan).build()
    nc = _CACHE[plan]
    in_maps = host_inputs(inp)
    res = run_bass_kernel_spmd(nc, in_maps, core_ids=list(range(NCORES)))
    out = np.stack([np.asarray(r["y"], dtype=np.float32) for r in res.results], axis=0)
    return out.reshape(16, SEQ, D)


def kernel(**inputs):
    return run(inputs)
```

```python
import math
from contextlib import ExitStack

import numpy as np
import concourse.bass as bass
import concourse.mybir as mybir
from concourse.bass_utils import run_bass_kernel_spmd
from concourse.alu_op_type import AluOpType as ALU

F32 = mybir.dt.float32
BF16 = mybir.dt.bfloat16
AF = mybir.ActivationFunctionType

NCORES = 8
D = 1024
SEQ = 2048
TOK = 4096
DFF = 2816
EPS = 1e-6
SLOT_ELEMS = 5632
NSLOT = 4
NPRM = 220

P_MIXPRE, P_MIXPOST, P_FFNPRE, P_FFNPOST = 0, 16, 32, 48
P_DWW = 64
P_DWB = 188
P_LNG = 192
P_LNB = 196
P_SCW = 200
P_GNG = 212


class Sched:
    def __init__(self, nc, es):
        self.nc = nc
        self.E = {"pe": nc.tensor, "act": nc.scalar, "dve": nc.vector, "pool": nc.gpsimd, "sp": nc.sync}
        self.psem = {}
        for e in ("pe", "act", "dve", "pool"):
            self.psem[e] = es.enter_context(nc.semaphore(f"p_{e}"))
        self.pcnt = {e: 0 for e in self.psem}
        self.waited = {e: {} for e in self.E}
        self.state = {}
        self.dsems = {}
        for q in ("sp", "pool"):
            self.dsems[q] = [[es.enter_context(nc.semaphore(f"d_{q}{i}")), 0, f"d_{q}{i}"] for i in range(10)]
        self.drr = {q: 0 for q in self.dsems}
        self.slotsem = [[es.enter_context(nc.semaphore(f"w_{i}")), 0, f"w_{i}"] for i in range(NSLOT)]

    def _collect(self, reads, writes):
        need = {}

        def add(n, s, v):
            if n not in need or need[n][1] < v:
                need[n] = (s, v)

        for k in reads:
            st = self.state.get(k)
            if st and st[0] is not None:
                add(*st[0])
        for k in writes:
            st = self.state.get(k)
            if st:
                if st[0] is not None:
                    add(*st[0])
                for n, (s, v) in st[1].items():
                    add(n, s, v)
        return need

    def _wait(self, e, need):
        for n, (s, v) in need.items():
            if self.waited[e].get(n, 0) >= v:
                continue
            self.E[e].wait_ge(s, v)
            self.waited[e][n] = v

    def _commit(self, tok, reads, writes):
        n, s, v = tok
        for k in reads:
            st = self.state.setdefault(k, [None, {}])
            st[1][n] = (s, v)
        for k in writes:
            self.state[k] = [tok, {}]

    def op(self, e, fn, reads=(), writes=()):
        need = self._collect(reads, writes)
        self._wait(e, need)
        ins = fn()
        self.pcnt[e] += 1
        ins.then_inc(self.psem[e], 1)
        tok = (f"p_{e}", self.psem[e], self.pcnt[e])
        self._commit(tok, reads, writes)
        return tok

    def mm(self, out, pairs, reads, wkey):
        need = self._collect(reads, [wkey])
        self._wait("pe", need)
        n = len(pairs)
        ins = None
        for i, (l, r) in enumerate(pairs):
            ins = self.nc.tensor.matmul(out, l, r, start=(i == 0), stop=(i == n - 1))
        self.pcnt["pe"] += 1
        ins.then_inc(self.psem["pe"], 1)
        tok = ("p_pe", self.psem["pe"], self.pcnt["pe"])
        self._commit(tok, reads, [wkey])
        return tok

    def mm_multi(self, groups, reads, wkey):
        need = self._collect(reads, [wkey])
        self._wait("pe", need)
        ins = None
        for out, pairs in groups:
            n = len(pairs)
            for i, (l, r) in enumerate(pairs):
                ins = self.nc.tensor.matmul(out, l, r, start=(i == 0), stop=(i == n - 1))
        self.pcnt["pe"] += 1
        ins.then_inc(self.psem["pe"], 1)
        tok = ("p_pe", self.psem["pe"], self.pcnt["pe"])
        self._commit(tok, reads, [wkey])
        return tok

    def tr(self, items, ident, reads, wkey):
        need = self._collect(reads, [wkey])
        self._wait("pe", need)
        ins = None
        for out, in_ in items:
            ins = self.nc.tensor.transpose(out, in_, ident)
        self.pcnt["pe"] += 1
        ins.then_inc(self.psem["pe"], 1)
        tok = ("p_pe", self.psem["pe"], self.pcnt["pe"])
        self._commit(tok, reads, [wkey])
        return tok

    def dma(self, q, out, in_, reads=(), writes=(), semrec=None, nonc=False):
        if semrec is None:
            semrec = self.dsems[q][self.drr[q]]
            self.drr[q] = (self.drr[q] + 1) % len(self.dsems[q])
            need = self._collect(reads, writes)
            if semrec[1] > 0:
                need[semrec[2]] = (semrec[0], semrec[1])
        else:
            need = self._collect(reads, writes)
        self._wait(q, need)
        if nonc:
            ins = self.E[q].dma_start(out=out, in_=in_, allow_slow_non_contiguous=True)
        else:
            ins = self.E[q].dma_start(out=out, in_=in_)
        semrec[1] += 16
        ins.then_inc(semrec[0], 16)
        tok = (semrec[2], semrec[0], semrec[1])
        self._commit(tok, reads, writes)
        return tok

    def barrier(self, engines=("pe", "act", "dve", "sp")):
        need = {}
        for e in self.psem:
            if self.pcnt[e] > 0:
                need[f"p_{e}"] = (self.psem[e], self.pcnt[e])
        for q in self.dsems:
            for rec in self.dsems[q]:
                if rec[1] > 0:
                    need[rec[2]] = (rec[0], rec[1])
        for e in engines:
            self._wait(e, need)


class Arena:
    def __init__(self, big, base_b, limit_b):
        self.big = big
        self.base = base_b
        self.limit = limit_b
        self.off = base_b

    def reset(self):
        self.off = self.base

    def alloc(self, shape, dt):
        esz = 2 if dt == BF16 else 4
        n = 1
        for s in shape[1:]:
            n *= s
        nb = (n * esz + 63) // 64 * 64
        assert self.off + nb <= self.limit, f"arena overflow {self.off + nb} > {self.limit}"
        a = self.big[0:shape[0], self.off // 2: self.off // 2 + n * esz // 2]
        self.off += nb
        if dt == F32:
            a = a.bitcast(F32)
        if len(shape) == 3:
            a = a.rearrange("p (a b) -> p a b", b=shape[2])
        elif len(shape) == 4:
            a = a.rearrange("p (a b c) -> p a b c", b=shape[2], c=shape[3])
        return a


class Builder:
    def __init__(self, plan):
        self.plan = plan
        self.nc = bass.Bass("TRN2", target_bir_lowering=False)
        self.es = ExitStack()

    def declare(self):
        nc = self.nc
        dt = lambda name, shape, kind="ExternalInput": nc.dram_tensor(name, shape, F32, kind=kind).ap()
        self.x_in = dt("x", [TOK, D])
        self.y_out = dt("y", [TOK, D], "ExternalOutput")
        self.xs = dt("xs", [TOK, D], "Internal")
        self.prm_d = dt("prm", [128, NPRM])
        self.cst_d = dt("cst", [128, 6 * 128])
        self.gvec_d = dt("gvec", [8, D])
        self.wa_d = dt("wa", [2, 17, 512])
        self.cv_w_in = dt("cv_w_in", [D, 2560])
        self.cv_w_out = dt("cv_w_out", [D, D])
        self.gla_w_in = dt("gla_w_in", [D, 3104])
        self.gla_w_out = dt("gla_w_out", [D, D])
        self.ffn_w_gu = dt("ffn_w_gu", [2, D, 2 * DFF])
        self.ffn_w_down = dt("ffn_w_down", [2, DFF, D])

    def setup(self):
        nc, es = self.nc, self.es
        self.S = Sched(nc, es)
        S = self.S
        total_b = 212800
        self.big = es.enter_context(nc.sbuf_tensor("big", [128, total_b // 2], BF16))
        self.ps = [es.enter_context(nc.psum_tensor(f"ps{i}", [128, 512], F32)) for i in range(8)]
        carve = Arena(self.big, 0, total_b)
        self.slots = [carve.alloc([128, SLOT_ELEMS], BF16) for _ in range(NSLOT)]
        self.identb = carve.alloc([128, 128], BF16)
        self.onesb = carve.alloc([128, 128], BF16)
        self.maskf = carve.alloc([128, 128], BF16)
        self.maskb = carve.alloc([128, 128], BF16)
        self.trif = carve.alloc([128, 128], F32)
        self.trib = carve.alloc([128, 128], F32)
        self.prm = carve.alloc([128, NPRM], F32)
        self.arena = Arena(self.big, carve.off, total_b)
        c = self.cst_d
        S.dma("pool", self.identb, c[:, 0:128], writes=["c0"])
        S.dma("pool", self.onesb, c[:, 128:256], writes=["c1"])
        S.dma("pool", self.maskf, c[:, 256:384], writes=["c2"])
        S.dma("pool", self.maskb, c[:, 384:512], writes=["c3"])
        S.dma("sp", self.trif, c[:, 512:640], writes=["c4"])
        S.dma("sp", self.trib, c[:, 640:768], writes=["c5"])
        S.dma("sp", self.prm, self.prm_d[:, :], writes=["c6"])
        S.barrier()
        self.wplan = []
        self.wnext_issue = 0
        self.wnext_use = 0
        self.psrr = {}

    def bank(self, role, banks):
        i = self.psrr.get(role, 0)
        self.psrr[role] = i + 1
        b = banks[i % len(banks)]
        return b, ("ps", b)

    def pcol(self, c0, n=1):
        return self.prm[:, c0:c0 + n]

    def w_issue(self, idx):
        if idx >= len(self.wplan):
            return
        S = self.S
        slot = idx % NSLOT
        rec = S.slotsem[slot]
        S._wait("pool", S._collect([], [("w", slot)]))
        for (eoff, shp, src) in self.wplan[idx]:
            n = shp[0] * shp[1]
            dst = self.slots[slot][:, eoff:eoff + n].rearrange("p (a b) -> p a b", b=shp[1])
            ins = self.nc.gpsimd.dma_start(out=dst, in_=src)
            rec[1] += 16
            ins.then_inc(rec[0], 16)
        S._commit((rec[2], rec[0], rec[1]), [], [("w", slot)])

    def w_prime(self):
        for i in range(NSLOT):
            self.w_issue(i)
        self.wnext_issue = NSLOT

    def w_get(self, tag):
        idx = self.wnext_use
        assert self.wtags[idx] == tag, (idx, self.wtags[idx], tag)
        self.wnext_use += 1
        slot = idx % NSLOT
        return self.slots[slot], ("w", slot)

    def w_release(self, n=1):
        for _ in range(n):
            self.w_issue(self.wnext_issue)
            self.wnext_issue += 1

    def add_load(self, tag, parts):
        self.wplan.append(parts)
        self.wtags.append(tag)

    @staticmethod
    def wsrc(w2d, r0, nr, c0, ncol):
        return w2d[r0:r0 + nr, c0:c0 + ncol].rearrange("(kc p) n -> p kc n", p=128)

    def make_plan(self):
        self.wtags = []
        for s in range(2):
            for sub in self.plan:
                if sub == "mix0":
                    w = self.cv_w_in
                    for nm, c0 in (("aval", 0), ("agate", 512), ("cgate", 1536), ("v", 2048), ("bgate", 1024)):
                        self.add_load(nm, [(0, (8, 512), self.wsrc(w, 0, D, c0, 512))])
                    for h in range(2):
                        self.add_load(f"wout{h}", [(0, (8, 512), self.wsrc(self.cv_w_out, 0, D, h * 512, 512))])
                elif sub == "mix1":
                    w = self.gla_w_in
                    self.add_load("gates", [(0, (8, 32), self.wsrc(w, 0, D, 3072, 32))])
                    for h in range(2):
                        self.add_load(f"r{h}", [(0, (8, 512), self.wsrc(w, 0, D, 2048 + h * 512, 512))])
                    for h in range(4):
                        self.add_load(f"qk{h}", [(0, (8, 128), self.wsrc(w, 0, D, h * 128, 128)),
                                                 (1024, (8, 128), self.wsrc(w, 0, D, 512 + h * 128, 128))])
                        self.add_load(f"v{h}", [(0, (8, 256), self.wsrc(w, 0, D, 1024 + h * 256, 256))])
                    for h in range(2):
                        self.add_load(f"wout{h}", [(0, (8, 512), self.wsrc(self.gla_w_out, 0, D, h * 512, 512))])
                elif sub in ("ffn0", "ffn1"):
                    l = int(sub[3])
                    wg = self.ffn_w_gu[l]
                    wd = self.ffn_w_down[l]
                    for hb in range(2):
                        for L in range(11):
                            self.add_load(f"gu{L}", [(0, (8, 256), self.wsrc(wg, 0, D, L * 256, 256)),
                                                     (2048, (8, 256), self.wsrc(wg, 0, D, DFF + L * 256, 256))])
                        for half in range(2):
                            for part in range(2):
                                self.add_load(f"dn{half}{part}",
                                              [(0, (11, 512), self.wsrc(wd, part * 1408, 1408, half * 512, 512))])

    def load_gb(self, A, row):
        g = A.alloc([128, D], F32)
        self.S.dma("sp", g, self.gvec_d[row:row + 1, :].to_broadcast([128, D]), writes=[("gb", row)])
        return g, ("gb", row)

    def pn_stats(self, xt, xkey, rs_col, rskey, ss_col, sskey):
        S, nc = self.S, self.nc
        S.op("act", lambda: nc.scalar.activation(out=self.sqj, in_=xt, func=AF.Square, accum_out=ss_col),
             reads=[xkey], writes=[sskey])
        S.op("act", lambda: nc.scalar.activation(out=rs_col, in_=ss_col, func=AF.Sqrt, scale=1.0 / D, bias=self.epsc),
             reads=[sskey], writes=[rskey])
        S.op("dve", lambda: nc.vector.reciprocal(out=rs_col, in_=rs_col), reads=[rskey], writes=[rskey])

    def pn_apply(self, xt, xkey, rs_col, rskey, gB, gkey, htm, htmkey, dst_cols, hkey):
        S, nc = self.S, self.nc
        S.op("dve", lambda: nc.vector.scalar_tensor_tensor(out=htm, in0=xt, scalar=rs_col, in1=gB,
                                                            op0=ALU.mult, op1=ALU.mult),
             reads=[xkey, rskey, gkey], writes=[htmkey])
        b, bkey = self.bank("pt", [0, 1])
        pv = self.ps[b][:, :].bitcast(BF16)
        S.tr([(pv[:, c * 128:(c + 1) * 128], htm[:, c * 128:(c + 1) * 128]) for c in range(8)], self.identb,
             reads=[htmkey], wkey=bkey)
        S.op("act", lambda: nc.scalar.copy(out=dst_cols, in_=pv.rearrange("p (c t) -> p c t", t=128)),
             reads=[bkey], writes=[hkey])

    def postnorm_tile(self, ops, okeys, xt, xkey, gB, gkey, t1, t1key, ssb, idx):
        S, nc = self.S, self.nc
        c0 = ssb[:, 4 * idx:4 * idx + 1]
        c1 = ssb[:, 4 * idx + 1:4 * idx + 2]
        c2 = ssb[:, 4 * idx + 2:4 * idx + 3]
        k = ("ssb", idx)
        S.op("act", lambda: nc.scalar.activation(out=self.sqj[:, 0:512], in_=ops[0], func=AF.Square, accum_out=c0),
             reads=[okeys[0]], writes=[(k, 0)])
        S.op("act", lambda: nc.scalar.activation(out=self.sqj[:, 512:1024], in_=ops[1], func=AF.Square, accum_out=c1),
             reads=[okeys[1]], writes=[(k, 1)])
        S.op("dve", lambda: nc.vector.tensor_tensor(out=c2, in0=c0, in1=c1, op=ALU.add),
             reads=[(k, 0), (k, 1)], writes=[(k, 2)])
        S.op("act", lambda: nc.scalar.activation(out=c2, in_=c2, func=AF.Sqrt, scale=1.0 / D, bias=self.epsc),
             reads=[(k, 2)], writes=[(k, 2)])
        S.op("dve", lambda: nc.vector.reciprocal(out=c2, in_=c2), reads=[(k, 2)], writes=[(k, 2)])
        for h in range(2):
            S.op("dve", lambda h=h: nc.vector.scalar_tensor_tensor(
                out=t1[:, h * 512:(h + 1) * 512], in0=ops[h], scalar=c2, in1=gB[:, h * 512:(h + 1) * 512],
                op0=ALU.mult, op1=ALU.mult), reads=[okeys[h], (k, 2), gkey], writes=[(t1key, h)])
        S.op("dve", lambda: nc.vector.tensor_tensor(out=xt, in0=xt, in1=t1, op=ALU.add),
             reads=[xkey, (t1key, 0), (t1key, 1)], writes=[xkey])

    def common_tmps(self, A):
        self.sqj = A.alloc([128, D], BF16)
        self.epsc = A.alloc([128, 1], F32)
        self.S.op("dve", lambda: self.nc.vector.memset(self.epsc, EPS), writes=["epsc"])
        self.S.barrier()

    def ffn(self, l, src, dst, seq):
        S, nc, A = self.S, self.nc, self.arena
        S.barrier()
        A.reset()
        self.common_tmps(A)
        gBpre, gpk = self.load_gb(A, 4 + l)
        gBpost, gqk = self.load_gb(A, 6 + l)
        XH = A.alloc([128, 8, D], F32)
        hT = A.alloc([128, 8, 1024], BF16)
        M0 = hT.rearrange("p a b -> p (a b)").bitcast(F32).rearrange("p (a b) -> p a b", b=512)
        ACTB = A.alloc([128, 22, 1024], BF16)
        HTM = [A.alloc([128, D], BF16) for _ in range(2)]
        SG = [A.alloc([128, 512], BF16) for _ in range(2)]
        T1 = [A.alloc([128, D], F32) for _ in range(2)]
        SS = A.alloc([128, 16], F32)
        RS = A.alloc([128, 16], F32)
        SSB = A.alloc([128, 64], F32)
        for hb in range(2):
            S.barrier()
            t0 = seq * SEQ + hb * 1024
            for i in range(8):
                r0 = t0 + i * 128
                S.dma("sp", XH[:, i, :], src[r0:r0 + 128, :], reads=[("xd", r0)], writes=[("XH", i)])
            for i in range(8):
                self.pn_stats(XH[:, i, :], ("XH", i), RS[:, i:i + 1], ("rs", i), SS[:, i:i + 1], ("ss", i))
            for i in range(8):
                self.pn_apply(XH[:, i, :], ("XH", i), RS[:, i:i + 1], ("rs", i), gBpre, gpk,
                              HTM[i % 2], ("htm", i % 2), hT[:, :, i * 128:(i + 1) * 128], ("hT", i, "w"))
            for L in range(11):
                W, wk = self.w_get(f"gu{L}")
                Wg = W[:, 0:2048].rearrange("p (a b) -> p a b", b=256)
                Wu = W[:, 2048:4096].rearrange("p (a b) -> p a b", b=256)
                for cc in range(2):
                    c = 2 * L + cc
                    for tb in range(2):
                        bg, kg = self.bank("g", [2, 3])
                        bu, ku = self.bank("u", [4, 5])
                        rhs = lambda kc: hT[:, kc, tb * 512:(tb + 1) * 512]
                        S.mm(self.ps[bg][:, :], [(Wg[:, kc, cc * 128:(cc + 1) * 128], rhs(kc)) for kc in range(8)],
                             reads=[wk] + [("hT", 4 * tb + q, "w") for q in range(4)], wkey=kg)
                        S.mm(self.ps[bu][:, :], [(Wu[:, kc, cc * 128:(cc + 1) * 128], rhs(kc)) for kc in range(8)],
                             reads=[wk] + [("hT", 4 * tb + q, "w") for q in range(4)], wkey=ku)
                        sg = SG[(2 * c + tb) % 2]
                        sgk = ("sg", (2 * c + tb) % 2)
                        S.op("act", lambda: nc.scalar.activation(out=sg, in_=self.ps[bg][:, :], func=AF.Silu),
                             reads=[kg], writes=[sgk])
                        S.op("dve", lambda: nc.vector.tensor_tensor(out=ACTB[:, c, tb * 512:(tb + 1) * 512], in0=sg,
                                                                    in1=self.ps[bu][:, :], op=ALU.mult),
                             reads=[sgk, ku], writes=[("actb", c, tb)])
                self.w_release()
            for half in range(2):
                Wa, wka = self.w_get(f"dn{half}0")
                Wb, wkb = self.w_get(f"dn{half}1")
                Wa3 = Wa[:, 0:5632].rearrange("p (a b) -> p a b", b=512)
                Wb3 = Wb[:, 0:5632].rearrange("p (a b) -> p a b", b=512)
                for i in range(8):
                    bf, kf = self.bank("f", [6, 7])
                    tb = i // 4
                    pairs = []
                    for kc in range(22):
                        w3 = Wa3 if kc < 11 else Wb3
                        pairs.append((ACTB[:, kc, i * 128:(i + 1) * 128], w3[:, kc % 11, :]))
                    S.mm(self.ps[bf][:, :], pairs, reads=[wka, wkb] + [("actb", kc, tb) for kc in range(22)], wkey=kf)
                    c0 = SSB[:, 4 * i + half:4 * i + half + 1]
                    k = ("ssb", i)
                    if half == 0:
                        S.op("act", lambda: nc.scalar.activation(out=self.sqj[:, 0:512], in_=self.ps[bf][:, :],
                                                                 func=AF.Square, accum_out=c0),
                             reads=[kf], writes=[(k, 0)])
                        S.op("act", lambda: nc.scalar.copy(out=M0[:, i, :], in_=self.ps[bf][:, :]),
                             reads=[kf], writes=[("m0", i)] + [("hT", q, "w") for q in range(8)])
                    else:
                        c2 = SSB[:, 4 * i + 2:4 * i + 3]
                        S.op("act", lambda: nc.scalar.activation(out=self.sqj[:, 512:1024], in_=self.ps[bf][:, :],
                                                                 func=AF.Square, accum_out=c0),
                             reads=[kf], writes=[(k, 1)])
                        S.op("dve", lambda: nc.vector.tensor_tensor(out=c2, in0=SSB[:, 4 * i:4 * i + 1], in1=c0,
                                                                    op=ALU.add),
                             reads=[(k, 0), (k, 1)], writes=[(k, 2)])
                        S.op("act", lambda: nc.scalar.activation(out=c2, in_=c2, func=AF.Sqrt, scale=1.0 / D,
                                                                 bias=self.epsc),
                             reads=[(k, 2)], writes=[(k, 2)])
                        S.op("dve", lambda: nc.vector.reciprocal(out=c2, in_=c2), reads=[(k, 2)], writes=[(k, 2)])
                        t1 = T1[i % 2]
                        tk = ("t1", i % 2)
                        S.op("dve", lambda: nc.vector.scalar_tensor_tensor(
                            out=t1[:, 512:1024], in0=self.ps[bf][:, :], scalar=c2, in1=gBpost[:, 512:1024],
                            op0=ALU.mult, op1=ALU.mult), reads=[kf, (k, 2), gqk], writes=[(tk, 1)])
                        S.op("dve", lambda: nc.vector.scalar_tensor_tensor(
                            out=t1[:, 0:512], in0=M0[:, i, :], scalar=c2, in1=gBpost[:, 0:512],
                            op0=ALU.mult, op1=ALU.mult), reads=[("m0", i), (k, 2), gqk], writes=[(tk, 0)])
                        S.op("dve", lambda: nc.vector.tensor_tensor(out=XH[:, i, :], in0=XH[:, i, :], in1=t1,
                                                                    op=ALU.add),
                             reads=[("XH", i), (tk, 0), (tk, 1)], writes=[("XH", i)])
                        r0 = t0 + i * 128
                        S.dma("sp", dst[r0:r0 + 128, :], XH[:, i, :], reads=[("XH", i)], writes=[("xd", r0)])
                self.w_release(2)

    def mixer_prenorm(self, A, src, seq, gB, gk, hT, ring):
        S = self.S
        HTM = [A.alloc([128, D], BF16) for _ in range(2)]
        SS = A.alloc([128, 16], F32)
        RS = A.alloc([128, 16], F32)
        R = len(ring)

        def load(i):
            r0 = seq * SEQ + i * 128
            S.dma("sp", ring[i % R], src[r0:r0 + 128, :], reads=[("xd", r0)], writes=[("XT", i % R)])

        def stats(i):
            self.pn_stats(ring[i % R], ("XT", i % R), RS[:, i:i + 1], ("rs", i), SS[:, i:i + 1], ("ss", i))

        for i in range(R - 1):
            load(i)
        stats(0)
        for i in range(16):
            if i + 1 < 16:
                stats(i + 1)
            self.pn_apply(ring[i % R], ("XT", i % R), RS[:, i:i + 1], ("rs", i), gB, gk, HTM[i % 2],
                          ("htm", i % 2), hT[:, :, i * 128:(i + 1) * 128], ("hT", i, "w"))
            if i + R - 1 < 16:
                load(i + R - 1)

    def mixer_out(self, A, CAT, catkeys, src, dst, seq, gB, gk, ring, T1):
        S, nc = self.S, self.nc
        SSB = A.alloc([128, 64], F32)
        W0, wk0 = self.w_get("wout0")
        W1, wk1 = self.w_get("wout1")
        W3 = [W0[:, 0:4096].rearrange("p (a b) -> p a b", b=512), W1[:, 0:4096].rearrange("p (a b) -> p a b", b=512)]
        wks = [wk0, wk1]
        R = len(ring)

        def load(i):
            r0 = seq * SEQ + i * 128
            S.dma("sp", ring[i % R], src[r0:r0 + 128, :], reads=[("xd", r0)], writes=[("XT", i % R)])

        for i in range(R - 1):
            load(i)
        for i in range(16):
            r0 = seq * SEQ + i * 128
            xt = ring[i % R]
            xk = ("XT", i % R)
            ops, oks = [], []
            for h in range(2):
                b, bk = self.bank("o", [2, 3, 4, 5])
                S.mm(self.ps[b][:, :], [(CAT[:, kc, i * 128:(i + 1) * 128], W3[h][:, kc, :]) for kc in range(8)],
                     reads=[wks[h]] + catkeys(i // 4), wkey=bk)
                ops.append(self.ps[b][:, :])
                oks.append(bk)
            self.postnorm_tile(ops, oks, xt, xk, gB, gk, T1, ("XT", 4), SSB, i)
            S.dma("sp", dst[r0:r0 + 128, :], xt, reads=[xk], writes=[("xd", r0)])
            if i + R - 1 < 16:
                load(i + R - 1)
        self.w_release(2)

    def conv_mixer(self, src, dst, seq):
        S, nc, A = self.S, self.nc, self.arena
        S.barrier()
        A.reset()
        self.common_tmps(A)
        gBpre, gpk = self.load_gb(A, 0)
        gBpost, gqk = self.load_gb(A, 2)
        hT = A.alloc([128, 8, SEQ], BF16)
        AB = A.alloc([128, 4, 2080], BF16)
        CV = A.alloc([128, 4, 2052], BF16)
        BG = A.alloc([128, 4, SEQ], BF16)
        DGs = [A.alloc([128, 31, 128], BF16) for _ in range(2)]
        DGB = A.alloc([128, 3, 128], BF16)
        SIG = [A.alloc([128, 512], F32) for _ in range(2)]
        VS = [A.alloc([128, 512], BF16) for _ in range(2)]
        YSQ = A.alloc([128, 4, 512], BF16)
        MEAN = A.alloc([128, 512], F32)
        MSQ = A.alloc([128, 512], F32)
        SDv = A.alloc([128, 512], F32)
        Dt = [A.alloc([128, 512], F32) for _ in range(2)]
        Zt = [A.alloc([128, 512], F32) for _ in range(2)]
        S.op("dve", lambda: nc.vector.memset(AB[:, :, 0:15], 0.0), writes=["abh0"])
        S.op("dve", lambda: nc.vector.memset(AB[:, :, 2063:2080], 0.0), writes=["abh1"])
        S.op("dve", lambda: nc.vector.memset(CV[:, :, 0:1], 0.0), writes=["cvh0"])
        S.op("dve", lambda: nc.vector.memset(CV[:, :, 2049:2052], 0.0), writes=["cvh1"])
        ring = [A.alloc([128, D], F32) for _ in range(4)]
        T1 = A.alloc([128, D], F32)
        self.mixer_prenorm(A, src, seq, gBpre, gpk, hT, ring + [T1])

        def proj(W, wk, j, tb, role, banks):
            b, bk = self.bank(role, banks)
            S.mm(self.ps[b][:, :], [(W[:, kc, j * 128:(j + 1) * 128], hT[:, kc, tb * 512:(tb + 1) * 512])
                                    for kc in range(8)], reads=[wk] + [("hT", 4 * tb + q_, "w") for q_ in range(4)], wkey=bk)
            return self.ps[b][:, :], bk

        def w3(tag):
            W, wk = self.w_get(tag)
            return W[:, 0:4096].rearrange("p (a b) -> p a b", b=512), wk

        Wv, wkv = w3("aval")
        Wg, wkg = w3("agate")
        n = 0
        for j in range(4):
            for tb in range(4):
                pv, kv = proj(Wv, wkv, j, tb, "pa", [2, 3])
                pg, kg = proj(Wg, wkg, j, tb, "pb", [4, 5])
                sg, sk = SIG[n % 2], ("sig", n % 2)
                S.op("act", lambda: nc.scalar.activation(out=sg, in_=pg, func=AF.Sigmoid), reads=[kg], writes=[sk])
                S.op("dve", lambda: nc.vector.tensor_tensor(out=AB[:, j, 15 + tb * 512:15 + (tb + 1) * 512],
                                                            in0=pv, in1=sg, op=ALU.mult),
                     reads=[kv, sk], writes=[("Ag", j, tb)])
                n += 1
        self.w_release(2)
        Wc, wkc = w3("cgate")
        Wvv, wkvv = w3("v")
        for j in range(4):
            for tb in range(4):
                pc, kc_ = proj(Wc, wkc, j, tb, "pa", [2, 3])
                pv, kv = proj(Wvv, wkvv, j, tb, "pb", [4, 5])
                vs, vk = VS[n % 2], ("vs", n % 2)
                S.op("act", lambda: nc.scalar.copy(out=vs, in_=pv), reads=[kv], writes=[vk])
                S.op("dve", lambda: nc.vector.tensor_tensor(out=CV[:, j, 1 + tb * 512:1 + (tb + 1) * 512],
                                                            in0=pc, in1=vs, op=ALU.mult),
                     reads=[kc_, vk], writes=[("CV", j, tb)])
                n += 1
        self.w_release(2)
        Wb, wkb = w3("bgate")
        for j in range(4):
            for tb in range(4):
                pb, kb = proj(Wb, wkb, j, tb, "pa", [2, 3])
                S.op("act", lambda: nc.scalar.copy(out=BG[:, j, tb * 512:(tb + 1) * 512], in_=pb),
                     reads=[kb], writes=[("BG", j, tb)])
        self.w_release(1)
        S.barrier()
        CAT = hT
        for j in range(4):
            DG = DGs[j % 2]
            for k in range(31):
                S.op("dve", lambda k=k: nc.vector.tensor_scalar(out=DG[:, k, :], in0=self.identb,
                                                                 scalar1=self.pcol(P_DWW + j * 31 + k), scalar2=None,
                                                                 op0=ALU.mult),
                     writes=[("DG", j % 2, k)])
            for tb in range(4):
                b, bk = self.bank("cv", [2, 3])
                rd = [("DG", j % 2, k) for k in range(31)] + [("Ag", j, t) for t in (tb - 1, tb, tb + 1) if 0 <= t < 4]
                rd += [("Ar", j, tb), ("Ar", j, tb + 1), "abh0", "abh1"]
                S.mm(self.ps[b][:, :], [(DG[:, k, :], AB[:, j, tb * 512 + k:tb * 512 + k + 512]) for k in range(31)],
                     reads=rd, wkey=bk)
                S.op("act", lambda: nc.scalar.activation(out=AB[:, j, tb * 512:(tb + 1) * 512], in_=self.ps[b][:, :],
                                                         func=AF.Identity, bias=self.pcol(P_DWB + j), scale=1.0),
                     reads=[bk], writes=[("Ar", j, tb)])
        for tb in range(4):
            for j in range(4):
                S.op("act", lambda j=j: nc.scalar.activation(out=YSQ[:, j, :], in_=AB[:, j, tb * 512:(tb + 1) * 512],
                                                             func=AF.Square),
                     reads=[("Ar", j, tb)], writes=[("ysq", j)])
            bm, km = self.bank("st", [6, 7])
            S.mm(self.ps[bm][:, :], [(self.onesb, AB[:, j, tb * 512:(tb + 1) * 512]) for j in range(4)],
                 reads=[("Ar", j, tb) for j in range(4)], wkey=km)
            be, ke = self.bank("st", [6, 7])
            S.mm(self.ps[be][:, :], [(self.onesb, YSQ[:, j, :]) for j in range(4)],
                 reads=[("ysq", j) for j in range(4)], wkey=ke)
            S.op("act", lambda: nc.scalar.activation(out=MEAN, in_=self.ps[bm][:, :], func=AF.Copy, scale=1.0 / 512),
                 reads=[km], writes=["mean"])
            S.op("act", lambda: nc.scalar.activation(out=MSQ, in_=self.ps[bm][:, :], func=AF.Square, scale=1.0 / 512),
                 reads=[km], writes=["msq"])
            S.op("dve", lambda: nc.vector.scalar_tensor_tensor(out=SDv, in0=self.ps[be][:, :], scalar=1.0 / 512,
                                                                in1=MSQ, op0=ALU.mult, op1=ALU.subtract),
                 reads=[ke, "msq"], writes=["sd"])
            S.op("act", lambda: nc.scalar.activation(out=SDv, in_=SDv, func=AF.Sqrt, bias=self.epsc, scale=1.0),
                 reads=["sd"], writes=["sd"])
            S.op("dve", lambda: nc.vector.reciprocal(out=SDv, in_=SDv), reads=["sd"], writes=["sd"])
            for j in range(4):
                d, dk_ = Dt[j % 2], ("dt", j % 2)
                z, zk = Zt[j % 2], ("zt", j % 2)
                S.op("dve", lambda: nc.vector.tensor_tensor(out=d, in0=AB[:, j, tb * 512:(tb + 1) * 512], in1=MEAN,
                                                            op=ALU.subtract),
                     reads=[("Ar", j, tb), "mean"], writes=[dk_])
                S.op("dve", lambda: nc.vector.tensor_tensor(out=z, in0=d, in1=SDv, op=ALU.mult),
                     reads=[dk_, "sd"], writes=[zk])
                S.op("act", lambda: nc.scalar.activation(out=CAT[:, j, tb * 512:(tb + 1) * 512], in_=z, func=AF.Silu,
                                                         scale=self.pcol(P_LNG + j), bias=self.pcol(P_LNB + j)),
                     reads=[zk], writes=[("cat", j, tb)])
        for j in range(4):
            for k in range(3):
                S.op("dve", lambda k=k: nc.vector.tensor_scalar(out=DGB[:, k, :], in0=self.identb,
                                                                 scalar1=self.pcol(P_SCW + j * 3 + k), scalar2=None,
                                                                 op0=ALU.mult),
                     writes=[("DGB", k)])
            for tb in range(4):
                b, bk = self.bank("cv", [2, 3])
                rd = [("DGB", k) for k in range(3)] + [("CV", j, t) for t in (tb - 1, tb, tb + 1) if 0 <= t < 4]
                rd += ["cvh0", "cvh1"]
                S.mm(self.ps[b][:, :], [(DGB[:, k, :], CV[:, j, tb * 512 + k:tb * 512 + k + 512]) for k in range(3)],
                     reads=rd, wkey=bk)
                S.op("dve", lambda: nc.vector.tensor_tensor(out=CAT[:, 4 + j, tb * 512:(tb + 1) * 512],
                                                            in0=self.ps[b][:, :], in1=BG[:, j, tb * 512:(tb + 1) * 512],
                                                            op=ALU.mult),
                     reads=[bk, ("BG", j, tb)], writes=[("cat", 4 + j, tb)])
        self.mixer_out(A, CAT, lambda tb: [("cat", c, tb) for c in range(8)], src, dst, seq, gBpost, gqk, ring, T1)

    def gla_mixer(self, src, dst, seq):
        S, nc, A = self.S, self.nc, self.arena
        S.barrier()
        A.reset()
        self.common_tmps(A)
        gBpre, gpk = self.load_gb(A, 1)
        gBpost, gqk = self.load_gb(A, 3)
        hT = A.alloc([128, 8, SEQ], BF16)
        OG = A.alloc([128, 8, SEQ], BF16)
        GT = [A.alloc([17, SEQ], BF16) for _ in range(2)]
        WA = [A.alloc([17, 512], BF16) for _ in range(2)]
        QT = A.alloc([128, SEQ], BF16)
        KT = A.alloc([128, SEQ], BF16)
        VTM = A.alloc([128, 16, 256], BF16)
        OF = A.alloc([128, 2, SEQ], BF16)
        Et = A.alloc([128, 512], F32)
        LP = [A.alloc([128, 512], F32) for _ in range(2)]
        EQ = [A.alloc([128, 512], F32) for _ in range(3)]
        EK = A.alloc([128, 512], F32)
        QTt = [A.alloc([128, 512], BF16) for _ in range(3)]
        KTt = [A.alloc([128, 512], BF16) for _ in range(2)]
        ST = [A.alloc([128, 512], BF16) for _ in range(2)]
        KTM = [A.alloc([128, 512], BF16) for _ in range(2)]
        Ust = A.alloc([128, 256], F32)
        DECC = A.alloc([128, 2], F32)
        SBF = [A.alloc([128, 256], BF16) for _ in range(3)]
        OS = A.alloc([128, 2, 512], F32)
        OSQ = self.sqj.rearrange("p (a b) -> p a b", b=512)
        RG = A.alloc([128, 512], F32)
        for d in range(2):
            S.dma("pool", WA[d], self.wa_d[d], writes=[("WA", d)])
            S.op("dve", lambda d=d: nc.vector.memset(GT[d], 1.0), writes=[("GT", d, t) for t in range(4)])
        XT3 = [A.alloc([128, D], F32) for _ in range(3)]
        T1 = A.alloc([128, D], F32)
        osv = OS.rearrange("p a b -> p (a b)")
        self.mixer_prenorm(A, src, seq, gBpre, gpk, hT, XT3 + [osv, T1])
        hk = lambda tb: [("hT", 4 * tb + q_, "w") for q_ in range(4)]
        W, wk = self.w_get("gates")
        Wg3 = W[:, 0:256].rearrange("p (a b) -> p a b", b=32)
        for tb in range(4):
            for d in range(2):
                b, bk = self.bank("pa", [2, 3])
                S.mm(self.ps[b][0:16, :], [(Wg3[:, kc, d * 16:(d + 1) * 16], hT[:, kc, tb * 512:(tb + 1) * 512])
                                            for kc in range(8)], reads=[wk] + hk(tb), wkey=bk)
                S.op("act", lambda: nc.scalar.copy(out=GT[d][0:16, tb * 512:(tb + 1) * 512], in_=self.ps[b][0:16, :]),
                     reads=[bk], writes=[("GT", d, tb)])
        self.w_release(1)
        for h in range(2):
            W, wk = self.w_get(f"r{h}")
            W3 = W[:, 0:4096].rearrange("p (a b) -> p a b", b=512)
            for cc in range(4):
                for tb in range(4):
                    b, bk = self.bank("pb", [4, 5])
                    S.mm(self.ps[b][:, :], [(W3[:, kc, cc * 128:(cc + 1) * 128], hT[:, kc, tb * 512:(tb + 1) * 512])
                                            for kc in range(8)], reads=[wk] + hk(tb), wkey=bk)
                    S.op("act", lambda: nc.scalar.activation(out=OG[:, h * 4 + cc, tb * 512:(tb + 1) * 512],
                                                             in_=self.ps[b][:, :], func=AF.Silu),
                         reads=[bk], writes=[("og", h * 4 + cc, tb)])
            self.w_release(1)
        gctr = [0]
        sctr = [0]
        for h in range(4):
            W, wk = self.w_get(f"qk{h}")
            Wq = W[:, 0:1024].rearrange("p (a b) -> p a b", b=128)
            Wk = W[:, 1024:2048].rearrange("p (a b) -> p a b", b=128)
            for tb in range(4):
                for (Wx, Xt, nm, sc) in ((Wq, QT, "QT", 128.0 ** -0.5), (Wk, KT, "KT", 1.0)):
                    b, bk = self.bank("pa", [2, 3])
                    S.mm(self.ps[b][:, :], [(Wx[:, kc, :], hT[:, kc, tb * 512:(tb + 1) * 512]) for kc in range(8)],
                         reads=[wk] + hk(tb), wkey=bk)
                    S.op("act", lambda: nc.scalar.activation(out=Xt[:, tb * 512:(tb + 1) * 512], in_=self.ps[b][:, :],
                                                             func=AF.Copy, scale=sc),
                         reads=[bk], writes=[(nm, tb)])
            self.w_release(1)
            W, wk = self.w_get(f"v{h}")
            Wv = W[:, 0:2048].rearrange("p (a b) -> p a b", b=256)
            for i in range(16):
                b, bk = self.bank("pb", [4, 5])
                S.mm(self.ps[b][:, 0:256], [(hT[:, kc, i * 128:(i + 1) * 128], Wv[:, kc, :]) for kc in range(8)],
                     reads=[wk, ("hT", i, "w")], wkey=bk)
                S.op("dve", lambda: nc.vector.tensor_copy(out=VTM[:, i, :], in_=self.ps[b][:, 0:256]),
                     reads=[bk], writes=[("VTM", i)])
            self.w_release(1)

            items = [(0, g) for g in range(4)] + [(1, g) for g in range(3, -1, -1)]
            col = lambda tm: slice(tm * 128, (tm + 1) * 128)

            def stA(d, gi, n):
                b, kz = self.bank("gz", [6, 7])
                zb = self.ps[b]
                S.mm_multi([(zb[:, col(tm)], [(GT[d][0:17, (gi * 4 + tm) * 128:(gi * 4 + tm + 1) * 128],
                                               WA[d][0:17, h * 128:(h + 1) * 128])]) for tm in range(4)],
                           reads=[("GT", d, gi), ("WA", d)], wkey=kz)
                S.op("act", lambda: nc.scalar.activation(out=Et, in_=zb[:, :], func=AF.Exp, scale=-1.0),
                     reads=[kz], writes=["E"])
                S.op("act", lambda: nc.scalar.activation(out=LP[n % 2], in_=Et, func=AF.Ln, bias=1.0, scale=1.0),
                     reads=["E"], writes=[("LP", n % 2)])

            def stB(d, gi, n):
                tri = self.trif if d == 0 else self.trib
                gs = slice(gi * 512, (gi + 1) * 512)
                b2, kc_ = self.bank("gz", [6, 7])
                cb = self.ps[b2]
                S.mm_multi([(cb[:, col(tm)], [(LP[n % 2][:, col(tm)], tri)]) for tm in range(4)],
                           reads=[("LP", n % 2)], wkey=kc_)
                S.op("act", lambda: nc.scalar.activation(out=EQ[n % 3], in_=cb[:, :], func=AF.Exp),
                     reads=[kc_], writes=[("EQ", n % 3)])
                S.op("act", lambda: nc.scalar.activation(out=EK, in_=cb[:, :], func=AF.Exp, scale=-1.0),
                     reads=[kc_], writes=["EK"])
                S.op("dve", lambda: nc.vector.tensor_tensor(out=QTt[n % 3], in0=QT[:, gs], in1=EQ[n % 3], op=ALU.mult),
                     reads=[("QT", gi), ("EQ", n % 3)], writes=[("QTt", n % 3)])
                S.op("dve", lambda: nc.vector.tensor_tensor(out=KTt[n % 2], in0=KT[:, gs], in1=EK, op=ALU.mult),
                     reads=[("KT", gi), "EK"], writes=[("KTt", n % 2)])

            def stC(d, gi, n):
                mask = self.maskf if d == 0 else self.maskb
                b3, ks = self.bank("sc", [0, 1])
                sb = self.ps[b3]
                S.mm_multi([(sb[:, col(tm)], [(KTt[n % 2][:, col(tm)], QTt[n % 3][:, col(tm)])]) for tm in range(4)],
                           reads=[("KTt", n % 2), ("QTt", n % 3)], wkey=ks)
                mask_b = mask.rearrange("p (o b) -> p o b", o=1).to_broadcast([128, 4, 128])
                S.op("dve", lambda: nc.vector.tensor_tensor(out=ST[n % 2].rearrange("p (a b) -> p a b", b=128),
                                                            in0=sb[:, :].rearrange("p (a b) -> p a b", b=128),
                                                            in1=mask_b, op=ALU.mult),
                     reads=[ks], writes=[("ST", n % 2)])
                b4, kt = self.bank("sc", [0, 1])
                tb16 = self.ps[b4][:, :].bitcast(BF16)
                S.tr([(tb16[:, col(tm)], KTt[n % 2][:, col(tm)]) for tm in range(4)], self.identb,
                     reads=[("KTt", n % 2)], wkey=kt)
                S.op("act", lambda: nc.scalar.copy(out=KTM[n % 2], in_=tb16[:, 0:512]), reads=[kt],
                     writes=[("KTM", n % 2)])

            def SD(d, gi, n, first_group, first_arrival):
                order = [0, 1, 2, 3] if d == 0 else [3, 2, 1, 0]
                lastc = 127 if d == 0 else 0
                p2, p3 = n % 2, n % 3
                decs = lambda tm: EQ[p3][:, tm * 128 + lastc:tm * 128 + lastc + 1]
                kvb = {}

                def kv(tm):
                    b6, kkv = self.bank("kv", [4, 5])
                    S.mm(self.ps[b6][:, 0:256], [(KTM[p2][:, col(tm)], VTM[:, gi * 4 + tm, :])],
                         reads=[("KTM", p2), ("VTM", gi * 4 + tm)], wkey=kkv)
                    kvb[tm] = (self.ps[b6][:, 0:256], kkv)

                kv(order[0])
                kv(order[1])
                obanks = {}
                for half in range(2):
                    b5, ko = self.bank("o", [2, 3])
                    obanks[half] = (self.ps[b5], ko)
                pend = {0: [], 1: []}
                for half in range(2):
                    S._wait("pe", S._collect([], [obanks[half][1]]))
                for t, tm in enumerate(order):
                    c = gi * 4 + tm
                    has_state = not (first_group and t == 0)
                    kvps, kkv = kvb[tm]
                    ob, ko = obanks[tm // 2]
                    tm2 = tm % 2
                    groups = []
                    for vc in range(2):
                        prs = [(VTM[:, c, vc * 128:(vc + 1) * 128], ST[p2][:, col(tm)])]
                        if has_state:
                            prs.append((SBF[sctr[0] % 3][:, vc * 128:(vc + 1) * 128], QTt[p3][:, col(tm)]))
                        groups.append((ob[:, vc * 256 + tm2 * 128:vc * 256 + tm2 * 128 + 128], prs))
                    rd = [("VTM", c), ("ST", p2), ("QTt", p3)] + ([("SBF", sctr[0] % 3)] if has_state else [])
                    S.mm_multi(groups, reads=rd, wkey=(ko, "part", tm2))
                    pend[tm // 2].append((ko, "part", tm2))
                    if t == 0:
                        if has_state:
                            S.op("dve", lambda: nc.vector.scalar_tensor_tensor(
                                out=Ust, in0=Ust, scalar=DECC[:, (n + 1) % 2:(n + 1) % 2 + 1], in1=kvps,
                                op0=ALU.mult, op1=ALU.add), reads=["U", ("DECC", (n + 1) % 2), kkv], writes=["U"])
                        else:
                            S.op("dve", lambda: nc.vector.tensor_copy(out=Ust, in_=kvps), reads=[kkv], writes=["U"])
                    else:
                        S.op("dve", lambda: nc.vector.scalar_tensor_tensor(out=Ust, in0=Ust, scalar=decs(order[t - 1]),
                                                                            in1=kvps, op0=ALU.mult, op1=ALU.add),
                             reads=["U", ("EQ", p3), kkv], writes=["U"])
                    nxt = (sctr[0] + 1) % 3
                    S.op("act", lambda: nc.scalar.activation(out=SBF[nxt], in_=Ust, func=AF.Copy, scale=decs(tm)),
                         reads=["U", ("EQ", p3)], writes=[("SBF", nxt)])
                    sctr[0] += 1
                    if t + 2 < 4:
                        kv(order[t + 2])
                S.op("act", lambda: nc.scalar.copy(out=DECC[:, n % 2:n % 2 + 1], in_=decs(order[3])),
                     reads=[("EQ", p3)], writes=[("DECC", n % 2)])
                gs = slice(gi * 512, (gi + 1) * 512)
                for half in range(2):
                    ob, ko = obanks[half]
                    o3 = ob[:, :].rearrange("p (a b) -> p a b", b=256)
                    cols = slice(gi * 512 + half * 256, gi * 512 + half * 256 + 256)
                    if first_arrival:
                        S.op("act", lambda: nc.scalar.copy(out=OF[:, :, cols], in_=o3), reads=pend[half],
                             writes=[("OF", gi, half), ko])
                    else:
                        S.op("dve", lambda: nc.vector.tensor_tensor(out=OS[:, :, half * 256:(half + 1) * 256], in0=o3,
                                                                    in1=OF[:, :, cols], op=ALU.add),
                             reads=pend[half] + [("OF", gi, half)], writes=[("XT", 3), ko])
                if not first_arrival:
                    S.op("act", lambda: nc.scalar.activation(out=OSQ, in_=OS, func=AF.Square),
                         reads=[("XT", 3)], writes=["OSQ"])
                    b7, kq = self.bank("gz", [6, 7])
                    qps = self.ps[b7]
                    S.mm(qps[:, :], [(self.onesb, OSQ[:, vc, :]) for vc in range(2)], reads=["OSQ"], wkey=kq)
                    S.op("act", lambda: nc.scalar.activation(out=RG, in_=qps[:, :], func=AF.Sqrt, scale=1.0 / 256,
                                                             bias=self.epsc),
                         reads=[kq], writes=["RG"])
                    S.op("dve", lambda: nc.vector.reciprocal(out=RG, in_=RG), reads=["RG"], writes=["RG"])
                    rg_b = RG.rearrange("p (o b) -> p o b", o=1).to_broadcast([128, 2, 512])
                    S.op("dve", lambda: nc.vector.tensor_tensor(out=OS, in0=OS, in1=rg_b, op=ALU.mult),
                         reads=[("XT", 3), "RG"], writes=[("XT", 3)])
                    for vc in range(2):
                        S.op("dve", lambda vc=vc: nc.vector.scalar_tensor_tensor(
                            out=OG[:, h * 2 + vc, gs], in0=OS[:, vc, :], scalar=self.pcol(P_GNG + h * 2 + vc),
                            in1=OG[:, h * 2 + vc, gs], op0=ALU.mult, op1=ALU.mult),
                            reads=[("XT", 3), ("og", h * 2 + vc, gi)], writes=[("og", h * 2 + vc, gi)])

            n0 = gctr[0]
            gctr[0] += len(items)
            L = len(items)
            stA(*items[0], n0)
            stB(*items[0], n0)
            stC(*items[0], n0)
            stA(*items[1], n0 + 1)
            stB(*items[1], n0 + 1)
            stA(*items[2], n0 + 2)
            for i_ in range(L):
                if i_ + 3 < L:
                    stA(*items[i_ + 3], n0 + i_ + 3)
                if i_ + 2 < L:
                    stB(*items[i_ + 2], n0 + i_ + 2)
                if i_ + 1 < L:
                    stC(*items[i_ + 1], n0 + i_ + 1)
                d, gi = items[i_]
                SD(d, gi, n0 + i_, first_group=(i_ == 0 or i_ == 4), first_arrival=(d == 0))
        self.mixer_out(A, OG, lambda tb: [("og", c, tb) for c in range(8)], src, dst, seq, gBpost, gqk, XT3 + [osv], T1)

    def build(self):
        self.declare()
        self.setup()
        self.make_plan()
        self.w_prime()
        S = self.S
        nsub = len(self.plan)
        for seq in range(2):
            for si, sub in enumerate(self.plan):
                src = self.x_in if si == 0 else self.xs
                dst = self.y_out if si == nsub - 1 else self.xs
                if sub == "mix0":
                    self.conv_mixer(src, dst, seq)
                elif sub == "mix1":
                    self.gla_mixer(src, dst, seq)
                else:
                    self.ffn(int(sub[3]), src, dst, seq)
        S.barrier(engines=("sp",))
        self.es.close()
        return self.nc


def host_inputs(inp):
    f = lambda a: np.ascontiguousarray(np.asarray(a, dtype=np.float32))
    pm = lambda v: f(v).reshape(-1, 128).T
    prm = np.zeros((128, NPRM), np.float32)
    for l in range(2):
        prm[:, P_MIXPRE + 8 * l:P_MIXPRE + 8 * l + 8] = pm(inp["mix_pre_g"][l])
        prm[:, P_MIXPOST + 8 * l:P_MIXPOST + 8 * l + 8] = pm(inp["mix_post_g"][l])
        prm[:, P_FFNPRE + 8 * l:P_FFNPRE + 8 * l + 8] = pm(inp["ffn_pre_g"][l])
        prm[:, P_FFNPOST + 8 * l:P_FFNPOST + 8 * l + 8] = pm(inp["ffn_post_g"][l])
    dww = f(inp["cv_dw_w"][0])
    prm[:, P_DWW:P_DWW + 124] = dww.reshape(31, 4, 128).transpose(2, 1, 0).reshape(128, 124)
    prm[:, P_DWB:P_DWB + 4] = pm(inp["cv_dw_b"][0])
    prm[:, P_LNG:P_LNG + 4] = pm(inp["cv_ln_g"][0])
    prm[:, P_LNB:P_LNB + 4] = pm(inp["cv_ln_b"][0])
    scw = f(inp["cv_sc_w"][0])
    prm[:, P_SCW:P_SCW + 12] = scw.reshape(3, 4, 128).transpose(2, 1, 0).reshape(128, 12)
    prm[:, P_GNG:P_GNG + 8] = pm(inp["gla_gn_g"][0])
    j = np.arange(128)[:, None]
    i = np.arange(128)[None, :]
    cst = np.concatenate([
        np.eye(128), np.ones((128, 128)), (j <= i) * 1.0, (j >= i) * 1.0,
        (j <= i) * (-1.0 / 16.0), (j >= i) * (-1.0 / 16.0)], axis=1).astype(np.float32)
    gvec = np.stack([f(inp["mix_pre_g"][0]), f(inp["mix_pre_g"][1]), f(inp["mix_post_g"][0]),
                     f(inp["mix_post_g"][1]), f(inp["ffn_pre_g"][0]), f(inp["ffn_pre_g"][1]),
                     f(inp["ffn_post_g"][0]), f(inp["ffn_post_g"][1])], axis=0)
    wa = np.stack([np.concatenate([f(inp["gla_wa2_f"][0]), f(inp["gla_ba2_f"])[0:1]], axis=0),
                   np.concatenate([f(inp["gla_wa2_b"][0]), f(inp["gla_ba2_b"])[0:1]], axis=0)], axis=0)
    shared = {
        "prm": prm, "cst": cst, "gvec": f(gvec), "wa": f(wa),
        "cv_w_in": f(inp["cv_w_in"][0]), "cv_w_out": f(inp["cv_w_out"][0]),
        "gla_w_in": f(inp["gla_w_in"][0]), "gla_w_out": f(inp["gla_w_out"][0]),
        "ffn_w_gu": f(inp["ffn_w_gu"]), "ffn_w_down": f(inp["ffn_w_down"]),
    }
    x = f(inp["x"]).reshape(NCORES, TOK, D)
    return [dict(shared, x=x[c]) for c in range(NCORES)]


_CACHE = {}


def run(inp, plan=("mix0", "ffn0", "mix1", "ffn1")):
    plan = tuple(plan)
    if plan not in _CACHE:
        _CACHE[plan] = Builder(plan).build()
    nc = _CACHE[plan]
    in_maps = host_inputs(inp)
    res = run_bass_kernel_spmd(nc, in_maps, core_ids=list(range(NCORES)))
    out = np.stack([np.asarray(r["y"], dtype=np.float32) for r in res.results], axis=0)
    return out.reshape(16, SEQ, D)


def kernel(**inputs):
    return run(inputs)
```

```python
import math
from contextlib import ExitStack

import numpy as np
import concourse.bass as bass
import concourse.mybir as mybir
from concourse.bass_utils import run_bass_kernel_spmd
from concourse.alu_op_type import AluOpType as ALU

F32 = mybir.dt.float32
BF16 = mybir.dt.bfloat16
AF = mybir.ActivationFunctionType

NCORES = 8
D = 1024
SEQ = 2048
TOK = 4096
DFF = 2816
EPS = 1e-6
SLOT_ELEMS = 5632
NSLOT = 4
NPRM = 220

P_MIXPRE, P_MIXPOST, P_FFNPRE, P_FFNPOST = 0, 16, 32, 48
P_DWW = 64
P_DWB = 188
P_LNG = 192
P_LNB = 196
P_SCW = 200
P_GNG = 212


class Sched:
    def __init__(self, nc, es):
        self.nc = nc
        self.E = {"pe": nc.tensor, "act": nc.scalar, "dve": nc.vector, "pool": nc.gpsimd, "sp": nc.sync}
        self.psem = {}
        for e in ("pe", "act", "dve", "pool"):
            self.psem[e] = es.enter_context(nc.semaphore(f"p_{e}"))
        self.pcnt = {e: 0 for e in self.psem}
        self.waited = {e: {} for e in self.E}
        self.state = {}
        self.dsems = {}
        for q in ("sp", "pool"):
            self.dsems[q] = [[es.enter_context(nc.semaphore(f"d_{q}{i}")), 0, f"d_{q}{i}"] for i in range(10)]
        self.drr = {q: 0 for q in self.dsems}
        self.slotsem = [[es.enter_context(nc.semaphore(f"w_{i}")), 0, f"w_{i}"] for i in range(NSLOT)]

    def _collect(self, reads, writes):
        need = {}

        def add(n, s, v):
            if n not in need or need[n][1] < v:
                need[n] = (s, v)

        for k in reads:
            st = self.state.get(k)
            if st and st[0] is not None:
                add(*st[0])
        for k in writes:
            st = self.state.get(k)
            if st:
                if st[0] is not None:
                    add(*st[0])
                for n, (s, v) in st[1].items():
                    add(n, s, v)
        return need

    def _wait(self, e, need):
        for n, (s, v) in need.items():
            if self.waited[e].get(n, 0) >= v:
                continue
            self.E[e].wait_ge(s, v)
            self.waited[e][n] = v

    def _commit(self, tok, reads, writes):
        n, s, v = tok
        for k in reads:
            st = self.state.setdefault(k, [None, {}])
            st[1][n] = (s, v)
        for k in writes:
            self.state[k] = [tok, {}]

    def op(self, e, fn, reads=(), writes=()):
        need = self._collect(reads, writes)
        self._wait(e, need)
        ins = fn()
        self.pcnt[e] += 1
        ins.then_inc(self.psem[e], 1)
        tok = (f"p_{e}", self.psem[e], self.pcnt[e])
        self._commit(tok, reads, writes)
        return tok

    def mm(self, out, pairs, reads, wkey):
        need = self._collect(reads, [wkey])
        self._wait("pe", need)
        n = len(pairs)
        ins = None
        for i, (l, r) in enumerate(pairs):
            ins = self.nc.tensor.matmul(out, l, r, start=(i == 0), stop=(i == n - 1))
        self.pcnt["pe"] += 1
        ins.then_inc(self.psem["pe"], 1)
        tok = ("p_pe", self.psem["pe"], self.pcnt["pe"])
        self._commit(tok, reads, [wkey])
        return tok

    def mm_multi(self, groups, reads, wkey):
        need = self._collect(reads, [wkey])
        self._wait("pe", need)
        ins = None
        for out, pairs in groups:
            n = len(pairs)
            for i, (l, r) in enumerate(pairs):
                ins = self.nc.tensor.matmul(out, l, r, start=(i == 0), stop=(i == n - 1))
        self.pcnt["pe"] += 1
        ins.then_inc(self.psem["pe"], 1)
        tok = ("p_pe", self.psem["pe"], self.pcnt["pe"])
        self._commit(tok, reads, [wkey])
        return tok

    def tr(self, items, ident, reads, wkey):
        need = self._collect(reads, [wkey])
        self._wait("pe", need)
        ins = None
        for out, in_ in items:
            ins = self.nc.tensor.transpose(out, in_, ident)
        self.pcnt["pe"] += 1
        ins.then_inc(self.psem["pe"], 1)
        tok = ("p_pe", self.psem["pe"], self.pcnt["pe"])
        self._commit(tok, reads, [wkey])
        return tok

    def dma(self, q, out, in_, reads=(), writes=(), semrec=None, nonc=False):
        if semrec is None:
            semrec = self.dsems[q][self.drr[q]]
            self.drr[q] = (self.drr[q] + 1) % len(self.dsems[q])
            need = self._collect(reads, writes)
            if semrec[1] > 0:
                need[semrec[2]] = (semrec[0], semrec[1])
        else:
            need = self._collect(reads, writes)
        self._wait(q, need)
        if nonc:
            ins = self.E[q].dma_start(out=out, in_=in_, allow_slow_non_contiguous=True)
        else:
            ins = self.E[q].dma_start(out=out, in_=in_)
        semrec[1] += 16
        ins.then_inc(semrec[0], 16)
        tok = (semrec[2], semrec[0], semrec[1])
        self._commit(tok, reads, writes)
        return tok

    def barrier(self, engines=("pe", "act", "dve", "sp")):
        need = {}
        for e in self.psem:
            if self.pcnt[e] > 0:
                need[f"p_{e}"] = (self.psem[e], self.pcnt[e])
        for q in self.dsems:
            for rec in self.dsems[q]:
                if rec[1] > 0:
                    need[rec[2]] = (rec[0], rec[1])
        for e in engines:
            self._wait(e, need)


class Arena:
    def __init__(self, big, base_b, limit_b):
        self.big = big
        self.base = base_b
        self.limit = limit_b
        self.off = base_b

    def reset(self):
        self.off = self.base

    def alloc(self, shape, dt):
        esz = 2 if dt == BF16 else 4
        n = 1
        for s in shape[1:]:
            n *= s
        nb = (n * esz + 63) // 64 * 64
        assert self.off + nb <= self.limit, f"arena overflow {self.off + nb} > {self.limit}"
        a = self.big[0:shape[0], self.off // 2: self.off // 2 + n * esz // 2]
        self.off += nb
        if dt == F32:
            a = a.bitcast(F32)
        if len(shape) == 3:
            a = a.rearrange("p (a b) -> p a b", b=shape[2])
        elif len(shape) == 4:
            a = a.rearrange("p (a b c) -> p a b c", b=shape[2], c=shape[3])
        return a


class Builder:
    def __init__(self, plan):
        self.plan = plan
        self.nc = bass.Bass("TRN2", target_bir_lowering=False)
        self.es = ExitStack()

    def declare(self):
        nc = self.nc
        dt = lambda name, shape, kind="ExternalInput": nc.dram_tensor(name, shape, F32, kind=kind).ap()
        self.x_in = dt("x", [TOK, D])
        self.y_out = dt("y", [TOK, D], "ExternalOutput")
        self.xs = dt("xs", [TOK, D], "Internal")
        self.prm_d = dt("prm", [128, NPRM])
        self.cst_d = dt("cst", [128, 6 * 128])
        self.gvec_d = dt("gvec", [8, D])
        self.wa_d = dt("wa", [2, 17, 512])
        self.cv_w_in = dt("cv_w_in", [D, 2560])
        self.cv_w_out = dt("cv_w_out", [D, D])
        self.gla_w_in = dt("gla_w_in", [D, 3104])
        self.gla_w_out = dt("gla_w_out", [D, D])
        self.ffn_w_gu = dt("ffn_w_gu", [2, D, 2 * DFF])
        self.ffn_w_down = dt("ffn_w_down", [2, DFF, D])

    def setup(self):
        nc, es = self.nc, self.es
        self.S = Sched(nc, es)
        S = self.S
        total_b = 212800
        self.big = es.enter_context(nc.sbuf_tensor("big", [128, total_b // 2], BF16))
        self.ps = [es.enter_context(nc.psum_tensor(f"ps{i}", [128, 512], F32)) for i in range(8)]
        carve = Arena(self.big, 0, total_b)
        self.slots = [carve.alloc([128, SLOT_ELEMS], BF16) for _ in range(NSLOT)]
        self.identb = carve.alloc([128, 128], BF16)
        self.onesb = carve.alloc([128, 128], BF16)
        self.maskf = carve.alloc([128, 128], BF16)
        self.maskb = carve.alloc([128, 128], BF16)
        self.trif = carve.alloc([128, 128], F32)
        self.trib = carve.alloc([128, 128], F32)
        self.prm = carve.alloc([128, NPRM], F32)
        self.arena = Arena(self.big, carve.off, total_b)
        c = self.cst_d
        S.dma("pool", self.identb, c[:, 0:128], writes=["c0"])
        S.dma("pool", self.onesb, c[:, 128:256], writes=["c1"])
        S.dma("pool", self.maskf, c[:, 256:384], writes=["c2"])
        S.dma("pool", self.maskb, c[:, 384:512], writes=["c3"])
        S.dma("sp", self.trif, c[:, 512:640], writes=["c4"])
        S.dma("sp", self.trib, c[:, 640:768], writes=["c5"])
        S.dma("sp", self.prm, self.prm_d[:, :], writes=["c6"])
        S.barrier()
        self.wplan = []
        self.wnext_issue = 0
        self.wnext_use = 0
        self.psrr = {}

    def bank(self, role, banks):
        i = self.psrr.get(role, 0)
        self.psrr[role] = i + 1
        b = banks[i % len(banks)]
        return b, ("ps", b)

    def pcol(self, c0, n=1):
        return self.prm[:, c0:c0 + n]

    def w_issue(self, idx):
        if idx >= len(self.wplan):
            return
        S = self.S
        slot = idx % NSLOT
        rec = S.slotsem[slot]
        S._wait("pool", S._collect([], [("w", slot)]))
        for (eoff, shp, src) in self.wplan[idx]:
            n = shp[0] * shp[1]
            dst = self.slots[slot][:, eoff:eoff + n].rearrange("p (a b) -> p a b", b=shp[1])
            ins = self.nc.gpsimd.dma_start(out=dst, in_=src)
            rec[1] += 16
            ins.then_inc(rec[0], 16)
        S._commit((rec[2], rec[0], rec[1]), [], [("w", slot)])

    def w_prime(self):
        for i in range(NSLOT):
            self.w_issue(i)
        self.wnext_issue = NSLOT

    def w_get(self, tag):
        idx = self.wnext_use
        assert self.wtags[idx] == tag, (idx, self.wtags[idx], tag)
        self.wnext_use += 1
        slot = idx % NSLOT
        return self.slots[slot], ("w", slot)

    def w_release(self, n=1):
        for _ in range(n):
            self.w_issue(self.wnext_issue)
            self.wnext_issue += 1

    def add_load(self, tag, parts):
        self.wplan.append(parts)
        self.wtags.append(tag)

    @staticmethod
    def wsrc(w2d, r0, nr, c0, ncol):
        return w2d[r0:r0 + nr, c0:c0 + ncol].rearrange("(kc p) n -> p kc n", p=128)

    def make_plan(self):
        self.wtags = []
        for s in range(2):
            for sub in self.plan:
                if sub == "mix0":
                    w = self.cv_w_in
                    for nm, c0 in (("aval", 0), ("agate", 512), ("cgate", 1536), ("v", 2048), ("bgate", 1024)):
                        self.add_load(nm, [(0, (8, 512), self.wsrc(w, 0, D, c0, 512))])
                    for h in range(2):
                        self.add_load(f"wout{h}", [(0, (8, 512), self.wsrc(self.cv_w_out, 0, D, h * 512, 512))])
                elif sub == "mix1":
                    w = self.gla_w_in
                    self.add_load("gates", [(0, (8, 32), self.wsrc(w, 0, D, 3072, 32))])
                    for h in range(2):
                        self.add_load(f"r{h}", [(0, (8, 512), self.wsrc(w, 0, D, 2048 + h * 512, 512))])
                    for h in range(4):
                        self.add_load(f"qk{h}", [(0, (8, 128), self.wsrc(w, 0, D, h * 128, 128)),
                                                 (1024, (8, 128), self.wsrc(w, 0, D, 512 + h * 128, 128))])
                        self.add_load(f"v{h}", [(0, (8, 256), self.wsrc(w, 0, D, 1024 + h * 256, 256))])
                    for h in range(2):
                        self.add_load(f"wout{h}", [(0, (8, 512), self.wsrc(self.gla_w_out, 0, D, h * 512, 512))])
                elif sub in ("ffn0", "ffn1"):
                    l = int(sub[3])
                    wg = self.ffn_w_gu[l]
                    wd = self.ffn_w_down[l]
                    for hb in range(2):
                        for L in range(11):
                            self.add_load(f"gu{L}", [(0, (8, 256), self.wsrc(wg, 0, D, L * 256, 256)),
                                                     (2048, (8, 256), self.wsrc(wg, 0, D, DFF + L * 256, 256))])
                        for half in range(2):
                            for part in range(2):
                                self.add_load(f"dn{half}{part}",
                                              [(0, (11, 512), self.wsrc(wd, part * 1408, 1408, half * 512, 512))])

    def load_gb(self, A, row):
        g = A.alloc([128, D], F32)
        self.S.dma("sp", g, self.gvec_d[row:row + 1, :].to_broadcast([128, D]), writes=[("gb", row)])
        return g, ("gb", row)

    def pn_stats(self, xt, xkey, rs_col, rskey, ss_col, sskey):
        S, nc = self.S, self.nc
        S.op("act", lambda: nc.scalar.activation(out=self.sqj, in_=xt, func=AF.Square, accum_out=ss_col),
             reads=[xkey], writes=[sskey])
        S.op("act", lambda: nc.scalar.activation(out=rs_col, in_=ss_col, func=AF.Sqrt, scale=1.0 / D, bias=self.epsc),
             reads=[sskey], writes=[rskey])
        S.op("dve", lambda: nc.vector.reciprocal(out=rs_col, in_=rs_col), reads=[rskey], writes=[rskey])

    def pn_apply(self, xt, xkey, rs_col, rskey, gB, gkey, htm, htmkey, dst_cols, hkey):
        S, nc = self.S, self.nc
        S.op("dve", lambda: nc.vector.scalar_tensor_tensor(out=htm, in0=xt, scalar=rs_col, in1=gB,
                                                            op0=ALU.mult, op1=ALU.mult),
             reads=[xkey, rskey, gkey], writes=[htmkey])
        b, bkey = self.bank("pt", [0, 1])
        pv = self.ps[b][:, :].bitcast(BF16)
        S.tr([(pv[:, c * 128:(c + 1) * 128], htm[:, c * 128:(c + 1) * 128]) for c in range(8)], self.identb,
             reads=[htmkey], wkey=bkey)
        S.op("act", lambda: nc.scalar.copy(out=dst_cols, in_=pv.rearrange("p (c t) -> p c t", t=128)),
             reads=[bkey], writes=[hkey])

    def postnorm_tile(self, ops, okeys, xt, xkey, gB, gkey, t1, t1key, ssb, idx):
        S, nc = self.S, self.nc
        c0 = ssb[:, 4 * idx:4 * idx + 1]
        c1 = ssb[:, 4 * idx + 1:4 * idx + 2]
        c2 = ssb[:, 4 * idx + 2:4 * idx + 3]
        k = ("ssb", idx)
        S.op("act", lambda: nc.scalar.activation(out=self.sqj[:, 0:512], in_=ops[0], func=AF.Square, accum_out=c0),
             reads=[okeys[0]], writes=[(k, 0)])
        S.op("act", lambda: nc.scalar.activation(out=self.sqj[:, 512:1024], in_=ops[1], func=AF.Square, accum_out=c1),
             reads=[okeys[1]], writes=[(k, 1)])
        S.op("dve", lambda: nc.vector.tensor_tensor(out=c2, in0=c0, in1=c1, op=ALU.add),
             reads=[(k, 0), (k, 1)], writes=[(k, 2)])
        S.op("act", lambda: nc.scalar.activation(out=c2, in_=c2, func=AF.Sqrt, scale=1.0 / D, bias=self.epsc),
             reads=[(k, 2)], writes=[(k, 2)])
        S.op("dve", lambda: nc.vector.reciprocal(out=c2, in_=c2), reads=[(k, 2)], writes=[(k, 2)])
        for h in range(2):
            S.op("dve", lambda h=h: nc.vector.scalar_tensor_tensor(
                out=t1[:, h * 512:(h + 1) * 512], in0=ops[h], scalar=c2, in1=gB[:, h * 512:(h + 1) * 512],
                op0=ALU.mult, op1=ALU.mult), reads=[okeys[h], (k, 2), gkey], writes=[(t1key, h)])
        S.op("dve", lambda: nc.vector.tensor_tensor(out=xt, in0=xt, in1=t1, op=ALU.add),
             reads=[xkey, (t1key, 0), (t1key, 1)], writes=[xkey])

    def common_tmps(self, A):
        self.sqj = A.alloc([128, D], BF16)
        self.epsc = A.alloc([128, 1], F32)
        self.S.op("dve", lambda: self.nc.vector.memset(self.epsc, EPS), writes=["epsc"])
        self.S.barrier()

    def ffn(self, l, src, dst, seq):
        S, nc, A = self.S, self.nc, self.arena
        S.barrier()
        A.reset()
        self.common_tmps(A)
        gBpre, gpk = self.load_gb(A, 4 + l)
        gBpost, gqk = self.load_gb(A, 6 + l)
        XH = A.alloc([128, 8, D], F32)
        hT = A.alloc([128, 8, 1024], BF16)
        M0 = hT.rearrange("p a b -> p (a b)").bitcast(F32).rearrange("p (a b) -> p a b", b=512)
        ACTB = A.alloc([128, 22, 1024], BF16)
        HTM = [A.alloc([128, D], BF16) for _ in range(2)]
        SG = [A.alloc([128, 512], BF16) for _ in range(2)]
        T1 = [A.alloc([128, D], F32) for _ in range(2)]
        SS = A.alloc([128, 16], F32)
        RS = A.alloc([128, 16], F32)
        SSB = A.alloc([128, 64], F32)
        for hb in range(2):
            S.barrier()
            t0 = seq * SEQ + hb * 1024
            for i in range(8):
                r0 = t0 + i * 128
                S.dma("sp", XH[:, i, :], src[r0:r0 + 128, :], reads=[("xd", r0)], writes=[("XH", i)])
            for i in range(8):
                self.pn_stats(XH[:, i, :], ("XH", i), RS[:, i:i + 1], ("rs", i), SS[:, i:i + 1], ("ss", i))
            for i in range(8):
                self.pn_apply(XH[:, i, :], ("XH", i), RS[:, i:i + 1], ("rs", i), gBpre, gpk,
                              HTM[i % 2], ("htm", i % 2), hT[:, :, i * 128:(i + 1) * 128], ("hT", i, "w"))
            for L in range(11):
                W, wk = self.w_get(f"gu{L}")
                Wg = W[:, 0:2048].rearrange("p (a b) -> p a b", b=256)
                Wu = W[:, 2048:4096].rearrange("p (a b) -> p a b", b=256)
                for cc in range(2):
                    c = 2 * L + cc
                    for tb in range(2):
                        bg, kg = self.bank("g", [2, 3])
                        bu, ku = self.bank("u", [4, 5])
                        rhs = lambda kc: hT[:, kc, tb * 512:(tb + 1) * 512]
                        S.mm(self.ps[bg][:, :], [(Wg[:, kc, cc * 128:(cc + 1) * 128], rhs(kc)) for kc in range(8)],
                             reads=[wk] + [("hT", 4 * tb + q, "w") for q in range(4)], wkey=kg)
                        S.mm(self.ps[bu][:, :], [(Wu[:, kc, cc * 128:(cc + 1) * 128], rhs(kc)) for kc in range(8)],
                             reads=[wk] + [("hT", 4 * tb + q, "w") for q in range(4)], wkey=ku)
                        sg = SG[(2 * c + tb) % 2]
                        sgk = ("sg", (2 * c + tb) % 2)
                        S.op("act", lambda: nc.scalar.activation(out=sg, in_=self.ps[bg][:, :], func=AF.Silu),
                             reads=[kg], writes=[sgk])
                        S.op("dve", lambda: nc.vector.tensor_tensor(out=ACTB[:, c, tb * 512:(tb + 1) * 512], in0=sg,
                                                                    in1=self.ps[bu][:, :], op=ALU.mult),
                             reads=[sgk, ku], writes=[("actb", c, tb)])
                self.w_release()
            for half in range(2):
                Wa, wka = self.w_get(f"dn{half}0")
                Wb, wkb = self.w_get(f"dn{half}1")
                Wa3 = Wa[:, 0:5632].rearrange("p (a b) -> p a b", b=512)
                Wb3 = Wb[:, 0:5632].rearrange("p (a b) -> p a b", b=512)
                for i in range(8):
                    bf, kf = self.bank("f", [6, 7])
                    tb = i // 4
                    pairs = []
                    for kc in range(22):
                        w3 = Wa3 if kc < 11 else Wb3
                        pairs.append((ACTB[:, kc, i * 128:(i + 1) * 128], w3[:, kc % 11, :]))
                    S.mm(self.ps[bf][:, :], pairs, reads=[wka, wkb] + [("actb", kc, tb) for kc in range(22)], wkey=kf)
                    c0 = SSB[:, 4 * i + half:4 * i + half + 1]
                    k = ("ssb", i)
                    if half == 0:
                        S.op("act", lambda: nc.scalar.activation(out=self.sqj[:, 0:512], in_=self.ps[bf][:, :],
                                                                 func=AF.Square, accum_out=c0),
                             reads=[kf], writes=[(k, 0)])
                        S.op("act", lambda: nc.scalar.copy(out=M0[:, i, :], in_=self.ps[bf][:, :]),
                             reads=[kf], writes=[("m0", i)] + [("hT", q, "w") for q in range(8)])
                    else:
                        c2 = SSB[:, 4 * i + 2:4 * i + 3]
                        S.op("act", lambda: nc.scalar.activation(out=self.sqj[:, 512:1024], in_=self.ps[bf][:, :],
                                                                 func=AF.Square, accum_out=c0),
                             reads=[kf], writes=[(k, 1)])
                        S.op("dve", lambda: nc.vector.tensor_tensor(out=c2, in0=SSB[:, 4 * i:4 * i + 1], in1=c0,
                                                                    op=ALU.add),
                             reads=[(k, 0), (k, 1)], writes=[(k, 2)])
                        S.op("act", lambda: nc.scalar.activation(out=c2, in_=c2, func=AF.Sqrt, scale=1.0 / D,
                                                                 bias=self.epsc),
                             reads=[(k, 2)], writes=[(k, 2)])
                        S.op("dve", lambda: nc.vector.reciprocal(out=c2, in_=c2), reads=[(k, 2)], writes=[(k, 2)])
                        t1 = T1[i % 2]
                        tk = ("t1", i % 2)
                        S.op("dve", lambda: nc.vector.scalar_tensor_tensor(
                            out=t1[:, 512:1024], in0=self.ps[bf][:, :], scalar=c2, in1=gBpost[:, 512:1024],
                            op0=ALU.mult, op1=ALU.mult), reads=[kf, (k, 2), gqk], writes=[(tk, 1)])
                        S.op("dve", lambda: nc.vector.scalar_tensor_tensor(
                            out=t1[:, 0:512], in0=M0[:, i, :], scalar=c2, in1=gBpost[:, 0:512],
                            op0=ALU.mult, op1=ALU.mult), reads=[("m0", i), (k, 2), gqk], writes=[(tk, 0)])
                        S.op("dve", lambda: nc.vector.tensor_tensor(out=XH[:, i, :], in0=XH[:, i, :], in1=t1,
                                                                    op=ALU.add),
                             reads=[("XH", i), (tk, 0), (tk, 1)], writes=[("XH", i)])
                        r0 = t0 + i * 128
                        S.dma("sp", dst[r0:r0 + 128, :], XH[:, i, :], reads=[("XH", i)], writes=[("xd", r0)])
                self.w_release(2)

    def mixer_prenorm(self, A, src, seq, gB, gk, hT, ring):
        S = self.S
        HTM = [A.alloc([128, D], BF16) for _ in range(2)]
        SS = A.alloc([128, 16], F32)
        RS = A.alloc([128, 16], F32)
        R = len(ring)

        def load(i):
            r0 = seq * SEQ + i * 128
            S.dma("sp", ring[i % R], src[r0:r0 + 128, :], reads=[("xd", r0)], writes=[("XT", i % R)])

        def stats(i):
            self.pn_stats(ring[i % R], ("XT", i % R), RS[:, i:i + 1], ("rs", i), SS[:, i:i + 1], ("ss", i))

        for i in range(R - 1):
            load(i)
        stats(0)
        for i in range(16):
            if i + 1 < 16:
                stats(i + 1)
            self.pn_apply(ring[i % R], ("XT", i % R), RS[:, i:i + 1], ("rs", i), gB, gk, HTM[i % 2],
                          ("htm", i % 2), hT[:, :, i * 128:(i + 1) * 128], ("hT", i, "w"))
            if i + R - 1 < 16:
                load(i + R - 1)

    def mixer_out(self, A, CAT, catkeys, src, dst, seq, gB, gk, ring, T1):
        S, nc = self.S, self.nc
        SSB = A.alloc([128, 64], F32)
        W0, wk0 = self.w_get("wout0")
        W1, wk1 = self.w_get("wout1")
        W3 = [W0[:, 0:4096].rearrange("p (a b) -> p a b", b=512), W1[:, 0:4096].rearrange("p (a b) -> p a b", b=512)]
        wks = [wk0, wk1]
        R = len(ring)

        def load(i):
            r0 = seq * SEQ + i * 128
            S.dma("sp", ring[i % R], src[r0:r0 + 128, :], reads=[("xd", r0)], writes=[("XT", i % R)])

        for i in range(R - 1):
            load(i)
        for i in range(16):
            r0 = seq * SEQ + i * 128
            xt = ring[i % R]
            xk = ("XT", i % R)
            ops, oks = [], []
            for h in range(2):
                b, bk = self.bank("o", [2, 3, 4, 5])
                S.mm(self.ps[b][:, :], [(CAT[:, kc, i * 128:(i + 1) * 128], W3[h][:, kc, :]) for kc in range(8)],
                     reads=[wks[h]] + catkeys(i // 4), wkey=bk)
                ops.append(self.ps[b][:, :])
                oks.append(bk)
            self.postnorm_tile(ops, oks, xt, xk, gB, gk, T1, ("XT", 4), SSB, i)
            S.dma("sp", dst[r0:r0 + 128, :], xt, reads=[xk], writes=[("xd", r0)])
            if i + R - 1 < 16:
                load(i + R - 1)
        self.w_release(2)

    def conv_mixer(self, src, dst, seq):
        S, nc, A = self.S, self.nc, self.arena
        S.barrier()
        A.reset()
        self.common_tmps(A)
        gBpre, gpk = self.load_gb(A, 0)
        gBpost, gqk = self.load_gb(A, 2)
        hT = A.alloc([128, 8, SEQ], BF16)
        AB = A.alloc([128, 4, 2080], BF16)
        CV = A.alloc([128, 4, 2052], BF16)
        BG = A.alloc([128, 4, SEQ], BF16)
        DGs = [A.alloc([128, 31, 128], BF16) for _ in range(2)]
        DGB = A.alloc([128, 3, 128], BF16)
        SIG = [A.alloc([128, 512], F32) for _ in range(2)]
        VS = [A.alloc([128, 512], BF16) for _ in range(2)]
        YSQ = A.alloc([128, 4, 512], BF16)
        MEAN = A.alloc([128, 512], F32)
        MSQ = A.alloc([128, 512], F32)
        SDv = A.alloc([128, 512], F32)
        Dt = [A.alloc([128, 512], F32) for _ in range(2)]
        Zt = [A.alloc([128, 512], F32) for _ in range(2)]
        S.op("dve", lambda: nc.vector.memset(AB[:, :, 0:15], 0.0), writes=["abh0"])
        S.op("dve", lambda: nc.vector.memset(AB[:, :, 2063:2080], 0.0), writes=["abh1"])
        S.op("dve", lambda: nc.vector.memset(CV[:, :, 0:1], 0.0), writes=["cvh0"])
        S.op("dve", lambda: nc.vector.memset(CV[:, :, 2049:2052], 0.0), writes=["cvh1"])
        ring = [A.alloc([128, D], F32) for _ in range(4)]
        T1 = A.alloc([128, D], F32)
        self.mixer_prenorm(A, src, seq, gBpre, gpk, hT, ring + [T1])

        def proj(W, wk, j, tb, role, banks):
            b, bk = self.bank(role, banks)
            S.mm(self.ps[b][:, :], [(W[:, kc, j * 128:(j + 1) * 128], hT[:, kc, tb * 512:(tb + 1) * 512])
                                    for kc in range(8)], reads=[wk] + [("hT", 4 * tb + q_, "w") for q_ in range(4)], wkey=bk)
            return self.ps[b][:, :], bk

        def w3(tag):
            W, wk = self.w_get(tag)
            return W[:, 0:4096].rearrange("p (a b) -> p a b", b=512), wk

        Wv, wkv = w3("aval")
        Wg, wkg = w3("agate")
        n = 0
        for j in range(4):
            for tb in range(4):
                pv, kv = proj(Wv, wkv, j, tb, "pa", [2, 3])
                pg, kg = proj(Wg, wkg, j, tb, "pb", [4, 5])
                sg, sk = SIG[n % 2], ("sig", n % 2)
                S.op("act", lambda: nc.scalar.activation(out=sg, in_=pg, func=AF.Sigmoid), reads=[kg], writes=[sk])
                S.op("dve", lambda: nc.vector.tensor_tensor(out=AB[:, j, 15 + tb * 512:15 + (tb + 1) * 512],
                                                            in0=pv, in1=sg, op=ALU.mult),
                     reads=[kv, sk], writes=[("Ag", j, tb)])
                n += 1
        self.w_release(2)
        Wc, wkc = w3("cgate")
        Wvv, wkvv = w3("v")
        for j in range(4):
            for tb in range(4):
                pc, kc_ = proj(Wc, wkc, j, tb, "pa", [2, 3])
                pv, kv = proj(Wvv, wkvv, j, tb, "pb", [4, 5])
                vs, vk = VS[n % 2], ("vs", n % 2)
                S.op("act", lambda: nc.scalar.copy(out=vs, in_=pv), reads=[kv], writes=[vk])
                S.op("dve", lambda: nc.vector.tensor_tensor(out=CV[:, j, 1 + tb * 512:1 + (tb + 1) * 512],
                                                            in0=pc, in1=vs, op=ALU.mult),
                     reads=[kc_, vk], writes=[("CV", j, tb)])
                n += 1
        self.w_release(2)
        Wb, wkb = w3("bgate")
        for j in range(4):
            for tb in range(4):
                pb, kb = proj(Wb, wkb, j, tb, "pa", [2, 3])
                S.op("act", lambda: nc.scalar.copy(out=BG[:, j, tb * 512:(tb + 1) * 512], in_=pb),
                     reads=[kb], writes=[("BG", j, tb)])
        self.w_release(1)
        S.barrier()
        CAT = hT
        for j in range(4):
            DG = DGs[j % 2]
            for k in range(31):
                S.op("dve", lambda k=k: nc.vector.tensor_scalar(out=DG[:, k, :], in0=self.identb,
                                                                 scalar1=self.pcol(P_DWW + j * 31 + k), scalar2=None,
                                                                 op0=ALU.mult),
                     writes=[("DG", j % 2, k)])
            for tb in range(4):
                b, bk = self.bank("cv", [2, 3])
                rd = [("DG", j % 2, k) for k in range(31)] + [("Ag", j, t) for t in (tb - 1, tb, tb + 1) if 0 <= t < 4]
                rd += [("Ar", j, tb), ("Ar", j, tb + 1), "abh0", "abh1"]
                S.mm(self.ps[b][:, :], [(DG[:, k, :], AB[:, j, tb * 512 + k:tb * 512 + k + 512]) for k in range(31)],
                     reads=rd, wkey=bk)
                S.op("act", lambda: nc.scalar.activation(out=AB[:, j, tb * 512:(tb + 1) * 512], in_=self.ps[b][:, :],
                                                         func=AF.Identity, bias=self.pcol(P_DWB + j), scale=1.0),
                     reads=[bk], writes=[("Ar", j, tb)])
        for tb in range(4):
            for j in range(4):
                S.op("act", lambda j=j: nc.scalar.activation(out=YSQ[:, j, :], in_=AB[:, j, tb * 512:(tb + 1) * 512],
                                                             func=AF.Square),
                     reads=[("Ar", j, tb)], writes=[("ysq", j)])
            bm, km = self.bank("st", [6, 7])
            S.mm(self.ps[bm][:, :], [(self.onesb, AB[:, j, tb * 512:(tb + 1) * 512]) for j in range(4)],
                 reads=[("Ar", j, tb) for j in range(4)], wkey=km)
            be, ke = self.bank("st", [6, 7])
            S.mm(self.ps[be][:, :], [(self.onesb, YSQ[:, j, :]) for j in range(4)],
                 reads=[("ysq", j) for j in range(4)], wkey=ke)
            S.op("act", lambda: nc.scalar.activation(out=MEAN, in_=self.ps[bm][:, :], func=AF.Copy, scale=1.0 / 512),
                 reads=[km], writes=["mean"])
            S.op("act", lambda: nc.scalar.activation(out=MSQ, in_=self.ps[bm][:, :], func=AF.Square, scale=1.0 / 512),
                 reads=[km], writes=["msq"])
            S.op("dve", lambda: nc.vector.scalar_tensor_tensor(out=SDv, in0=self.ps[be][:, :], scalar=1.0 / 512,
                                                                in1=MSQ, op0=ALU.mult, op1=ALU.subtract),
                 reads=[ke, "msq"], writes=["sd"])
            S.op("act", lambda: nc.scalar.activation(out=SDv, in_=SDv, func=AF.Sqrt, bias=self.epsc, scale=1.0),
                 reads=["sd"], writes=["sd"])
            S.op("dve", lambda: nc.vector.reciprocal(out=SDv, in_=SDv), reads=["sd"], writes=["sd"])
            for j in range(4):
                d, dk_ = Dt[j % 2], ("dt", j % 2)
                z, zk = Zt[j % 2], ("zt", j % 2)
                S.op("dve", lambda: nc.vector.tensor_tensor(out=d, in0=AB[:, j, tb * 512:(tb + 1) * 512], in1=MEAN,
                                                            op=ALU.subtract),
                     reads=[("Ar", j, tb), "mean"], writes=[dk_])
                S.op("dve", lambda: nc.vector.tensor_tensor(out=z, in0=d, in1=SDv, op=ALU.mult),
                     reads=[dk_, "sd"], writes=[zk])
                S.op("act", lambda: nc.scalar.activation(out=CAT[:, j, tb * 512:(tb + 1) * 512], in_=z, func=AF.Silu,
                                                         scale=self.pcol(P_LNG + j), bias=self.pcol(P_LNB + j)),
                     reads=[zk], writes=[("cat", j, tb)])
        for j in range(4):
            for k in range(3):
                S.op("dve", lambda k=k: nc.vector.tensor_scalar(out=DGB[:, k, :], in0=self.identb,
                                                                 scalar1=self.pcol(P_SCW + j * 3 + k), scalar2=None,
                                                                 op0=ALU.mult),
                     writes=[("DGB", k)])
            for tb in range(4):
                b, bk = self.bank("cv", [2, 3])
                rd = [("DGB", k) for k in range(3)] + [("CV", j, t) for t in (tb - 1, tb, tb + 1) if 0 <= t < 4]
                rd += ["cvh0", "cvh1"]
                S.mm(self.ps[b][:, :], [(DGB[:, k, :], CV[:, j, tb * 512 + k:tb * 512 + k + 512]) for k in range(3)],
                     reads=rd, wkey=bk)
                S.op("dve", lambda: nc.vector.tensor_tensor(out=CAT[:, 4 + j, tb * 512:(tb + 1) * 512],
                                                            in0=self.ps[b][:, :], in1=BG[:, j, tb * 512:(tb + 1) * 512],
                                                            op=ALU.mult),
                     reads=[bk, ("BG", j, tb)], writes=[("cat", 4 + j, tb)])
        self.mixer_out(A, CAT, lambda tb: [("cat", c, tb) for c in range(8)], src, dst, seq, gBpost, gqk, ring, T1)

    def gla_mixer(self, src, dst, seq):
        S, nc, A = self.S, self.nc, self.arena
        S.barrier()
        A.reset()
        self.common_tmps(A)
        gBpre, gpk = self.load_gb(A, 1)
        gBpost, gqk = self.load_gb(A, 3)
        hT = A.alloc([128, 8, SEQ], BF16)
        OG = A.alloc([128, 8, SEQ], BF16)
        GT = [A.alloc([17, SEQ], BF16) for _ in range(2)]
        WA = [A.alloc([17, 512], BF16) for _ in range(2)]
        QT = A.alloc([128, SEQ], BF16)
        KT = A.alloc([128, SEQ], BF16)
        VTM = A.alloc([128, 16, 256], BF16)
        OF = A.alloc([128, 2, SEQ], BF16)
        Et = A.alloc([128, 512], F32)
        LP = [A.alloc([128, 512], F32) for _ in range(2)]
        EQ = [A.alloc([128, 512], F32) for _ in range(3)]
        EK = A.alloc([128, 512], F32)
        QTt = [A.alloc([128, 512], BF16) for _ in range(3)]
        KTt = [A.alloc([128, 512], BF16) for _ in range(2)]
        ST = [A.alloc([128, 512], BF16) for _ in range(2)]
        KTM = [A.alloc([128, 512], BF16) for _ in range(2)]
        DECC = A.alloc([128, 2], F32)
        SBF = [A.alloc([128, 256], BF16) for _ in range(3)]
        OS = A.alloc([128, 2, 512], F32)
        OSQ = self.sqj.rearrange("p (a b) -> p a b", b=512)
        RG = A.alloc([128, 512], F32)
        for d in range(2):
            S.dma("pool", WA[d], self.wa_d[d], writes=[("WA", d)])
            S.op("dve", lambda d=d: nc.vector.memset(GT[d], 1.0), writes=[("GT", d, t) for t in range(4)])
        XT3 = [A.alloc([128, D], F32) for _ in range(3)]
        T1 = A.alloc([128, D], F32)
        osv = OS.rearrange("p a b -> p (a b)")
        Ub = [T1[:, 0:256], T1[:, 256:512]]
        uctr = [0]
        self.mixer_prenorm(A, src, seq, gBpre, gpk, hT, XT3 + [osv, T1])
        hk = lambda tb: [("hT", 4 * tb + q_, "w") for q_ in range(4)]
        W, wk = self.w_get("gates")
        Wg3 = W[:, 0:256].rearrange("p (a b) -> p a b", b=32)
        for tb in range(4):
            for d in range(2):
                b, bk = self.bank("pa", [2, 3])
                S.mm(self.ps[b][0:16, :], [(Wg3[:, kc, d * 16:(d + 1) * 16], hT[:, kc, tb * 512:(tb + 1) * 512])
                                            for kc in range(8)], reads=[wk] + hk(tb), wkey=bk)
                S.op("act", lambda: nc.scalar.copy(out=GT[d][0:16, tb * 512:(tb + 1) * 512], in_=self.ps[b][0:16, :]),
                     reads=[bk], writes=[("GT", d, tb)])
        self.w_release(1)
        for h in range(2):
            W, wk = self.w_get(f"r{h}")
            W3 = W[:, 0:4096].rearrange("p (a b) -> p a b", b=512)
            for cc in range(4):
                for tb in range(4):
                    b, bk = self.bank("pb", [4, 5])
                    S.mm(self.ps[b][:, :], [(W3[:, kc, cc * 128:(cc + 1) * 128], hT[:, kc, tb * 512:(tb + 1) * 512])
                                            for kc in range(8)], reads=[wk] + hk(tb), wkey=bk)
                    S.op("act", lambda: nc.scalar.activation(out=OG[:, h * 4 + cc, tb * 512:(tb + 1) * 512],
                                                             in_=self.ps[b][:, :], func=AF.Silu),
                         reads=[bk], writes=[("og", h * 4 + cc, tb)])
            self.w_release(1)
        gctr = [0]
        sctr = [0]
        for h in range(4):
            W, wk = self.w_get(f"qk{h}")
            Wq = W[:, 0:1024].rearrange("p (a b) -> p a b", b=128)
            Wk = W[:, 1024:2048].rearrange("p (a b) -> p a b", b=128)
            for tb in range(4):
                for (Wx, Xt, nm, sc) in ((Wq, QT, "QT", 128.0 ** -0.5), (Wk, KT, "KT", 1.0)):
                    b, bk = self.bank("pa", [2, 3])
                    S.mm(self.ps[b][:, :], [(Wx[:, kc, :], hT[:, kc, tb * 512:(tb + 1) * 512]) for kc in range(8)],
                         reads=[wk] + hk(tb), wkey=bk)
                    S.op("act", lambda: nc.scalar.activation(out=Xt[:, tb * 512:(tb + 1) * 512], in_=self.ps[b][:, :],
                                                             func=AF.Copy, scale=sc),
                         reads=[bk], writes=[(nm, tb)])
            self.w_release(1)
            W, wk = self.w_get(f"v{h}")
            Wv = W[:, 0:2048].rearrange("p (a b) -> p a b", b=256)
            for i in range(16):
                b, bk = self.bank("pb", [4, 5])
                S.mm(self.ps[b][:, 0:256], [(hT[:, kc, i * 128:(i + 1) * 128], Wv[:, kc, :]) for kc in range(8)],
                     reads=[wk, ("hT", i, "w")], wkey=bk)
                S.op("dve", lambda: nc.vector.tensor_copy(out=VTM[:, i, :], in_=self.ps[b][:, 0:256]),
                     reads=[bk], writes=[("VTM", i)])
            self.w_release(1)

            items = [(0, g) for g in range(4)] + [(1, g) for g in range(3, -1, -1)]
            col = lambda tm: slice(tm * 128, (tm + 1) * 128)

            def stA(d, gi, n):
                b, kz = self.bank("gz", [6, 7])
                zb = self.ps[b]
                S.mm_multi([(zb[:, col(tm)], [(GT[d][0:17, (gi * 4 + tm) * 128:(gi * 4 + tm + 1) * 128],
                                               WA[d][0:17, h * 128:(h + 1) * 128])]) for tm in range(4)],
                           reads=[("GT", d, gi), ("WA", d)], wkey=kz)
                S.op("act", lambda: nc.scalar.activation(out=Et, in_=zb[:, :], func=AF.Exp, scale=-1.0),
                     reads=[kz], writes=["E"])
                S.op("act", lambda: nc.scalar.activation(out=LP[n % 2], in_=Et, func=AF.Ln, bias=1.0, scale=1.0),
                     reads=["E"], writes=[("LP", n % 2)])

            def stB(d, gi, n):
                tri = self.trif if d == 0 else self.trib
                gs = slice(gi * 512, (gi + 1) * 512)
                b2, kc_ = self.bank("gz", [6, 7])
                cb = self.ps[b2]
                S.mm_multi([(cb[:, col(tm)], [(LP[n % 2][:, col(tm)], tri)]) for tm in range(4)],
                           reads=[("LP", n % 2)], wkey=kc_)
                S.op("act", lambda: nc.scalar.activation(out=EQ[n % 3], in_=cb[:, :], func=AF.Exp),
                     reads=[kc_], writes=[("EQ", n % 3)])
                S.op("act", lambda: nc.scalar.activation(out=EK, in_=cb[:, :], func=AF.Exp, scale=-1.0),
                     reads=[kc_], writes=["EK"])
                S.op("dve", lambda: nc.vector.tensor_tensor(out=QTt[n % 3], in0=QT[:, gs], in1=EQ[n % 3], op=ALU.mult),
                     reads=[("QT", gi), ("EQ", n % 3)], writes=[("QTt", n % 3)])
                S.op("dve", lambda: nc.vector.tensor_tensor(out=KTt[n % 2], in0=KT[:, gs], in1=EK, op=ALU.mult),
                     reads=[("KT", gi), "EK"], writes=[("KTt", n % 2)])

            def stC(d, gi, n):
                mask = self.maskf if d == 0 else self.maskb
                b3, ks = self.bank("sc", [0, 1])
                sb = self.ps[b3]
                S.mm_multi([(sb[:, col(tm)], [(KTt[n % 2][:, col(tm)], QTt[n % 3][:, col(tm)])]) for tm in range(4)],
                           reads=[("KTt", n % 2), ("QTt", n % 3)], wkey=ks)
                mask_b = mask.rearrange("p (o b) -> p o b", o=1).to_broadcast([128, 4, 128])
                S.op("dve", lambda: nc.vector.tensor_tensor(out=ST[n % 2].rearrange("p (a b) -> p a b", b=128),
                                                            in0=sb[:, :].rearrange("p (a b) -> p a b", b=128),
                                                            in1=mask_b, op=ALU.mult),
                     reads=[ks], writes=[("ST", n % 2)])
                b4, kt = self.bank("sc", [0, 1])
                tb16 = self.ps[b4][:, :].bitcast(BF16)
                S.tr([(tb16[:, col(tm)], KTt[n % 2][:, col(tm)]) for tm in range(4)], self.identb,
                     reads=[("KTt", n % 2)], wkey=kt)
                S.op("act", lambda: nc.scalar.copy(out=KTM[n % 2], in_=tb16[:, 0:512]), reads=[kt],
                     writes=[("KTM", n % 2)])

            def SD(d, gi, n, first_group, first_arrival):
                order = [0, 1, 2, 3] if d == 0 else [3, 2, 1, 0]
                lastc = 127 if d == 0 else 0
                p2, p3 = n % 2, n % 3
                decs = lambda tm: EQ[p3][:, tm * 128 + lastc:tm * 128 + lastc + 1]
                kvb = {}

                def kv(tm):
                    b6, kkv = self.bank("kv", [4, 5])
                    S.mm(self.ps[b6][:, 0:256], [(KTM[p2][:, col(tm)], VTM[:, gi * 4 + tm, :])],
                         reads=[("KTM", p2), ("VTM", gi * 4 + tm)], wkey=kkv)
                    kvb[tm] = (self.ps[b6][:, 0:256], kkv)

                kv(order[0])
                kv(order[1])
                obanks = {}
                for half in range(2):
                    b5, ko = self.bank("o", [2, 3])
                    obanks[half] = (self.ps[b5], ko)
                pend = {0: [], 1: []}
                for half in range(2):
                    S._wait("pe", S._collect([], [obanks[half][1]]))
                for t, tm in enumerate(order):
                    c = gi * 4 + tm
                    has_state = not (first_group and t == 0)
                    kvps, kkv = kvb[tm]
                    ob, ko = obanks[tm // 2]
                    tm2 = tm % 2
                    groups = []
                    for vc in range(2):
                        prs = [(VTM[:, c, vc * 128:(vc + 1) * 128], ST[p2][:, col(tm)])]
                        if has_state:
                            prs.append((SBF[sctr[0] % 3][:, vc * 128:(vc + 1) * 128], QTt[p3][:, col(tm)]))
                        groups.append((ob[:, vc * 256 + tm2 * 128:vc * 256 + tm2 * 128 + 128], prs))
                    rd = [("VTM", c), ("ST", p2), ("QTt", p3)] + ([("SBF", sctr[0] % 3)] if has_state else [])
                    S.mm_multi(groups, reads=rd, wkey=(ko, "part", tm2))
                    pend[tm // 2].append((ko, "part", tm2))
                    uo, un = uctr[0] % 2, (uctr[0] + 1) % 2
                    uctr[0] += 1
                    if t == 0:
                        if has_state:
                            S.op("dve", lambda: nc.vector.scalar_tensor_tensor(
                                out=Ub[un], in0=Ub[uo], scalar=DECC[:, (n + 1) % 2:(n + 1) % 2 + 1], in1=kvps,
                                op0=ALU.mult, op1=ALU.add), reads=[("U", uo), ("DECC", (n + 1) % 2), kkv],
                                writes=[("U", un)])
                        else:
                            S.op("dve", lambda: nc.vector.tensor_copy(out=Ub[un], in_=kvps), reads=[kkv],
                                 writes=[("U", un), ("XT", 4)])
                    else:
                        S.op("dve", lambda: nc.vector.scalar_tensor_tensor(out=Ub[un], in0=Ub[uo],
                                                                            scalar=decs(order[t - 1]),
                                                                            in1=kvps, op0=ALU.mult, op1=ALU.add),
                             reads=[("U", uo), ("EQ", p3), kkv], writes=[("U", un)])
                    nxt = (sctr[0] + 1) % 3
                    S.op("act", lambda: nc.scalar.activation(out=SBF[nxt], in_=Ub[un], func=AF.Copy, scale=decs(tm)),
                         reads=[("U", un), ("EQ", p3)], writes=[("SBF", nxt)])
                    sctr[0] += 1
                    if t + 2 < 4:
                        kv(order[t + 2])
                S.op("act", lambda: nc.scalar.copy(out=DECC[:, n % 2:n % 2 + 1], in_=decs(order[3])),
                     reads=[("EQ", p3)], writes=[("DECC", n % 2)])
                gs = slice(gi * 512, (gi + 1) * 512)
                for half in range(2):
                    ob, ko = obanks[half]
                    o3 = ob[:, :].rearrange("p (a b) -> p a b", b=256)
                    cols = slice(gi * 512 + half * 256, gi * 512 + half * 256 + 256)
                    if first_arrival:
                        S.op("act", lambda: nc.scalar.copy(out=OF[:, :, cols], in_=o3), reads=pend[half],
                             writes=[("OF", gi, half), ko])
                    else:
                        S.op("dve", lambda: nc.vector.tensor_tensor(out=OS[:, :, half * 256:(half + 1) * 256], in0=o3,
                                                                    in1=OF[:, :, cols], op=ALU.add),
                             reads=pend[half] + [("OF", gi, half)], writes=[("XT", 3), ko])
                if not first_arrival:
                    S.op("act", lambda: nc.scalar.activation(out=OSQ, in_=OS, func=AF.Square),
                         reads=[("XT", 3)], writes=["OSQ"])
                    b7, kq = self.bank("gz", [6, 7])
                    qps = self.ps[b7]
                    S.mm(qps[:, :], [(self.onesb, OSQ[:, vc, :]) for vc in range(2)], reads=["OSQ"], wkey=kq)
                    S.op("act", lambda: nc.scalar.activation(out=RG, in_=qps[:, :], func=AF.Ln, scale=1.0 / 256,
                                                             bias=self.epsc),
                         reads=[kq], writes=["RG"])
                    S.op("act", lambda: nc.scalar.activation(out=RG, in_=RG, func=AF.Exp, scale=-0.5),
                         reads=["RG"], writes=["RG"])
                    rg_b = RG.rearrange("p (o b) -> p o b", o=1).to_broadcast([128, 2, 512])
                    S.op("dve", lambda: nc.vector.tensor_tensor(out=OS, in0=OS, in1=rg_b, op=ALU.mult),
                         reads=[("XT", 3), "RG"], writes=[("XT", 3)])
                    for vc in range(2):
                        S.op("dve", lambda vc=vc: nc.vector.scalar_tensor_tensor(
                            out=OG[:, h * 2 + vc, gs], in0=OS[:, vc, :], scalar=self.pcol(P_GNG + h * 2 + vc),
                            in1=OG[:, h * 2 + vc, gs], op0=ALU.mult, op1=ALU.mult),
                            reads=[("XT", 3), ("og", h * 2 + vc, gi)], writes=[("og", h * 2 + vc, gi)])

            n0 = gctr[0]
            gctr[0] += len(items)
            L = len(items)
            stA(*items[0], n0)
            stB(*items[0], n0)
            stC(*items[0], n0)
            stA(*items[1], n0 + 1)
            stB(*items[1], n0 + 1)
            stA(*items[2], n0 + 2)
            for i_ in range(L):
                if i_ + 3 < L:
                    stA(*items[i_ + 3], n0 + i_ + 3)
                if i_ + 2 < L:
                    stB(*items[i_ + 2], n0 + i_ + 2)
                if i_ + 1 < L:
                    stC(*items[i_ + 1], n0 + i_ + 1)
                d, gi = items[i_]
                SD(d, gi, n0 + i_, first_group=(i_ == 0 or i_ == 4), first_arrival=(d == 0))
        S.barrier()
        self.mixer_out(A, OG, lambda tb: [("og", c, tb) for c in range(8)], src, dst, seq, gBpost, gqk, XT3 + [osv], T1)

    def build(self):
        self.declare()
        self.setup()
        self.make_plan()
        self.w_prime()
        S = self.S
        nsub = len(self.plan)
        for seq in range(2):
            for si, sub in enumerate(self.plan):
                src = self.x_in if si == 0 else self.xs
                dst = self.y_out if si == nsub - 1 else self.xs
                if sub == "mix0":
                    self.conv_mixer(src, dst, seq)
                elif sub == "mix1":
                    self.gla_mixer(src, dst, seq)
                else:
                    self.ffn(int(sub[3]), src, dst, seq)
        S.barrier(engines=("sp",))
        self.es.close()
        return self.nc


def host_inputs(inp):
    f = lambda a: np.ascontiguousarray(np.asarray(a, dtype=np.float32))
    pm = lambda v: f(v).reshape(-1, 128).T
    prm = np.zeros((128, NPRM), np.float32)
    for l in range(2):
        prm[:, P_MIXPRE + 8 * l:P_MIXPRE + 8 * l + 8] = pm(inp["mix_pre_g"][l])
        prm[:, P_MIXPOST + 8 * l:P_MIXPOST + 8 * l + 8] = pm(inp["mix_post_g"][l])
        prm[:, P_FFNPRE + 8 * l:P_FFNPRE + 8 * l + 8] = pm(inp["ffn_pre_g"][l])
        prm[:, P_FFNPOST + 8 * l:P_FFNPOST + 8 * l + 8] = pm(inp["ffn_post_g"][l])
    dww = f(inp["cv_dw_w"][0])
    prm[:, P_DWW:P_DWW + 124] = dww.reshape(31, 4, 128).transpose(2, 1, 0).reshape(128, 124)
    prm[:, P_DWB:P_DWB + 4] = pm(inp["cv_dw_b"][0])
    prm[:, P_LNG:P_LNG + 4] = pm(inp["cv_ln_g"][0])
    prm[:, P_LNB:P_LNB + 4] = pm(inp["cv_ln_b"][0])
    scw = f(inp["cv_sc_w"][0])
    prm[:, P_SCW:P_SCW + 12] = scw.reshape(3, 4, 128).transpose(2, 1, 0).reshape(128, 12)
    prm[:, P_GNG:P_GNG + 8] = pm(inp["gla_gn_g"][0])
    j = np.arange(128)[:, None]
    i = np.arange(128)[None, :]
    cst = np.concatenate([
        np.eye(128), np.ones((128, 128)), (j <= i) * 1.0, (j >= i) * 1.0,
        (j <= i) * (-1.0 / 16.0), (j >= i) * (-1.0 / 16.0)], axis=1).astype(np.float32)
    gvec = np.stack([f(inp["mix_pre_g"][0]), f(inp["mix_pre_g"][1]), f(inp["mix_post_g"][0]),
                     f(inp["mix_post_g"][1]), f(inp["ffn_pre_g"][0]), f(inp["ffn_pre_g"][1]),
                     f(inp["ffn_post_g"][0]), f(inp["ffn_post_g"][1])], axis=0)
    wa = np.stack([np.concatenate([f(inp["gla_wa2_f"][0]), f(inp["gla_ba2_f"])[0:1]], axis=0),
                   np.concatenate([f(inp["gla_wa2_b"][0]), f(inp["gla_ba2_b"])[0:1]], axis=0)], axis=0)
    shared = {
        "prm": prm, "cst": cst, "gvec": f(gvec), "wa": f(wa),
        "cv_w_in": f(inp["cv_w_in"][0]), "cv_w_out": f(inp["cv_w_out"][0]),
        "gla_w_in": f(inp["gla_w_in"][0]), "gla_w_out": f(inp["gla_w_out"][0]),
        "ffn_w_gu": f(inp["ffn_w_gu"]), "ffn_w_down": f(inp["ffn_w_down"]),
    }
    x = f(inp["x"]).reshape(NCORES, TOK, D)
    return [dict(shared, x=x[c]) for c in range(NCORES)]


_CACHE = {}


def run(inp, plan=("mix0", "ffn0", "mix1", "ffn1")):
    plan = tuple(plan)
    if plan not in _CACHE:
        _CACHE[plan] = Builder(plan).build()
    nc = _CACHE[plan]
    in_maps = host_inputs(inp)
    res = run_bass_kernel_spmd(nc, in_maps, core_ids=list(range(NCORES)))
    out = np.stack([np.asarray(r["y"], dtype=np.float32) for r in res.results], axis=0)
    return out.reshape(16, SEQ, D)


def kernel(**inputs):
    return run(inputs)
```

```python
import math
from contextlib import ExitStack

import numpy as np
import concourse.bass as bass
import concourse.mybir as mybir
from concourse.bass_utils import run_bass_kernel_spmd
from concourse.alu_op_type import AluOpType as ALU

F32 = mybir.dt.float32
BF16 = mybir.dt.bfloat16
AF = mybir.ActivationFunctionType

NCORES = 8
D = 1024
SEQ = 2048
TOK = 4096
DFF = 2816
EPS = 1e-6
SLOT_ELEMS = 5632
NSLOT = 4
NPRM = 220

P_MIXPRE, P_MIXPOST, P_FFNPRE, P_FFNPOST = 0, 16, 32, 48
P_DWW = 64
P_DWB = 188
P_LNG = 192
P_LNB = 196
P_SCW = 200
P_GNG = 212


class Sched:
    def __init__(self, nc, es):
        self.nc = nc
        self.E = {"pe": nc.tensor, "act": nc.scalar, "dve": nc.vector, "pool": nc.gpsimd, "sp": nc.sync}
        self.psem = {}
        for e in ("pe", "act", "dve", "pool"):
            self.psem[e] = es.enter_context(nc.semaphore(f"p_{e}"))
        self.pcnt = {e: 0 for e in self.psem}
        self.waited = {e: {} for e in self.E}
        self.state = {}
        self.dsems = {}
        for q in ("sp", "pool"):
            self.dsems[q] = [[es.enter_context(nc.semaphore(f"d_{q}{i}")), 0, f"d_{q}{i}"] for i in range(10)]
        self.drr = {q: 0 for q in self.dsems}
        self.slotsem = [[es.enter_context(nc.semaphore(f"w_{i}")), 0, f"w_{i}"] for i in range(NSLOT)]

    def _collect(self, reads, writes):
        need = {}

        def add(n, s, v):
            if n not in need or need[n][1] < v:
                need[n] = (s, v)

        for k in reads:
            st = self.state.get(k)
            if st and st[0] is not None:
                add(*st[0])
        for k in writes:
            st = self.state.get(k)
            if st:
                if st[0] is not None:
                    add(*st[0])
                for n, (s, v) in st[1].items():
                    add(n, s, v)
        return need

    def _wait(self, e, need):
        for n, (s, v) in need.items():
            if self.waited[e].get(n, 0) >= v:
                continue
            self.E[e].wait_ge(s, v)
            self.waited[e][n] = v

    def _commit(self, tok, reads, writes):
        n, s, v = tok
        for k in reads:
            st = self.state.setdefault(k, [None, {}])
            st[1][n] = (s, v)
        for k in writes:
            self.state[k] = [tok, {}]

    def op(self, e, fn, reads=(), writes=()):
        need = self._collect(reads, writes)
        self._wait(e, need)
        ins = fn()
        self.pcnt[e] += 1
        ins.then_inc(self.psem[e], 1)
        tok = (f"p_{e}", self.psem[e], self.pcnt[e])
        self._commit(tok, reads, writes)
        return tok

    def mm(self, out, pairs, reads, wkey):
        need = self._collect(reads, [wkey])
        self._wait("pe", need)
        n = len(pairs)
        ins = None
        for i, (l, r) in enumerate(pairs):
            ins = self.nc.tensor.matmul(out, l, r, start=(i == 0), stop=(i == n - 1))
        self.pcnt["pe"] += 1
        ins.then_inc(self.psem["pe"], 1)
        tok = ("p_pe", self.psem["pe"], self.pcnt["pe"])
        self._commit(tok, reads, [wkey])
        return tok

    def mm_multi(self, groups, reads, wkey):
        need = self._collect(reads, [wkey])
        self._wait("pe", need)
        ins = None
        for out, pairs in groups:
            n = len(pairs)
            for i, (l, r) in enumerate(pairs):
                ins = self.nc.tensor.matmul(out, l, r, start=(i == 0), stop=(i == n - 1))
        self.pcnt["pe"] += 1
        ins.then_inc(self.psem["pe"], 1)
        tok = ("p_pe", self.psem["pe"], self.pcnt["pe"])
        self._commit(tok, reads, [wkey])
        return tok

    def tr(self, items, ident, reads, wkey):
        need = self._collect(reads, [wkey])
        self._wait("pe", need)
        ins = None
        for out, in_ in items:
            ins = self.nc.tensor.transpose(out, in_, ident)
        self.pcnt["pe"] += 1
        ins.then_inc(self.psem["pe"], 1)
        tok = ("p_pe", self.psem["pe"], self.pcnt["pe"])
        self._commit(tok, reads, [wkey])
        return tok

    def dma(self, q, out, in_, reads=(), writes=(), semrec=None, nonc=False):
        if semrec is None:
            semrec = self.dsems[q][self.drr[q]]
            self.drr[q] = (self.drr[q] + 1) % len(self.dsems[q])
            need = self._collect(reads, writes)
            if semrec[1] > 0:
                need[semrec[2]] = (semrec[0], semrec[1])
        else:
            need = self._collect(reads, writes)
        self._wait(q, need)
        if nonc:
            ins = self.E[q].dma_start(out=out, in_=in_, allow_slow_non_contiguous=True)
        else:
            ins = self.E[q].dma_start(out=out, in_=in_)
        semrec[1] += 16
        ins.then_inc(semrec[0], 16)
        tok = (semrec[2], semrec[0], semrec[1])
        self._commit(tok, reads, writes)
        return tok

    def barrier(self, engines=("pe", "act", "dve", "sp")):
        need = {}
        for e in self.psem:
            if self.pcnt[e] > 0:
                need[f"p_{e}"] = (self.psem[e], self.pcnt[e])
        for q in self.dsems:
            for rec in self.dsems[q]:
                if rec[1] > 0:
                    need[rec[2]] = (rec[0], rec[1])
        for e in engines:
            self._wait(e, need)


class Arena:
    def __init__(self, big, base_b, limit_b):
        self.big = big
        self.base = base_b
        self.limit = limit_b
        self.off = base_b

    def reset(self):
        self.off = self.base

    def alloc(self, shape, dt):
        esz = 2 if dt == BF16 else 4
        n = 1
        for s in shape[1:]:
            n *= s
        nb = (n * esz + 63) // 64 * 64
        assert self.off + nb <= self.limit, f"arena overflow {self.off + nb} > {self.limit}"
        a = self.big[0:shape[0], self.off // 2: self.off // 2 + n * esz // 2]
        self.off += nb
        if dt == F32:
            a = a.bitcast(F32)
        if len(shape) == 3:
            a = a.rearrange("p (a b) -> p a b", b=shape[2])
        elif len(shape) == 4:
            a = a.rearrange("p (a b c) -> p a b c", b=shape[2], c=shape[3])
        return a


class Builder:
    def __init__(self, plan):
        self.plan = plan
        self.nc = bass.Bass("TRN2", target_bir_lowering=False)
        self.es = ExitStack()

    def declare(self):
        nc = self.nc
        dt = lambda name, shape, kind="ExternalInput": nc.dram_tensor(name, shape, F32, kind=kind).ap()
        self.x_in = dt("x", [TOK, D])
        self.y_out = dt("y", [TOK, D], "ExternalOutput")
        self.xs = dt("xs", [TOK, D], "Internal")
        self.prm_d = dt("prm", [128, NPRM])
        self.cst_d = dt("cst", [128, 6 * 128])
        self.gvec_d = dt("gvec", [8, D])
        self.wa_d = dt("wa", [2, 17, 512])
        self.cv_w_in = dt("cv_w_in", [D, 2560])
        self.cv_w_out = dt("cv_w_out", [D, D])
        self.gla_w_in = dt("gla_w_in", [D, 3104])
        self.gla_w_out = dt("gla_w_out", [D, D])
        self.ffn_w_gu = dt("ffn_w_gu", [2, D, 2 * DFF])
        self.ffn_w_down = dt("ffn_w_down", [2, DFF, D])

    def setup(self):
        nc, es = self.nc, self.es
        self.S = Sched(nc, es)
        S = self.S
        total_b = 212800
        self.big = es.enter_context(nc.sbuf_tensor("big", [128, total_b // 2], BF16))
        self.ps = [es.enter_context(nc.psum_tensor(f"ps{i}", [128, 512], F32)) for i in range(8)]
        carve = Arena(self.big, 0, total_b)
        self.slots = [carve.alloc([128, SLOT_ELEMS], BF16) for _ in range(NSLOT)]
        self.identb = carve.alloc([128, 128], BF16)
        self.onesb = carve.alloc([128, 128], BF16)
        self.maskf = carve.alloc([128, 128], BF16)
        self.maskb = carve.alloc([128, 128], BF16)
        self.trif = carve.alloc([128, 128], F32)
        self.trib = carve.alloc([128, 128], F32)
        self.prm = carve.alloc([128, NPRM], F32)
        self.arena = Arena(self.big, carve.off, total_b)
        c = self.cst_d
        S.dma("pool", self.identb, c[:, 0:128], writes=["c0"])
        S.dma("pool", self.onesb, c[:, 128:256], writes=["c1"])
        S.dma("pool", self.maskf, c[:, 256:384], writes=["c2"])
        S.dma("pool", self.maskb, c[:, 384:512], writes=["c3"])
        S.dma("sp", self.trif, c[:, 512:640], writes=["c4"])
        S.dma("sp", self.trib, c[:, 640:768], writes=["c5"])
        S.dma("sp", self.prm, self.prm_d[:, :], writes=["c6"])
        S.barrier()
        self.wplan = []
        self.wnext_issue = 0
        self.wnext_use = 0
        self.psrr = {}

    def bank(self, role, banks):
        i = self.psrr.get(role, 0)
        self.psrr[role] = i + 1
        b = banks[i % len(banks)]
        return b, ("ps", b)

    def pcol(self, c0, n=1):
        return self.prm[:, c0:c0 + n]

    def w_issue(self, idx):
        if idx >= len(self.wplan):
            return
        S = self.S
        slot = idx % NSLOT
        rec = S.slotsem[slot]
        S._wait("pool", S._collect([], [("w", slot)]))
        for (eoff, shp, src) in self.wplan[idx]:
            n = shp[0] * shp[1]
            dst = self.slots[slot][:, eoff:eoff + n].rearrange("p (a b) -> p a b", b=shp[1])
            ins = self.nc.gpsimd.dma_start(out=dst, in_=src)
            rec[1] += 16
            ins.then_inc(rec[0], 16)
        S._commit((rec[2], rec[0], rec[1]), [], [("w", slot)])

    def w_prime(self):
        for i in range(NSLOT):
            self.w_issue(i)
        self.wnext_issue = NSLOT

    def w_get(self, tag):
        idx = self.wnext_use
        assert self.wtags[idx] == tag, (idx, self.wtags[idx], tag)
        self.wnext_use += 1
        slot = idx % NSLOT
        return self.slots[slot], ("w", slot)

    def w_release(self, n=1):
        for _ in range(n):
            self.w_issue(self.wnext_issue)
            self.wnext_issue += 1

    def add_load(self, tag, parts):
        self.wplan.append(parts)
        self.wtags.append(tag)

    @staticmethod
    def wsrc(w2d, r0, nr, c0, ncol):
        return w2d[r0:r0 + nr, c0:c0 + ncol].rearrange("(kc p) n -> p kc n", p=128)

    def make_plan(self):
        self.wtags = []
        for s in range(2):
            for sub in self.plan:
                if sub == "mix0":
                    w = self.cv_w_in
                    for nm, c0 in (("aval", 0), ("agate", 512), ("cgate", 1536), ("v", 2048), ("bgate", 1024)):
                        self.add_load(nm, [(0, (8, 512), self.wsrc(w, 0, D, c0, 512))])
                    for h in range(2):
                        self.add_load(f"wout{h}", [(0, (8, 512), self.wsrc(self.cv_w_out, 0, D, h * 512, 512))])
                elif sub == "mix1":
                    w = self.gla_w_in
                    self.add_load("gates", [(0, (8, 32), self.wsrc(w, 0, D, 3072, 32))])
                    for h in range(2):
                        self.add_load(f"r{h}", [(0, (8, 512), self.wsrc(w, 0, D, 2048 + h * 512, 512))])
                    for h in range(4):
                        self.add_load(f"qk{h}", [(0, (8, 128), self.wsrc(w, 0, D, h * 128, 128)),
                                                 (1024, (8, 128), self.wsrc(w, 0, D, 512 + h * 128, 128))])
                        self.add_load(f"v{h}", [(0, (8, 256), self.wsrc(w, 0, D, 1024 + h * 256, 256))])
                    for h in range(2):
                        self.add_load(f"wout{h}", [(0, (8, 512), self.wsrc(self.gla_w_out, 0, D, h * 512, 512))])
                elif sub in ("ffn0", "ffn1"):
                    l = int(sub[3])
                    wg = self.ffn_w_gu[l]
                    wd = self.ffn_w_down[l]
                    for hb in range(2):
                        for L in range(11):
                            self.add_load(f"gu{L}", [(0, (8, 256), self.wsrc(wg, 0, D, L * 256, 256)),
                                                     (2048, (8, 256), self.wsrc(wg, 0, D, DFF + L * 256, 256))])
                        for half in range(2):
                            for part in range(2):
                                self.add_load(f"dn{half}{part}",
                                              [(0, (11, 512), self.wsrc(wd, part * 1408, 1408, half * 512, 512))])

    def load_gb(self, A, row):
        g = A.alloc([128, D], F32)
        self.S.dma("sp", g, self.gvec_d[row:row + 1, :].to_broadcast([128, D]), writes=[("gb", row)])
        return g, ("gb", row)

    def pn_stats(self, xt, xkey, rs_col, rskey, ss_col, sskey):
        S, nc = self.S, self.nc
        S.op("act", lambda: nc.scalar.activation(out=self.sqj, in_=xt, func=AF.Square, accum_out=ss_col),
             reads=[xkey], writes=[sskey])
        S.op("act", lambda: nc.scalar.activation(out=rs_col, in_=ss_col, func=AF.Sqrt, scale=1.0 / D, bias=self.epsc),
             reads=[sskey], writes=[rskey])
        S.op("dve", lambda: nc.vector.reciprocal(out=rs_col, in_=rs_col), reads=[rskey], writes=[rskey])

    def pn_apply_a(self, xt, xkey, rs_col, rskey, gB, gkey, htm, htmkey):
        S, nc = self.S, self.nc
        S.op("dve", lambda: nc.vector.scalar_tensor_tensor(out=htm, in0=xt, scalar=rs_col, in1=gB,
                                                            op0=ALU.mult, op1=ALU.mult),
             reads=[xkey, rskey, gkey], writes=[htmkey])
        b, bkey = self.bank("pt", [0, 1])
        pv = self.ps[b][:, :].bitcast(BF16)
        S.tr([(pv[:, c * 128:(c + 1) * 128], htm[:, c * 128:(c + 1) * 128]) for c in range(8)], self.identb,
             reads=[htmkey], wkey=bkey)
        return pv, bkey

    def pn_apply_b(self, pv, bkey, dst_cols, hkey):
        S, nc = self.S, self.nc
        S.op("dve", lambda: nc.vector.tensor_copy(out=dst_cols, in_=pv.rearrange("p (c t) -> p c t", t=128)),
             reads=[bkey], writes=[hkey])

    def postnorm_tile(self, ops, okeys, xt, xkey, gB, gkey, t1, t1key, ssb, idx):
        S, nc = self.S, self.nc
        c0 = ssb[:, 4 * idx:4 * idx + 1]
        c1 = ssb[:, 4 * idx + 1:4 * idx + 2]
        c2 = ssb[:, 4 * idx + 2:4 * idx + 3]
        k = ("ssb", idx)
        S.op("act", lambda: nc.scalar.activation(out=self.sqj[:, 0:512], in_=ops[0], func=AF.Square, accum_out=c0),
             reads=[okeys[0]], writes=[(k, 0)])
        S.op("act", lambda: nc.scalar.activation(out=self.sqj[:, 512:1024], in_=ops[1], func=AF.Square, accum_out=c1),
             reads=[okeys[1]], writes=[(k, 1)])
        S.op("dve", lambda: nc.vector.tensor_tensor(out=c2, in0=c0, in1=c1, op=ALU.add),
             reads=[(k, 0), (k, 1)], writes=[(k, 2)])
        S.op("act", lambda: nc.scalar.activation(out=c2, in_=c2, func=AF.Sqrt, scale=1.0 / D, bias=self.epsc),
             reads=[(k, 2)], writes=[(k, 2)])
        S.op("dve", lambda: nc.vector.reciprocal(out=c2, in_=c2), reads=[(k, 2)], writes=[(k, 2)])
        for h in range(2):
            S.op("dve", lambda h=h: nc.vector.scalar_tensor_tensor(
                out=t1[:, h * 512:(h + 1) * 512], in0=ops[h], scalar=c2, in1=gB[:, h * 512:(h + 1) * 512],
                op0=ALU.mult, op1=ALU.mult), reads=[okeys[h], (k, 2), gkey], writes=[(t1key, h)])
        S.op("dve", lambda: nc.vector.tensor_tensor(out=xt, in0=xt, in1=t1, op=ALU.add),
             reads=[xkey, (t1key, 0), (t1key, 1)], writes=[xkey])

    def common_tmps(self, A):
        self.sqj = A.alloc([128, D], BF16)
        self.epsc = A.alloc([128, 1], F32)
        self.S.op("dve", lambda: self.nc.vector.memset(self.epsc, EPS), writes=["epsc"])
        self.S.barrier()

    def ffn(self, l, src, dst, seq):
        S, nc, A = self.S, self.nc, self.arena
        S.barrier()
        A.reset()
        self.common_tmps(A)
        gBpre, gpk = self.load_gb(A, 4 + l)
        gBpost, gqk = self.load_gb(A, 6 + l)
        XH = A.alloc([128, 8, D], F32)
        hT = A.alloc([128, 8, 1024], BF16)
        M0 = A.alloc([128, 8, 512], F32)
        ACTB = A.alloc([128, 22, 1024], BF16)
        HTM = [A.alloc([128, D], BF16) for _ in range(2)]
        SG = [A.alloc([128, 512], BF16) for _ in range(2)]
        T1 = [A.alloc([128, D], F32) for _ in range(2)]
        SS = A.alloc([128, 16], F32)
        RS = A.alloc([128, 16], F32)
        SSB = A.alloc([128, 64], F32)
        for hb in range(2):
            t0 = seq * SEQ + hb * 1024
            for i in range(8):
                r0 = t0 + i * 128
                S.dma("sp", XH[:, i, :], src[r0:r0 + 128, :], reads=[("xd", r0)], writes=[("XH", i)])
            for i in range(8):
                self.pn_stats(XH[:, i, :], ("XH", i), RS[:, i:i + 1], ("rs", i), SS[:, i:i + 1], ("ss", i))
            pend = None
            for i in range(8):
                cur = self.pn_apply_a(XH[:, i, :], ("XH", i), RS[:, i:i + 1], ("rs", i), gBpre, gpk,
                                      HTM[i % 2], ("htm", i % 2))
                if pend is not None:
                    self.pn_apply_b(*pend)
                pend = (cur[0], cur[1], hT[:, :, i * 128:(i + 1) * 128], ("hT", i, "w"))
            self.pn_apply_b(*pend)
            for L in range(11):
                W, wk = self.w_get(f"gu{L}")
                Wg = W[:, 0:2048].rearrange("p (a b) -> p a b", b=256)
                Wu = W[:, 2048:4096].rearrange("p (a b) -> p a b", b=256)
                for cc in range(2):
                    c = 2 * L + cc
                    for tb in range(2):
                        bg, kg = self.bank("g", [2, 3])
                        bu, ku = self.bank("u", [4, 5])
                        rhs = lambda kc: hT[:, kc, tb * 512:(tb + 1) * 512]
                        S.mm(self.ps[bg][:, :], [(Wg[:, kc, cc * 128:(cc + 1) * 128], rhs(kc)) for kc in range(8)],
                             reads=[wk] + [("hT", 4 * tb + q, "w") for q in range(4)], wkey=kg)
                        S.mm(self.ps[bu][:, :], [(Wu[:, kc, cc * 128:(cc + 1) * 128], rhs(kc)) for kc in range(8)],
                             reads=[wk] + [("hT", 4 * tb + q, "w") for q in range(4)], wkey=ku)
                        sg = SG[(2 * c + tb) % 2]
                        sgk = ("sg", (2 * c + tb) % 2)
                        S.op("act", lambda: nc.scalar.activation(out=sg, in_=self.ps[bg][:, :], func=AF.Silu),
                             reads=[kg], writes=[sgk])
                        S.op("dve", lambda: nc.vector.tensor_tensor(out=ACTB[:, c, tb * 512:(tb + 1) * 512], in0=sg,
                                                                    in1=self.ps[bu][:, :], op=ALU.mult),
                             reads=[sgk, ku], writes=[("actb", c, tb)])
                self.w_release()
            for half in range(2):
                Wa, wka = self.w_get(f"dn{half}0")
                Wb, wkb = self.w_get(f"dn{half}1")
                Wa3 = Wa[:, 0:5632].rearrange("p (a b) -> p a b", b=512)
                Wb3 = Wb[:, 0:5632].rearrange("p (a b) -> p a b", b=512)
                for i in range(8):
                    bf, kf = self.bank("f", [6, 7])
                    tb = i // 4
                    pairs = []
                    for kc in range(22):
                        w3 = Wa3 if kc < 11 else Wb3
                        pairs.append((ACTB[:, kc, i * 128:(i + 1) * 128], w3[:, kc % 11, :]))
                    S.mm(self.ps[bf][:, :], pairs, reads=[wka, wkb] + [("actb", kc, tb) for kc in range(22)], wkey=kf)
                    c0 = SSB[:, 4 * i + half:4 * i + half + 1]
                    k = ("ssb", i)
                    if half == 0:
                        S.op("act", lambda: nc.scalar.activation(out=self.sqj[:, 0:512], in_=self.ps[bf][:, :],
                                                                 func=AF.Square, accum_out=c0),
                             reads=[kf], writes=[(k, 0)])
                        S.op("act", lambda: nc.scalar.copy(out=M0[:, i, :], in_=self.ps[bf][:, :]),
                             reads=[kf], writes=[("m0", i)])
                    else:
                        c2 = SSB[:, 4 * i + 2:4 * i + 3]
                        S.op("act", lambda: nc.scalar.activation(out=self.sqj[:, 512:1024], in_=self.ps[bf][:, :],
                                                                 func=AF.Square, accum_out=c0),
                             reads=[kf], writes=[(k, 1)])
                        S.op("dve", lambda: nc.vector.tensor_tensor(out=c2, in0=SSB[:, 4 * i:4 * i + 1], in1=c0,
                                                                    op=ALU.add),
                             reads=[(k, 0), (k, 1)], writes=[(k, 2)])
                        S.op("act", lambda: nc.scalar.activation(out=c2, in_=c2, func=AF.Sqrt, scale=1.0 / D,
                                                                 bias=self.epsc),
                             reads=[(k, 2)], writes=[(k, 2)])
                        S.op("dve", lambda: nc.vector.reciprocal(out=c2, in_=c2), reads=[(k, 2)], writes=[(k, 2)])
                        t1 = T1[i % 2]
                        tk = ("t1", i % 2)
                        S.op("dve", lambda: nc.vector.scalar_tensor_tensor(
                            out=t1[:, 512:1024], in0=self.ps[bf][:, :], scalar=c2, in1=gBpost[:, 512:1024],
                            op0=ALU.mult, op1=ALU.mult), reads=[kf, (k, 2), gqk], writes=[(tk, 1)])
                        S.op("dve", lambda: nc.vector.scalar_tensor_tensor(
                            out=t1[:, 0:512], in0=M0[:, i, :], scalar=c2, in1=gBpost[:, 0:512],
                            op0=ALU.mult, op1=ALU.mult), reads=[("m0", i), (k, 2), gqk], writes=[(tk, 0)])
                        S.op("dve", lambda: nc.vector.tensor_tensor(out=XH[:, i, :], in0=XH[:, i, :], in1=t1,
                                                                    op=ALU.add),
                             reads=[("XH", i), (tk, 0), (tk, 1)], writes=[("XH", i)])
                        r0 = t0 + i * 128
                        S.dma("sp", dst[r0:r0 + 128, :], XH[:, i, :], reads=[("XH", i)], writes=[("xd", r0)])
                self.w_release(2)

    def mixer_prenorm(self, A, src, seq, gB, gk, hT, ring):
        S = self.S
        HTM = [A.alloc([128, D], BF16) for _ in range(2)]
        SS = A.alloc([128, 16], F32)
        RS = A.alloc([128, 16], F32)
        R = len(ring)

        def load(i):
            r0 = seq * SEQ + i * 128
            S.dma("sp", ring[i % R], src[r0:r0 + 128, :], reads=[("xd", r0)], writes=[("XT", i % R)])

        def stats(i):
            self.pn_stats(ring[i % R], ("XT", i % R), RS[:, i:i + 1], ("rs", i), SS[:, i:i + 1], ("ss", i))

        for i in range(R - 1):
            load(i)
        stats(0)
        pend = None
        for i in range(16):
            if i + 1 < 16:
                stats(i + 1)
            cur = self.pn_apply_a(ring[i % R], ("XT", i % R), RS[:, i:i + 1], ("rs", i), gB, gk, HTM[i % 2],
                                  ("htm", i % 2))
            if pend is not None:
                self.pn_apply_b(*pend)
            pend = (cur[0], cur[1], hT[:, :, i * 128:(i + 1) * 128], ("hT", i, "w"))
            if i + R - 1 < 16:
                load(i + R - 1)
        self.pn_apply_b(*pend)

    def mixer_out(self, A, CAT, catkeys, src, dst, seq, gB, gk, ring, T1):
        S, nc = self.S, self.nc
        SSB = A.alloc([128, 64], F32)
        W0, wk0 = self.w_get("wout0")
        W1, wk1 = self.w_get("wout1")
        W3 = [W0[:, 0:4096].rearrange("p (a b) -> p a b", b=512), W1[:, 0:4096].rearrange("p (a b) -> p a b", b=512)]
        wks = [wk0, wk1]
        R = len(ring)

        def load(i):
            r0 = seq * SEQ + i * 128
            S.dma("sp", ring[i % R], src[r0:r0 + 128, :], reads=[("xd", r0)], writes=[("XT", i % R)])

        for i in range(R - 1):
            load(i)
        tiles = {}

        def p1a(i):
            ops, oks = [], []
            for h in range(2):
                b, bk = self.bank("o", [2, 3, 4, 5, 6, 7])
                S.mm(self.ps[b][:, :], [(CAT[:, kc, i * 128:(i + 1) * 128], W3[h][:, kc, :]) for kc in range(8)],
                     reads=[wks[h]] + catkeys(i // 4), wkey=bk)
                ops.append(self.ps[b][:, :])
                oks.append(bk)
            tiles[i] = (ops, oks)
            c0 = SSB[:, 4 * i:4 * i + 1]
            c1 = SSB[:, 4 * i + 1:4 * i + 2]
            c2 = SSB[:, 4 * i + 2:4 * i + 3]
            k = ("ssb", i)
            S.op("act", lambda: nc.scalar.activation(out=self.sqj[:, 0:512], in_=ops[0], func=AF.Square, accum_out=c0),
                 reads=[oks[0]], writes=[(k, 0)])
            S.op("act", lambda: nc.scalar.activation(out=self.sqj[:, 512:1024], in_=ops[1], func=AF.Square,
                                                     accum_out=c1),
                 reads=[oks[1]], writes=[(k, 1)])
            S.op("dve", lambda: nc.vector.tensor_tensor(out=c2, in0=c0, in1=c1, op=ALU.add),
                 reads=[(k, 0), (k, 1)], writes=[(k, 2)])

        def p1b(i):
            c2 = SSB[:, 4 * i + 2:4 * i + 3]
            k = ("ssb", i)
            S.op("act", lambda: nc.scalar.activation(out=c2, in_=c2, func=AF.Sqrt, scale=1.0 / D, bias=self.epsc),
                 reads=[(k, 2)], writes=[(k, 2)])
            S.op("dve", lambda: nc.vector.reciprocal(out=c2, in_=c2), reads=[(k, 2)], writes=[(k, 2)])

        def p3(i):
            r0 = seq * SEQ + i * 128
            xt = ring[i % R]
            xk = ("XT", i % R)
            ops, oks = tiles.pop(i)
            c2 = SSB[:, 4 * i + 2:4 * i + 3]
            k = ("ssb", i)
            for h in range(2):
                S.op("dve", lambda h=h: nc.vector.scalar_tensor_tensor(
                    out=T1[:, h * 512:(h + 1) * 512], in0=ops[h], scalar=c2, in1=gB[:, h * 512:(h + 1) * 512],
                    op0=ALU.mult, op1=ALU.mult), reads=[oks[h], (k, 2), gk], writes=[(("XT", 4), h)])
            S.op("dve", lambda: nc.vector.tensor_tensor(out=xt, in0=xt, in1=T1, op=ALU.add),
                 reads=[xk, (("XT", 4), 0), (("XT", 4), 1)], writes=[xk])
            S.dma("sp", dst[r0:r0 + 128, :], xt, reads=[xk], writes=[("xd", r0)])
            if i + R - 1 < 16:
                load(i + R - 1)

        p1a(0)
        p1a(1)
        p1b(0)
        for i in range(16):
            if i + 2 < 16:
                p1a(i + 2)
            if i + 1 < 16:
                p1b(i + 1)
            p3(i)
        self.w_release(2)

    def conv_mixer(self, src, dst, seq):
        S, nc, A = self.S, self.nc, self.arena
        S.barrier()
        A.reset()
        self.common_tmps(A)
        gBpre, gpk = self.load_gb(A, 0)
        gBpost, gqk = self.load_gb(A, 2)
        hT = A.alloc([128, 8, SEQ], BF16)
        AB = A.alloc([128, 4, 2080], BF16)
        CV = A.alloc([128, 4, 2052], BF16)
        BG = A.alloc([128, 4, SEQ], BF16)
        DGs = [A.alloc([128, 31, 128], BF16) for _ in range(2)]
        DGB = A.alloc([128, 3, 128], BF16)
        SIG = [A.alloc([128, 512], F32) for _ in range(2)]
        VS = [A.alloc([128, 512], BF16) for _ in range(2)]
        YSQ = A.alloc([128, 4, 512], BF16)
        MEAN = A.alloc([128, 512], F32)
        MSQ = A.alloc([128, 512], F32)
        SDv = A.alloc([128, 512], F32)
        Dt = [A.alloc([128, 512], F32) for _ in range(2)]
        Zt = [A.alloc([128, 512], F32) for _ in range(2)]
        S.op("dve", lambda: nc.vector.memset(AB[:, :, 0:15], 0.0), writes=["abh0"])
        S.op("dve", lambda: nc.vector.memset(AB[:, :, 2063:2080], 0.0), writes=["abh1"])
        S.op("dve", lambda: nc.vector.memset(CV[:, :, 0:1], 0.0), writes=["cvh0"])
        S.op("dve", lambda: nc.vector.memset(CV[:, :, 2049:2052], 0.0), writes=["cvh1"])
        ring = [A.alloc([128, D], F32) for _ in range(4)]
        T1 = A.alloc([128, D], F32)
        self.mixer_prenorm(A, src, seq, gBpre, gpk, hT, ring + [T1])

        def proj(W, wk, j, tb, role, banks):
            b, bk = self.bank(role, banks)
            S.mm(self.ps[b][:, :], [(W[:, kc, j * 128:(j + 1) * 128], hT[:, kc, tb * 512:(tb + 1) * 512])
                                    for kc in range(8)], reads=[wk] + [("hT", 4 * tb + q_, "w") for q_ in range(4)], wkey=bk)
            return self.ps[b][:, :], bk

        def w3(tag):
            W, wk = self.w_get(tag)
            return W[:, 0:4096].rearrange("p (a b) -> p a b", b=512), wk

        Wv, wkv = w3("aval")
        Wg, wkg = w3("agate")
        n = 0
        for j in range(4):
            for tb in range(4):
                pv, kv = proj(Wv, wkv, j, tb, "pa", [2, 3])
                pg, kg = proj(Wg, wkg, j, tb, "pb", [4, 5])
                sg, sk = SIG[n % 2], ("sig", n % 2)
                S.op("act", lambda: nc.scalar.activation(out=sg, in_=pg, func=AF.Sigmoid), reads=[kg], writes=[sk])
                S.op("dve", lambda: nc.vector.tensor_tensor(out=AB[:, j, 15 + tb * 512:15 + (tb + 1) * 512],
                                                            in0=pv, in1=sg, op=ALU.mult),
                     reads=[kv, sk], writes=[("Ag", j, tb)])
                n += 1
        self.w_release(2)
        Wc, wkc = w3("cgate")
        Wvv, wkvv = w3("v")
        for j in range(4):
            for tb in range(4):
                pc, kc_ = proj(Wc, wkc, j, tb, "pa", [2, 3])
                pv, kv = proj(Wvv, wkvv, j, tb, "pb", [4, 5])
                vs, vk = VS[n % 2], ("vs", n % 2)
                S.op("act", lambda: nc.scalar.copy(out=vs, in_=pv), reads=[kv], writes=[vk])
                S.op("dve", lambda: nc.vector.tensor_tensor(out=CV[:, j, 1 + tb * 512:1 + (tb + 1) * 512],
                                                            in0=pc, in1=vs, op=ALU.mult),
                     reads=[kc_, vk], writes=[("CV", j, tb)])
                n += 1
        self.w_release(2)
        Wb, wkb = w3("bgate")
        for j in range(4):
            for tb in range(4):
                pb, kb = proj(Wb, wkb, j, tb, "pa", [2, 3])
                S.op("act", lambda: nc.scalar.copy(out=BG[:, j, tb * 512:(tb + 1) * 512], in_=pb),
                     reads=[kb], writes=[("BG", j, tb)])
        self.w_release(1)
        S.barrier()
        CAT = hT
        for j in range(4):
            DG = DGs[j % 2]
            for k in range(31):
                S.op("dve", lambda k=k: nc.vector.tensor_scalar(out=DG[:, k, :], in0=self.identb,
                                                                 scalar1=self.pcol(P_DWW + j * 31 + k), scalar2=None,
                                                                 op0=ALU.mult),
                     writes=[("DG", j % 2, k)])
            for tb in range(4):
                b, bk = self.bank("cv", [2, 3])
                rd = [("DG", j % 2, k) for k in range(31)] + [("Ag", j, t) for t in (tb - 1, tb, tb + 1) if 0 <= t < 4]
                rd += [("Ar", j, tb), ("Ar", j, tb + 1), "abh0", "abh1"]
                S.mm(self.ps[b][:, :], [(DG[:, k, :], AB[:, j, tb * 512 + k:tb * 512 + k + 512]) for k in range(31)],
                     reads=rd, wkey=bk)
                S.op("act", lambda: nc.scalar.activation(out=AB[:, j, tb * 512:(tb + 1) * 512], in_=self.ps[b][:, :],
                                                         func=AF.Identity, bias=self.pcol(P_DWB + j), scale=1.0),
                     reads=[bk], writes=[("Ar", j, tb)])
        for tb in range(4):
            for j in range(4):
                S.op("act", lambda j=j: nc.scalar.activation(out=YSQ[:, j, :], in_=AB[:, j, tb * 512:(tb + 1) * 512],
                                                             func=AF.Square),
                     reads=[("Ar", j, tb)], writes=[("ysq", j)])
            bm, km = self.bank("st", [6, 7])
            S.mm(self.ps[bm][:, :], [(self.onesb, AB[:, j, tb * 512:(tb + 1) * 512]) for j in range(4)],
                 reads=[("Ar", j, tb) for j in range(4)], wkey=km)
            be, ke = self.bank("st", [6, 7])
            S.mm(self.ps[be][:, :], [(self.onesb, YSQ[:, j, :]) for j in range(4)],
                 reads=[("ysq", j) for j in range(4)], wkey=ke)
            S.op("act", lambda: nc.scalar.activation(out=MEAN, in_=self.ps[bm][:, :], func=AF.Copy, scale=1.0 / 512),
                 reads=[km], writes=["mean"])
            S.op("act", lambda: nc.scalar.activation(out=MSQ, in_=self.ps[bm][:, :], func=AF.Square, scale=1.0 / 512),
                 reads=[km], writes=["msq"])
            S.op("dve", lambda: nc.vector.scalar_tensor_tensor(out=SDv, in0=self.ps[be][:, :], scalar=1.0 / 512,
                                                                in1=MSQ, op0=ALU.mult, op1=ALU.subtract),
                 reads=[ke, "msq"], writes=["sd"])
            S.op("act", lambda: nc.scalar.activation(out=SDv, in_=SDv, func=AF.Sqrt, bias=self.epsc, scale=1.0),
                 reads=["sd"], writes=["sd"])
            S.op("dve", lambda: nc.vector.reciprocal(out=SDv, in_=SDv), reads=["sd"], writes=["sd"])
            for j in range(4):
                d, dk_ = Dt[j % 2], ("dt", j % 2)
                z, zk = Zt[j % 2], ("zt", j % 2)
                S.op("dve", lambda: nc.vector.tensor_tensor(out=d, in0=AB[:, j, tb * 512:(tb + 1) * 512], in1=MEAN,
                                                            op=ALU.subtract),
                     reads=[("Ar", j, tb), "mean"], writes=[dk_])
                S.op("dve", lambda: nc.vector.tensor_tensor(out=z, in0=d, in1=SDv, op=ALU.mult),
                     reads=[dk_, "sd"], writes=[zk])
                S.op("act", lambda: nc.scalar.activation(out=CAT[:, j, tb * 512:(tb + 1) * 512], in_=z, func=AF.Silu,
                                                         scale=self.pcol(P_LNG + j), bias=self.pcol(P_LNB + j)),
                     reads=[zk], writes=[("cat", j, tb)])
        for j in range(4):
            for k in range(3):
                S.op("dve", lambda k=k: nc.vector.tensor_scalar(out=DGB[:, k, :], in0=self.identb,
                                                                 scalar1=self.pcol(P_SCW + j * 3 + k), scalar2=None,
                                                                 op0=ALU.mult),
                     writes=[("DGB", k)])
            for tb in range(4):
                b, bk = self.bank("cv", [2, 3])
                rd = [("DGB", k) for k in range(3)] + [("CV", j, t) for t in (tb - 1, tb, tb + 1) if 0 <= t < 4]
                rd += ["cvh0", "cvh1"]
                S.mm(self.ps[b][:, :], [(DGB[:, k, :], CV[:, j, tb * 512 + k:tb * 512 + k + 512]) for k in range(3)],
                     reads=rd, wkey=bk)
                S.op("dve", lambda: nc.vector.tensor_tensor(out=CAT[:, 4 + j, tb * 512:(tb + 1) * 512],
                                                            in0=self.ps[b][:, :], in1=BG[:, j, tb * 512:(tb + 1) * 512],
                                                            op=ALU.mult),
                     reads=[bk, ("BG", j, tb)], writes=[("cat", 4 + j, tb)])
        self.mixer_out(A, CAT, lambda tb: [("cat", c, tb) for c in range(8)], src, dst, seq, gBpost, gqk, ring, T1)

    def gla_mixer(self, src, dst, seq):
        S, nc, A = self.S, self.nc, self.arena
        S.barrier()
        A.reset()
        self.common_tmps(A)
        gBpre, gpk = self.load_gb(A, 1)
        gBpost, gqk = self.load_gb(A, 3)
        hT = A.alloc([128, 8, SEQ], BF16)
        OG = A.alloc([128, 8, SEQ], BF16)
        GT = [A.alloc([17, SEQ], BF16) for _ in range(2)]
        WA = [A.alloc([17, 512], BF16) for _ in range(2)]
        QT = A.alloc([128, SEQ], BF16)
        KT = A.alloc([128, SEQ], BF16)
        VTM = A.alloc([128, 16, 256], BF16)
        OF = A.alloc([128, 2, SEQ], BF16)
        Et = A.alloc([128, 512], F32)
        LP = [A.alloc([128, 512], F32) for _ in range(2)]
        EQ = [A.alloc([128, 512], F32) for _ in range(3)]
        EK = A.alloc([128, 512], F32)
        QTt = [A.alloc([128, 512], BF16) for _ in range(3)]
        KTt = [A.alloc([128, 512], BF16) for _ in range(2)]
        ST = [A.alloc([128, 512], BF16) for _ in range(2)]
        KTM = [A.alloc([128, 512], BF16) for _ in range(2)]
        DECC = A.alloc([128, 2], F32)
        SBF = [A.alloc([128, 256], BF16) for _ in range(3)]
        OS = A.alloc([128, 2, 512], F32)
        OSQ = self.sqj.rearrange("p (a b) -> p a b", b=512)
        RG = A.alloc([128, 512], F32)
        for d in range(2):
            S.dma("pool", WA[d], self.wa_d[d], writes=[("WA", d)])
            S.op("dve", lambda d=d: nc.vector.memset(GT[d], 1.0), writes=[("GT", d, t) for t in range(4)])
        XT3 = [A.alloc([128, D], F32) for _ in range(3)]
        T1 = A.alloc([128, D], F32)
        osv = OS.rearrange("p a b -> p (a b)")
        Ub = [T1[:, 0:256], T1[:, 256:512]]
        uctr = [0]
        self.mixer_prenorm(A, src, seq, gBpre, gpk, hT, XT3 + [osv, T1])
        hk = lambda tb: [("hT", 4 * tb + q_, "w") for q_ in range(4)]
        W, wk = self.w_get("gates")
        Wg3 = W[:, 0:256].rearrange("p (a b) -> p a b", b=32)
        for tb in range(4):
            for d in range(2):
                b, bk = self.bank("pa", [2, 3])
                S.mm(self.ps[b][0:16, :], [(Wg3[:, kc, d * 16:(d + 1) * 16], hT[:, kc, tb * 512:(tb + 1) * 512])
                                            for kc in range(8)], reads=[wk] + hk(tb), wkey=bk)
                S.op("act", lambda: nc.scalar.copy(out=GT[d][0:16, tb * 512:(tb + 1) * 512], in_=self.ps[b][0:16, :]),
                     reads=[bk], writes=[("GT", d, tb)])
        self.w_release(1)
        for h in range(2):
            W, wk = self.w_get(f"r{h}")
            W3 = W[:, 0:4096].rearrange("p (a b) -> p a b", b=512)
            for cc in range(4):
                for tb in range(4):
                    b, bk = self.bank("pb", [4, 5])
                    S.mm(self.ps[b][:, :], [(W3[:, kc, cc * 128:(cc + 1) * 128], hT[:, kc, tb * 512:(tb + 1) * 512])
                                            for kc in range(8)], reads=[wk] + hk(tb), wkey=bk)
                    S.op("act", lambda: nc.scalar.activation(out=OG[:, h * 4 + cc, tb * 512:(tb + 1) * 512],
                                                             in_=self.ps[b][:, :], func=AF.Silu),
                         reads=[bk], writes=[("og", h * 4 + cc, tb)])
            self.w_release(1)
        gctr = [0]
        sctr = [0]
        for h in range(4):
            W, wk = self.w_get(f"qk{h}")
            Wq = W[:, 0:1024].rearrange("p (a b) -> p a b", b=128)
            Wk = W[:, 1024:2048].rearrange("p (a b) -> p a b", b=128)
            for tb in range(4):
                for (Wx, Xt, nm, sc) in ((Wq, QT, "QT", 128.0 ** -0.5), (Wk, KT, "KT", 1.0)):
                    b, bk = self.bank("pa", [2, 3])
                    S.mm(self.ps[b][:, :], [(Wx[:, kc, :], hT[:, kc, tb * 512:(tb + 1) * 512]) for kc in range(8)],
                         reads=[wk] + hk(tb), wkey=bk)
                    S.op("act", lambda: nc.scalar.activation(out=Xt[:, tb * 512:(tb + 1) * 512], in_=self.ps[b][:, :],
                                                             func=AF.Copy, scale=sc),
                         reads=[bk], writes=[(nm, tb)])
            self.w_release(1)
            W, wk = self.w_get(f"v{h}")
            Wv = W[:, 0:2048].rearrange("p (a b) -> p a b", b=256)
            for i in range(16):
                b, bk = self.bank("pb", [4, 5])
                S.mm(self.ps[b][:, 0:256], [(hT[:, kc, i * 128:(i + 1) * 128], Wv[:, kc, :]) for kc in range(8)],
                     reads=[wk, ("hT", i, "w")], wkey=bk)
                S.op("dve", lambda: nc.vector.tensor_copy(out=VTM[:, i, :], in_=self.ps[b][:, 0:256]),
                     reads=[bk], writes=[("VTM", i)])
            self.w_release(1)

            items = [(0, g) for g in range(4)] + [(1, g) for g in range(3, -1, -1)]
            col = lambda tm: slice(tm * 128, (tm + 1) * 128)

            def stA(d, gi, n):
                b, kz = self.bank("gz", [6, 7])
                zb = self.ps[b]
                S.mm_multi([(zb[:, col(tm)], [(GT[d][0:17, (gi * 4 + tm) * 128:(gi * 4 + tm + 1) * 128],
                                               WA[d][0:17, h * 128:(h + 1) * 128])]) for tm in range(4)],
                           reads=[("GT", d, gi), ("WA", d)], wkey=kz)
                S.op("act", lambda: nc.scalar.activation(out=Et, in_=zb[:, :], func=AF.Exp, scale=-1.0),
                     reads=[kz], writes=["E"])
                S.op("act", lambda: nc.scalar.activation(out=LP[n % 2], in_=Et, func=AF.Ln, bias=1.0, scale=1.0),
                     reads=["E"], writes=[("LP", n % 2)])

            def stB(d, gi, n):
                tri = self.trif if d == 0 else self.trib
                gs = slice(gi * 512, (gi + 1) * 512)
                b2, kc_ = self.bank("gz", [6, 7])
                cb = self.ps[b2]
                S.mm_multi([(cb[:, col(tm)], [(LP[n % 2][:, col(tm)], tri)]) for tm in range(4)],
                           reads=[("LP", n % 2)], wkey=kc_)
                S.op("act", lambda: nc.scalar.activation(out=EQ[n % 3], in_=cb[:, :], func=AF.Exp),
                     reads=[kc_], writes=[("EQ", n % 3)])
                S.op("act", lambda: nc.scalar.activation(out=EK, in_=cb[:, :], func=AF.Exp, scale=-1.0),
                     reads=[kc_], writes=["EK"])
                S.op("dve", lambda: nc.vector.tensor_tensor(out=QTt[n % 3], in0=QT[:, gs], in1=EQ[n % 3], op=ALU.mult),
                     reads=[("QT", gi), ("EQ", n % 3)], writes=[("QTt", n % 3)])
                S.op("dve", lambda: nc.vector.tensor_tensor(out=KTt[n % 2], in0=KT[:, gs], in1=EK, op=ALU.mult),
                     reads=[("KT", gi), "EK"], writes=[("KTt", n % 2)])

            def stC(d, gi, n):
                mask = self.maskf if d == 0 else self.maskb
                b3, ks = self.bank("sc", [0, 1])
                sb = self.ps[b3]
                S.mm_multi([(sb[:, col(tm)], [(KTt[n % 2][:, col(tm)], QTt[n % 3][:, col(tm)])]) for tm in range(4)],
                           reads=[("KTt", n % 2), ("QTt", n % 3)], wkey=ks)
                mask_b = mask.rearrange("p (o b) -> p o b", o=1).to_broadcast([128, 4, 128])
                S.op("dve", lambda: nc.vector.tensor_tensor(out=ST[n % 2].rearrange("p (a b) -> p a b", b=128),
                                                            in0=sb[:, :].rearrange("p (a b) -> p a b", b=128),
                                                            in1=mask_b, op=ALU.mult),
                     reads=[ks], writes=[("ST", n % 2)])
                b4, kt = self.bank("sc", [0, 1])
                tb16 = self.ps[b4][:, :].bitcast(BF16)
                S.tr([(tb16[:, col(tm)], KTt[n % 2][:, col(tm)]) for tm in range(4)], self.identb,
                     reads=[("KTt", n % 2)], wkey=kt)
                S.op("act", lambda: nc.scalar.copy(out=KTM[n % 2], in_=tb16[:, 0:512]), reads=[kt],
                     writes=[("KTM", n % 2)])

            def SD(d, gi, n, first_group, first_arrival):
                order = [0, 1, 2, 3] if d == 0 else [3, 2, 1, 0]
                lastc = 127 if d == 0 else 0
                p2, p3 = n % 2, n % 3
                decs = lambda tm: EQ[p3][:, tm * 128 + lastc:tm * 128 + lastc + 1]
                kvb = {}

                def kv(tm):
                    b6, kkv = self.bank("kv", [4, 5])
                    S.mm(self.ps[b6][:, 0:256], [(KTM[p2][:, col(tm)], VTM[:, gi * 4 + tm, :])],
                         reads=[("KTM", p2), ("VTM", gi * 4 + tm)], wkey=kkv)
                    kvb[tm] = (self.ps[b6][:, 0:256], kkv)

                kv(order[0])
                kv(order[1])
                obanks = {}
                for half in range(2):
                    b5, ko = self.bank("o", [2, 3])
                    obanks[half] = (self.ps[b5], ko)
                pend = {0: [], 1: []}
                for half in range(2):
                    S._wait("pe", S._collect([], [obanks[half][1]]))
                for t, tm in enumerate(order):
                    c = gi * 4 + tm
                    has_state = not (first_group and t == 0)
                    kvps, kkv = kvb[tm]
                    ob, ko = obanks[tm // 2]
                    tm2 = tm % 2
                    groups = []
                    for vc in range(2):
                        prs = [(VTM[:, c, vc * 128:(vc + 1) * 128], ST[p2][:, col(tm)])]
                        if has_state:
                            prs.append((SBF[sctr[0] % 3][:, vc * 128:(vc + 1) * 128], QTt[p3][:, col(tm)]))
                        groups.append((ob[:, vc * 256 + tm2 * 128:vc * 256 + tm2 * 128 + 128], prs))
                    rd = [("VTM", c), ("ST", p2), ("QTt", p3)] + ([("SBF", sctr[0] % 3)] if has_state else [])
                    S.mm_multi(groups, reads=rd, wkey=(ko, "part", tm2))
                    pend[tm // 2].append((ko, "part", tm2))
                    uo, un = uctr[0] % 2, (uctr[0] + 1) % 2
                    uctr[0] += 1
                    if t == 0:
                        if has_state:
                            S.op("dve", lambda: nc.vector.scalar_tensor_tensor(
                                out=Ub[un], in0=Ub[uo], scalar=DECC[:, (n + 1) % 2:(n + 1) % 2 + 1], in1=kvps,
                                op0=ALU.mult, op1=ALU.add), reads=[("U", uo), ("DECC", (n + 1) % 2), kkv],
                                writes=[("U", un)])
                        else:
                            S.op("dve", lambda: nc.vector.tensor_copy(out=Ub[un], in_=kvps), reads=[kkv],
                                 writes=[("U", un), ("XT", 4)])
                    else:
                        S.op("dve", lambda: nc.vector.scalar_tensor_tensor(out=Ub[un], in0=Ub[uo],
                                                                            scalar=decs(order[t - 1]),
                                                                            in1=kvps, op0=ALU.mult, op1=ALU.add),
                             reads=[("U", uo), ("EQ", p3), kkv], writes=[("U", un)])
                    nxt = (sctr[0] + 1) % 3
                    S.op("act", lambda: nc.scalar.activation(out=SBF[nxt], in_=Ub[un], func=AF.Copy, scale=decs(tm)),
                         reads=[("U", un), ("EQ", p3)], writes=[("SBF", nxt)])
                    sctr[0] += 1
                    if t + 2 < 4:
                        kv(order[t + 2])
                S.op("act", lambda: nc.scalar.copy(out=DECC[:, n % 2:n % 2 + 1], in_=decs(order[3])),
                     reads=[("EQ", p3)], writes=[("DECC", n % 2)])
                gs = slice(gi * 512, (gi + 1) * 512)
                for half in range(2):
                    ob, ko = obanks[half]
                    o3 = ob[:, :].rearrange("p (a b) -> p a b", b=256)
                    cols = slice(gi * 512 + half * 256, gi * 512 + half * 256 + 256)
                    if first_arrival:
                        S.op("act", lambda: nc.scalar.copy(out=OF[:, :, cols], in_=o3), reads=pend[half],
                             writes=[("OF", gi, half), ko])
                    else:
                        S.op("dve", lambda: nc.vector.tensor_tensor(out=OS[:, :, half * 256:(half + 1) * 256], in0=o3,
                                                                    in1=OF[:, :, cols], op=ALU.add),
                             reads=pend[half] + [("OF", gi, half)], writes=[("XT", 3), ko])
                if not first_arrival:
                    S.op("act", lambda: nc.scalar.activation(out=OSQ, in_=OS, func=AF.Square),
                         reads=[("XT", 3)], writes=["OSQ"])
                    b7, kq = self.bank("gz", [6, 7])
                    qps = self.ps[b7]
                    S.mm(qps[:, :], [(self.onesb, OSQ[:, vc, :]) for vc in range(2)], reads=["OSQ"], wkey=kq)
                    S.op("act", lambda: nc.scalar.activation(out=RG, in_=qps[:, :], func=AF.Ln, scale=1.0 / 256,
                                                             bias=self.epsc),
                         reads=[kq], writes=["RG"])
                    S.op("act", lambda: nc.scalar.activation(out=RG, in_=RG, func=AF.Exp, scale=-0.5),
                         reads=["RG"], writes=["RG"])
                    rg_b = RG.rearrange("p (o b) -> p o b", o=1).to_broadcast([128, 2, 512])
                    S.op("dve", lambda: nc.vector.tensor_tensor(out=OS, in0=OS, in1=rg_b, op=ALU.mult),
                         reads=[("XT", 3), "RG"], writes=[("XT", 3)])
                    for vc in range(2):
                        S.op("dve", lambda vc=vc: nc.vector.scalar_tensor_tensor(
                            out=OG[:, h * 2 + vc, gs], in0=OS[:, vc, :], scalar=self.pcol(P_GNG + h * 2 + vc),
                            in1=OG[:, h * 2 + vc, gs], op0=ALU.mult, op1=ALU.mult),
                            reads=[("XT", 3), ("og", h * 2 + vc, gi)], writes=[("og", h * 2 + vc, gi)])

            n0 = gctr[0]
            gctr[0] += len(items)
            L = len(items)
            stA(*items[0], n0)
            stB(*items[0], n0)
            stC(*items[0], n0)
            stA(*items[1], n0 + 1)
            stB(*items[1], n0 + 1)
            stA(*items[2], n0 + 2)
            for i_ in range(L):
                if i_ + 3 < L:
                    stA(*items[i_ + 3], n0 + i_ + 3)
                if i_ + 2 < L:
                    stB(*items[i_ + 2], n0 + i_ + 2)
                if i_ + 1 < L:
                    stC(*items[i_ + 1], n0 + i_ + 1)
                d, gi = items[i_]
                SD(d, gi, n0 + i_, first_group=(i_ == 0 or i_ == 4), first_arrival=(d == 0))
        S.barrier()
        self.mixer_out(A, OG, lambda tb: [("og", c, tb) for c in range(8)], src, dst, seq, gBpost, gqk, XT3 + [osv], T1)

    def build(self):
        self.declare()
        self.setup()
        self.make_plan()
        self.w_prime()
        S = self.S
        nsub = len(self.plan)
        for seq in range(2):
            for si, sub in enumerate(self.plan):
                src = self.x_in if si == 0 else self.xs
                dst = self.y_out if si == nsub - 1 else self.xs
                if sub == "mix0":
                    self.conv_mixer(src, dst, seq)
                elif sub == "mix1":
                    self.gla_mixer(src, dst, seq)
                else:
                    self.ffn(int(sub[3]), src, dst, seq)
        S.barrier(engines=("sp",))
        self.es.close()
        return self.nc


def host_inputs(inp):
    f = lambda a: np.ascontiguousarray(np.asarray(a, dtype=np.float32))
    pm = lambda v: f(v).reshape(-1, 128).T
    prm = np.zeros((128, NPRM), np.float32)
    for l in range(2):
        prm[:, P_MIXPRE + 8 * l:P_MIXPRE + 8 * l + 8] = pm(inp["mix_pre_g"][l])
        prm[:, P_MIXPOST + 8 * l:P_MIXPOST + 8 * l + 8] = pm(inp["mix_post_g"][l])
        prm[:, P_FFNPRE + 8 * l:P_FFNPRE + 8 * l + 8] = pm(inp["ffn_pre_g"][l])
        prm[:, P_FFNPOST + 8 * l:P_FFNPOST + 8 * l + 8] = pm(inp["ffn_post_g"][l])
    dww = f(inp["cv_dw_w"][0])
    prm[:, P_DWW:P_DWW + 124] = dww.reshape(31, 4, 128).transpose(2, 1, 0).reshape(128, 124)
    prm[:, P_DWB:P_DWB + 4] = pm(inp["cv_dw_b"][0])
    prm[:, P_LNG:P_LNG + 4] = pm(inp["cv_ln_g"][0])
    prm[:, P_LNB:P_LNB + 4] = pm(inp["cv_ln_b"][0])
    scw = f(inp["cv_sc_w"][0])
    prm[:, P_SCW:P_SCW + 12] = scw.reshape(3, 4, 128).transpose(2, 1, 0).reshape(128, 12)
    prm[:, P_GNG:P_GNG + 8] = pm(inp["gla_gn_g"][0])
    j = np.arange(128)[:, None]
    i = np.arange(128)[None, :]
    cst = np.concatenate([
        np.eye(128), np.ones((128, 128)), (j <= i) * 1.0, (j >= i) * 1.0,
        (j <= i) * (-1.0 / 16.0), (j >= i) * (-1.0 / 16.0)], axis=1).astype(np.float32)
    gvec = np.stack([f(inp["mix_pre_g"][0]), f(inp["mix_pre_g"][1]), f(inp["mix_post_g"][0]),
                     f(inp["mix_post_g"][1]), f(inp["ffn_pre_g"][0]), f(inp["ffn_pre_g"][1]),
                     f(inp["ffn_post_g"][0]), f(inp["ffn_post_g"][1])], axis=0)
    wa = np.stack([np.concatenate([f(inp["gla_wa2_f"][0]), f(inp["gla_ba2_f"])[0:1]], axis=0),
                   np.concatenate([f(inp["gla_wa2_b"][0]), f(inp["gla_ba2_b"])[0:1]], axis=0)], axis=0)
    shared = {
        "prm": prm, "cst": cst, "gvec": f(gvec), "wa": f(wa),
        "cv_w_in": f(inp["cv_w_in"][0]), "cv_w_out": f(inp["cv_w_out"][0]),
        "gla_w_in": f(inp["gla_w_in"][0]), "gla_w_out": f(inp["gla_w_out"][0]),
        "ffn_w_gu": f(inp["ffn_w_gu"]), "ffn_w_down": f(inp["ffn_w_down"]),
    }
    x = f(inp["x"]).reshape(NCORES, TOK, D)
    return [dict(shared, x=x[c]) for c in range(NCORES)]


_CACHE = {}


def run(inp, plan=("mix0", "ffn0", "mix1", "ffn1")):
    plan = tuple(plan)
    if plan not in _CACHE:
        _CACHE[plan] = Builder(plan).build()
    nc = _CACHE[plan]
    in_maps = host_inputs(inp)
    res = run_bass_kernel_spmd(nc, in_maps, core_ids=list(range(NCORES)))
    out = np.stack([np.asarray(r["y"], dtype=np.float32) for r in res.results], axis=0)
    return out.reshape(16, SEQ, D)


def kernel(**inputs):
    return run(inputs)
```

```python
import math
from contextlib import ExitStack

import numpy as np
import concourse.bass as bass
import concourse.mybir as mybir
from concourse.bass_utils import run_bass_kernel_spmd
from concourse.alu_op_type import AluOpType as ALU

F32 = mybir.dt.float32
BF16 = mybir.dt.bfloat16
AF = mybir.ActivationFunctionType

NCORES = 8
D = 1024
SEQ = 2048
TOK = 4096
DFF = 2816
EPS = 1e-6
SLOT_ELEMS = 5632
NSLOT = 4
NPRM = 220

P_MIXPRE, P_MIXPOST, P_FFNPRE, P_FFNPOST = 0, 16, 32, 48
P_DWW = 64
P_DWB = 188
P_LNG = 192
P_LNB = 196
P_SCW = 200
P_GNG = 212


class Sched:
    def __init__(self, nc, es):
        self.nc = nc
        self.E = {"pe": nc.tensor, "act": nc.scalar, "dve": nc.vector, "pool": nc.gpsimd, "sp": nc.sync}
        self.psem = {}
        for e in ("pe", "act", "dve", "pool"):
            self.psem[e] = es.enter_context(nc.semaphore(f"p_{e}"))
        self.pcnt = {e: 0 for e in self.psem}
        self.waited = {e: {} for e in self.E}
        self.state = {}
        self.dsems = {}
        for q in ("sp", "pool"):
            self.dsems[q] = [[es.enter_context(nc.semaphore(f"d_{q}{i}")), 0, f"d_{q}{i}"] for i in range(10)]
        self.drr = {q: 0 for q in self.dsems}
        self.slotsem = [[es.enter_context(nc.semaphore(f"w_{i}")), 0, f"w_{i}"] for i in range(NSLOT)]

    def _collect(self, reads, writes):
        need = {}

        def add(n, s, v):
            if n not in need or need[n][1] < v:
                need[n] = (s, v)

        for k in reads:
            st = self.state.get(k)
            if st and st[0] is not None:
                add(*st[0])
        for k in writes:
            st = self.state.get(k)
            if st:
                if st[0] is not None:
                    add(*st[0])
                for n, (s, v) in st[1].items():
                    add(n, s, v)
        return need

    def _wait(self, e, need):
        for n, (s, v) in need.items():
            if self.waited[e].get(n, 0) >= v:
                continue
            self.E[e].wait_ge(s, v)
            self.waited[e][n] = v

    def _commit(self, tok, reads, writes):
        n, s, v = tok
        for k in reads:
            st = self.state.setdefault(k, [None, {}])
            st[1][n] = (s, v)
        for k in writes:
            self.state[k] = [tok, {}]

    def op(self, e, fn, reads=(), writes=()):
        need = self._collect(reads, writes)
        self._wait(e, need)
        ins = fn()
        self.pcnt[e] += 1
        ins.then_inc(self.psem[e], 1)
        tok = (f"p_{e}", self.psem[e], self.pcnt[e])
        self._commit(tok, reads, writes)
        return tok

    def mm(self, out, pairs, reads, wkey):
        need = self._collect(reads, [wkey])
        self._wait("pe", need)
        n = len(pairs)
        ins = None
        for i, (l, r) in enumerate(pairs):
            ins = self.nc.tensor.matmul(out, l, r, start=(i == 0), stop=(i == n - 1))
        self.pcnt["pe"] += 1
        ins.then_inc(self.psem["pe"], 1)
        tok = ("p_pe", self.psem["pe"], self.pcnt["pe"])
        self._commit(tok, reads, [wkey])
        return tok

    def mm_multi(self, groups, reads, wkey):
        need = self._collect(reads, [wkey])
        self._wait("pe", need)
        ins = None
        for out, pairs in groups:
            n = len(pairs)
            for i, (l, r) in enumerate(pairs):
                ins = self.nc.tensor.matmul(out, l, r, start=(i == 0), stop=(i == n - 1))
        self.pcnt["pe"] += 1
        ins.then_inc(self.psem["pe"], 1)
        tok = ("p_pe", self.psem["pe"], self.pcnt["pe"])
        self._commit(tok, reads, [wkey])
        return tok

    def tr(self, items, ident, reads, wkey):
        need = self._collect(reads, [wkey])
        self._wait("pe", need)
        ins = None
        for out, in_ in items:
            ins = self.nc.tensor.transpose(out, in_, ident)
        self.pcnt["pe"] += 1
        ins.then_inc(self.psem["pe"], 1)
        tok = ("p_pe", self.psem["pe"], self.pcnt["pe"])
        self._commit(tok, reads, [wkey])
        return tok

    def dma(self, q, out, in_, reads=(), writes=(), semrec=None, nonc=False):
        if semrec is None:
            semrec = self.dsems[q][self.drr[q]]
            self.drr[q] = (self.drr[q] + 1) % len(self.dsems[q])
            need = self._collect(reads, writes)
            if semrec[1] > 0:
                need[semrec[2]] = (semrec[0], semrec[1])
        else:
            need = self._collect(reads, writes)
        self._wait(q, need)
        if nonc:
            ins = self.E[q].dma_start(out=out, in_=in_, allow_slow_non_contiguous=True)
        else:
            ins = self.E[q].dma_start(out=out, in_=in_)
        semrec[1] += 16
        ins.then_inc(semrec[0], 16)
        tok = (semrec[2], semrec[0], semrec[1])
        self._commit(tok, reads, writes)
        return tok

    def barrier(self, engines=("pe", "act", "dve", "sp")):
        need = {}
        for e in self.psem:
            if self.pcnt[e] > 0:
                need[f"p_{e}"] = (self.psem[e], self.pcnt[e])
        for q in self.dsems:
            for rec in self.dsems[q]:
                if rec[1] > 0:
                    need[rec[2]] = (rec[0], rec[1])
        for e in engines:
            self._wait(e, need)


class Arena:
    def __init__(self, big, base_b, limit_b):
        self.big = big
        self.base = base_b
        self.limit = limit_b
        self.off = base_b

    def reset(self):
        self.off = self.base

    def alloc(self, shape, dt):
        esz = 2 if dt == BF16 else 4
        n = 1
        for s in shape[1:]:
            n *= s
        nb = (n * esz + 63) // 64 * 64
        assert self.off + nb <= self.limit, f"arena overflow {self.off + nb} > {self.limit}"
        a = self.big[0:shape[0], self.off // 2: self.off // 2 + n * esz // 2]
        self.off += nb
        if dt == F32:
            a = a.bitcast(F32)
        if len(shape) == 3:
            a = a.rearrange("p (a b) -> p a b", b=shape[2])
        elif len(shape) == 4:
            a = a.rearrange("p (a b c) -> p a b c", b=shape[2], c=shape[3])
        return a


class Builder:
    def __init__(self, plan):
        self.plan = plan
        self.nc = bass.Bass("TRN2", target_bir_lowering=False)
        self.es = ExitStack()

    def declare(self):
        nc = self.nc
        dt = lambda name, shape, kind="ExternalInput": nc.dram_tensor(name, shape, F32, kind=kind).ap()
        self.x_in = dt("x", [TOK, D])
        self.y_out = dt("y", [TOK, D], "ExternalOutput")
        self.xs = dt("xs", [TOK, D], "Internal")
        self.prm_d = dt("prm", [128, NPRM])
        self.cst_d = dt("cst", [128, 6 * 128])
        self.gvec_d = dt("gvec", [8, D])
        self.wa_d = dt("wa", [2, 17, 512])
        self.cv_w_in = dt("cv_w_in", [D, 2560])
        self.cv_w_out = dt("cv_w_out", [D, D])
        self.gla_w_in = dt("gla_w_in", [D, 3104])
        self.gla_w_out = dt("gla_w_out", [D, D])
        self.ffn_w_gu = dt("ffn_w_gu", [2, D, 2 * DFF])
        self.ffn_w_down = dt("ffn_w_down", [2, DFF, D])

    def setup(self):
        nc, es = self.nc, self.es
        self.S = Sched(nc, es)
        S = self.S
        total_b = 212800
        self.big = es.enter_context(nc.sbuf_tensor("big", [128, total_b // 2], BF16))
        self.ps = [es.enter_context(nc.psum_tensor(f"ps{i}", [128, 512], F32)) for i in range(8)]
        carve = Arena(self.big, 0, total_b)
        self.slots = [carve.alloc([128, SLOT_ELEMS], BF16) for _ in range(NSLOT)]
        self.identb = carve.alloc([128, 128], BF16)
        self.onesb = carve.alloc([128, 128], BF16)
        self.maskf = carve.alloc([128, 128], BF16)
        self.maskb = carve.alloc([128, 128], BF16)
        self.trif = carve.alloc([128, 128], F32)
        self.trib = carve.alloc([128, 128], F32)
        self.prm = carve.alloc([128, NPRM], F32)
        self.arena = Arena(self.big, carve.off, total_b)
        c = self.cst_d
        S.dma("pool", self.identb, c[:, 0:128], writes=["c0"])
        S.dma("pool", self.onesb, c[:, 128:256], writes=["c1"])
        S.dma("pool", self.maskf, c[:, 256:384], writes=["c2"])
        S.dma("pool", self.maskb, c[:, 384:512], writes=["c3"])
        S.dma("sp", self.trif, c[:, 512:640], writes=["c4"])
        S.dma("sp", self.trib, c[:, 640:768], writes=["c5"])
        S.dma("sp", self.prm, self.prm_d[:, :], writes=["c6"])
        S.barrier()
        self.wplan = []
        self.wnext_issue = 0
        self.wnext_use = 0
        self.psrr = {}

    def bank(self, role, banks):
        i = self.psrr.get(role, 0)
        self.psrr[role] = i + 1
        b = banks[i % len(banks)]
        return b, ("ps", b)

    def pcol(self, c0, n=1):
        return self.prm[:, c0:c0 + n]

    def w_issue(self, idx):
        if idx >= len(self.wplan):
            return
        S = self.S
        slot = idx % NSLOT
        rec = S.slotsem[slot]
        S._wait("pool", S._collect([], [("w", slot)]))
        for (eoff, shp, src) in self.wplan[idx]:
            n = shp[0] * shp[1]
            dst = self.slots[slot][:, eoff:eoff + n].rearrange("p (a b) -> p a b", b=shp[1])
            ins = self.nc.gpsimd.dma_start(out=dst, in_=src)
            rec[1] += 16
            ins.then_inc(rec[0], 16)
        S._commit((rec[2], rec[0], rec[1]), [], [("w", slot)])

    def w_prime(self):
        for i in range(NSLOT):
            self.w_issue(i)
        self.wnext_issue = NSLOT

    def w_get(self, tag):
        idx = self.wnext_use
        assert self.wtags[idx] == tag, (idx, self.wtags[idx], tag)
        self.wnext_use += 1
        slot = idx % NSLOT
        return self.slots[slot], ("w", slot)

    def w_release(self, n=1):
        for _ in range(n):
            self.w_issue(self.wnext_issue)
            self.wnext_issue += 1

    def add_load(self, tag, parts):
        self.wplan.append(parts)
        self.wtags.append(tag)

    @staticmethod
    def wsrc(w2d, r0, nr, c0, ncol):
        return w2d[r0:r0 + nr, c0:c0 + ncol].rearrange("(kc p) n -> p kc n", p=128)

    def make_plan(self):
        self.wtags = []
        for s in range(2):
            for sub in self.plan:
                if sub == "mix0":
                    w = self.cv_w_in
                    for nm, c0 in (("aval", 0), ("agate", 512), ("cgate", 1536), ("v", 2048), ("bgate", 1024)):
                        self.add_load(nm, [(0, (8, 512), self.wsrc(w, 0, D, c0, 512))])
                    for h in range(2):
                        self.add_load(f"wout{h}", [(0, (8, 512), self.wsrc(self.cv_w_out, 0, D, h * 512, 512))])
                elif sub == "mix1":
                    w = self.gla_w_in
                    self.add_load("gates", [(0, (8, 32), self.wsrc(w, 0, D, 3072, 32))])
                    for h in range(2):
                        self.add_load(f"r{h}", [(0, (8, 512), self.wsrc(w, 0, D, 2048 + h * 512, 512))])
                    for h in range(4):
                        self.add_load(f"qk{h}", [(0, (8, 128), self.wsrc(w, 0, D, h * 128, 128)),
                                                 (1024, (8, 128), self.wsrc(w, 0, D, 512 + h * 128, 128))])
                        self.add_load(f"v{h}", [(0, (8, 256), self.wsrc(w, 0, D, 1024 + h * 256, 256))])
                    for h in range(2):
                        self.add_load(f"wout{h}", [(0, (8, 512), self.wsrc(self.gla_w_out, 0, D, h * 512, 512))])
                elif sub in ("ffn0", "ffn1"):
                    l = int(sub[3])
                    wg = self.ffn_w_gu[l]
                    wd = self.ffn_w_down[l]
                    for hb in range(2):
                        for L in range(11):
                            self.add_load(f"gu{L}", [(0, (8, 256), self.wsrc(wg, 0, D, L * 256, 256)),
                                                     (2048, (8, 256), self.wsrc(wg, 0, D, DFF + L * 256, 256))])
                        for half in range(2):
                            for part in range(2):
                                self.add_load(f"dn{half}{part}",
                                              [(0, (11, 512), self.wsrc(wd, part * 1408, 1408, half * 512, 512))])

    def load_gb(self, A, row):
        g = A.alloc([128, D], F32)
        self.S.dma("sp", g, self.gvec_d[row:row + 1, :].to_broadcast([128, D]), writes=[("gb", row)])
        return g, ("gb", row)

    def pn_stats(self, xt, xkey, rs_col, rskey, ss_col, sskey):
        S, nc = self.S, self.nc
        S.op("act", lambda: nc.scalar.activation(out=self.sqj, in_=xt, func=AF.Square, accum_out=ss_col),
             reads=[xkey], writes=[sskey])
        S.op("act", lambda: nc.scalar.activation(out=rs_col, in_=ss_col, func=AF.Sqrt, scale=1.0 / D, bias=self.epsc),
             reads=[sskey], writes=[rskey])
        S.op("dve", lambda: nc.vector.reciprocal(out=rs_col, in_=rs_col), reads=[rskey], writes=[rskey])

    def pn_apply_a(self, xt, xkey, rs_col, rskey, gB, gkey, htm, htmkey):
        S, nc = self.S, self.nc
        S.op("dve", lambda: nc.vector.scalar_tensor_tensor(out=htm, in0=xt, scalar=rs_col, in1=gB,
                                                            op0=ALU.mult, op1=ALU.mult),
             reads=[xkey, rskey, gkey], writes=[htmkey])
        b, bkey = self.bank("pt", [0, 1])
        pv = self.ps[b][:, :].bitcast(BF16)
        S.tr([(pv[:, c * 128:(c + 1) * 128], htm[:, c * 128:(c + 1) * 128]) for c in range(8)], self.identb,
             reads=[htmkey], wkey=bkey)
        return pv, bkey

    def pn_apply_b(self, pv, bkey, dst_cols, hkey):
        S, nc = self.S, self.nc
        S.op("dve", lambda: nc.vector.tensor_copy(out=dst_cols, in_=pv.rearrange("p (c t) -> p c t", t=128)),
             reads=[bkey], writes=[hkey])

    def postnorm_tile(self, ops, okeys, xt, xkey, gB, gkey, t1, t1key, ssb, idx):
        S, nc = self.S, self.nc
        c0 = ssb[:, 4 * idx:4 * idx + 1]
        c1 = ssb[:, 4 * idx + 1:4 * idx + 2]
        c2 = ssb[:, 4 * idx + 2:4 * idx + 3]
        k = ("ssb", idx)
        S.op("act", lambda: nc.scalar.activation(out=self.sqj[:, 0:512], in_=ops[0], func=AF.Square, accum_out=c0),
             reads=[okeys[0]], writes=[(k, 0)])
        S.op("act", lambda: nc.scalar.activation(out=self.sqj[:, 512:1024], in_=ops[1], func=AF.Square, accum_out=c1),
             reads=[okeys[1]], writes=[(k, 1)])
        S.op("dve", lambda: nc.vector.tensor_tensor(out=c2, in0=c0, in1=c1, op=ALU.add),
             reads=[(k, 0), (k, 1)], writes=[(k, 2)])
        S.op("act", lambda: nc.scalar.activation(out=c2, in_=c2, func=AF.Sqrt, scale=1.0 / D, bias=self.epsc),
             reads=[(k, 2)], writes=[(k, 2)])
        S.op("dve", lambda: nc.vector.reciprocal(out=c2, in_=c2), reads=[(k, 2)], writes=[(k, 2)])
        for h in range(2):
            S.op("dve", lambda h=h: nc.vector.scalar_tensor_tensor(
                out=t1[:, h * 512:(h + 1) * 512], in0=ops[h], scalar=c2, in1=gB[:, h * 512:(h + 1) * 512],
                op0=ALU.mult, op1=ALU.mult), reads=[okeys[h], (k, 2), gkey], writes=[(t1key, h)])
        S.op("dve", lambda: nc.vector.tensor_tensor(out=xt, in0=xt, in1=t1, op=ALU.add),
             reads=[xkey, (t1key, 0), (t1key, 1)], writes=[xkey])

    def common_tmps(self, A):
        self.sqj = A.alloc([128, D], BF16)
        self.epsc = A.alloc([128, 1], F32)
        self.S.op("dve", lambda: self.nc.vector.memset(self.epsc, EPS), writes=["epsc"])
        self.S.barrier()

    def ffn(self, l, src, dst, seq):
        S, nc, A = self.S, self.nc, self.arena
        S.barrier()
        A.reset()
        self.common_tmps(A)
        gBpre, gpk = self.load_gb(A, 4 + l)
        gBpost, gqk = self.load_gb(A, 6 + l)
        XH = A.alloc([128, 8, D], F32)
        hT = A.alloc([128, 8, 1024], BF16)
        M0 = A.alloc([128, 8, 512], F32)
        ACTB = A.alloc([128, 22, 1024], BF16)
        HTM = [A.alloc([128, D], BF16) for _ in range(2)]
        SG = [A.alloc([128, 512], BF16) for _ in range(2)]
        T1 = [A.alloc([128, D], F32) for _ in range(2)]
        SS = A.alloc([128, 16], F32)
        RS = A.alloc([128, 16], F32)
        SSB = A.alloc([128, 64], F32)
        for hb in range(2):
            t0 = seq * SEQ + hb * 1024
            for i in range(8):
                r0 = t0 + i * 128
                S.dma("sp", XH[:, i, :], src[r0:r0 + 128, :], reads=[("xd", r0)], writes=[("XH", i)])
            for i in range(8):
                self.pn_stats(XH[:, i, :], ("XH", i), RS[:, i:i + 1], ("rs", i), SS[:, i:i + 1], ("ss", i))
            pend = None
            for i in range(8):
                cur = self.pn_apply_a(XH[:, i, :], ("XH", i), RS[:, i:i + 1], ("rs", i), gBpre, gpk,
                                      HTM[i % 2], ("htm", i % 2))
                if pend is not None:
                    self.pn_apply_b(*pend)
                pend = (cur[0], cur[1], hT[:, :, i * 128:(i + 1) * 128], ("hT", i, "w"))
            self.pn_apply_b(*pend)
            for L in range(11):
                W, wk = self.w_get(f"gu{L}")
                Wg = W[:, 0:2048].rearrange("p (a b) -> p a b", b=256)
                Wu = W[:, 2048:4096].rearrange("p (a b) -> p a b", b=256)
                for cc in range(2):
                    c = 2 * L + cc
                    for tb in range(2):
                        bg, kg = self.bank("g", [2, 3])
                        bu, ku = self.bank("u", [4, 5])
                        rhs = lambda kc: hT[:, kc, tb * 512:(tb + 1) * 512]
                        S.mm(self.ps[bg][:, :], [(Wg[:, kc, cc * 128:(cc + 1) * 128], rhs(kc)) for kc in range(8)],
                             reads=[wk] + [("hT", 4 * tb + q, "w") for q in range(4)], wkey=kg)
                        S.mm(self.ps[bu][:, :], [(Wu[:, kc, cc * 128:(cc + 1) * 128], rhs(kc)) for kc in range(8)],
                             reads=[wk] + [("hT", 4 * tb + q, "w") for q in range(4)], wkey=ku)
                        sg = SG[(2 * c + tb) % 2]
                        sgk = ("sg", (2 * c + tb) % 2)
                        S.op("act", lambda: nc.scalar.activation(out=sg, in_=self.ps[bg][:, :], func=AF.Silu),
                             reads=[kg], writes=[sgk])
                        S.op("dve", lambda: nc.vector.tensor_tensor(out=ACTB[:, c, tb * 512:(tb + 1) * 512], in0=sg,
                                                                    in1=self.ps[bu][:, :], op=ALU.mult),
                             reads=[sgk, ku], writes=[("actb", c, tb)])
                self.w_release()
            for half in range(2):
                Wa, wka = self.w_get(f"dn{half}0")
                Wb, wkb = self.w_get(f"dn{half}1")
                Wa3 = Wa[:, 0:5632].rearrange("p (a b) -> p a b", b=512)
                Wb3 = Wb[:, 0:5632].rearrange("p (a b) -> p a b", b=512)
                for i in range(8):
                    bf, kf = self.bank("f", [6, 7])
                    tb = i // 4
                    pairs = []
                    for kc in range(22):
                        w3 = Wa3 if kc < 11 else Wb3
                        pairs.append((ACTB[:, kc, i * 128:(i + 1) * 128], w3[:, kc % 11, :]))
                    S.mm(self.ps[bf][:, :], pairs, reads=[wka, wkb] + [("actb", kc, tb) for kc in range(22)], wkey=kf)
                    c0 = SSB[:, 4 * i + half:4 * i + half + 1]
                    k = ("ssb", i)
                    if half == 0:
                        S.op("act", lambda: nc.scalar.activation(out=self.sqj[:, 0:512], in_=self.ps[bf][:, :],
                                                                 func=AF.Square, accum_out=c0),
                             reads=[kf], writes=[(k, 0)])
                        S.op("act", lambda: nc.scalar.copy(out=M0[:, i, :], in_=self.ps[bf][:, :]),
                             reads=[kf], writes=[("m0", i)])
                    else:
                        c2 = SSB[:, 4 * i + 2:4 * i + 3]
                        S.op("act", lambda: nc.scalar.activation(out=self.sqj[:, 512:1024], in_=self.ps[bf][:, :],
                                                                 func=AF.Square, accum_out=c0),
                             reads=[kf], writes=[(k, 1)])
                        S.op("dve", lambda: nc.vector.tensor_tensor(out=c2, in0=SSB[:, 4 * i:4 * i + 1], in1=c0,
                                                                    op=ALU.add),
                             reads=[(k, 0), (k, 1)], writes=[(k, 2)])
                        S.op("act", lambda: nc.scalar.activation(out=c2, in_=c2, func=AF.Sqrt, scale=1.0 / D,
                                                                 bias=self.epsc),
                             reads=[(k, 2)], writes=[(k, 2)])
                        S.op("dve", lambda: nc.vector.reciprocal(out=c2, in_=c2), reads=[(k, 2)], writes=[(k, 2)])
                        t1 = T1[i % 2]
                        tk = ("t1", i % 2)
                        S.op("dve", lambda: nc.vector.scalar_tensor_tensor(
                            out=t1[:, 512:1024], in0=self.ps[bf][:, :], scalar=c2, in1=gBpost[:, 512:1024],
                            op0=ALU.mult, op1=ALU.mult), reads=[kf, (k, 2), gqk], writes=[(tk, 1)])
                        S.op("dve", lambda: nc.vector.scalar_tensor_tensor(
                            out=t1[:, 0:512], in0=M0[:, i, :], scalar=c2, in1=gBpost[:, 0:512],
                            op0=ALU.mult, op1=ALU.mult), reads=[("m0", i), (k, 2), gqk], writes=[(tk, 0)])
                        S.op("dve", lambda: nc.vector.tensor_tensor(out=XH[:, i, :], in0=XH[:, i, :], in1=t1,
                                                                    op=ALU.add),
                             reads=[("XH", i), (tk, 0), (tk, 1)], writes=[("XH", i)])
                        r0 = t0 + i * 128
                        S.dma("sp", dst[r0:r0 + 128, :], XH[:, i, :], reads=[("XH", i)], writes=[("xd", r0)])
                self.w_release(2)

    def mixer_prenorm(self, A, src, seq, gB, gk, hT, ring):
        S = self.S
        HTM = [A.alloc([128, D], BF16) for _ in range(2)]
        SS = A.alloc([128, 16], F32)
        RS = A.alloc([128, 16], F32)
        R = len(ring)

        def load(i):
            r0 = seq * SEQ + i * 128
            S.dma("sp", ring[i % R], src[r0:r0 + 128, :], reads=[("xd", r0)], writes=[("XT", i % R)])

        def stats(i):
            self.pn_stats(ring[i % R], ("XT", i % R), RS[:, i:i + 1], ("rs", i), SS[:, i:i + 1], ("ss", i))

        for i in range(R - 1):
            load(i)
        stats(0)
        pend = None
        for i in range(16):
            if i + 1 < 16:
                stats(i + 1)
            cur = self.pn_apply_a(ring[i % R], ("XT", i % R), RS[:, i:i + 1], ("rs", i), gB, gk, HTM[i % 2],
                                  ("htm", i % 2))
            if pend is not None:
                self.pn_apply_b(*pend)
            pend = (cur[0], cur[1], hT[:, :, i * 128:(i + 1) * 128], ("hT", i, "w"))
            if i + R - 1 < 16:
                load(i + R - 1)
        self.pn_apply_b(*pend)

    def mixer_out(self, A, CAT, catkeys, src, dst, seq, gB, gk, ring, T1):
        S, nc = self.S, self.nc
        SSB = A.alloc([128, 64], F32)
        W0, wk0 = self.w_get("wout0")
        W1, wk1 = self.w_get("wout1")
        W3 = [W0[:, 0:4096].rearrange("p (a b) -> p a b", b=512), W1[:, 0:4096].rearrange("p (a b) -> p a b", b=512)]
        wks = [wk0, wk1]
        R = len(ring)

        def load(i):
            r0 = seq * SEQ + i * 128
            S.dma("sp", ring[i % R], src[r0:r0 + 128, :], reads=[("xd", r0)], writes=[("XT", i % R)])

        for i in range(R - 1):
            load(i)
        tiles = {}

        def p1a(i):
            ops, oks = [], []
            for h in range(2):
                b, bk = self.bank("o", [2, 3, 4, 5, 6, 7])
                S.mm(self.ps[b][:, :], [(CAT[:, kc, i * 128:(i + 1) * 128], W3[h][:, kc, :]) for kc in range(8)],
                     reads=[wks[h]] + catkeys(i // 4), wkey=bk)
                ops.append(self.ps[b][:, :])
                oks.append(bk)
            tiles[i] = (ops, oks)
            c0 = SSB[:, 4 * i:4 * i + 1]
            c1 = SSB[:, 4 * i + 1:4 * i + 2]
            c2 = SSB[:, 4 * i + 2:4 * i + 3]
            k = ("ssb", i)
            S.op("act", lambda: nc.scalar.activation(out=self.sqj[:, 0:512], in_=ops[0], func=AF.Square, accum_out=c0),
                 reads=[oks[0]], writes=[(k, 0)])
            S.op("act", lambda: nc.scalar.activation(out=self.sqj[:, 512:1024], in_=ops[1], func=AF.Square,
                                                     accum_out=c1),
                 reads=[oks[1]], writes=[(k, 1)])

        def p1c(i):
            c0 = SSB[:, 4 * i:4 * i + 1]
            c1 = SSB[:, 4 * i + 1:4 * i + 2]
            c2 = SSB[:, 4 * i + 2:4 * i + 3]
            k = ("ssb", i)
            S.op("dve", lambda: nc.vector.tensor_tensor(out=c2, in0=c0, in1=c1, op=ALU.add),
                 reads=[(k, 0), (k, 1)], writes=[(k, 2)])

        def p1b(i):
            c2 = SSB[:, 4 * i + 2:4 * i + 3]
            k = ("ssb", i)
            S.op("act", lambda: nc.scalar.activation(out=c2, in_=c2, func=AF.Sqrt, scale=1.0 / D, bias=self.epsc),
                 reads=[(k, 2)], writes=[(k, 2)])
            S.op("dve", lambda: nc.vector.reciprocal(out=c2, in_=c2), reads=[(k, 2)], writes=[(k, 2)])

        def p3(i):
            r0 = seq * SEQ + i * 128
            xt = ring[i % R]
            xk = ("XT", i % R)
            ops, oks = tiles.pop(i)
            c2 = SSB[:, 4 * i + 2:4 * i + 3]
            k = ("ssb", i)
            for h in range(2):
                S.op("dve", lambda h=h: nc.vector.scalar_tensor_tensor(
                    out=T1[:, h * 512:(h + 1) * 512], in0=ops[h], scalar=c2, in1=gB[:, h * 512:(h + 1) * 512],
                    op0=ALU.mult, op1=ALU.mult), reads=[oks[h], (k, 2), gk], writes=[(("XT", 4), h)])
            S.op("dve", lambda: nc.vector.tensor_tensor(out=xt, in0=xt, in1=T1, op=ALU.add),
                 reads=[xk, (("XT", 4), 0), (("XT", 4), 1)], writes=[xk])
            S.dma("sp", dst[r0:r0 + 128, :], xt, reads=[xk], writes=[("xd", r0)])
            if i + R - 1 < 16:
                load(i + R - 1)

        p1a(0)
        p1c(0)
        p1b(0)
        p1a(1)
        p1c(1)
        for i in range(16):
            if i + 1 < 16:
                p1b(i + 1)
            if i + 2 < 16:
                p1a(i + 2)
            p3(i)
            if i + 2 < 16:
                p1c(i + 2)
        self.w_release(2)

    def conv_mixer(self, src, dst, seq):
        S, nc, A = self.S, self.nc, self.arena
        S.barrier()
        A.reset()
        self.common_tmps(A)
        gBpre, gpk = self.load_gb(A, 0)
        gBpost, gqk = self.load_gb(A, 2)
        hT = A.alloc([128, 8, SEQ], BF16)
        AB = A.alloc([128, 4, 2080], BF16)
        CV = A.alloc([128, 4, 2052], BF16)
        BG = A.alloc([128, 4, SEQ], BF16)
        DGs = [A.alloc([128, 31, 128], BF16) for _ in range(2)]
        DGB = A.alloc([128, 3, 128], BF16)
        SIG = [A.alloc([128, 512], F32) for _ in range(2)]
        VS = [A.alloc([128, 512], BF16) for _ in range(2)]
        YSQ = A.alloc([128, 4, 512], BF16)
        MEAN = A.alloc([128, 512], F32)
        MSQ = A.alloc([128, 512], F32)
        SDv = A.alloc([128, 512], F32)
        Dt = [A.alloc([128, 512], F32) for _ in range(2)]
        Zt = [A.alloc([128, 512], F32) for _ in range(2)]
        S.op("dve", lambda: nc.vector.memset(AB[:, :, 0:15], 0.0), writes=["abh0"])
        S.op("dve", lambda: nc.vector.memset(AB[:, :, 2063:2080], 0.0), writes=["abh1"])
        S.op("dve", lambda: nc.vector.memset(CV[:, :, 0:1], 0.0), writes=["cvh0"])
        S.op("dve", lambda: nc.vector.memset(CV[:, :, 2049:2052], 0.0), writes=["cvh1"])
        ring = [A.alloc([128, D], F32) for _ in range(4)]
        T1 = A.alloc([128, D], F32)
        self.mixer_prenorm(A, src, seq, gBpre, gpk, hT, ring + [T1])

        def proj(W, wk, j, tb, role, banks):
            b, bk = self.bank(role, banks)
            S.mm(self.ps[b][:, :], [(W[:, kc, j * 128:(j + 1) * 128], hT[:, kc, tb * 512:(tb + 1) * 512])
                                    for kc in range(8)], reads=[wk] + [("hT", 4 * tb + q_, "w") for q_ in range(4)], wkey=bk)
            return self.ps[b][:, :], bk

        def w3(tag):
            W, wk = self.w_get(tag)
            return W[:, 0:4096].rearrange("p (a b) -> p a b", b=512), wk

        Wv, wkv = w3("aval")
        Wg, wkg = w3("agate")
        n = 0
        for j in range(4):
            for tb in range(4):
                pv, kv = proj(Wv, wkv, j, tb, "pa", [2, 3])
                pg, kg = proj(Wg, wkg, j, tb, "pb", [4, 5])
                sg, sk = SIG[n % 2], ("sig", n % 2)
                S.op("act", lambda: nc.scalar.activation(out=sg, in_=pg, func=AF.Sigmoid), reads=[kg], writes=[sk])
                S.op("dve", lambda: nc.vector.tensor_tensor(out=AB[:, j, 15 + tb * 512:15 + (tb + 1) * 512],
                                                            in0=pv, in1=sg, op=ALU.mult),
                     reads=[kv, sk], writes=[("Ag", j, tb)])
                n += 1
        self.w_release(2)
        Wc, wkc = w3("cgate")
        Wvv, wkvv = w3("v")
        for j in range(4):
            for tb in range(4):
                pc, kc_ = proj(Wc, wkc, j, tb, "pa", [2, 3])
                pv, kv = proj(Wvv, wkvv, j, tb, "pb", [4, 5])
                vs, vk = VS[n % 2], ("vs", n % 2)
                S.op("act", lambda: nc.scalar.copy(out=vs, in_=pv), reads=[kv], writes=[vk])
                S.op("dve", lambda: nc.vector.tensor_tensor(out=CV[:, j, 1 + tb * 512:1 + (tb + 1) * 512],
                                                            in0=pc, in1=vs, op=ALU.mult),
                     reads=[kc_, vk], writes=[("CV", j, tb)])
                n += 1
        self.w_release(2)
        Wb, wkb = w3("bgate")
        for j in range(4):
            for tb in range(4):
                pb, kb = proj(Wb, wkb, j, tb, "pa", [2, 3])
                S.op("act", lambda: nc.scalar.copy(out=BG[:, j, tb * 512:(tb + 1) * 512], in_=pb),
                     reads=[kb], writes=[("BG", j, tb)])
        self.w_release(1)
        S.barrier()
        CAT = hT
        for j in range(4):
            DG = DGs[j % 2]
            for k in range(31):
                S.op("dve", lambda k=k: nc.vector.tensor_scalar(out=DG[:, k, :], in0=self.identb,
                                                                 scalar1=self.pcol(P_DWW + j * 31 + k), scalar2=None,
                                                                 op0=ALU.mult),
                     writes=[("DG", j % 2, k)])
            for tb in range(4):
                b, bk = self.bank("cv", [2, 3])
                rd = [("DG", j % 2, k) for k in range(31)] + [("Ag", j, t) for t in (tb - 1, tb, tb + 1) if 0 <= t < 4]
                rd += [("Ar", j, tb), ("Ar", j, tb + 1), "abh0", "abh1"]
                S.mm(self.ps[b][:, :], [(DG[:, k, :], AB[:, j, tb * 512 + k:tb * 512 + k + 512]) for k in range(31)],
                     reads=rd, wkey=bk)
                S.op("act", lambda: nc.scalar.activation(out=AB[:, j, tb * 512:(tb + 1) * 512], in_=self.ps[b][:, :],
                                                         func=AF.Identity, bias=self.pcol(P_DWB + j), scale=1.0),
                     reads=[bk], writes=[("Ar", j, tb)])
        for tb in range(4):
            for j in range(4):
                S.op("act", lambda j=j: nc.scalar.activation(out=YSQ[:, j, :], in_=AB[:, j, tb * 512:(tb + 1) * 512],
                                                             func=AF.Square),
                     reads=[("Ar", j, tb)], writes=[("ysq", j)])
            bm, km = self.bank("st", [6, 7])
            S.mm(self.ps[bm][:, :], [(self.onesb, AB[:, j, tb * 512:(tb + 1) * 512]) for j in range(4)],
                 reads=[("Ar", j, tb) for j in range(4)], wkey=km)
            be, ke = self.bank("st", [6, 7])
            S.mm(self.ps[be][:, :], [(self.onesb, YSQ[:, j, :]) for j in range(4)],
                 reads=[("ysq", j) for j in range(4)], wkey=ke)
            S.op("act", lambda: nc.scalar.activation(out=MEAN, in_=self.ps[bm][:, :], func=AF.Copy, scale=1.0 / 512),
                 reads=[km], writes=["mean"])
            S.op("act", lambda: nc.scalar.activation(out=MSQ, in_=self.ps[bm][:, :], func=AF.Square, scale=1.0 / 512),
                 reads=[km], writes=["msq"])
            S.op("dve", lambda: nc.vector.scalar_tensor_tensor(out=SDv, in0=self.ps[be][:, :], scalar=1.0 / 512,
                                                                in1=MSQ, op0=ALU.mult, op1=ALU.subtract),
                 reads=[ke, "msq"], writes=["sd"])
            S.op("act", lambda: nc.scalar.activation(out=SDv, in_=SDv, func=AF.Sqrt, bias=self.epsc, scale=1.0),
                 reads=["sd"], writes=["sd"])
            S.op("dve", lambda: nc.vector.reciprocal(out=SDv, in_=SDv), reads=["sd"], writes=["sd"])
            for j in range(4):
                d, dk_ = Dt[j % 2], ("dt", j % 2)
                z, zk = Zt[j % 2], ("zt", j % 2)
                S.op("dve", lambda: nc.vector.tensor_tensor(out=d, in0=AB[:, j, tb * 512:(tb + 1) * 512], in1=MEAN,
                                                            op=ALU.subtract),
                     reads=[("Ar", j, tb), "mean"], writes=[dk_])
                S.op("dve", lambda: nc.vector.tensor_tensor(out=z, in0=d, in1=SDv, op=ALU.mult),
                     reads=[dk_, "sd"], writes=[zk])
                S.op("act", lambda: nc.scalar.activation(out=CAT[:, j, tb * 512:(tb + 1) * 512], in_=z, func=AF.Silu,
                                                         scale=self.pcol(P_LNG + j), bias=self.pcol(P_LNB + j)),
                     reads=[zk], writes=[("cat", j, tb)])
        for j in range(4):
            for k in range(3):
                S.op("dve", lambda k=k: nc.vector.tensor_scalar(out=DGB[:, k, :], in0=self.identb,
                                                                 scalar1=self.pcol(P_SCW + j * 3 + k), scalar2=None,
                                                                 op0=ALU.mult),
                     writes=[("DGB", k)])
            for tb in range(4):
                b, bk = self.bank("cv", [2, 3])
                rd = [("DGB", k) for k in range(3)] + [("CV", j, t) for t in (tb - 1, tb, tb + 1) if 0 <= t < 4]
                rd += ["cvh0", "cvh1"]
                S.mm(self.ps[b][:, :], [(DGB[:, k, :], CV[:, j, tb * 512 + k:tb * 512 + k + 512]) for k in range(3)],
                     reads=rd, wkey=bk)
                S.op("dve", lambda: nc.vector.tensor_tensor(out=CAT[:, 4 + j, tb * 512:(tb + 1) * 512],
                                                            in0=self.ps[b][:, :], in1=BG[:, j, tb * 512:(tb + 1) * 512],
                                                            op=ALU.mult),
                     reads=[bk, ("BG", j, tb)], writes=[("cat", 4 + j, tb)])
        self.mixer_out(A, CAT, lambda tb: [("cat", c, tb) for c in range(8)], src, dst, seq, gBpost, gqk, ring, T1)

    def gla_mixer(self, src, dst, seq):
        S, nc, A = self.S, self.nc, self.arena
        S.barrier()
        A.reset()
        self.common_tmps(A)
        gBpre, gpk = self.load_gb(A, 1)
        gBpost, gqk = self.load_gb(A, 3)
        hT = A.alloc([128, 8, SEQ], BF16)
        OG = A.alloc([128, 8, SEQ], BF16)
        GT = [A.alloc([17, SEQ], BF16) for _ in range(2)]
        WA = [A.alloc([17, 512], BF16) for _ in range(2)]
        QT = A.alloc([128, SEQ], BF16)
        KT = A.alloc([128, SEQ], BF16)
        VTM = A.alloc([128, 16, 256], BF16)
        OF = A.alloc([128, 2, SEQ], BF16)
        Et = A.alloc([128, 512], F32)
        LP = [A.alloc([128, 512], F32) for _ in range(2)]
        EQ = [A.alloc([128, 512], F32) for _ in range(3)]
        EK = A.alloc([128, 512], F32)
        QTt = [A.alloc([128, 512], BF16) for _ in range(3)]
        KTt = [A.alloc([128, 512], BF16) for _ in range(2)]
        ST = [A.alloc([128, 512], BF16) for _ in range(2)]
        KTM = [A.alloc([128, 512], BF16) for _ in range(2)]
        DECC = A.alloc([128, 2], F32)
        SBF = [A.alloc([128, 256], BF16) for _ in range(3)]
        OS = A.alloc([128, 2, 512], F32)
        OSQ = self.sqj.rearrange("p (a b) -> p a b", b=512)
        RG = A.alloc([128, 512], F32)
        for d in range(2):
            S.dma("pool", WA[d], self.wa_d[d], writes=[("WA", d)])
            S.op("dve", lambda d=d: nc.vector.memset(GT[d], 1.0), writes=[("GT", d, t) for t in range(4)])
        XT3 = [A.alloc([128, D], F32) for _ in range(3)]
        T1 = A.alloc([128, D], F32)
        osv = OS.rearrange("p a b -> p (a b)")
        Ub = [T1[:, 0:256], T1[:, 256:512]]
        uctr = [0]
        self.mixer_prenorm(A, src, seq, gBpre, gpk, hT, XT3 + [osv, T1])
        hk = lambda tb: [("hT", 4 * tb + q_, "w") for q_ in range(4)]
        W, wk = self.w_get("gates")
        Wg3 = W[:, 0:256].rearrange("p (a b) -> p a b", b=32)
        for tb in range(4):
            for d in range(2):
                b, bk = self.bank("pa", [2, 3])
                S.mm(self.ps[b][0:16, :], [(Wg3[:, kc, d * 16:(d + 1) * 16], hT[:, kc, tb * 512:(tb + 1) * 512])
                                            for kc in range(8)], reads=[wk] + hk(tb), wkey=bk)
                S.op("act", lambda: nc.scalar.copy(out=GT[d][0:16, tb * 512:(tb + 1) * 512], in_=self.ps[b][0:16, :]),
                     reads=[bk], writes=[("GT", d, tb)])
        self.w_release(1)
        for h in range(2):
            W, wk = self.w_get(f"r{h}")
            W3 = W[:, 0:4096].rearrange("p (a b) -> p a b", b=512)
            for cc in range(4):
                for tb in range(4):
                    b, bk = self.bank("pb", [4, 5])
                    S.mm(self.ps[b][:, :], [(W3[:, kc, cc * 128:(cc + 1) * 128], hT[:, kc, tb * 512:(tb + 1) * 512])
                                            for kc in range(8)], reads=[wk] + hk(tb), wkey=bk)
                    S.op("act", lambda: nc.scalar.activation(out=OG[:, h * 4 + cc, tb * 512:(tb + 1) * 512],
                                                             in_=self.ps[b][:, :], func=AF.Silu),
                         reads=[bk], writes=[("og", h * 4 + cc, tb)])
            self.w_release(1)
        gctr = [0]
        sctr = [0]
        for h in range(4):
            W, wk = self.w_get(f"qk{h}")
            Wq = W[:, 0:1024].rearrange("p (a b) -> p a b", b=128)
            Wk = W[:, 1024:2048].rearrange("p (a b) -> p a b", b=128)
            for tb in range(4):
                for (Wx, Xt, nm, sc) in ((Wq, QT, "QT", 128.0 ** -0.5), (Wk, KT, "KT", 1.0)):
                    b, bk = self.bank("pa", [2, 3])
                    S.mm(self.ps[b][:, :], [(Wx[:, kc, :], hT[:, kc, tb * 512:(tb + 1) * 512]) for kc in range(8)],
                         reads=[wk] + hk(tb), wkey=bk)
                    S.op("act", lambda: nc.scalar.activation(out=Xt[:, tb * 512:(tb + 1) * 512], in_=self.ps[b][:, :],
                                                             func=AF.Copy, scale=sc),
                         reads=[bk], writes=[(nm, tb)])
            self.w_release(1)
            W, wk = self.w_get(f"v{h}")
            Wv = W[:, 0:2048].rearrange("p (a b) -> p a b", b=256)
            for i in range(16):
                b, bk = self.bank("pb", [4, 5])
                S.mm(self.ps[b][:, 0:256], [(hT[:, kc, i * 128:(i + 1) * 128], Wv[:, kc, :]) for kc in range(8)],
                     reads=[wk, ("hT", i, "w")], wkey=bk)
                S.op("dve", lambda: nc.vector.tensor_copy(out=VTM[:, i, :], in_=self.ps[b][:, 0:256]),
                     reads=[bk], writes=[("VTM", i)])
            self.w_release(1)

            items = [(0, g) for g in range(4)] + [(1, g) for g in range(3, -1, -1)]
            col = lambda tm: slice(tm * 128, (tm + 1) * 128)

            def stA(d, gi, n):
                b, kz = self.bank("gz", [6, 7])
                zb = self.ps[b]
                S.mm_multi([(zb[:, col(tm)], [(GT[d][0:17, (gi * 4 + tm) * 128:(gi * 4 + tm + 1) * 128],
                                               WA[d][0:17, h * 128:(h + 1) * 128])]) for tm in range(4)],
                           reads=[("GT", d, gi), ("WA", d)], wkey=kz)
                S.op("act", lambda: nc.scalar.activation(out=Et, in_=zb[:, :], func=AF.Exp, scale=-1.0),
                     reads=[kz], writes=["E"])
                S.op("act", lambda: nc.scalar.activation(out=LP[n % 2], in_=Et, func=AF.Ln, bias=1.0, scale=1.0),
                     reads=["E"], writes=[("LP", n % 2)])

            def stB(d, gi, n):
                tri = self.trif if d == 0 else self.trib
                gs = slice(gi * 512, (gi + 1) * 512)
                b2, kc_ = self.bank("gz", [6, 7])
                cb = self.ps[b2]
                S.mm_multi([(cb[:, col(tm)], [(LP[n % 2][:, col(tm)], tri)]) for tm in range(4)],
                           reads=[("LP", n % 2)], wkey=kc_)
                S.op("act", lambda: nc.scalar.activation(out=EQ[n % 3], in_=cb[:, :], func=AF.Exp),
                     reads=[kc_], writes=[("EQ", n % 3)])
                S.op("act", lambda: nc.scalar.activation(out=EK, in_=cb[:, :], func=AF.Exp, scale=-1.0),
                     reads=[kc_], writes=["EK"])
                S.op("dve", lambda: nc.vector.tensor_tensor(out=QTt[n % 3], in0=QT[:, gs], in1=EQ[n % 3], op=ALU.mult),
                     reads=[("QT", gi), ("EQ", n % 3)], writes=[("QTt", n % 3)])
                S.op("dve", lambda: nc.vector.tensor_tensor(out=KTt[n % 2], in0=KT[:, gs], in1=EK, op=ALU.mult),
                     reads=[("KT", gi), "EK"], writes=[("KTt", n % 2)])

            def stC(d, gi, n):
                mask = self.maskf if d == 0 else self.maskb
                b3, ks = self.bank("sc", [0, 1])
                sb = self.ps[b3]
                S.mm_multi([(sb[:, col(tm)], [(KTt[n % 2][:, col(tm)], QTt[n % 3][:, col(tm)])]) for tm in range(4)],
                           reads=[("KTt", n % 2), ("QTt", n % 3)], wkey=ks)
                mask_b = mask.rearrange("p (o b) -> p o b", o=1).to_broadcast([128, 4, 128])
                S.op("dve", lambda: nc.vector.tensor_tensor(out=ST[n % 2].rearrange("p (a b) -> p a b", b=128),
                                                            in0=sb[:, :].rearrange("p (a b) -> p a b", b=128),
                                                            in1=mask_b, op=ALU.mult),
                     reads=[ks], writes=[("ST", n % 2)])
                b4, kt = self.bank("sc", [0, 1])
                tb16 = self.ps[b4][:, :].bitcast(BF16)
                S.tr([(tb16[:, col(tm)], KTt[n % 2][:, col(tm)]) for tm in range(4)], self.identb,
                     reads=[("KTt", n % 2)], wkey=kt)
                S.op("act", lambda: nc.scalar.copy(out=KTM[n % 2], in_=tb16[:, 0:512]), reads=[kt],
                     writes=[("KTM", n % 2)])

            def SD(d, gi, n, first_group, first_arrival):
                order = [0, 1, 2, 3] if d == 0 else [3, 2, 1, 0]
                lastc = 127 if d == 0 else 0
                p2, p3 = n % 2, n % 3
                decs = lambda tm: EQ[p3][:, tm * 128 + lastc:tm * 128 + lastc + 1]
                kvb = {}

                def kv(tm):
                    b6, kkv = self.bank("kv", [4, 5])
                    S.mm(self.ps[b6][:, 0:256], [(KTM[p2][:, col(tm)], VTM[:, gi * 4 + tm, :])],
                         reads=[("KTM", p2), ("VTM", gi * 4 + tm)], wkey=kkv)
                    kvb[tm] = (self.ps[b6][:, 0:256], kkv)

                kv(order[0])
                kv(order[1])
                obanks = {}
                for half in range(2):
                    b5, ko = self.bank("o", [2, 3])
                    obanks[half] = (self.ps[b5], ko)
                pend = {0: [], 1: []}
                for half in range(2):
                    S._wait("pe", S._collect([], [obanks[half][1]]))
                for t, tm in enumerate(order):
                    c = gi * 4 + tm
                    has_state = not (first_group and t == 0)
                    kvps, kkv = kvb[tm]
                    ob, ko = obanks[tm // 2]
                    tm2 = tm % 2
                    groups = []
                    for vc in range(2):
                        prs = [(VTM[:, c, vc * 128:(vc + 1) * 128], ST[p2][:, col(tm)])]
                        if has_state:
                            prs.append((SBF[sctr[0] % 3][:, vc * 128:(vc + 1) * 128], QTt[p3][:, col(tm)]))
                        groups.append((ob[:, vc * 256 + tm2 * 128:vc * 256 + tm2 * 128 + 128], prs))
                    rd = [("VTM", c), ("ST", p2), ("QTt", p3)] + ([("SBF", sctr[0] % 3)] if has_state else [])
                    S.mm_multi(groups, reads=rd, wkey=(ko, "part", tm2))
                    pend[tm // 2].append((ko, "part", tm2))
                    uo, un = uctr[0] % 2, (uctr[0] + 1) % 2
                    uctr[0] += 1
                    if t == 0:
                        if has_state:
                            S.op("dve", lambda: nc.vector.scalar_tensor_tensor(
                                out=Ub[un], in0=Ub[uo], scalar=DECC[:, (n + 1) % 2:(n + 1) % 2 + 1], in1=kvps,
                                op0=ALU.mult, op1=ALU.add), reads=[("U", uo), ("DECC", (n + 1) % 2), kkv],
                                writes=[("U", un)])
                        else:
                            S.op("dve", lambda: nc.vector.tensor_copy(out=Ub[un], in_=kvps), reads=[kkv],
                                 writes=[("U", un), ("XT", 4)])
                    else:
                        S.op("dve", lambda: nc.vector.scalar_tensor_tensor(out=Ub[un], in0=Ub[uo],
                                                                            scalar=decs(order[t - 1]),
                                                                            in1=kvps, op0=ALU.mult, op1=ALU.add),
                             reads=[("U", uo), ("EQ", p3), kkv], writes=[("U", un)])
                    nxt = (sctr[0] + 1) % 3
                    S.op("act", lambda: nc.scalar.activation(out=SBF[nxt], in_=Ub[un], func=AF.Copy, scale=decs(tm)),
                         reads=[("U", un), ("EQ", p3)], writes=[("SBF", nxt)])
                    sctr[0] += 1
                    if t + 2 < 4:
                        kv(order[t + 2])
                S.op("act", lambda: nc.scalar.copy(out=DECC[:, n % 2:n % 2 + 1], in_=decs(order[3])),
                     reads=[("EQ", p3)], writes=[("DECC", n % 2)])
                gs = slice(gi * 512, (gi + 1) * 512)
                for half in range(2):
                    ob, ko = obanks[half]
                    o3 = ob[:, :].rearrange("p (a b) -> p a b", b=256)
                    cols = slice(gi * 512 + half * 256, gi * 512 + half * 256 + 256)
                    if first_arrival:
                        S.op("act", lambda: nc.scalar.copy(out=OF[:, :, cols], in_=o3), reads=pend[half],
                             writes=[("OF", gi, half), ko])
                    else:
                        S.op("dve", lambda: nc.vector.tensor_tensor(out=OS[:, :, half * 256:(half + 1) * 256], in0=o3,
                                                                    in1=OF[:, :, cols], op=ALU.add),
                             reads=pend[half] + [("OF", gi, half)], writes=[("XT", 3), ko])
                if not first_arrival:
                    S.op("act", lambda: nc.scalar.activation(out=OSQ, in_=OS, func=AF.Square),
                         reads=[("XT", 3)], writes=["OSQ"])
                    b7, kq = self.bank("gz", [6, 7])
                    qps = self.ps[b7]
                    S.mm(qps[:, :], [(self.onesb, OSQ[:, vc, :]) for vc in range(2)], reads=["OSQ"], wkey=kq)
                    S.op("act", lambda: nc.scalar.activation(out=RG, in_=qps[:, :], func=AF.Ln, scale=1.0 / 256,
                                                             bias=self.epsc),
                         reads=[kq], writes=["RG"])
                    S.op("act", lambda: nc.scalar.activation(out=RG, in_=RG, func=AF.Exp, scale=-0.5),
                         reads=["RG"], writes=["RG"])
                    rg_b = RG.rearrange("p (o b) -> p o b", o=1).to_broadcast([128, 2, 512])
                    S.op("dve", lambda: nc.vector.tensor_tensor(out=OS, in0=OS, in1=rg_b, op=ALU.mult),
                         reads=[("XT", 3), "RG"], writes=[("XT", 3)])
                    for vc in range(2):
                        S.op("dve", lambda vc=vc: nc.vector.scalar_tensor_tensor(
                            out=OG[:, h * 2 + vc, gs], in0=OS[:, vc, :], scalar=self.pcol(P_GNG + h * 2 + vc),
                            in1=OG[:, h * 2 + vc, gs], op0=ALU.mult, op1=ALU.mult),
                            reads=[("XT", 3), ("og", h * 2 + vc, gi)], writes=[("og", h * 2 + vc, gi)])

            n0 = gctr[0]
            gctr[0] += len(items)
            L = len(items)
            stA(*items[0], n0)
            stB(*items[0], n0)
            stC(*items[0], n0)
            stA(*items[1], n0 + 1)
            stB(*items[1], n0 + 1)
            stA(*items[2], n0 + 2)
            for i_ in range(L):
                if i_ + 3 < L:
                    stA(*items[i_ + 3], n0 + i_ + 3)
                if i_ + 2 < L:
                    stB(*items[i_ + 2], n0 + i_ + 2)
                if i_ + 1 < L:
                    stC(*items[i_ + 1], n0 + i_ + 1)
                d, gi = items[i_]
                SD(d, gi, n0 + i_, first_group=(i_ == 0 or i_ == 4), first_arrival=(d == 0))
        S.barrier()
        self.mixer_out(A, OG, lambda tb: [("og", c, tb) for c in range(8)], src, dst, seq, gBpost, gqk, XT3 + [osv], T1)

    def build(self):
        self.declare()
        self.setup()
        self.make_plan()
        self.w_prime()
        S = self.S
        nsub = len(self.plan)
        for seq in range(2):
            for si, sub in enumerate(self.plan):
                src = self.x_in if si == 0 else self.xs
                dst = self.y_out if si == nsub - 1 else self.xs
                if sub == "mix0":
                    self.conv_mixer(src, dst, seq)
                elif sub == "mix1":
                    self.gla_mixer(src, dst, seq)
                else:
                    self.ffn(int(sub[3]), src, dst, seq)
        S.barrier(engines=("sp",))
        self.es.close()
        return self.nc


def host_inputs(inp):
    f = lambda a: np.ascontiguousarray(np.asarray(a, dtype=np.float32))
    pm = lambda v: f(v).reshape(-1, 128).T
    prm = np.zeros((128, NPRM), np.float32)
    for l in range(2):
        prm[:, P_MIXPRE + 8 * l:P_MIXPRE + 8 * l + 8] = pm(inp["mix_pre_g"][l])
        prm[:, P_MIXPOST + 8 * l:P_MIXPOST + 8 * l + 8] = pm(inp["mix_post_g"][l])
        prm[:, P_FFNPRE + 8 * l:P_FFNPRE + 8 * l + 8] = pm(inp["ffn_pre_g"][l])
        prm[:, P_FFNPOST + 8 * l:P_FFNPOST + 8 * l + 8] = pm(inp["ffn_post_g"][l])
    dww = f(inp["cv_dw_w"][0])
    prm[:, P_DWW:P_DWW + 124] = dww.reshape(31, 4, 128).transpose(2, 1, 0).reshape(128, 124)
    prm[:, P_DWB:P_DWB + 4] = pm(inp["cv_dw_b"][0])
    prm[:, P_LNG:P_LNG + 4] = pm(inp["cv_ln_g"][0])
    prm[:, P_LNB:P_LNB + 4] = pm(inp["cv_ln_b"][0])
    scw = f(inp["cv_sc_w"][0])
    prm[:, P_SCW:P_SCW + 12] = scw.reshape(3, 4, 128).transpose(2, 1, 0).reshape(128, 12)
    prm[:, P_GNG:P_GNG + 8] = pm(inp["gla_gn_g"][0])
    j = np.arange(128)[:, None]
    i = np.arange(128)[None, :]
    cst = np.concatenate([
        np.eye(128), np.ones((128, 128)), (j <= i) * 1.0, (j >= i) * 1.0,
        (j <= i) * (-1.0 / 16.0), (j >= i) * (-1.0 / 16.0)], axis=1).astype(np.float32)
    gvec = np.stack([f(inp["mix_pre_g"][0]), f(inp["mix_pre_g"][1]), f(inp["mix_post_g"][0]),
                     f(inp["mix_post_g"][1]), f(inp["ffn_pre_g"][0]), f(inp["ffn_pre_g"][1]),
                     f(inp["ffn_post_g"][0]), f(inp["ffn_post_g"][1])], axis=0)
    wa = np.stack([np.concatenate([f(inp["gla_wa2_f"][0]), f(inp["gla_ba2_f"])[0:1]], axis=0),
                   np.concatenate([f(inp["gla_wa2_b"][0]), f(inp["gla_ba2_b"])[0:1]], axis=0)], axis=0)
    shared = {
        "prm": prm, "cst": cst, "gvec": f(gvec), "wa": f(wa),
        "cv_w_in": f(inp["cv_w_in"][0]), "cv_w_out": f(inp["cv_w_out"][0]),
        "gla_w_in": f(inp["gla_w_in"][0]), "gla_w_out": f(inp["gla_w_out"][0]),
        "ffn_w_gu": f(inp["ffn_w_gu"]), "ffn_w_down": f(inp["ffn_w_down"]),
    }
    x = f(inp["x"]).reshape(NCORES, TOK, D)
    return [dict(shared, x=x[c]) for c in range(NCORES)]


_CACHE = {}


def run(inp, plan=("mix0", "ffn0", "mix1", "ffn1")):
    plan = tuple(plan)
    if plan not in _CACHE:
        _CACHE[plan] = Builder(plan).build()
    nc = _CACHE[plan]
    in_maps = host_inputs(inp)
    res = run_bass_kernel_spmd(nc, in_maps, core_ids=list(range(NCORES)))
    out = np.stack([np.asarray(r["y"], dtype=np.float32) for r in res.results], axis=0)
    return out.reshape(16, SEQ, D)


def kernel(**inputs):
    return run(inputs)
```

```python
import math
from contextlib import ExitStack

import numpy as np
import concourse.bass as bass
import concourse.mybir as mybir
from concourse.bass_utils import run_bass_kernel_spmd
from concourse.alu_op_type import AluOpType as ALU

F32 = mybir.dt.float32
BF16 = mybir.dt.bfloat16
AF = mybir.ActivationFunctionType

NCORES = 8
D = 1024
SEQ = 2048
TOK = 4096
DFF = 2816
EPS = 1e-6
SLOT_ELEMS = 5632
NSLOT = 4
NPRM = 220

P_MIXPRE, P_MIXPOST, P_FFNPRE, P_FFNPOST = 0, 16, 32, 48
P_DWW = 64
P_DWB = 188
P_LNG = 192
P_LNB = 196
P_SCW = 200
P_GNG = 212


class Sched:
    def __init__(self, nc, es):
        self.nc = nc
        self.E = {"pe": nc.tensor, "act": nc.scalar, "dve": nc.vector, "pool": nc.gpsimd, "sp": nc.sync}
        self.psem = {}
        for e in ("pe", "act", "dve", "pool"):
            self.psem[e] = es.enter_context(nc.semaphore(f"p_{e}"))
        self.pcnt = {e: 0 for e in self.psem}
        self.waited = {e: {} for e in self.E}
        self.state = {}
        self.dsems = {}
        for q in ("sp", "pool"):
            self.dsems[q] = [[es.enter_context(nc.semaphore(f"d_{q}{i}")), 0, f"d_{q}{i}"] for i in range(10)]
        self.drr = {q: 0 for q in self.dsems}
        self.slotsem = [[es.enter_context(nc.semaphore(f"w_{i}")), 0, f"w_{i}"] for i in range(NSLOT)]

    def _collect(self, reads, writes):
        need = {}

        def add(n, s, v):
            if n not in need or need[n][1] < v:
                need[n] = (s, v)

        for k in reads:
            st = self.state.get(k)
            if st and st[0] is not None:
                add(*st[0])
        for k in writes:
            st = self.state.get(k)
            if st:
                if st[0] is not None:
                    add(*st[0])
                for n, (s, v) in st[1].items():
                    add(n, s, v)
        return need

    def _wait(self, e, need):
        for n, (s, v) in need.items():
            if self.waited[e].get(n, 0) >= v:
                continue
            self.E[e].wait_ge(s, v)
            self.waited[e][n] = v

    def _commit(self, tok, reads, writes):
        n, s, v = tok
        for k in reads:
            st = self.state.setdefault(k, [None, {}])
            st[1][n] = (s, v)
        for k in writes:
            self.state[k] = [tok, {}]

    def op(self, e, fn, reads=(), writes=()):
        need = self._collect(reads, writes)
        self._wait(e, need)
        ins = fn()
        self.pcnt[e] += 1
        ins.then_inc(self.psem[e], 1)
        tok = (f"p_{e}", self.psem[e], self.pcnt[e])
        self._commit(tok, reads, writes)
        return tok

    def mm(self, out, pairs, reads, wkey):
        need = self._collect(reads, [wkey])
        self._wait("pe", need)
        n = len(pairs)
        ins = None
        for i, (l, r) in enumerate(pairs):
            ins = self.nc.tensor.matmul(out, l, r, start=(i == 0), stop=(i == n - 1))
        self.pcnt["pe"] += 1
        ins.then_inc(self.psem["pe"], 1)
        tok = ("p_pe", self.psem["pe"], self.pcnt["pe"])
        self._commit(tok, reads, [wkey])
        return tok

    def mm_multi(self, groups, reads, wkey):
        need = self._collect(reads, [wkey])
        self._wait("pe", need)
        ins = None
        for out, pairs in groups:
            n = len(pairs)
            for i, (l, r) in enumerate(pairs):
                ins = self.nc.tensor.matmul(out, l, r, start=(i == 0), stop=(i == n - 1))
        self.pcnt["pe"] += 1
        ins.then_inc(self.psem["pe"], 1)
        tok = ("p_pe", self.psem["pe"], self.pcnt["pe"])
        self._commit(tok, reads, [wkey])
        return tok

    def tr(self, items, ident, reads, wkey):
        need = self._collect(reads, [wkey])
        self._wait("pe", need)
        ins = None
        for out, in_ in items:
            ins = self.nc.tensor.transpose(out, in_, ident)
        self.pcnt["pe"] += 1
        ins.then_inc(self.psem["pe"], 1)
        tok = ("p_pe", self.psem["pe"], self.pcnt["pe"])
        self._commit(tok, reads, [wkey])
        return tok

    def dma(self, q, out, in_, reads=(), writes=(), semrec=None, nonc=False):
        if semrec is None:
            semrec = self.dsems[q][self.drr[q]]
            self.drr[q] = (self.drr[q] + 1) % len(self.dsems[q])
            need = self._collect(reads, writes)
            if semrec[1] > 0:
                need[semrec[2]] = (semrec[0], semrec[1])
        else:
            need = self._collect(reads, writes)
        self._wait(q, need)
        if nonc:
            ins = self.E[q].dma_start(out=out, in_=in_, allow_slow_non_contiguous=True)
        else:
            ins = self.E[q].dma_start(out=out, in_=in_)
        semrec[1] += 16
        ins.then_inc(semrec[0], 16)
        tok = (semrec[2], semrec[0], semrec[1])
        self._commit(tok, reads, writes)
        return tok

    def barrier(self, engines=("pe", "act", "dve", "sp")):
        need = {}
        for e in self.psem:
            if self.pcnt[e] > 0:
                need[f"p_{e}"] = (self.psem[e], self.pcnt[e])
        for q in self.dsems:
            for rec in self.dsems[q]:
                if rec[1] > 0:
                    need[rec[2]] = (rec[0], rec[1])
        for e in engines:
            self._wait(e, need)


class Arena:
    def __init__(self, big, base_b, limit_b):
        self.big = big
        self.base = base_b
        self.limit = limit_b
        self.off = base_b

    def reset(self):
        self.off = self.base

    def alloc(self, shape, dt):
        esz = 2 if dt == BF16 else 4
        n = 1
        for s in shape[1:]:
            n *= s
        nb = (n * esz + 63) // 64 * 64
        assert self.off + nb <= self.limit, f"arena overflow {self.off + nb} > {self.limit}"
        a = self.big[0:shape[0], self.off // 2: self.off // 2 + n * esz // 2]
        self.off += nb
        if dt == F32:
            a = a.bitcast(F32)
        if len(shape) == 3:
            a = a.rearrange("p (a b) -> p a b", b=shape[2])
        elif len(shape) == 4:
            a = a.rearrange("p (a b c) -> p a b c", b=shape[2], c=shape[3])
        return a


class Builder:
    def __init__(self, plan):
        self.plan = plan
        self.nc = bass.Bass("TRN2", target_bir_lowering=False)
        self.es = ExitStack()

    def declare(self):
        nc = self.nc
        dt = lambda name, shape, kind="ExternalInput": nc.dram_tensor(name, shape, F32, kind=kind).ap()
        self.x_in = dt("x", [TOK, D])
        self.y_out = dt("y", [TOK, D], "ExternalOutput")
        self.xs = dt("xs", [TOK, D], "Internal")
        self.prm_d = dt("prm", [128, NPRM])
        self.cst_d = dt("cst", [128, 6 * 128])
        self.gvec_d = dt("gvec", [8, D])
        self.wa_d = dt("wa", [2, 17, 512])
        self.cv_w_in = dt("cv_w_in", [D, 2560])
        self.cv_w_out = dt("cv_w_out", [D, D])
        self.gla_w_in = dt("gla_w_in", [D, 3104])
        self.gla_w_out = dt("gla_w_out", [D, D])
        self.ffn_w_gu = dt("ffn_w_gu", [2, D, 2 * DFF])
        self.ffn_w_down = dt("ffn_w_down", [2, DFF, D])

    def setup(self):
        nc, es = self.nc, self.es
        self.S = Sched(nc, es)
        S = self.S
        total_b = 212800
        self.big = es.enter_context(nc.sbuf_tensor("big", [128, total_b // 2], BF16))
        self.ps = [es.enter_context(nc.psum_tensor(f"ps{i}", [128, 512], F32)) for i in range(8)]
        carve = Arena(self.big, 0, total_b)
        self.slots = [carve.alloc([128, SLOT_ELEMS], BF16) for _ in range(NSLOT)]
        self.identb = carve.alloc([128, 128], BF16)
        self.onesb = carve.alloc([128, 128], BF16)
        self.maskf = carve.alloc([128, 128], BF16)
        self.maskb = carve.alloc([128, 128], BF16)
        self.trif = carve.alloc([128, 128], F32)
        self.trib = carve.alloc([128, 128], F32)
        self.prm = carve.alloc([128, NPRM], F32)
        self.arena = Arena(self.big, carve.off, total_b)
        c = self.cst_d
        S.dma("pool", self.identb, c[:, 0:128], writes=["c0"])
        S.dma("pool", self.onesb, c[:, 128:256], writes=["c1"])
        S.dma("pool", self.maskf, c[:, 256:384], writes=["c2"])
        S.dma("pool", self.maskb, c[:, 384:512], writes=["c3"])
        S.dma("sp", self.trif, c[:, 512:640], writes=["c4"])
        S.dma("sp", self.trib, c[:, 640:768], writes=["c5"])
        S.dma("sp", self.prm, self.prm_d[:, :], writes=["c6"])
        S.barrier()
        self.wplan = []
        self.wnext_issue = 0
        self.wnext_use = 0
        self.psrr = {}

    def bank(self, role, banks):
        i = self.psrr.get(role, 0)
        self.psrr[role] = i + 1
        b = banks[i % len(banks)]
        return b, ("ps", b)

    def pcol(self, c0, n=1):
        return self.prm[:, c0:c0 + n]

    def w_issue(self, idx):
        if idx >= len(self.wplan):
            return
        S = self.S
        slot = idx % NSLOT
        rec = S.slotsem[slot]
        S._wait("pool", S._collect([], [("w", slot)]))
        for (eoff, shp, src) in self.wplan[idx]:
            n = shp[0] * shp[1]
            dst = self.slots[slot][:, eoff:eoff + n].rearrange("p (a b) -> p a b", b=shp[1])
            ins = self.nc.gpsimd.dma_start(out=dst, in_=src)
            rec[1] += 16
            ins.then_inc(rec[0], 16)
        S._commit((rec[2], rec[0], rec[1]), [], [("w", slot)])

    def w_prime(self):
        for i in range(NSLOT):
            self.w_issue(i)
        self.wnext_issue = NSLOT

    def w_get(self, tag):
        idx = self.wnext_use
        assert self.wtags[idx] == tag, (idx, self.wtags[idx], tag)
        self.wnext_use += 1
        slot = idx % NSLOT
        return self.slots[slot], ("w", slot)

    def w_release(self, n=1):
        for _ in range(n):
            self.w_issue(self.wnext_issue)
            self.wnext_issue += 1

    def add_load(self, tag, parts):
        self.wplan.append(parts)
        self.wtags.append(tag)

    @staticmethod
    def wsrc(w2d, r0, nr, c0, ncol):
        return w2d[r0:r0 + nr, c0:c0 + ncol].rearrange("(kc p) n -> p kc n", p=128)

    def make_plan(self):
        self.wtags = []
        for s in range(2):
            for sub in self.plan:
                if sub == "mix0":
                    w = self.cv_w_in
                    for nm, c0 in (("aval", 0), ("agate", 512), ("cgate", 1536), ("v", 2048), ("bgate", 1024)):
                        self.add_load(nm, [(0, (8, 512), self.wsrc(w, 0, D, c0, 512))])
                    for h in range(2):
                        self.add_load(f"wout{h}", [(0, (8, 512), self.wsrc(self.cv_w_out, 0, D, h * 512, 512))])
                elif sub == "mix1":
                    w = self.gla_w_in
                    self.add_load("gates", [(0, (8, 32), self.wsrc(w, 0, D, 3072, 32))])
                    for h in range(2):
                        self.add_load(f"r{h}", [(0, (8, 512), self.wsrc(w, 0, D, 2048 + h * 512, 512))])
                    for h in range(4):
                        self.add_load(f"qk{h}", [(0, (8, 128), self.wsrc(w, 0, D, h * 128, 128)),
                                                 (1024, (8, 128), self.wsrc(w, 0, D, 512 + h * 128, 128))])
                        self.add_load(f"v{h}", [(0, (8, 256), self.wsrc(w, 0, D, 1024 + h * 256, 256))])
                    for h in range(2):
                        self.add_load(f"wout{h}", [(0, (8, 512), self.wsrc(self.gla_w_out, 0, D, h * 512, 512))])
                elif sub in ("ffn0", "ffn1"):
                    l = int(sub[3])
                    wg = self.ffn_w_gu[l]
                    wd = self.ffn_w_down[l]
                    for hb in range(2):
                        for L in range(11):
                            self.add_load(f"gu{L}", [(0, (8, 256), self.wsrc(wg, 0, D, L * 256, 256)),
                                                     (2048, (8, 256), self.wsrc(wg, 0, D, DFF + L * 256, 256))])
                        for half in range(2):
                            for part in range(2):
                                self.add_load(f"dn{half}{part}",
                                              [(0, (11, 512), self.wsrc(wd, part * 1408, 1408, half * 512, 512))])

    def load_gb(self, A, row):
        g = A.alloc([128, D], F32)
        self.S.dma("sp", g, self.gvec_d[row:row + 1, :].to_broadcast([128, D]), writes=[("gb", row)])
        return g, ("gb", row)

    def pn_stats(self, xt, xkey, rs_col, rskey, ss_col, sskey):
        S, nc = self.S, self.nc
        S.op("act", lambda: nc.scalar.activation(out=self.sqj, in_=xt, func=AF.Square, accum_out=ss_col),
             reads=[xkey], writes=[sskey])
        S.op("act", lambda: nc.scalar.activation(out=rs_col, in_=ss_col, func=AF.Sqrt, scale=1.0 / D, bias=self.epsc),
             reads=[sskey], writes=[rskey])
        S.op("dve", lambda: nc.vector.reciprocal(out=rs_col, in_=rs_col), reads=[rskey], writes=[rskey])

    def pn_apply_a(self, xt, xkey, rs_col, rskey, gB, gkey, htm, htmkey):
        S, nc = self.S, self.nc
        S.op("dve", lambda: nc.vector.scalar_tensor_tensor(out=htm, in0=xt, scalar=rs_col, in1=gB,
                                                            op0=ALU.mult, op1=ALU.mult),
             reads=[xkey, rskey, gkey], writes=[htmkey])
        b, bkey = self.bank("pt", [0, 1])
        pv = self.ps[b][:, :].bitcast(BF16)
        S.tr([(pv[:, c * 128:(c + 1) * 128], htm[:, c * 128:(c + 1) * 128]) for c in range(8)], self.identb,
             reads=[htmkey], wkey=bkey)
        return pv, bkey

    def pn_apply_b(self, pv, bkey, dst_cols, hkey):
        S, nc = self.S, self.nc
        S.op("dve", lambda: nc.vector.tensor_copy(out=dst_cols, in_=pv.rearrange("p (c t) -> p c t", t=128)),
             reads=[bkey], writes=[hkey])

    def postnorm_tile(self, ops, okeys, xt, xkey, gB, gkey, t1, t1key, ssb, idx):
        S, nc = self.S, self.nc
        c0 = ssb[:, 4 * idx:4 * idx + 1]
        c1 = ssb[:, 4 * idx + 1:4 * idx + 2]
        c2 = ssb[:, 4 * idx + 2:4 * idx + 3]
        k = ("ssb", idx)
        S.op("act", lambda: nc.scalar.activation(out=self.sqj[:, 0:512], in_=ops[0], func=AF.Square, accum_out=c0),
             reads=[okeys[0]], writes=[(k, 0)])
        S.op("act", lambda: nc.scalar.activation(out=self.sqj[:, 512:1024], in_=ops[1], func=AF.Square, accum_out=c1),
             reads=[okeys[1]], writes=[(k, 1)])
        S.op("dve", lambda: nc.vector.tensor_tensor(out=c2, in0=c0, in1=c1, op=ALU.add),
             reads=[(k, 0), (k, 1)], writes=[(k, 2)])
        S.op("act", lambda: nc.scalar.activation(out=c2, in_=c2, func=AF.Sqrt, scale=1.0 / D, bias=self.epsc),
             reads=[(k, 2)], writes=[(k, 2)])
        S.op("dve", lambda: nc.vector.reciprocal(out=c2, in_=c2), reads=[(k, 2)], writes=[(k, 2)])
        for h in range(2):
            S.op("dve", lambda h=h: nc.vector.scalar_tensor_tensor(
                out=t1[:, h * 512:(h + 1) * 512], in0=ops[h], scalar=c2, in1=gB[:, h * 512:(h + 1) * 512],
                op0=ALU.mult, op1=ALU.mult), reads=[okeys[h], (k, 2), gkey], writes=[(t1key, h)])
        S.op("dve", lambda: nc.vector.tensor_tensor(out=xt, in0=xt, in1=t1, op=ALU.add),
             reads=[xkey, (t1key, 0), (t1key, 1)], writes=[xkey])

    def common_tmps(self, A):
        self.sqj = A.alloc([128, D], BF16)
        self.epsc = A.alloc([128, 1], F32)
        self.S.op("dve", lambda: self.nc.vector.memset(self.epsc, EPS), writes=["epsc"])
        self.S.barrier()

    def ffn(self, l, src, dst, seq):
        S, nc, A = self.S, self.nc, self.arena
        S.barrier()
        A.reset()
        self.common_tmps(A)
        gBpre, gpk = self.load_gb(A, 4 + l)
        gBpost, gqk = self.load_gb(A, 6 + l)
        XHs = [A.alloc([128, 8, D], F32) for _ in range(2)]
        hT = A.alloc([128, 8, 1024], BF16)
        M0 = A.alloc([128, 8, 512], F32)
        ACTB = A.alloc([128, 22, 1024], BF16)
        HTM = [A.alloc([128, D], BF16) for _ in range(2)]
        SG = [A.alloc([128, 512], BF16) for _ in range(2)]
        T1 = [A.alloc([128, D], F32) for _ in range(1)]
        SS = A.alloc([128, 16], F32)
        RS = A.alloc([128, 16], F32)
        SSB = A.alloc([128, 64], F32)

        def loads(hb):
            for i in range(8):
                r0 = seq * SEQ + hb * 1024 + i * 128
                S.dma("sp", XHs[hb][:, i, :], src[r0:r0 + 128, :], reads=[("xd", r0)], writes=[("XH", hb, i)])

        def stats(hb):
            for i in range(8):
                j = hb * 8 + i
                self.pn_stats(XHs[hb][:, i, :], ("XH", hb, i), RS[:, j:j + 1], ("rs", j), SS[:, j:j + 1], ("ss", j))

        pend = [None]

        def apply_tile(hb, i):
            j = hb * 8 + i
            cur = self.pn_apply_a(XHs[hb][:, i, :], ("XH", hb, i), RS[:, j:j + 1], ("rs", j), gBpre, gpk,
                                  HTM[i % 2], ("htm", i % 2))
            if pend[0] is not None:
                self.pn_apply_b(*pend[0])
            pend[0] = (cur[0], cur[1], hT[:, :, i * 128:(i + 1) * 128], ("hT", i, "w"))

        def apply_flush():
            if pend[0] is not None:
                self.pn_apply_b(*pend[0])
                pend[0] = None

        def phase_a():
            for L in range(11):
                W, wk = self.w_get(f"gu{L}")
                Wg = W[:, 0:2048].rearrange("p (a b) -> p a b", b=256)
                Wu = W[:, 2048:4096].rearrange("p (a b) -> p a b", b=256)
                for cc in range(2):
                    c = 2 * L + cc
                    for tb in range(2):
                        bg, kg = self.bank("g", [2, 3])
                        bu, ku = self.bank("u", [4, 5])
                        rhs = lambda kc: hT[:, kc, tb * 512:(tb + 1) * 512]
                        S.mm(self.ps[bg][:, :], [(Wg[:, kc, cc * 128:(cc + 1) * 128], rhs(kc)) for kc in range(8)],
                             reads=[wk] + [("hT", 4 * tb + q, "w") for q in range(4)], wkey=kg)
                        S.mm(self.ps[bu][:, :], [(Wu[:, kc, cc * 128:(cc + 1) * 128], rhs(kc)) for kc in range(8)],
                             reads=[wk] + [("hT", 4 * tb + q, "w") for q in range(4)], wkey=ku)
                        sg = SG[(2 * c + tb) % 2]
                        sgk = ("sg", (2 * c + tb) % 2)
                        S.op("act", lambda: nc.scalar.activation(out=sg, in_=self.ps[bg][:, :], func=AF.Silu),
                             reads=[kg], writes=[sgk])
                        S.op("dve", lambda: nc.vector.tensor_tensor(out=ACTB[:, c, tb * 512:(tb + 1) * 512], in0=sg,
                                                                    in1=self.ps[bu][:, :], op=ALU.mult),
                             reads=[sgk, ku], writes=[("actb", c, tb)])
                self.w_release()

        def phase_b(hb, hook):
            XH = XHs[hb]
            t0 = seq * SEQ + hb * 1024
            for half in range(2):
                Wa, wka = self.w_get(f"dn{half}0")
                Wb, wkb = self.w_get(f"dn{half}1")
                Wa3 = Wa[:, 0:5632].rearrange("p (a b) -> p a b", b=512)
                Wb3 = Wb[:, 0:5632].rearrange("p (a b) -> p a b", b=512)
                for i in range(8):
                    bf, kf = self.bank("f", [6, 7])
                    tb = i // 4
                    pairs = []
                    for kc in range(22):
                        w3 = Wa3 if kc < 11 else Wb3
                        pairs.append((ACTB[:, kc, i * 128:(i + 1) * 128], w3[:, kc % 11, :]))
                    S.mm(self.ps[bf][:, :], pairs, reads=[wka, wkb] + [("actb", kc, tb) for kc in range(22)], wkey=kf)
                    c0 = SSB[:, 4 * i + half:4 * i + half + 1]
                    k = ("ssb", i)
                    if half == 0:
                        S.op("act", lambda: nc.scalar.activation(out=self.sqj[:, 0:512], in_=self.ps[bf][:, :],
                                                                 func=AF.Square, accum_out=c0),
                             reads=[kf], writes=[(k, 0)])
                        S.op("act", lambda: nc.scalar.copy(out=M0[:, i, :], in_=self.ps[bf][:, :]),
                             reads=[kf], writes=[("m0", i)])
                        if hook is not None:
                            hook(i)
                    else:
                        c2 = SSB[:, 4 * i + 2:4 * i + 3]
                        S.op("act", lambda: nc.scalar.activation(out=self.sqj[:, 512:1024], in_=self.ps[bf][:, :],
                                                                 func=AF.Square, accum_out=c0),
                             reads=[kf], writes=[(k, 1)])
                        S.op("dve", lambda: nc.vector.tensor_tensor(out=c2, in0=SSB[:, 4 * i:4 * i + 1], in1=c0,
                                                                    op=ALU.add),
                             reads=[(k, 0), (k, 1)], writes=[(k, 2)])
                        S.op("act", lambda: nc.scalar.activation(out=c2, in_=c2, func=AF.Sqrt, scale=1.0 / D,
                                                                 bias=self.epsc),
                             reads=[(k, 2)], writes=[(k, 2)])
                        S.op("dve", lambda: nc.vector.reciprocal(out=c2, in_=c2), reads=[(k, 2)], writes=[(k, 2)])
                        t1 = T1[0]
                        tk = ("t1", 0)
                        S.op("dve", lambda: nc.vector.scalar_tensor_tensor(
                            out=t1[:, 512:1024], in0=self.ps[bf][:, :], scalar=c2, in1=gBpost[:, 512:1024],
                            op0=ALU.mult, op1=ALU.mult), reads=[kf, (k, 2), gqk], writes=[(tk, 1)])
                        S.op("dve", lambda: nc.vector.scalar_tensor_tensor(
                            out=t1[:, 0:512], in0=M0[:, i, :], scalar=c2, in1=gBpost[:, 0:512],
                            op0=ALU.mult, op1=ALU.mult), reads=[("m0", i), (k, 2), gqk], writes=[(tk, 0)])
                        S.op("dve", lambda: nc.vector.tensor_tensor(out=XH[:, i, :], in0=XH[:, i, :], in1=t1,
                                                                    op=ALU.add),
                             reads=[("XH", hb, i), (tk, 0), (tk, 1)], writes=[("XH", hb, i)])
                        r0 = t0 + i * 128
                        S.dma("sp", dst[r0:r0 + 128, :], XH[:, i, :], reads=[("XH", hb, i)], writes=[("xd", r0)])
                self.w_release(2)

        loads(0)
        loads(1)
        stats(0)
        for i in range(8):
            apply_tile(0, i)
        apply_flush()
        phase_a()
        stats(1)
        def hook(i):
            apply_tile(1, i)
            if i == 7:
                apply_flush()

        phase_b(0, hook)
        phase_a()
        phase_b(1, None)

    def mixer_prenorm(self, A, src, seq, gB, gk, hT, ring):
        S = self.S
        HTM = [A.alloc([128, D], BF16) for _ in range(2)]
        SS = A.alloc([128, 16], F32)
        RS = A.alloc([128, 16], F32)
        R = len(ring)

        def load(i):
            r0 = seq * SEQ + i * 128
            S.dma("sp", ring[i % R], src[r0:r0 + 128, :], reads=[("xd", r0)], writes=[("XT", i % R)])

        def stats(i):
            self.pn_stats(ring[i % R], ("XT", i % R), RS[:, i:i + 1], ("rs", i), SS[:, i:i + 1], ("ss", i))

        for i in range(R - 1):
            load(i)
        stats(0)
        pend = None
        for i in range(16):
            if i + 1 < 16:
                stats(i + 1)
            cur = self.pn_apply_a(ring[i % R], ("XT", i % R), RS[:, i:i + 1], ("rs", i), gB, gk, HTM[i % 2],
                                  ("htm", i % 2))
            if pend is not None:
                self.pn_apply_b(*pend)
            pend = (cur[0], cur[1], hT[:, :, i * 128:(i + 1) * 128], ("hT", i, "w"))
            if i + R - 1 < 16:
                load(i + R - 1)
        self.pn_apply_b(*pend)

    def mixer_out(self, A, CAT, catkeys, src, dst, seq, gB, gk, ring, T1):
        S, nc = self.S, self.nc
        SSB = A.alloc([128, 64], F32)
        W0, wk0 = self.w_get("wout0")
        W1, wk1 = self.w_get("wout1")
        W3 = [W0[:, 0:4096].rearrange("p (a b) -> p a b", b=512), W1[:, 0:4096].rearrange("p (a b) -> p a b", b=512)]
        wks = [wk0, wk1]
        R = len(ring)

        def load(i):
            r0 = seq * SEQ + i * 128
            S.dma("sp", ring[i % R], src[r0:r0 + 128, :], reads=[("xd", r0)], writes=[("XT", i % R)])

        for i in range(R - 1):
            load(i)
        tiles = {}

        def p1a(i):
            ops, oks = [], []
            for h in range(2):
                b, bk = self.bank("o", [2, 3, 4, 5, 6, 7])
                S.mm(self.ps[b][:, :], [(CAT[:, kc, i * 128:(i + 1) * 128], W3[h][:, kc, :]) for kc in range(8)],
                     reads=[wks[h]] + catkeys(i // 4), wkey=bk)
                ops.append(self.ps[b][:, :])
                oks.append(bk)
            tiles[i] = (ops, oks)
            c0 = SSB[:, 4 * i:4 * i + 1]
            c1 = SSB[:, 4 * i + 1:4 * i + 2]
            c2 = SSB[:, 4 * i + 2:4 * i + 3]
            k = ("ssb", i)
            S.op("act", lambda: nc.scalar.activation(out=self.sqj[:, 0:512], in_=ops[0], func=AF.Square, accum_out=c0),
                 reads=[oks[0]], writes=[(k, 0)])
            S.op("act", lambda: nc.scalar.activation(out=self.sqj[:, 512:1024], in_=ops[1], func=AF.Square,
                                                     accum_out=c1),
                 reads=[oks[1]], writes=[(k, 1)])

        def p1c(i):
            c0 = SSB[:, 4 * i:4 * i + 1]
            c1 = SSB[:, 4 * i + 1:4 * i + 2]
            c2 = SSB[:, 4 * i + 2:4 * i + 3]
            k = ("ssb", i)
            S.op("dve", lambda: nc.vector.tensor_tensor(out=c2, in0=c0, in1=c1, op=ALU.add),
                 reads=[(k, 0), (k, 1)], writes=[(k, 2)])

        def p1b(i):
            c2 = SSB[:, 4 * i + 2:4 * i + 3]
            k = ("ssb", i)
            S.op("act", lambda: nc.scalar.activation(out=c2, in_=c2, func=AF.Sqrt, scale=1.0 / D, bias=self.epsc),
                 reads=[(k, 2)], writes=[(k, 2)])
            S.op("dve", lambda: nc.vector.reciprocal(out=c2, in_=c2), reads=[(k, 2)], writes=[(k, 2)])

        def p3(i):
            r0 = seq * SEQ + i * 128
            xt = ring[i % R]
            xk = ("XT", i % R)
            ops, oks = tiles.pop(i)
            c2 = SSB[:, 4 * i + 2:4 * i + 3]
            k = ("ssb", i)
            for h in range(2):
                S.op("dve", lambda h=h: nc.vector.scalar_tensor_tensor(
                    out=T1[:, h * 512:(h + 1) * 512], in0=ops[h], scalar=c2, in1=gB[:, h * 512:(h + 1) * 512],
                    op0=ALU.mult, op1=ALU.mult), reads=[oks[h], (k, 2), gk], writes=[(("XT", 4), h)])
            S.op("dve", lambda: nc.vector.tensor_tensor(out=xt, in0=xt, in1=T1, op=ALU.add),
                 reads=[xk, (("XT", 4), 0), (("XT", 4), 1)], writes=[xk])
            S.dma("sp", dst[r0:r0 + 128, :], xt, reads=[xk], writes=[("xd", r0)])
            if i + R - 1 < 16:
                load(i + R - 1)

        p1a(0)
        p1c(0)
        p1b(0)
        p1a(1)
        p1c(1)
        for i in range(16):
            if i + 1 < 16:
                p1b(i + 1)
            if i + 2 < 16:
                p1a(i + 2)
            p3(i)
            if i + 2 < 16:
                p1c(i + 2)
        self.w_release(2)

    def conv_mixer(self, src, dst, seq):
        S, nc, A = self.S, self.nc, self.arena
        S.barrier()
        A.reset()
        self.common_tmps(A)
        gBpre, gpk = self.load_gb(A, 0)
        gBpost, gqk = self.load_gb(A, 2)
        hT = A.alloc([128, 8, SEQ], BF16)
        AB = A.alloc([128, 4, 2080], BF16)
        CV = A.alloc([128, 4, 2052], BF16)
        BG = A.alloc([128, 4, SEQ], BF16)
        DGs = [A.alloc([128, 31, 128], BF16) for _ in range(2)]
        DGB = A.alloc([128, 3, 128], BF16)
        SIG = [A.alloc([128, 512], F32) for _ in range(2)]
        VS = [A.alloc([128, 512], BF16) for _ in range(2)]
        YSQ = A.alloc([128, 4, 512], BF16)
        MEAN = A.alloc([128, 512], F32)
        MSQ = A.alloc([128, 512], F32)
        SDv = A.alloc([128, 512], F32)
        Dt = [A.alloc([128, 512], F32) for _ in range(2)]
        Zt = [A.alloc([128, 512], F32) for _ in range(2)]
        S.op("dve", lambda: nc.vector.memset(AB[:, :, 0:15], 0.0), writes=["abh0"])
        S.op("dve", lambda: nc.vector.memset(AB[:, :, 2063:2080], 0.0), writes=["abh1"])
        S.op("dve", lambda: nc.vector.memset(CV[:, :, 0:1], 0.0), writes=["cvh0"])
        S.op("dve", lambda: nc.vector.memset(CV[:, :, 2049:2052], 0.0), writes=["cvh1"])
        ring = [A.alloc([128, D], F32) for _ in range(4)]
        T1 = A.alloc([128, D], F32)
        self.mixer_prenorm(A, src, seq, gBpre, gpk, hT, ring + [T1])

        def proj(W, wk, j, tb, role, banks):
            b, bk = self.bank(role, banks)
            S.mm(self.ps[b][:, :], [(W[:, kc, j * 128:(j + 1) * 128], hT[:, kc, tb * 512:(tb + 1) * 512])
                                    for kc in range(8)], reads=[wk] + [("hT", 4 * tb + q_, "w") for q_ in range(4)], wkey=bk)
            return self.ps[b][:, :], bk

        def w3(tag):
            W, wk = self.w_get(tag)
            return W[:, 0:4096].rearrange("p (a b) -> p a b", b=512), wk

        Wv, wkv = w3("aval")
        Wg, wkg = w3("agate")
        n = 0
        for j in range(4):
            for tb in range(4):
                pv, kv = proj(Wv, wkv, j, tb, "pa", [2, 3])
                pg, kg = proj(Wg, wkg, j, tb, "pb", [4, 5])
                sg, sk = SIG[n % 2], ("sig", n % 2)
                S.op("act", lambda: nc.scalar.activation(out=sg, in_=pg, func=AF.Sigmoid), reads=[kg], writes=[sk])
                S.op("dve", lambda: nc.vector.tensor_tensor(out=AB[:, j, 15 + tb * 512:15 + (tb + 1) * 512],
                                                            in0=pv, in1=sg, op=ALU.mult),
                     reads=[kv, sk], writes=[("Ag", j, tb)])
                n += 1
        self.w_release(2)
        Wc, wkc = w3("cgate")
        Wvv, wkvv = w3("v")
        for j in range(4):
            for tb in range(4):
                pc, kc_ = proj(Wc, wkc, j, tb, "pa", [2, 3])
                pv, kv = proj(Wvv, wkvv, j, tb, "pb", [4, 5])
                vs, vk = VS[n % 2], ("vs", n % 2)
                S.op("act", lambda: nc.scalar.copy(out=vs, in_=pv), reads=[kv], writes=[vk])
                S.op("dve", lambda: nc.vector.tensor_tensor(out=CV[:, j, 1 + tb * 512:1 + (tb + 1) * 512],
                                                            in0=pc, in1=vs, op=ALU.mult),
                     reads=[kc_, vk], writes=[("CV", j, tb)])
                n += 1
        self.w_release(2)
        Wb, wkb = w3("bgate")
        for j in range(4):
            for tb in range(4):
                pb, kb = proj(Wb, wkb, j, tb, "pa", [2, 3])
                S.op("act", lambda: nc.scalar.copy(out=BG[:, j, tb * 512:(tb + 1) * 512], in_=pb),
                     reads=[kb], writes=[("BG", j, tb)])
        self.w_release(1)
        S.barrier()
        CAT = hT
        for j in range(4):
            DG = DGs[j % 2]
            for k in range(31):
                S.op("dve", lambda k=k: nc.vector.tensor_scalar(out=DG[:, k, :], in0=self.identb,
                                                                 scalar1=self.pcol(P_DWW + j * 31 + k), scalar2=None,
                                                                 op0=ALU.mult),
                     writes=[("DG", j % 2, k)])
            for tb in range(4):
                b, bk = self.bank("cv", [2, 3])
                rd = [("DG", j % 2, k) for k in range(31)] + [("Ag", j, t) for t in (tb - 1, tb, tb + 1) if 0 <= t < 4]
                rd += [("Ar", j, tb), ("Ar", j, tb + 1), "abh0", "abh1"]
                S.mm(self.ps[b][:, :], [(DG[:, k, :], AB[:, j, tb * 512 + k:tb * 512 + k + 512]) for k in range(31)],
                     reads=rd, wkey=bk)
                S.op("act", lambda: nc.scalar.activation(out=AB[:, j, tb * 512:(tb + 1) * 512], in_=self.ps[b][:, :],
                                                         func=AF.Identity, bias=self.pcol(P_DWB + j), scale=1.0),
                     reads=[bk], writes=[("Ar", j, tb)])
        for tb in range(4):
            for j in range(4):
                S.op("act", lambda j=j: nc.scalar.activation(out=YSQ[:, j, :], in_=AB[:, j, tb * 512:(tb + 1) * 512],
                                                             func=AF.Square),
                     reads=[("Ar", j, tb)], writes=[("ysq", j)])
            bm, km = self.bank("st", [6, 7])
            S.mm(self.ps[bm][:, :], [(self.onesb, AB[:, j, tb * 512:(tb + 1) * 512]) for j in range(4)],
                 reads=[("Ar", j, tb) for j in range(4)], wkey=km)
            be, ke = self.bank("st", [6, 7])
            S.mm(self.ps[be][:, :], [(self.onesb, YSQ[:, j, :]) for j in range(4)],
                 reads=[("ysq", j) for j in range(4)], wkey=ke)
            S.op("act", lambda: nc.scalar.activation(out=MEAN, in_=self.ps[bm][:, :], func=AF.Copy, scale=1.0 / 512),
                 reads=[km], writes=["mean"])
            S.op("act", lambda: nc.scalar.activation(out=MSQ, in_=self.ps[bm][:, :], func=AF.Square, scale=1.0 / 512),
                 reads=[km], writes=["msq"])
            S.op("dve", lambda: nc.vector.scalar_tensor_tensor(out=SDv, in0=self.ps[be][:, :], scalar=1.0 / 512,
                                                                in1=MSQ, op0=ALU.mult, op1=ALU.subtract),
                 reads=[ke, "msq"], writes=["sd"])
            S.op("act", lambda: nc.scalar.activation(out=SDv, in_=SDv, func=AF.Sqrt, bias=self.epsc, scale=1.0),
                 reads=["sd"], writes=["sd"])
            S.op("dve", lambda: nc.vector.reciprocal(out=SDv, in_=SDv), reads=["sd"], writes=["sd"])
            for j in range(4):
                d, dk_ = Dt[j % 2], ("dt", j % 2)
                z, zk = Zt[j % 2], ("zt", j % 2)
                S.op("dve", lambda: nc.vector.tensor_tensor(out=d, in0=AB[:, j, tb * 512:(tb + 1) * 512], in1=MEAN,
                                                            op=ALU.subtract),
                     reads=[("Ar", j, tb), "mean"], writes=[dk_])
                S.op("dve", lambda: nc.vector.tensor_tensor(out=z, in0=d, in1=SDv, op=ALU.mult),
                     reads=[dk_, "sd"], writes=[zk])
                S.op("act", lambda: nc.scalar.activation(out=CAT[:, j, tb * 512:(tb + 1) * 512], in_=z, func=AF.Silu,
                                                         scale=self.pcol(P_LNG + j), bias=self.pcol(P_LNB + j)),
                     reads=[zk], writes=[("cat", j, tb)])
        for j in range(4):
            for k in range(3):
                S.op("dve", lambda k=k: nc.vector.tensor_scalar(out=DGB[:, k, :], in0=self.identb,
                                                                 scalar1=self.pcol(P_SCW + j * 3 + k), scalar2=None,
                                                                 op0=ALU.mult),
                     writes=[("DGB", k)])
            for tb in range(4):
                b, bk = self.bank("cv", [2, 3])
                rd = [("DGB", k) for k in range(3)] + [("CV", j, t) for t in (tb - 1, tb, tb + 1) if 0 <= t < 4]
                rd += ["cvh0", "cvh1"]
                S.mm(self.ps[b][:, :], [(DGB[:, k, :], CV[:, j, tb * 512 + k:tb * 512 + k + 512]) for k in range(3)],
                     reads=rd, wkey=bk)
                S.op("dve", lambda: nc.vector.tensor_tensor(out=CAT[:, 4 + j, tb * 512:(tb + 1) * 512],
                                                            in0=self.ps[b][:, :], in1=BG[:, j, tb * 512:(tb + 1) * 512],
                                                            op=ALU.mult),
                     reads=[bk, ("BG", j, tb)], writes=[("cat", 4 + j, tb)])
        self.mixer_out(A, CAT, lambda tb: [("cat", c, tb) for c in range(8)], src, dst, seq, gBpost, gqk, ring, T1)

    def gla_mixer(self, src, dst, seq):
        S, nc, A = self.S, self.nc, self.arena
        S.barrier()
        A.reset()
        self.common_tmps(A)
        gBpre, gpk = self.load_gb(A, 1)
        gBpost, gqk = self.load_gb(A, 3)
        hT = A.alloc([128, 8, SEQ], BF16)
        OG = A.alloc([128, 8, SEQ], BF16)
        GT = [A.alloc([17, SEQ], BF16) for _ in range(2)]
        WA = [A.alloc([17, 512], BF16) for _ in range(2)]
        QT = A.alloc([128, SEQ], BF16)
        KT = A.alloc([128, SEQ], BF16)
        VTM = A.alloc([128, 16, 256], BF16)
        OF = A.alloc([128, 2, SEQ], BF16)
        Et = A.alloc([128, 512], F32)
        LP = [A.alloc([128, 512], F32) for _ in range(2)]
        EQ = [A.alloc([128, 512], F32) for _ in range(3)]
        EK = A.alloc([128, 512], F32)
        QTt = [A.alloc([128, 512], BF16) for _ in range(3)]
        KTt = [A.alloc([128, 512], BF16) for _ in range(2)]
        ST = [A.alloc([128, 512], BF16) for _ in range(2)]
        KTM = [A.alloc([128, 512], BF16) for _ in range(2)]
        DECC = A.alloc([128, 2], F32)
        SBF = [A.alloc([128, 256], BF16) for _ in range(3)]
        OS = A.alloc([128, 2, 512], F32)
        OSQ = self.sqj.rearrange("p (a b) -> p a b", b=512)
        RG = A.alloc([128, 512], F32)
        for d in range(2):
            S.dma("pool", WA[d], self.wa_d[d], writes=[("WA", d)])
            S.op("dve", lambda d=d: nc.vector.memset(GT[d], 1.0), writes=[("GT", d, t) for t in range(4)])
        XT3 = [A.alloc([128, D], F32) for _ in range(3)]
        T1 = A.alloc([128, D], F32)
        osv = OS.rearrange("p a b -> p (a b)")
        Ub = [T1[:, 0:256], T1[:, 256:512]]
        uctr = [0]
        self.mixer_prenorm(A, src, seq, gBpre, gpk, hT, XT3 + [osv, T1])
        hk = lambda tb: [("hT", 4 * tb + q_, "w") for q_ in range(4)]
        W, wk = self.w_get("gates")
        Wg3 = W[:, 0:256].rearrange("p (a b) -> p a b", b=32)
        for tb in range(4):
            for d in range(2):
                b, bk = self.bank("pa", [2, 3])
                S.mm(self.ps[b][0:16, :], [(Wg3[:, kc, d * 16:(d + 1) * 16], hT[:, kc, tb * 512:(tb + 1) * 512])
                                            for kc in range(8)], reads=[wk] + hk(tb), wkey=bk)
                S.op("act", lambda: nc.scalar.copy(out=GT[d][0:16, tb * 512:(tb + 1) * 512], in_=self.ps[b][0:16, :]),
                     reads=[bk], writes=[("GT", d, tb)])
        self.w_release(1)
        for h in range(2):
            W, wk = self.w_get(f"r{h}")
            W3 = W[:, 0:4096].rearrange("p (a b) -> p a b", b=512)
            for cc in range(4):
                for tb in range(4):
                    b, bk = self.bank("pb", [4, 5])
                    S.mm(self.ps[b][:, :], [(W3[:, kc, cc * 128:(cc + 1) * 128], hT[:, kc, tb * 512:(tb + 1) * 512])
                                            for kc in range(8)], reads=[wk] + hk(tb), wkey=bk)
                    S.op("act", lambda: nc.scalar.activation(out=OG[:, h * 4 + cc, tb * 512:(tb + 1) * 512],
                                                             in_=self.ps[b][:, :], func=AF.Silu),
                         reads=[bk], writes=[("og", h * 4 + cc, tb)])
            self.w_release(1)
        gctr = [0]
        sctr = [0]
        for h in range(4):
            W, wk = self.w_get(f"qk{h}")
            Wq = W[:, 0:1024].rearrange("p (a b) -> p a b", b=128)
            Wk = W[:, 1024:2048].rearrange("p (a b) -> p a b", b=128)
            for tb in range(4):
                for (Wx, Xt, nm, sc) in ((Wq, QT, "QT", 128.0 ** -0.5), (Wk, KT, "KT", 1.0)):
                    b, bk = self.bank("pa", [2, 3])
                    S.mm(self.ps[b][:, :], [(Wx[:, kc, :], hT[:, kc, tb * 512:(tb + 1) * 512]) for kc in range(8)],
                         reads=[wk] + hk(tb), wkey=bk)
                    S.op("act", lambda: nc.scalar.activation(out=Xt[:, tb * 512:(tb + 1) * 512], in_=self.ps[b][:, :],
                                                             func=AF.Copy, scale=sc),
                         reads=[bk], writes=[(nm, tb)])
            self.w_release(1)
            W, wk = self.w_get(f"v{h}")
            Wv = W[:, 0:2048].rearrange("p (a b) -> p a b", b=256)
            for i in range(16):
                b, bk = self.bank("pb", [4, 5])
                S.mm(self.ps[b][:, 0:256], [(hT[:, kc, i * 128:(i + 1) * 128], Wv[:, kc, :]) for kc in range(8)],
                     reads=[wk, ("hT", i, "w")], wkey=bk)
                S.op("dve", lambda: nc.vector.tensor_copy(out=VTM[:, i, :], in_=self.ps[b][:, 0:256]),
                     reads=[bk], writes=[("VTM", i)])
            self.w_release(1)

            items = [(0, g) for g in range(4)] + [(1, g) for g in range(3, -1, -1)]
            col = lambda tm: slice(tm * 128, (tm + 1) * 128)

            def stA(d, gi, n):
                b, kz = self.bank("gz", [6, 7])
                zb = self.ps[b]
                S.mm_multi([(zb[:, col(tm)], [(GT[d][0:17, (gi * 4 + tm) * 128:(gi * 4 + tm + 1) * 128],
                                               WA[d][0:17, h * 128:(h + 1) * 128])]) for tm in range(4)],
                           reads=[("GT", d, gi), ("WA", d)], wkey=kz)
                S.op("act", lambda: nc.scalar.activation(out=Et, in_=zb[:, :], func=AF.Exp, scale=-1.0),
                     reads=[kz], writes=["E"])
                S.op("act", lambda: nc.scalar.activation(out=LP[n % 2], in_=Et, func=AF.Ln, bias=1.0, scale=1.0),
                     reads=["E"], writes=[("LP", n % 2)])

            def stB(d, gi, n):
                tri = self.trif if d == 0 else self.trib
                gs = slice(gi * 512, (gi + 1) * 512)
                b2, kc_ = self.bank("gz", [6, 7])
                cb = self.ps[b2]
                S.mm_multi([(cb[:, col(tm)], [(LP[n % 2][:, col(tm)], tri)]) for tm in range(4)],
                           reads=[("LP", n % 2)], wkey=kc_)
                S.op("act", lambda: nc.scalar.activation(out=EQ[n % 3], in_=cb[:, :], func=AF.Exp),
                     reads=[kc_], writes=[("EQ", n % 3)])
                S.op("act", lambda: nc.scalar.activation(out=EK, in_=cb[:, :], func=AF.Exp, scale=-1.0),
                     reads=[kc_], writes=["EK"])
                S.op("dve", lambda: nc.vector.tensor_tensor(out=QTt[n % 3], in0=QT[:, gs], in1=EQ[n % 3], op=ALU.mult),
                     reads=[("QT", gi), ("EQ", n % 3)], writes=[("QTt", n % 3)])
                S.op("dve", lambda: nc.vector.tensor_tensor(out=KTt[n % 2], in0=KT[:, gs], in1=EK, op=ALU.mult),
                     reads=[("KT", gi), "EK"], writes=[("KTt", n % 2)])

            def stC(d, gi, n):
                mask = self.maskf if d == 0 else self.maskb
                b3, ks = self.bank("sc", [0, 1])
                sb = self.ps[b3]
                S.mm_multi([(sb[:, col(tm)], [(KTt[n % 2][:, col(tm)], QTt[n % 3][:, col(tm)])]) for tm in range(4)],
                           reads=[("KTt", n % 2), ("QTt", n % 3)], wkey=ks)
                mask_b = mask.rearrange("p (o b) -> p o b", o=1).to_broadcast([128, 4, 128])
                S.op("dve", lambda: nc.vector.tensor_tensor(out=ST[n % 2].rearrange("p (a b) -> p a b", b=128),
                                                            in0=sb[:, :].rearrange("p (a b) -> p a b", b=128),
                                                            in1=mask_b, op=ALU.mult),
                     reads=[ks], writes=[("ST", n % 2)])
                b4, kt = self.bank("sc", [0, 1])
                tb16 = self.ps[b4][:, :].bitcast(BF16)
                S.tr([(tb16[:, col(tm)], KTt[n % 2][:, col(tm)]) for tm in range(4)], self.identb,
                     reads=[("KTt", n % 2)], wkey=kt)
                S.op("act", lambda: nc.scalar.copy(out=KTM[n % 2], in_=tb16[:, 0:512]), reads=[kt],
                     writes=[("KTM", n % 2)])

            def SD(d, gi, n, first_group, first_arrival):
                order = [0, 1, 2, 3] if d == 0 else [3, 2, 1, 0]
                lastc = 127 if d == 0 else 0
                p2, p3 = n % 2, n % 3
                decs = lambda tm: EQ[p3][:, tm * 128 + lastc:tm * 128 + lastc + 1]
                kvb = {}

                def kv(tm):
                    b6, kkv = self.bank("kv", [4, 5])
                    S.mm(self.ps[b6][:, 0:256], [(KTM[p2][:, col(tm)], VTM[:, gi * 4 + tm, :])],
                         reads=[("KTM", p2), ("VTM", gi * 4 + tm)], wkey=kkv)
                    kvb[tm] = (self.ps[b6][:, 0:256], kkv)

                kv(order[0])
                kv(order[1])
                obanks = {}
                for half in range(2):
                    b5, ko = self.bank("o", [2, 3])
                    obanks[half] = (self.ps[b5], ko)
                pend = {0: [], 1: []}
                for half in range(2):
                    S._wait("pe", S._collect([], [obanks[half][1]]))
                for t, tm in enumerate(order):
                    c = gi * 4 + tm
                    has_state = not (first_group and t == 0)
                    kvps, kkv = kvb[tm]
                    ob, ko = obanks[tm // 2]
                    tm2 = tm % 2
                    groups = []
                    for vc in range(2):
                        prs = [(VTM[:, c, vc * 128:(vc + 1) * 128], ST[p2][:, col(tm)])]
                        if has_state:
                            prs.append((SBF[sctr[0] % 3][:, vc * 128:(vc + 1) * 128], QTt[p3][:, col(tm)]))
                        groups.append((ob[:, vc * 256 + tm2 * 128:vc * 256 + tm2 * 128 + 128], prs))
                    rd = [("VTM", c), ("ST", p2), ("QTt", p3)] + ([("SBF", sctr[0] % 3)] if has_state else [])
                    S.mm_multi(groups, reads=rd, wkey=(ko, "part", tm2))
                    pend[tm // 2].append((ko, "part", tm2))
                    uo, un = uctr[0] % 2, (uctr[0] + 1) % 2
                    uctr[0] += 1
                    if t == 0:
                        if has_state:
                            S.op("dve", lambda: nc.vector.scalar_tensor_tensor(
                                out=Ub[un], in0=Ub[uo], scalar=DECC[:, (n + 1) % 2:(n + 1) % 2 + 1], in1=kvps,
                                op0=ALU.mult, op1=ALU.add), reads=[("U", uo), ("DECC", (n + 1) % 2), kkv],
                                writes=[("U", un)])
                        else:
                            S.op("dve", lambda: nc.vector.tensor_copy(out=Ub[un], in_=kvps), reads=[kkv],
                                 writes=[("U", un), ("XT", 4)])
                    else:
                        S.op("dve", lambda: nc.vector.scalar_tensor_tensor(out=Ub[un], in0=Ub[uo],
                                                                            scalar=decs(order[t - 1]),
                                                                            in1=kvps, op0=ALU.mult, op1=ALU.add),
                             reads=[("U", uo), ("EQ", p3), kkv], writes=[("U", un)])
                    nxt = (sctr[0] + 1) % 3
                    S.op("act", lambda: nc.scalar.activation(out=SBF[nxt], in_=Ub[un], func=AF.Copy, scale=decs(tm)),
                         reads=[("U", un), ("EQ", p3)], writes=[("SBF", nxt)])
                    sctr[0] += 1
                    if t + 2 < 4:
                        kv(order[t + 2])
                S.op("act", lambda: nc.scalar.copy(out=DECC[:, n % 2:n % 2 + 1], in_=decs(order[3])),
                     reads=[("EQ", p3)], writes=[("DECC", n % 2)])
                gs = slice(gi * 512, (gi + 1) * 512)
                for half in range(2):
                    ob, ko = obanks[half]
                    o3 = ob[:, :].rearrange("p (a b) -> p a b", b=256)
                    cols = slice(gi * 512 + half * 256, gi * 512 + half * 256 + 256)
                    if first_arrival:
                        S.op("act", lambda: nc.scalar.copy(out=OF[:, :, cols], in_=o3), reads=pend[half],
                             writes=[("OF", gi, half), ko])
                    else:
                        S.op("dve", lambda: nc.vector.tensor_tensor(out=OS[:, :, half * 256:(half + 1) * 256], in0=o3,
                                                                    in1=OF[:, :, cols], op=ALU.add),
                             reads=pend[half] + [("OF", gi, half)], writes=[("XT", 3), ko])
                if not first_arrival:
                    S.op("act", lambda: nc.scalar.activation(out=OSQ, in_=OS, func=AF.Square),
                         reads=[("XT", 3)], writes=["OSQ"])
                    b7, kq = self.bank("gz", [6, 7])
                    qps = self.ps[b7]
                    S.mm(qps[:, :], [(self.onesb, OSQ[:, vc, :]) for vc in range(2)], reads=["OSQ"], wkey=kq)
                    S.op("act", lambda: nc.scalar.activation(out=RG, in_=qps[:, :], func=AF.Ln, scale=1.0 / 256,
                                                             bias=self.epsc),
                         reads=[kq], writes=["RG"])
                    S.op("act", lambda: nc.scalar.activation(out=RG, in_=RG, func=AF.Exp, scale=-0.5),
                         reads=["RG"], writes=["RG"])
                    rg_b = RG.rearrange("p (o b) -> p o b", o=1).to_broadcast([128, 2, 512])
                    S.op("dve", lambda: nc.vector.tensor_tensor(out=OS, in0=OS, in1=rg_b, op=ALU.mult),
                         reads=[("XT", 3), "RG"], writes=[("XT", 3)])
                    for vc in range(2):
                        S.op("dve", lambda vc=vc: nc.vector.scalar_tensor_tensor(
                            out=OG[:, h * 2 + vc, gs], in0=OS[:, vc, :], scalar=self.pcol(P_GNG + h * 2 + vc),
                            in1=OG[:, h * 2 + vc, gs], op0=ALU.mult, op1=ALU.mult),
                            reads=[("XT", 3), ("og", h * 2 + vc, gi)], writes=[("og", h * 2 + vc, gi)])

            n0 = gctr[0]
            gctr[0] += len(items)
            L = len(items)
            stA(*items[0], n0)
            stB(*items[0], n0)
            stC(*items[0], n0)
            stA(*items[1], n0 + 1)
            stB(*items[1], n0 + 1)
            stA(*items[2], n0 + 2)
            for i_ in range(L):
                if i_ + 3 < L:
                    stA(*items[i_ + 3], n0 + i_ + 3)
                if i_ + 2 < L:
                    stB(*items[i_ + 2], n0 + i_ + 2)
                if i_ + 1 < L:
                    stC(*items[i_ + 1], n0 + i_ + 1)
                d, gi = items[i_]
                SD(d, gi, n0 + i_, first_group=(i_ == 0 or i_ == 4), first_arrival=(d == 0))
        S.barrier()
        self.mixer_out(A, OG, lambda tb: [("og", c, tb) for c in range(8)], src, dst, seq, gBpost, gqk, XT3 + [osv], T1)

    def build(self):
        self.declare()
        self.setup()
        self.make_plan()
        self.w_prime()
        S = self.S
        nsub = len(self.plan)
        for seq in range(2):
            for si, sub in enumerate(self.plan):
                src = self.x_in if si == 0 else self.xs
                dst = self.y_out if si == nsub - 1 else self.xs
                if sub == "mix0":
                    self.conv_mixer(src, dst, seq)
                elif sub == "mix1":
                    self.gla_mixer(src, dst, seq)
                else:
                    self.ffn(int(sub[3]), src, dst, seq)
        S.barrier(engines=("sp",))
        self.es.close()
        return self.nc


def host_inputs(inp):
    f = lambda a: np.ascontiguousarray(np.asarray(a, dtype=np.float32))
    pm = lambda v: f(v).reshape(-1, 128).T
    prm = np.zeros((128, NPRM), np.float32)
    for l in range(2):
        prm[:, P_MIXPRE + 8 * l:P_MIXPRE + 8 * l + 8] = pm(inp["mix_pre_g"][l])
        prm[:, P_MIXPOST + 8 * l:P_MIXPOST + 8 * l + 8] = pm(inp["mix_post_g"][l])
        prm[:, P_FFNPRE + 8 * l:P_FFNPRE + 8 * l + 8] = pm(inp["ffn_pre_g"][l])
        prm[:, P_FFNPOST + 8 * l:P_FFNPOST + 8 * l + 8] = pm(inp["ffn_post_g"][l])
    dww = f(inp["cv_dw_w"][0])
    prm[:, P_DWW:P_DWW + 124] = dww.reshape(31, 4, 128).transpose(2, 1, 0).reshape(128, 124)
    prm[:, P_DWB:P_DWB + 4] = pm(inp["cv_dw_b"][0])
    prm[:, P_LNG:P_LNG + 4] = pm(inp["cv_ln_g"][0])
    prm[:, P_LNB:P_LNB + 4] = pm(inp["cv_ln_b"][0])
    scw = f(inp["cv_sc_w"][0])
    prm[:, P_SCW:P_SCW + 12] = scw.reshape(3, 4, 128).transpose(2, 1, 0).reshape(128, 12)
    prm[:, P_GNG:P_GNG + 8] = pm(inp["gla_gn_g"][0])
    j = np.arange(128)[:, None]
    i = np.arange(128)[None, :]
    cst = np.concatenate([
        np.eye(128), np.ones((128, 128)), (j <= i) * 1.0, (j >= i) * 1.0,
        (j <= i) * (-1.0 / 16.0), (j >= i) * (-1.0 / 16.0)], axis=1).astype(np.float32)
    gvec = np.stack([f(inp["mix_pre_g"][0]), f(inp["mix_pre_g"][1]), f(inp["mix_post_g"][0]),
                     f(inp["mix_post_g"][1]), f(inp["ffn_pre_g"][0]), f(inp["ffn_pre_g"][1]),
                     f(inp["ffn_post_g"][0]), f(inp["ffn_post_g"][1])], axis=0)
    wa = np.stack([np.concatenate([f(inp["gla_wa2_f"][0]), f(inp["gla_ba2_f"])[0:1]], axis=0),
                   np.concatenate([f(inp["gla_wa2_b"][0]), f(inp["gla_ba2_b"])[0:1]], axis=0)], axis=0)
    shared = {
        "prm": prm, "cst": cst, "gvec": f(gvec), "wa": f(wa),
        "cv_w_in": f(inp["cv_w_in"][0]), "cv_w_out": f(inp["cv_w_out"][0]),
        "gla_w_in": f(inp["gla_w_in"][0]), "gla_w_out": f(inp["gla_w_out"][0]),
        "ffn_w_gu": f(inp["ffn_w_gu"]), "ffn_w_down": f(inp["ffn_w_down"]),
    }
    x = f(inp["x"]).reshape(NCORES, TOK, D)
    return [dict(shared, x=x[c]) for c in range(NCORES)]


_CACHE = {}


def run(inp, plan=("mix0", "ffn0", "mix1", "ffn1")):
    plan = tuple(plan)
    if plan not in _CACHE:
        _CACHE[plan] = Builder(plan).build()
    nc = _CACHE[plan]
    in_maps = host_inputs(inp)
    res = run_bass_kernel_spmd(nc, in_maps, core_ids=list(range(NCORES)))
    out = np.stack([np.asarray(r["y"], dtype=np.float32) for r in res.results], axis=0)
    return out.reshape(16, SEQ, D)


def kernel(**inputs):
    return run(inputs)
```
